# Optimizing a Trainium2 kernel written in Bass

```python
import jax, jax.numpy as jnp
from jax import lax
import numpy as np

D_MODEL = 1024
BATCH = 2
SEQ = 16384
DEPTH = 1
DEC_BATCH = 1
DEC_SEQ = 16384
PAST_LEN = 128

N_MEM = 256
RET_HEADS = 4
RET_DK = D_MODEL // RET_HEADS
RET_DV = D_MODEL // RET_HEADS
RET_WIDTH = RET_HEADS * RET_DK
RET_CHUNK = 128
ROPE_THETA = 10000.0
FOUR_GROUPS = 4
FOUR_WIDTH = D_MODEL
FOUR_GROUP_DIM = FOUR_WIDTH // FOUR_GROUPS
CA_HEADS = 4
CA_DIM = D_MODEL // CA_HEADS
D_FF = 4 * D_MODEL
EPS = 1e-6
GN_EPS = 1e-5
IN_WIDTH = 4 * RET_WIDTH + FOUR_WIDTH + 2 * D_MODEL
IN_SPLITS = [RET_WIDTH, 2 * RET_WIDTH, 3 * RET_WIDTH, 4 * RET_WIDTH,
             4 * RET_WIDTH + FOUR_WIDTH, 4 * RET_WIDTH + FOUR_WIDTH + D_MODEL]

kernel_name = "hybrid_retention_fnet_encoder"


def rms_norm(x, w):
    x32 = x.astype(jnp.float32)
    y = x32 * lax.rsqrt(jnp.mean(x32 * x32, axis=-1, keepdims=True) + EPS)
    return (y * w.astype(jnp.float32)).astype(x.dtype)


def rotary(x):
    s, d = x.shape[2], x.shape[3]
    inv = ROPE_THETA ** (-jnp.arange(0, d, 2, dtype=jnp.float32) / d)
    ang = jnp.arange(s, dtype=jnp.float32)[:, None] * inv[None, :]
    cos, sin = jnp.cos(ang), jnp.sin(ang)
    x1, x2 = x[..., : d // 2], x[..., d // 2:]
    return jnp.concatenate([x1 * cos - x2 * sin, x2 * cos + x1 * sin], axis=-1)


def retention_scan(q, k, v, log_gamma, include_diag):
    b, h, s, dk = q.shape
    dv = v.shape[-1]
    c = RET_CHUNK
    n_chunks = s // c
    pos = jnp.arange(c, dtype=jnp.float32)
    diff = pos[:, None] - pos[None, :]
    mask = (diff >= 0) if include_diag else (diff > 0)
    lg = log_gamma[:, None, None]
    decay_in = jnp.where(mask[None], jnp.exp(lg * jnp.where(mask, diff, 0.0)[None]), 0.0)
    xi = jnp.exp(log_gamma[:, None] * (pos + 1.0)[None, :])
    zeta = jnp.exp(log_gamma[:, None] * (c - 1.0 - pos)[None, :])
    g_chunk = jnp.exp(log_gamma * c)

    def to_chunks(t):
        return t.reshape(b, h, n_chunks, c, t.shape[-1]).transpose(2, 0, 1, 3, 4)

    def step(state, inp):
        qc, kc, vc = inp
        scores = jnp.einsum('bhid,bhjd->bhij', qc, kc) * decay_in[None]
        inner = jnp.einsum('bhij,bhjv->bhiv', scores, vc)
        cross = jnp.einsum('bhid,bhdv->bhiv', qc, state) * xi[None, :, :, None]
        new_state = state * g_chunk[None, :, None, None] + jnp.einsum(
            'bhjd,bhjv->bhdv', kc * zeta[None, :, :, None], vc)
        return new_state, inner + cross

    state0 = jnp.zeros((b, h, dk, dv), jnp.float32)
    _, ys = lax.scan(step, state0, (to_chunks(q), to_chunks(k), to_chunks(v)))
    return ys.transpose(1, 2, 0, 3, 4).reshape(b, h, s, dv)


def bidirectional_retention(q, k, v, decay_fwd, decay_bwd):
    lg_f = jax.nn.log_sigmoid(decay_fwd.astype(jnp.float32))
    lg_b = jax.nn.log_sigmoid(decay_bwd.astype(jnp.float32))
    y_f = retention_scan(q, k, v, lg_f, True)
    flip = lambda t: jnp.flip(t, axis=2)
    y_b = flip(retention_scan(flip(q), flip(k), flip(v), lg_b, False))
    return y_f + y_b


def encoder_layer(x, mem, norm_mix_w, w_in, ret_decay_fwd, ret_decay_bwd, ret_gn_w,
                  w_ret_out, w_four_out, w_mix_out, norm_ca_w, norm_mem_w,
                  w_cq, w_ck, w_cv, w_co, norm_mlp_w, w_up, w_down):
    b, s, _ = x.shape
    xn = rms_norm(x, norm_mix_w)
    proj = xn @ w_in
    q, k, v, g, u, gate_r, gate_f = jnp.split(proj, IN_SPLITS, axis=-1)

    def heads(t, d):
        return t.reshape(b, s, RET_HEADS, d).transpose(0, 2, 1, 3).astype(jnp.float32)

    q = rotary(heads(q, RET_DK)) * (RET_DK ** -0.5)
    k = rotary(heads(k, RET_DK))
    v = heads(v, RET_DV)
    r = bidirectional_retention(q, k, v, ret_decay_fwd, ret_decay_bwd)
    mu = jnp.mean(r, axis=-1, keepdims=True)
    var = jnp.mean(jnp.square(r - mu), axis=-1, keepdims=True)
    r = (r - mu) * lax.rsqrt(var + GN_EPS)
    r = r.transpose(0, 2, 1, 3).reshape(b, s, RET_WIDTH) * ret_gn_w.astype(jnp.float32)
    r = (jax.nn.silu(g.astype(jnp.float32)) * r).astype(x.dtype)
    ret_branch = r @ w_ret_out

    uf = u.reshape(b, s, FOUR_GROUPS, FOUR_GROUP_DIM).astype(jnp.float32)
    uf = jnp.fft.fft2(uf, axes=(1, 3), norm="ortho").real
    four_branch = uf.reshape(b, s, FOUR_WIDTH).astype(x.dtype) @ w_four_out

    merged = jax.nn.sigmoid(gate_r) * ret_branch + jax.nn.sigmoid(gate_f) * four_branch
    x = x + merged @ w_mix_out

    hq = rms_norm(x, norm_ca_w) @ w_cq
    mn = rms_norm(mem, norm_mem_w)
    cq = hq.reshape(b, s, CA_HEADS, CA_DIM)
    ck = (mn @ w_ck).reshape(b, mem.shape[1], CA_HEADS, CA_DIM)
    cv = (mn @ w_cv).reshape(b, mem.shape[1], CA_HEADS, CA_DIM)
    logits = jnp.einsum('bshd,bmhd->bhsm', cq, ck).astype(jnp.float32) * (CA_DIM ** -0.5)
    probs = jax.nn.softmax(logits, axis=-1).astype(x.dtype)
    att = jnp.einsum('bhsm,bmhd->bshd', probs, cv).reshape(b, s, D_MODEL)
    x = x + att @ w_co

    hid = jnp.square(jax.nn.relu(rms_norm(x, norm_mlp_w) @ w_up))
    x = x + hid @ w_down
    return x


def trunk(x, mem, norm_mix_w, w_in, ret_decay_fwd, ret_decay_bwd, ret_gn_w,
          w_ret_out, w_four_out, w_mix_out, norm_ca_w, norm_mem_w,
          w_cq, w_ck, w_cv, w_co, norm_mlp_w, w_up, w_down, norm_final_w):
    for l in range(DEPTH):
        x = encoder_layer(x, mem, norm_mix_w[l], w_in[l], ret_decay_fwd[l], ret_decay_bwd[l],
                          ret_gn_w[l], w_ret_out[l], w_four_out[l], w_mix_out[l],
                          norm_ca_w[l], norm_mem_w[l], w_cq[l], w_ck[l], w_cv[l], w_co[l],
                          norm_mlp_w[l], w_up[l], w_down[l])
    return rms_norm(x, norm_final_w)


def setup_inputs(seed: int = 0) -> dict:
    key = jax.random.key(seed)
    ks = jax.random.split(key, 24)
    f32 = jnp.float32

    def w(k, shape, fan_in):
        return jax.random.normal(k, shape, f32) * (fan_in ** -0.5)

    def gain(k, shape):
        return jnp.ones(shape, f32) + 0.01 * jax.random.normal(k, shape, f32)

    base_decay = jnp.log(2.0 ** (5.0 + jnp.arange(RET_HEADS, dtype=f32)) - 1.0)
    return {
        "x_prompt": jax.random.normal(ks[0], (BATCH, SEQ, D_MODEL), f32),
        "x_sample": jax.random.normal(ks[1], (DEC_BATCH, DEC_SEQ, D_MODEL), f32),
        "mem_prompt": jax.random.normal(ks[2], (BATCH, N_MEM, D_MODEL), f32),
        "mem_sample": jax.random.normal(ks[3], (DEC_BATCH, N_MEM, D_MODEL), f32),
        "norm_mix_w": gain(ks[4], (DEPTH, D_MODEL)),
        "w_in": w(ks[5], (DEPTH, D_MODEL, IN_WIDTH), D_MODEL),
        "ret_decay_fwd": base_decay[None, :] + 0.1 * jax.random.normal(ks[6], (DEPTH, RET_HEADS), f32),
        "ret_decay_bwd": base_decay[None, :] + 0.1 * jax.random.normal(ks[7], (DEPTH, RET_HEADS), f32),
        "ret_gn_w": gain(ks[8], (DEPTH, RET_WIDTH)),
        "w_ret_out": w(ks[9], (DEPTH, RET_WIDTH, D_MODEL), RET_WIDTH),
        "w_four_out": w(ks[10], (DEPTH, FOUR_WIDTH, D_MODEL), FOUR_WIDTH),
        "w_mix_out": w(ks[11], (DEPTH, D_MODEL, D_MODEL), D_MODEL),
        "norm_ca_w": gain(ks[12], (DEPTH, D_MODEL)),
        "norm_mem_w": gain(ks[13], (DEPTH, D_MODEL)),
        "w_cq": w(ks[14], (DEPTH, D_MODEL, D_MODEL), D_MODEL),
        "w_ck": w(ks[15], (DEPTH, D_MODEL, D_MODEL), D_MODEL),
        "w_cv": w(ks[16], (DEPTH, D_MODEL, D_MODEL), D_MODEL),
        "w_co": w(ks[17], (DEPTH, D_MODEL, D_MODEL), D_MODEL),
        "norm_mlp_w": gain(ks[18], (DEPTH, D_MODEL)),
        "w_up": w(ks[19], (DEPTH, D_MODEL, D_FF), D_MODEL),
        "w_down": w(ks[20], (DEPTH, D_FF, D_MODEL), D_FF),
        "norm_final_w": gain(ks[21], (D_MODEL,)),
    }


def reference(x_prompt, x_sample, mem_prompt, mem_sample, norm_mix_w, w_in, ret_decay_fwd,
              ret_decay_bwd, ret_gn_w, w_ret_out, w_four_out, w_mix_out, norm_ca_w, norm_mem_w,
              w_cq, w_ck, w_cv, w_co, norm_mlp_w, w_up, w_down, norm_final_w):
    y_prompt = trunk(x_prompt, mem_prompt, norm_mix_w, w_in, ret_decay_fwd, ret_decay_bwd,
                     ret_gn_w, w_ret_out, w_four_out, w_mix_out, norm_ca_w, norm_mem_w,
                     w_cq, w_ck, w_cv, w_co, norm_mlp_w, w_up, w_down, norm_final_w)
    y_sample = trunk(x_sample, mem_sample, norm_mix_w, w_in, ret_decay_fwd, ret_decay_bwd,
                     ret_gn_w, w_ret_out, w_four_out, w_mix_out, norm_ca_w, norm_mem_w,
                     w_cq, w_ck, w_cv, w_co, norm_mlp_w, w_up, w_down, norm_final_w)
    return (y_prompt, y_sample)
```

```python
import contextlib
import numpy as np
import ml_dtypes
import concourse.bass as bass
import concourse.mybir as mybir
from concourse.bass_utils import run_bass_kernel_spmd

F32 = mybir.dt.float32
BF16 = mybir.dt.bfloat16
AF = mybir.ActivationFunctionType
ALU = mybir.AluOpType
AX = mybir.AxisListType

D = 1024
SEQ = 16384
NM = 64
NO = 64
INW = 7168
DFF = 4096
NMEM = 256


class Prog:
    NPOOL = 8

    def __init__(self, nc):
        self.nc = nc
        self.eng = {"pe": nc.tensor, "act": nc.scalar, "dve": nc.vector, "pool": nc.gpsimd, "sp": nc.sync}
        self.sems = {}
        self.cnt = {}
        self.known = {e: {} for e in self.eng}
        self.carry = {e: {} for e in self.eng}
        self.dma_rr = {e: 0 for e in self.eng}
        self.n_inst = 0
        self._reset()

    def _reset(self):
        self.ops = []
        self.lastw = {}
        self.lastr = {}
        self.last_on_sem = {}

    def getsem(self, name):
        if name not in self.sems:
            self.sems[name] = self.nc.alloc_semaphore(name=name)
        return self.sems[name]

    def op(self, eng, fn, reads=(), writes=(), dma=False, ndma=1):
        idx = len(self.ops)
        deps = set()
        for k in reads:
            deps.update(self.lastw.get(k, {}).values())
        for k in writes:
            deps.update(self.lastw.get(k, {}).values())
            deps.update(self.lastr.get(k, {}).values())
        semname = None
        if dma:
            semname = "d_%s_%d" % (eng, self.dma_rr[eng] % self.NPOOL)
            self.dma_rr[eng] += 1
            if semname in self.last_on_sem:
                deps.add(self.last_on_sem[semname])
            self.last_on_sem[semname] = idx
        self.ops.append(dict(eng=eng, fn=fn, deps=deps, dma=dma, semname=semname, ndma=ndma, waited=False, tok=None))
        ek = (eng, semname)
        for k in reads:
            self.lastr.setdefault(k, {})[ek] = idx
        for k in writes:
            self.lastw[k] = {ek: idx}
            self.lastr[k] = {}
        return idx

    def flush(self, final=False):
        ops = self.ops
        last_eng = {}
        for i, o in enumerate(ops):
            if not o["dma"]:
                last_eng[o["eng"]] = i
            for d in o["deps"]:
                od = ops[d]
                if (not od["dma"]) and od["eng"] == o["eng"] == "pe":
                    continue
                od["waited"] = True
        for i in last_eng.values():
            ops[i]["waited"] = True
        for o in ops:
            if o["dma"]:
                o["waited"] = True
        for o in ops:
            if not o["waited"]:
                continue
            if o["dma"]:
                nm = o["semname"]
                self.cnt[nm] = self.cnt.get(nm, 0) + 16 * o["ndma"]
            else:
                nm = "e_" + o["eng"]
                self.cnt[nm] = self.cnt.get(nm, 0) + 1
            o["tok"] = (nm, self.cnt[nm])
        for o in ops:
            e = o["eng"]
            engobj = self.eng[e]
            need = dict(self.carry[e])
            self.carry[e] = {}
            for d in o["deps"]:
                od = ops[d]
                if od["tok"] is None:
                    continue
                if (not od["dma"]) and od["eng"] == e == "pe":
                    continue
                nm, v = od["tok"]
                need[nm] = max(need.get(nm, 0), v)
            for nm, v in need.items():
                if self.known[e].get(nm, 0) >= v:
                    continue
                engobj.wait_ge(self.getsem(nm), v)
                self.known[e][nm] = v
                self.n_inst += 1
            r = o["fn"](engobj)
            self.n_inst += 1
            if o["tok"] is not None:
                nm, v = o["tok"]
                if o["dma"]:
                    insts = r if isinstance(r, (list, tuple)) else [r]
                    assert len(insts) == o["ndma"], (len(insts), o["ndma"])
                    for ins in insts:
                        ins.then_inc(self.getsem(nm), 16)
                else:
                    r.then_inc(self.getsem(nm), 1)
        allc = dict(self.cnt)
        for e in self.eng:
            self.carry[e] = dict(allc)
        if final:
            engobj = self.eng["sp"]
            for nm, v in allc.items():
                if self.known["sp"].get(nm, 0) < v:
                    engobj.wait_ge(self.getsem(nm), v)
                    self.known["sp"][nm] = v
        self._reset()


O_E0F, O_MF, O_E0B, O_MB, O_XF, O_XB = 0, 128, 256, 384, 512, 640
O_ZF, O_ZB, O_ZO = 768, 1024, 1280
O_MSKF, O_MSKB, O_EPS6, O_EPS5, O_ONE = 1536, 1537, 1538, 1539, 1540
O_NH = 1544
DKW = 1552


def build(dbg=False, phases=("p1", "p2", "p3", "p4")):
    nc = bass.Bass("TRN2", target_bir_lowering=False)

    def din(name, shape, dt=F32):
        return nc.dram_tensor(name, shape, dt, kind="ExternalInput").ap()

    def dscr(name, shape, dt):
        return nc.dram_tensor(name, shape, dt, kind="ExternalOutput" if dbg else "Internal").ap()

    xm = din("xm", [NM * 128, D])
    xo = din("xo", [NO * 128, D])
    mem = din("mem", [NMEM, D])
    w_in = din("w_in", [D, INW])
    w_ro = din("w_ro", [D, D])
    w_4 = din("w_4", [D, D])
    w_mx = din("w_mx", [D, D])
    w_cq = din("w_cq", [D, D])
    w_ck = din("w_ck", [D, D])
    w_cv = din("w_cv", [D, D])
    w_co = din("w_co", [D, D])
    w_up = din("w_up", [D, DFF])
    w_dn = din("w_dn", [DFF, D])
    nmix_d = din("nmix", [128, D])
    nca_d = din("nca", [128, D])
    nmem_d = din("nmem", [128, D])
    nmlp_d = din("nmlp", [128, D])
    nfin_d = din("nfin", [128, D])
    gnw_d = din("gnw", [128, D])
    dec_d = din("dec", [128, 12])
    dk_d = din("dk", [128, DKW])
    ident_d = din("ident", [128, 128], BF16)
    rotm_d = din("rotm", [NM * 128, 256])
    roto_d = din("roto", [NO * 128, 256])
    fcs_d = din("fcs", [128, 2, 256], BF16)
    mt_d = din("mt", [128, 128, 2, 64], BF16)
    cs_d = din("cs", [128, 2, 512], BF16)
    y_out = nc.dram_tensor("y", [NM * 128, D], F32, kind="ExternalOutput").ap()

    NT = (NM + NO) * 128
    Qd = dscr("Qd", [NM * 128, D], BF16)
    Kd = dscr("Kd", [NM * 128, D], BF16)
    Vd = dscr("Vd", [NM * 128, D], BF16)
    Gd = dscr("Gd", [NM * 128, D], BF16)
    GRd = dscr("GRd", [NM * 128, D], BF16)
    GFd = dscr("GFd", [NM * 128, D], BF16)
    KOd = dscr("KOd", [NO * 128, D], BF16)
    VOd = dscr("VOd", [NO * 128, D], BF16)
    Zd = dscr("Zd", [4, 2, 2, NT, 128], BF16)
    YFd = dscr("YFd", [NM * 128, D], F32)
    YBd = dscr("YBd", [NM * 128, D], F32)
    UFd = dscr("UFd", [D, NM * 128], BF16)
    X2d = dscr("X2d", [NM * 128, D], F32)

    P = Prog(nc)
    with contextlib.ExitStack() as gs:
        ident = gs.enter_context(nc.sbuf_tensor("ident_s", [128, 128], BF16))
        dk = gs.enter_context(nc.sbuf_tensor("dk_s", [128, DKW], F32))
        ckT = gs.enter_context(nc.sbuf_tensor("ckT", [128, 8, NMEM], BF16))
        cv = gs.enter_context(nc.sbuf_tensor("cv", [128, 2, D], BF16))
        P.op("sp", lambda e: e.dma_start(out=ident[:], in_=ident_d[:, :]), writes=["ident"], dma=True)
        P.op("sp", lambda e: e.dma_start(out=dk[:], in_=dk_d[:, :]), writes=["dk"], dma=True)
        eps6 = dk[:, O_EPS6:O_EPS6 + 1]
        eps5 = dk[:, O_EPS5:O_EPS5 + 1]
        one_ap = dk[:, O_ONE:O_ONE + 1]

        def rstd_ops(st, key, x_ap, xkey, junk, jkey):
            P.op("dve", lambda e: e.memset(st[:], 0.0), writes=[key])
            P.op("act", lambda e: e.activation(out=junk, in_=x_ap, func=AF.Square, accum_out=st[:, 0:1]),
                 reads=[xkey, key], writes=[jkey, key])
            P.op("act", lambda e: e.activation(out=st[:, 1:2], in_=st[:, 0:1], func=AF.Sqrt, scale=1.0 / D, bias=eps6),
                 reads=[key, "dk"], writes=[key])
            P.op("dve", lambda e: e.reciprocal(out=st[:, 2:3], in_=st[:, 1:2]), reads=[key], writes=[key])

        def rstd_pow(st, key, x_ap, xkey, junk, jkey):
            P.op("dve", lambda e: e.memset(st[:], 0.0), writes=[key])
            P.op("act", lambda e: e.activation(out=junk, in_=x_ap, func=AF.Square, accum_out=st[:, 0:1]),
                 reads=[xkey, key], writes=[jkey, key])
            P.op("dve", lambda e: e.tensor_scalar(out=st[:, 1:2], in0=st[:, 0:1], scalar1=1.0 / D, scalar2=eps6, op0=ALU.mult, op1=ALU.add), reads=[key, "dk"], writes=[key])
            P.op("act", lambda e: e.activation(out=st[:, 3:4], in_=st[:, 1:2], func=AF.Ln), reads=[key], writes=[key])
            P.op("act", lambda e: e.activation(out=st[:, 2:3], in_=st[:, 3:4], func=AF.Exp, scale=-0.5), reads=[key], writes=[key])

        def transposes8(src, skey, pT, pkey):
            for k in range(8):
                P.op("pe", lambda e, k=k: e.transpose(out=pT[:, k * 128:(k + 1) * 128], in_=src[:, k * 128:(k + 1) * 128], identity=ident[:]),
                     reads=[skey, "ident"], writes=[pkey])

        if "p1" in phases:
            with contextlib.ExitStack() as es:
                sb = lambda n, s, d: es.enter_context(nc.sbuf_tensor(n, s, d))
                ps = lambda n, s, d: es.enter_context(nc.psum_tensor(n, s, d))
                win = sb("win", [128, 8, INW], BF16)
                nmix = sb("nmix_s", [128, D], F32)
                css = sb("css", [128, 2, 512], BF16)
                xs = [sb("xs%d" % i, [128, D], F32) for i in range(2)]
                rt = [sb("rt%d" % i, [128, 256], F32) for i in range(3)]
                st_ = [sb("st1_%d" % i, [128, 8], F32) for i in range(2)]
                xn_ = [sb("xn1_%d" % i, [128, D], BF16) for i in range(2)]
                xnT_ = [sb("xnT1_%d" % i, [128, D], BF16) for i in range(2)]
                qkf = [sb("qkf%d" % i, [128, D], F32) for i in range(2)]
                tmp = [sb("tmp%d" % i, [128, 4, 128], F32) for i in range(4)]
                outs = {nm: [sb("o_%s%d" % (nm, i), [128, D], BF16) for i in range(2)] for nm in ("q", "k", "v")}
                outs.update({nm: [sb("o_%s0" % nm, [128, D], BF16)] for nm in ("g", "gr", "gf")})
                ub = sb("ub", [128, D], BF16)
                uT = sb("uT", [128, D], BF16)
                zt = [sb("zt%d" % i, [128, 4, 512], BF16) for i in range(2)]
                pT = ps("pT1", [128, D], BF16)
                pT2 = ps("pT1b", [128, D], BF16)
                pm = [ps("pm1_%d" % i, [128, 512], F32) for i in range(6)]
                for k in range(8):
                    P.op("pool", lambda e, k=k: e.dma_start(out=win[:, k, :], in_=w_in[k * 128:(k + 1) * 128, :]),
                         writes=["win%d" % k], dma=True)
                P.op("sp", lambda e: e.dma_start(out=nmix[:], in_=nmix_d[:, :]), writes=["nmix"], dma=True)
                P.op("sp", lambda e: e.dma_start(out=css[:], in_=cs_d[:, :, :]), writes=["css"], dma=True)
                bi = [0]

                def p1_load_x(ti):
                    mine = ti < NM
                    sl = ti % 2
                    src = xm if mine else xo
                    r0 = (ti if mine else ti - NM) * 128
                    P.op("sp", lambda e: e.dma_start(out=xs[sl][:], in_=src[r0:r0 + 128, :]), writes=["xs%d" % sl], dma=True)

                def p1_load_rt(ti):
                    mine = ti < NM
                    sl = ti % 3
                    rsrc = rotm_d if mine else roto_d
                    r0 = (ti if mine else ti - NM) * 128
                    P.op("sp", lambda e: e.dma_start(out=rt[sl][:], in_=rsrc[r0:r0 + 128, :]), writes=["rt%d" % sl], dma=True)

                def rotary(srcf, skey, dst, dkey, sl):
                    X = srcf[:].rearrange("p (h t f) -> p h t f", h=4, t=2)
                    O = dst[:].rearrange("p (h t f) -> p h t f", h=4, t=2)
                    cosb = rt[sl][:, 0:128].rearrange("p (o f) -> p o f", o=1).broadcast_to([128, 4, 128])
                    sinb = rt[sl][:, 128:256].rearrange("p (o f) -> p o f", o=1).broadcast_to([128, 4, 128])
                    rk = "rt%d" % sl
                    P.op("pool", lambda e: e.tensor_tensor(out=tmp[0][:], in0=X[:, :, 0, :], in1=cosb, op=ALU.mult), reads=[skey, rk], writes=["tmp0"])
                    P.op("dve", lambda e: e.tensor_tensor(out=tmp[1][:], in0=X[:, :, 1, :], in1=sinb, op=ALU.mult), reads=[skey, rk], writes=["tmp1"])
                    P.op("dve", lambda e: e.tensor_tensor(out=O[:, :, 0, :], in0=tmp[0][:], in1=tmp[1][:], op=ALU.subtract), reads=["tmp0", "tmp1"], writes=[dkey])
                    P.op("pool", lambda e: e.tensor_tensor(out=tmp[2][:], in0=X[:, :, 1, :], in1=cosb, op=ALU.mult), reads=[skey, rk], writes=["tmp2"])
                    P.op("dve", lambda e: e.tensor_tensor(out=tmp[3][:], in0=X[:, :, 0, :], in1=sinb, op=ALU.mult), reads=[skey, rk], writes=["tmp3"])
                    P.op("pool", lambda e: e.tensor_tensor(out=O[:, :, 1, :], in0=tmp[2][:], in1=tmp[3][:], op=ALU.add), reads=["tmp2", "tmp3", dkey], writes=[dkey])

                def p1_A_stages(ti):
                    sl = ti % 2
                    xk = "xs%d" % sl
                    st, xn, xnT = st_[sl], xn_[sl], xnT_[sl]
                    ks_, kn_, kt_ = "st1_%d" % sl, "xn1_%d" % sl, "xnT1_%d" % sl

                    def a0():
                        P.op("dve", lambda e: e.memset(st[:], 0.0), writes=[ks_])
                        P.op("act", lambda e: e.activation(out=xn[:], in_=xs[sl][:], func=AF.Square, accum_out=st[:, 0:1]), reads=[xk, ks_], writes=[kn_, ks_])

                    def a1():
                        P.op("act", lambda e: e.activation(out=st[:, 1:2], in_=st[:, 0:1], func=AF.Sqrt, scale=1.0 / D, bias=eps6), reads=[ks_, "dk"], writes=[ks_])

                    def a2():
                        P.op("dve", lambda e: e.reciprocal(out=st[:, 2:3], in_=st[:, 1:2]), reads=[ks_], writes=[ks_])

                    def a3():
                        P.op("dve", lambda e: e.scalar_tensor_tensor(out=xn[:], in0=xs[sl][:], scalar=st[:, 2:3], in1=nmix[:], op0=ALU.mult, op1=ALU.mult),
                             reads=[xk, ks_, "nmix"], writes=[kn_])

                    def a4():
                        transposes8(xn, kn_, pT, "pT1")

                    def a5():
                        P.op("dve", lambda e: e.tensor_copy(out=xnT[:], in_=pT[:]), reads=["pT1"], writes=[kt_])
                    return [a0, a1, a2, a3, a4, a5]

                def p1_tail_stages(ti):
                    zr0 = ti * 128
                    zsl = ti % 2

                    def t0():
                        transposes8(ub, "ub", pT2, "pT1b")
                        P.op("act", lambda e: e.activation(out=uT[:], in_=pT2[:], func=AF.Copy), reads=["pT1b"], writes=["uT"])

                    def t1():
                        for g in range(4):
                            b = bi[0] % 6
                            bi[0] += 1
                            bank, bkey = pm[b], "pm1_%d" % b
                            for kk in range(2):
                                P.op("pe", lambda e, g=g, kk=kk, bank=bank: e.matmul(bank[:], lhsT=uT[:, (2 * g + kk) * 128:(2 * g + kk + 1) * 128], rhs=css[:, kk, :], start=(kk == 0), stop=(kk == 1)),
                                     reads=["uT", "css"], writes=[bkey])
                            if g % 2 == 0:
                                P.op("act", lambda e, g=g, bank=bank: e.activation(out=zt[zsl][:, g, :], in_=bank[:], func=AF.Copy), reads=[bkey], writes=["zt%d" % zsl])
                            else:
                                P.op("dve", lambda e, g=g, bank=bank: e.tensor_copy(out=zt[zsl][:, g, :], in_=bank[:]), reads=[bkey], writes=["zt%d" % zsl])
                        for g in range(4):
                            P.op("sp", lambda e, g=g: e.dma_start(
                                out=Zd[g, :, :, zr0:zr0 + 128, :].rearrange("ri hf t c -> t (ri hf) c"),
                                in_=zt[zsl][:, g, :].rearrange("p (rh c) -> p rh c", rh=4)),
                                reads=["zt%d" % zsl], writes=["Zd"], dma=True)
                    return [t0, t1]

                def p1_compute(ti):
                    mine = ti < NM
                    sl = ti % 2
                    r0 = (ti if mine else ti - NM) * 128
                    zr0 = ti * 128
                    xnT = xnT_[sl]
                    kt_ = "xnT1_%d" % sl
                    slices = list(range(14)) if mine else [2, 3, 4, 5, 8, 9]
                    nxt_st = p1_A_stages(ti + 1) if ti + 1 < NM + NO else []
                    hook = ({1: 0, 2: 1, 3: 2, 5: 3, 8: 4, 10: 5} if mine else {0: 0, 1: 1, 2: 2, 3: 3, 4: 4, 5: 5})
                    prev_tail = p1_tail_stages(ti - 1) if ti > 0 else []
                    thook = ({4: 0, 7: 1} if mine else {1: 0, 3: 1})
                    for si_, n in enumerate(slices):
                        if si_ in hook and nxt_st:
                            nxt_st[hook[si_]]()
                        if si_ in thook and prev_tail:
                            prev_tail[thook[si_]]()
                        b = bi[0] % 6
                        bi[0] += 1
                        bank, bkey = pm[b], "pm1_%d" % b
                        for k in range(8):
                            P.op("pe", lambda e, k=k, n=n, bank=bank: e.matmul(bank[:], lhsT=xnT[:, k * 128:(k + 1) * 128], rhs=win[:, k, n * 512:(n + 1) * 512], start=(k == 0), stop=(k == 7)),
                                 reads=[kt_, "win%d" % k], writes=[bkey])
                        hf = n % 2
                        cs_ = slice(hf * 512, (hf + 1) * 512)
                        if n in (0, 1):
                            P.op("act", lambda e, bank=bank, cs_=cs_: e.activation(out=qkf[0][:, cs_], in_=bank[:], func=AF.Copy, scale=1.0 / 16.0), reads=[bkey], writes=["qkf0"])
                            if n == 1:
                                rotary(qkf[0], "qkf0", outs["q"][sl], "o_q%d" % sl, ti % 3)
                        elif n in (2, 3):
                            P.op("act", lambda e, bank=bank, cs_=cs_: e.activation(out=qkf[1][:, cs_], in_=bank[:], func=AF.Copy), reads=[bkey], writes=["qkf1"])
                            if n == 3:
                                rotary(qkf[1], "qkf1", outs["k"][sl], "o_k%d" % sl, ti % 3)
                        else:
                            nm_, eng = {4: ("v", "dve"), 5: ("v", "dve"), 6: ("g", "act"), 7: ("g", "act"), 8: ("u", "dve"), 9: ("u", "dve"),
                                        10: ("gr", "act"), 11: ("gr", "act"), 12: ("gf", "dve"), 13: ("gf", "dve")}[n]
                            if nm_ == "u":
                                dst, dkey = ub, "ub"
                            elif nm_ == "v":
                                dst, dkey = outs["v"][sl], "o_v%d" % sl
                            else:
                                dst, dkey = outs[nm_][0], "o_%s0" % nm_
                            if eng == "act":
                                P.op("act", lambda e, bank=bank, cs_=cs_, dst=dst: e.activation(out=dst[:, cs_], in_=bank[:], func=AF.Copy), reads=[bkey], writes=[dkey])
                            else:
                                P.op("dve", lambda e, bank=bank, cs_=cs_, dst=dst: e.tensor_copy(out=dst[:, cs_], in_=bank[:]), reads=[bkey], writes=[dkey])
                    if mine:
                        for nm_, dd in (("q", Qd), ("k", Kd), ("v", Vd)):
                            P.op("sp", lambda e, nm_=nm_, dd=dd: e.dma_start(out=dd[r0:r0 + 128, :], in_=outs[nm_][sl][:]), reads=["o_%s%d" % (nm_, sl)], writes=["dram_" + nm_], dma=True)
                        for nm_, dd in (("g", Gd), ("gr", GRd), ("gf", GFd)):
                            P.op("sp", lambda e, nm_=nm_, dd=dd: e.dma_start(out=dd[r0:r0 + 128, :], in_=outs[nm_][0][:]), reads=["o_%s0" % nm_], writes=["dram_" + nm_], dma=True)
                    else:
                        for nm_, dd in (("k", KOd), ("v", VOd)):
                            P.op("sp", lambda e, nm_=nm_, dd=dd: e.dma_start(out=dd[r0:r0 + 128, :], in_=outs[nm_][sl][:]), reads=["o_%s%d" % (nm_, sl)], writes=["dram_o" + nm_], dma=True)

                p1_load_x(0)
                p1_load_rt(0)
                p1_load_x(1)
                for f_ in p1_A_stages(0):
                    f_()
                for ti in range(NM + NO):
                    if ti + 2 < NM + NO:
                        p1_load_x(ti + 2)
                    if ti + 1 < NM + NO:
                        p1_load_rt(ti + 1)
                    p1_compute(ti)
                for f_ in p1_tail_stages(NM + NO - 1):
                    f_()
                P.flush()

        if "p2" in phases:
            with contextlib.ExitStack() as es:
                sb = lambda n, s, d: es.enter_context(nc.sbuf_tensor(n, s, d))
                ps = lambda n, s, d: es.enter_context(nc.psum_tensor(n, s, d))
                dec = sb("dec_s", [128, 12], F32)
                lg = sb("lg", [128, 12], F32)
                gch = sb("gch", [128, 12], F32)
                DT = [sb("DT%d" % i, [128, 4, 128], F32) for i in range(2)]
                xi = [sb("xi%d" % i, [128, 8, 128], F32) for i in range(2)]
                zeta = [sb("zeta%d" % i, [128, 4, 256], F32) for i in range(3)]
                tmpd = sb("tmpd", [128, 128], F32)
                S = [sb("S%d" % i, [128, 8, 256], F32) for i in range(3)]
                Sbf = [sb("Sbf%d" % i, [128, 8, 256], BF16) for i in range(2)]
                qs = [[sb("q2_%d%d" % (d_, i), [128, D], BF16) for i in range(2)] for d_ in range(2)]
                ks = [[sb("k2_%d%d" % (d_, i), [128, D], BF16) for i in range(2)] for d_ in range(2)]
                vs = [[sb("v2_%d%d" % (d_, i), [128, D], BF16) for i in range(2)] for d_ in range(2)]
                qT = [sb("qT%d" % i, [128, D], BF16) for i in range(2)]
                qxT = [sb("qxT%d" % i, [128, D], BF16) for i in range(2)]
                kT = [sb("kT%d" % i, [128, D], BF16) for i in range(2)]
                kz = [sb("kz%d" % i, [128, D], BF16) for i in range(2)]
                PT = [sb("PT%d" % i, [128, 512], BF16) for i in range(2)]
                ysb = [sb("ysb%d" % i, [128, D], F32) for i in range(2)]
                pTt = [ps("pTt%d" % i, [128, D], BF16) for i in range(2)]
                pS = [ps("pS%d" % i, [128, 512], F32) for i in range(2)]
                py = [ps("py%d" % i, [128, 512], F32) for i in range(2)]
                pst = [ps("pst%d" % i, [128, 512], F32) for i in range(2)]

                P.op("sp", lambda e: e.dma_start(out=dec[:], in_=dec_d[:, :]), writes=["dec"], dma=True)
                P.op("act", lambda e: e.activation(out=lg[:], in_=dec[:], func=AF.Exp, scale=-1.0), reads=["dec"], writes=["lg"])
                P.op("act", lambda e: e.activation(out=lg[:], in_=lg[:], func=AF.Ln, bias=one_ap, scale=1.0), reads=["lg", "dk"], writes=["lg"])
                P.op("dve", lambda e: e.tensor_scalar(out=lg[:], in0=lg[:], scalar1=-1.0, scalar2=None, op0=ALU.mult), reads=["lg"], writes=["lg"])
                P.op("act", lambda e: e.activation(out=gch[:], in_=lg[:], func=AF.Exp, scale=128.0), reads=["lg"], writes=["gch"])
                for d_ in range(2):
                    oe, om, ox = (O_E0F, O_MF, O_XF) if d_ == 0 else (O_E0B, O_MB, O_XB)
                    for hd in range(4):
                        col = 4 * d_ + hd
                        P.op("act", lambda e, oe=oe, col=col: e.activation(out=tmpd[:], in_=dk[:, oe:oe + 128], func=AF.Exp, scale=lg[:, col:col + 1]), reads=["dk", "lg"], writes=["tmpd"])
                        P.op("dve", lambda e, om=om, d_=d_, hd=hd: e.tensor_tensor(out=DT[d_][:, hd, :], in0=tmpd[:], in1=dk[:, om:om + 128], op=ALU.mult), reads=["tmpd", "dk"], writes=["DT%d" % d_])
                        for hf in range(2):
                            P.op("act", lambda e, ox=ox, col=col, d_=d_, hd=hd, hf=hf: e.activation(out=xi[d_][:, 2 * hd + hf, :], in_=dk[:, ox:ox + 128], func=AF.Exp, scale=lg[:, col:col + 1]), reads=["dk", "lg"], writes=["xi%d" % d_])
                for z_, oz in enumerate((O_ZF, O_ZB, O_ZO)):
                    for hd in range(4):
                        col = 4 * z_ + hd
                        P.op("act", lambda e, z_=z_, oz=oz, hd=hd, col=col: e.activation(out=zeta[z_][:, hd, :], in_=dk[:, oz:oz + 256], func=AF.Exp, scale=lg[:, col:col + 1]), reads=["dk", "lg"], writes=["zeta%d" % z_])
                P.op("dve", lambda e: e.memset(S[2][:], 0.0), writes=["S2"])
                import os
                P2STOP = int(os.environ.get("P2STOP", "9"))

                def state_update(si, kzt, kzkey, vt, vkey, gcol0, bset):
                    bank, bkey = pst[bset], "pst%d" % bset
                    for hd in range(4):
                        for hf in range(2):
                            P.op("pe", lambda e, hd=hd, hf=hf: e.matmul(bank[:, hf * 256:(hf + 1) * 256], lhsT=kzt[:, hd * 256 + hf * 128:hd * 256 + hf * 128 + 128], rhs=vt[:, hd * 256:(hd + 1) * 256], start=True, stop=True),
                                 reads=[kzkey, vkey], writes=[bkey])
                        sview = S[si][:, 2 * hd:2 * hd + 2, :].rearrange("p a b -> p (a b)")
                        P.op("dve", lambda e, hd=hd, sview=sview: e.scalar_tensor_tensor(out=sview, in0=sview, scalar=gch[:, gcol0 + hd:gcol0 + hd + 1], in1=bank[:], op0=ALU.mult, op1=ALU.add),
                             reads=["S%d" % si, "gch", bkey], writes=["S%d" % si])

                def p2a_load(j):
                    sl = j % 2
                    P.op("sp", lambda e: e.dma_start(out=ks[0][sl][:], in_=KOd[j * 128:(j + 1) * 128, :]), writes=["k2_0%d" % sl], dma=True)
                    P.op("sp", lambda e: e.dma_start(out=vs[0][sl][:], in_=VOd[j * 128:(j + 1) * 128, :]), writes=["v2_0%d" % sl], dma=True)
                if P2STOP >= 2:
                    p2a_load(0)
                for j in range(NO if P2STOP >= 2 else 0):
                    if j + 1 < NO:
                        p2a_load(j + 1)
                    sl = j % 2
                    bs_ = j % 2
                    P.op("pool", lambda e, sl=sl, bs_=bs_: e.tensor_tensor(out=kz[bs_][:], in0=ks[0][sl][:], in1=zeta[2][:].rearrange("p a b -> p (a b)"), op=ALU.mult), reads=["k2_0%d" % sl, "zeta2"], writes=["kz%d" % bs_])
                    state_update(2, kz[bs_], "kz%d" % bs_, vs[0][sl], "v2_0%d" % sl, 8, bs_)
                P.op("dve", lambda e: e.tensor_scalar(out=S[0][:], in0=S[2][:], scalar1=dk[:, O_MSKF:O_MSKF + 1], scalar2=None, op0=ALU.mult), reads=["S2", "dk"], writes=["S0"])
                P.op("dve", lambda e: e.tensor_scalar(out=S[1][:], in0=S[2][:], scalar1=dk[:, O_MSKB:O_MSKB + 1], scalar2=None, op0=ALU.mult), reads=["S2", "dk"], writes=["S1"])
                for d_ in range(2):
                    P.op("act", lambda e, d_=d_: e.activation(out=Sbf[d_][:], in_=S[d_][:], func=AF.Copy), reads=["S%d" % d_], writes=["Sbf%d" % d_])
                P.flush()

                def p2_load(d_, c, sl):
                    r0 = c * 128
                    P.op("sp", lambda e: e.dma_start(out=qs[d_][sl][:], in_=Qd[r0:r0 + 128, :]), writes=["q2_%d%d" % (d_, sl)], dma=True)
                    P.op("sp", lambda e: e.dma_start(out=ks[d_][sl][:], in_=Kd[r0:r0 + 128, :]), writes=["k2_%d%d" % (d_, sl)], dma=True)
                    P.op("sp", lambda e: e.dma_start(out=vs[d_][sl][:], in_=Vd[r0:r0 + 128, :]), writes=["v2_%d%d" % (d_, sl)], dma=True)

                def p2_stages(d_, c, sl):
                    q_, k_, v_ = qs[d_][sl], ks[d_][sl], vs[d_][sl]
                    qk_, kk_, vk_ = "q2_%d%d" % (d_, sl), "k2_%d%d" % (d_, sl), "v2_%d%d" % (d_, sl)
                    ds = str(d_)
                    T_, kT_ = pTt[d_], "pTt" + ds

                    def s0():
                        transposes8(q_, qk_, T_, kT_)
                        P.op("act", lambda e: e.activation(out=qT[d_][:], in_=T_[:], func=AF.Copy), reads=[kT_], writes=["qT" + ds])
                        P.op("dve", lambda e: e.tensor_tensor(out=qxT[d_][:], in0=qT[d_][:], in1=xi[d_][:].rearrange("p a b -> p (a b)"), op=ALU.mult), reads=["qT" + ds, "xi" + ds], writes=["qxT" + ds])

                    def s1():
                        transposes8(k_, kk_, T_, kT_)
                        P.op("act", lambda e: e.activation(out=kT[d_][:], in_=T_[:], func=AF.Copy), reads=[kT_], writes=["kT" + ds])
                        P.op("pool", lambda e: e.tensor_tensor(out=kz[d_][:], in0=k_[:], in1=zeta[d_][:].rearrange("p a b -> p (a b)"), op=ALU.mult), reads=[kk_, "zeta" + ds], writes=["kz" + ds])

                    def s2():
                        for hd in range(4):
                            for hf in range(2):
                                m = 2 * hd + hf
                                P.op("pe", lambda e, hd=hd, hf=hf, m=m: e.matmul(pS[d_][:, hd * 128:(hd + 1) * 128], lhsT=kT[d_][:, m * 128:(m + 1) * 128], rhs=qT[d_][:, m * 128:(m + 1) * 128], start=(hf == 0), stop=(hf == 1)),
                                     reads=["kT" + ds, "qT" + ds], writes=["pS" + ds])
                        P.op("dve", lambda e: e.tensor_tensor(out=PT[d_][:], in0=pS[d_][:], in1=DT[d_][:].rearrange("p a b -> p (a b)"), op=ALU.mult), reads=["pS" + ds, "DT" + ds], writes=["PT" + ds])

                    def ystage(half):
                        bank, bkey = py[d_], "py" + ds
                        for hd in (2 * half, 2 * half + 1):
                            co = (hd % 2) * 256
                            P.op("pe", lambda e, hd=hd, co=co: e.matmul(bank[:, co:co + 256], lhsT=PT[d_][:, hd * 128:(hd + 1) * 128], rhs=v_[:, hd * 256:(hd + 1) * 256], start=True, stop=False),
                                 reads=["PT" + ds, vk_], writes=[bkey])
                            for hf in range(2):
                                m = 2 * hd + hf
                                P.op("pe", lambda e, m=m, hf=hf, co=co: e.matmul(bank[:, co:co + 256], lhsT=qxT[d_][:, m * 128:(m + 1) * 128], rhs=Sbf[d_][:, m, :], start=False, stop=(hf == 1)),
                                     reads=["qxT" + ds, "Sbf" + ds], writes=[bkey])
                        if half == 0:
                            P.op("act", lambda e: e.activation(out=ysb[d_][:, 0:512], in_=bank[:], func=AF.Copy), reads=[bkey], writes=["ysb" + ds])
                        else:
                            P.op("dve", lambda e: e.tensor_copy(out=ysb[d_][:, 512:1024], in_=bank[:]), reads=[bkey], writes=["ysb" + ds])
                            yd = YFd if d_ == 0 else YBd
                            P.op("sp", lambda e: e.dma_start(out=yd[c * 128:(c + 1) * 128, :], in_=ysb[d_][:]), reads=["ysb" + ds], writes=["dram_y" + ds], dma=True)

                    def s5():
                        state_update(d_, kz[d_], "kz" + ds, v_, vk_, 4 * d_, d_)
                        for hh in range(2):
                            P.op("act", lambda e, hh=hh: e.activation(out=Sbf[d_][:, 4 * hh:4 * hh + 4, :], in_=S[d_][:, 4 * hh:4 * hh + 4, :], func=AF.Copy), reads=["S" + ds], writes=["Sbf" + ds])

                    return [s0, s1, s2, lambda: ystage(0), lambda: ystage(1), s5]

                if P2STOP >= 3:
                    p2_load(0, 0, 0)
                    p2_load(1, NM - 1, 0)
                for t in range(NM if P2STOP >= 3 else 0):
                    sl = t % 2
                    if t + 1 < NM:
                        p2_load(0, t + 1, 1 - sl)
                        p2_load(1, NM - 2 - t, 1 - sl)
                    sf_ = p2_stages(0, t, sl)
                    sb_ = p2_stages(1, NM - 1 - t, sl)
                    for a_, b_ in zip(sf_, sb_):
                        a_()
                        b_()
                P.flush()

        if "p3" in phases:
            with contextlib.ExitStack() as es:
                sb = lambda n, s, d: es.enter_context(nc.sbuf_tensor(n, s, d))
                ps = lambda n, s, d: es.enter_context(nc.psum_tensor(n, s, d))
                ZL = sb("ZL", [128, 2, 128, 128], BF16)
                T = sb("Tt", [128, 128, 256], BF16)
                mts = sb("mts", [128, 128, 2, 64], BF16)
                ufs = sb("ufs", [128, 64, 128], BF16)
                fcs = sb("fcs_s", [128, 2, 256], BF16)
                pb = [ps("pb%d" % i, [128, 512], F32) for i in range(4)]
                pc = [ps("pc%d" % i, [128, 512], F32) for i in range(4)]
                P.op("sp", lambda e: e.dma_start(out=fcs[:], in_=fcs_d[:, :, :]), writes=["fcs"], dma=True)
                for dq in range(4):
                    P.op("sp", lambda e, dq=dq: e.dma_start(out=mts[:, dq * 32:(dq + 1) * 32, :, :], in_=mt_d[:, dq * 32:(dq + 1) * 32, :, :]), writes=["mts"], dma=True)
                for s in range(8):
                    for ri in range(2):
                        for hb in range(2):
                            P.op("sp", lambda e, s=s, ri=ri, hb=hb: e.dma_start(
                                out=ZL[:, ri, hb * 64:(hb + 1) * 64, :],
                                in_=Zd[s // 2, ri, s % 2, :, :].rearrange("(l b) c -> l b c", b=128)[:, hb * 64:(hb + 1) * 64, :]),
                                reads=["Zd"], writes=["ZL"], dma=True)
                    for cp in range(64):
                        bank, bkey = pb[cp % 4], "pb%d" % (cp % 4)
                        for q_ in range(2):
                            ch = 2 * cp + q_
                            P.op("pe", lambda e, ch=ch, q_=q_, bank=bank: e.matmul(bank[:, q_ * 256:(q_ + 1) * 256], lhsT=ZL[:, 0, :, ch], rhs=fcs[:, 0, :], start=True, stop=False),
                                 reads=["ZL", "fcs"], writes=[bkey])
                            P.op("pe", lambda e, ch=ch, q_=q_, bank=bank: e.matmul(bank[:, q_ * 256:(q_ + 1) * 256], lhsT=ZL[:, 1, :, ch], rhs=fcs[:, 1, :], start=False, stop=True),
                                 reads=["ZL", "fcs"], writes=[bkey])
                        tv = T[:, 2 * cp:2 * cp + 2, :].rearrange("p a b -> p (a b)")
                        if cp % 2 == 0:
                            P.op("act", lambda e, bank=bank, tv=tv: e.activation(out=tv, in_=bank[:], func=AF.Copy), reads=[bkey], writes=["Tt"])
                        else:
                            P.op("dve", lambda e, bank=bank, tv=tv: e.tensor_copy(out=tv, in_=bank[:]), reads=[bkey], writes=["Tt"])
                    for db in range(16):
                        bank, bkey = pc[db % 4], "pc%d" % (db % 4)
                        for dd in range(8):
                            d_ = db * 8 + dd
                            P.op("pe", lambda e, d_=d_, dd=dd, bank=bank: e.matmul(bank[:, dd * 64:(dd + 1) * 64], lhsT=T[:, :, d_], rhs=mts[:, d_, 0, :], start=True, stop=False),
                                 reads=["Tt", "mts"], writes=[bkey])
                            P.op("pe", lambda e, d_=d_, dd=dd, bank=bank: e.matmul(bank[:, dd * 64:(dd + 1) * 64], lhsT=T[:, :, 128 + d_], rhs=mts[:, d_, 1, :], start=False, stop=True),
                                 reads=["Tt", "mts"], writes=[bkey])
                        ov = ufs[:, :, db * 8:(db + 1) * 8]
                        iv = bank[:].rearrange("p (dd c) -> p c dd", dd=8)
                        if db % 2 == 0:
                            P.op("act", lambda e, ov=ov, iv=iv: e.activation(out=ov, in_=iv, func=AF.Copy, scale=1.0 / 2048.0), reads=[bkey], writes=["ufs"])
                        else:
                            P.op("dve", lambda e, ov=ov, iv=iv: e.tensor_scalar(out=ov, in0=iv, scalar1=1.0 / 2048.0, scalar2=None, op0=ALU.mult), reads=[bkey], writes=["ufs"])
                    P.op("sp", lambda e, s=s: e.dma_start(out=UFd[s * 128:(s + 1) * 128, :], in_=ufs[:].rearrange("p c d -> p (c d)")), reads=["ufs"], writes=["UFd"], dma=True)
                P.flush()

        if "p4" in phases or "p4a" in phases:
            with contextlib.ExitStack() as es:
                sb = lambda n, s, d: es.enter_context(nc.sbuf_tensor(n, s, d))
                ps = lambda n, s, d: es.enter_context(nc.psum_tensor(n, s, d))
                wck = sb("wck", [128, 8, D], BF16)
                wcv = sb("wcv", [128, 8, D], BF16)
                nmem = sb("nmem_s", [128, D], F32)
                ms = [sb("ms%d" % i, [128, D], F32) for i in range(2)]
                st = sb("st0", [128, 8], F32)
                mn = sb("mn", [128, D], BF16)
                mnT = sb("mnT", [128, 8, NMEM], BF16)
                pT = ps("pT0", [128, D], BF16)
                pm = [ps("pm0_%d" % i, [128, 512], F32) for i in range(4)]
                for wsb, wd, nm_ in ((wck, w_ck, "wck"), (wcv, w_cv, "wcv")):
                    for kq in range(4):
                        P.op("pool", lambda e, wsb=wsb, wd=wd, kq=kq: e.dma_start(out=wsb[:, 2 * kq:2 * kq + 2, :], in_=wd[kq * 256:(kq + 1) * 256, :].rearrange("(k p) n -> p k n", p=128)), writes=[nm_], dma=True)
                P.op("sp", lambda e: e.dma_start(out=nmem[:], in_=nmem_d[:, :]), writes=["nmem"], dma=True)
                for t in range(2):
                    P.op("sp", lambda e, t=t: e.dma_start(out=ms[t][:], in_=mem[t * 128:(t + 1) * 128, :]), writes=["ms%d" % t], dma=True)
                for t in range(2):
                    rstd_ops(st, "st0", ms[t][:], "ms%d" % t, mn[:], "mn")
                    P.op("dve", lambda e, t=t: e.scalar_tensor_tensor(out=mn[:], in0=ms[t][:], scalar=st[:, 2:3], in1=nmem[:], op0=ALU.mult, op1=ALU.mult), reads=["ms%d" % t, "st0", "nmem"], writes=["mn"])
                    transposes8(mn, "mn", pT, "pT0")
                    P.op("dve", lambda e, t=t: e.tensor_copy(out=mnT[:, :, t * 128:(t + 1) * 128], in_=pT[:].rearrange("p (k c) -> p k c", k=8)), reads=["pT0"], writes=["mnT"])
                bi0 = 0
                for m in range(8):
                    bank, bkey = pm[bi0 % 4], "pm0_%d" % (bi0 % 4)
                    bi0 += 1
                    for k in range(8):
                        P.op("pe", lambda e, m=m, k=k, bank=bank: e.matmul(bank[:, 0:NMEM], lhsT=wck[:, k, m * 128:(m + 1) * 128], rhs=mnT[:, k, :], start=(k == 0), stop=(k == 7)), reads=["wck", "mnT"], writes=[bkey])
                    P.op("act", lambda e, m=m, bank=bank: e.activation(out=ckT[:, m, :], in_=bank[:, 0:NMEM], func=AF.Copy), reads=[bkey], writes=["ckT"])
                for t in range(2):
                    for n in range(2):
                        bank, bkey = pm[bi0 % 4], "pm0_%d" % (bi0 % 4)
                        bi0 += 1
                        for k in range(8):
                            P.op("pe", lambda e, t=t, n=n, k=k, bank=bank: e.matmul(bank[:], lhsT=mnT[:, k, t * 128:(t + 1) * 128], rhs=wcv[:, k, n * 512:(n + 1) * 512], start=(k == 0), stop=(k == 7)), reads=["wcv", "mnT"], writes=[bkey])
                        P.op("dve", lambda e, t=t, n=n, bank=bank: e.tensor_copy(out=cv[:, t, n * 512:(n + 1) * 512], in_=bank[:]), reads=[bkey], writes=["cv"])
                P.flush()

            with contextlib.ExitStack() as es:
                sb = lambda n, s, d: es.enter_context(nc.sbuf_tensor(n, s, d))
                ps = lambda n, s, d: es.enter_context(nc.psum_tensor(n, s, d))
                wts = {}
                for nm_, wd in (("wro", w_ro), ("w4", w_4), ("wmx", w_mx), ("wcq", w_cq), ("wco", w_co)):
                    wts[nm_] = sb(nm_, [128, 8, D], BF16)
                    for kq in range(4):
                        P.op("pool", lambda e, nm_=nm_, wd=wd, kq=kq: e.dma_start(out=wts[nm_][:, 2 * kq:2 * kq + 2, :], in_=wd[kq * 256:(kq + 1) * 256, :].rearrange("(k p) n -> p k n", p=128)), writes=[nm_], dma=True)
                gnw = sb("gnw_s", [128, D], F32)
                nca = sb("nca_s", [128, D], F32)
                P.op("sp", lambda e: e.dma_start(out=gnw[:], in_=gnw_d[:, :]), writes=["gnw"], dma=True)
                P.op("sp", lambda e: e.dma_start(out=nca[:], in_=nca_d[:, :]), writes=["nca"], dma=True)
                yf = [sb("yf%d" % i, [128, D], F32) for i in range(2)]
                yb = [sb("yb%d" % i, [128, D], F32) for i in range(2)]
                xt = [sb("xt%d" % i, [128, D], F32) for i in range(2)]
                gt = [sb("gt%d" % i, [128, D], BF16) for i in range(2)]
                grt = [sb("grt%d" % i, [128, D], BF16) for i in range(2)]
                gft = [sb("gft%d" % i, [128, D], BF16) for i in range(2)]
                uft = [sb("uft%d" % i, [128, 8, 128], BF16) for i in range(2)]
                SETS = []
                for i in range(2):
                    SETS.append(dict(
                        A=sb("A4_%d" % i, [128, D], F32), B=sb("B4_%d" % i, [128, D], F32), C=sb("C4_%d" % i, [128, D], F32),
                        x1=sb("x1_%d" % i, [128, D], F32), x2=sb("x2_%d" % i, [128, D], F32),
                        b=[sb("b4_%d_%d" % (i, j), [128, D], BF16) for j in range(4)],
                        bs=sb("bs4_%d" % i, [128, 4, 6], F32), mv=sb("mv4_%d" % i, [128, 4, 2], F32),
                        sm=sb("sm4_%d" % i, [128, 16], F32), st=sb("st4_%d" % i, [128, 8], F32),
                        T=ps("pT4_%d" % i, [128, D], BF16), X=[ps("pX%d_%d" % (j, i), [128, 512], F32) for j in range(2)],
                        Y=ps("pY_%d" % i, [128, 512], F32)))
                UFv = UFd.rearrange("(s p) t -> p s t", p=128)

                def p4a_load(c):
                    sl = c % 2
                    r0 = c * 128
                    for tl, dd, nm_ in ((yf, YFd, "yf"), (yb, YBd, "yb"), (gt, Gd, "gt")):
                        P.op("sp", lambda e, tl=tl, dd=dd: e.dma_start(out=tl[sl][:], in_=dd[r0:r0 + 128, :]), writes=["%s%d" % (nm_, sl)], dma=True)
                    P.op("sp", lambda e: e.dma_start(out=uft[sl][:], in_=UFv[:, :, r0:r0 + 128]), writes=["uft%d" % sl], dma=True)
                    for tl, dd, nm_ in ((grt, GRd, "grt"), (gft, GFd, "gft"), (xt, xm, "xt")):
                        P.op("sp", lambda e, tl=tl, dd=dd: e.dma_start(out=tl[sl][:], in_=dd[r0:r0 + 128, :]), writes=["%s%d" % (nm_, sl)], dma=True)

                def p4a_stages(c):
                    i = c % 2
                    sl = i
                    S_ = SETS[i]
                    A, B, Cc, x1, x2, bb, bs, mv, sm, st, T_, X, Y = (S_[k_] for k_ in ("A", "B", "C", "x1", "x2", "b", "bs", "mv", "sm", "st", "T", "X", "Y"))
                    s_ = str(i)
                    kA, kB, kC, kx1, kx2, kbs, kmv, ksm, kst, kT, kY = ("A4_" + s_, "B4_" + s_, "C4_" + s_, "x1_" + s_, "x2_" + s_, "bs4_" + s_, "mv4_" + s_, "sm4_" + s_, "st4_" + s_, "pT4_" + s_, "pY_" + s_)
                    kX = ["pX0_" + s_, "pX1_" + s_]
                    kb = ["b4_%s_%d" % (s_, j) for j in range(4)]
                    r0 = c * 128

                    def mm16(lhs_fn, lkey, w, wkey):
                        for n in range(2):
                            for k in range(8):
                                P.op("pe", lambda e, n=n, k=k: e.matmul(X[n][:], lhsT=lhs_fn(k), rhs=w[:, k, n * 512:(n + 1) * 512], start=(k == 0), stop=(k == 7)),
                                     reads=[lkey, wkey], writes=[kX[n]])

                    def tr(src, skey, dst, dkey, eng):
                        for k in range(8):
                            P.op("pe", lambda e, k=k: e.transpose(out=T_[:, k * 128:(k + 1) * 128], in_=src[:, k * 128:(k + 1) * 128], identity=ident[:]), reads=[skey, "ident"], writes=[kT])
                        if eng == "act":
                            P.op("act", lambda e: e.activation(out=dst[:], in_=T_[:], func=AF.Copy), reads=[kT], writes=[dkey])
                        else:
                            P.op("dve", lambda e: e.tensor_copy(out=dst[:], in_=T_[:]), reads=[kT], writes=[dkey])

                    def s_gn1():
                        P.op("dve", lambda e: e.tensor_tensor(out=A[:], in0=yf[sl][:], in1=yb[sl][:], op=ALU.add), reads=["yf" + s_, "yb" + s_], writes=[kA])
                        for hd in range(4):
                            P.op("dve", lambda e, hd=hd: e.bn_stats(out=bs[:, hd, :], in_=A[:, hd * 256:(hd + 1) * 256]), reads=[kA], writes=[kbs])
                            P.op("dve", lambda e, hd=hd: e.bn_aggr(out=mv[:, hd, :], in_=bs[:, hd, :]), reads=[kbs], writes=[kmv])
                        P.op("dve", lambda e: e.tensor_scalar(out=sm[:, 0:4], in0=mv[:, :, 1], scalar1=eps5, scalar2=None, op0=ALU.add), reads=[kmv, "dk"], writes=[ksm])
                        P.op("act", lambda e: e.activation(out=sm[:, 0:4], in_=sm[:, 0:4], func=AF.Ln), reads=[ksm], writes=[ksm])
                        P.op("act", lambda e: e.activation(out=sm[:, 4:8], in_=sm[:, 0:4], func=AF.Exp, scale=-0.5), reads=[ksm], writes=[ksm])
                        P.op("act", lambda e: e.activation(out=Cc[:], in_=gt[sl][:], func=AF.Sigmoid), reads=["gt" + s_], writes=[kC])
                        P.op("dve", lambda e: e.tensor_tensor(out=Cc[:], in0=Cc[:], in1=gt[sl][:], op=ALU.mult), reads=[kC, "gt" + s_], writes=[kC])

                    def s_gn2():
                        for hd in range(4):
                            P.op("dve", lambda e, hd=hd: e.tensor_scalar(out=B[:, hd * 256:(hd + 1) * 256], in0=A[:, hd * 256:(hd + 1) * 256], scalar1=mv[:, hd, 0:1], scalar2=sm[:, 4 + hd:5 + hd], op0=ALU.subtract, op1=ALU.mult),
                                 reads=[kA, kmv, ksm], writes=[kB])
                        P.op("dve", lambda e: e.tensor_tensor(out=B[:], in0=B[:], in1=gnw[:], op=ALU.mult), reads=[kB, "gnw"], writes=[kB])
                        P.op("dve", lambda e: e.tensor_tensor(out=bb[0][:], in0=B[:], in1=Cc[:], op=ALU.mult), reads=[kB, kC], writes=[kb[0]])
                        P.op("act", lambda e: e.activation(out=A[:], in_=grt[sl][:], func=AF.Sigmoid), reads=["grt" + s_], writes=[kA])
                        P.op("act", lambda e: e.activation(out=B[:], in_=gft[sl][:], func=AF.Sigmoid), reads=["gft" + s_], writes=[kB])

                    def s_tr_r():
                        tr(bb[0], kb[0], bb[1], kb[1], "act")

                    def s_ret():
                        mm16(lambda k: bb[1][:, k * 128:(k + 1) * 128], kb[1], wts["wro"], "wro")

                    def s_four():
                        for n in range(2):
                            cs_ = slice(n * 512, (n + 1) * 512)
                            for k in range(8):
                                P.op("pe", lambda e, n=n, k=k: e.matmul(Y[:], lhsT=uft[sl][:, k, :], rhs=wts["w4"][:, k, n * 512:(n + 1) * 512], start=(k == 0), stop=(k == 7)), reads=["uft" + s_, "w4"], writes=[kY])
                            P.op("dve", lambda e, cs_=cs_: e.tensor_tensor(out=B[:, cs_], in0=Y[:], in1=B[:, cs_], op=ALU.mult), reads=[kY, kB], writes=[kB])

                    def s_merge():
                        for n in range(2):
                            cs_ = slice(n * 512, (n + 1) * 512)
                            P.op("dve", lambda e, n=n, cs_=cs_: e.tensor_tensor(out=A[:, cs_], in0=X[n][:], in1=A[:, cs_], op=ALU.mult), reads=[kX[n], kA], writes=[kA])
                        P.op("dve", lambda e: e.tensor_tensor(out=bb[2][:], in0=A[:], in1=B[:], op=ALU.add), reads=[kA, kB], writes=[kb[2]])

                    def s_tr_m():
                        tr(bb[2], kb[2], bb[3], kb[3], "act")

                    def s_mix():
                        mm16(lambda k: bb[3][:, k * 128:(k + 1) * 128], kb[3], wts["wmx"], "wmx")
                        for n in range(2):
                            cs_ = slice(n * 512, (n + 1) * 512)
                            P.op("dve", lambda e, n=n, cs_=cs_: e.tensor_tensor(out=x1[:, cs_], in0=X[n][:], in1=xt[sl][:, cs_], op=ALU.add), reads=[kX[n], "xt" + s_], writes=[kx1])

                    def s_norm():
                        rstd_pow(st, kst, x1[:], kx1, Cc[:], kC)
                        P.op("dve", lambda e: e.scalar_tensor_tensor(out=bb[0][:], in0=x1[:], scalar=st[:, 2:3], in1=nca[:], op0=ALU.mult, op1=ALU.mult), reads=[kx1, kst, "nca"], writes=[kb[0]])

                    def s_tr_x():
                        tr(bb[0], kb[0], bb[1], kb[1], "dve")

                    def s_hq():
                        for m in range(8):
                            bank, bkey = X[m // 4], kX[m // 4]
                            co = (m % 4) * 128
                            for k in range(8):
                                P.op("pe", lambda e, m=m, k=k, bank=bank, co=co: e.matmul(bank[:, co:co + 128], lhsT=wts["wcq"][:, k, m * 128:(m + 1) * 128], rhs=bb[1][:, k * 128:(k + 1) * 128], start=(k == 0), stop=(k == 7)),
                                     reads=["wcq", kb[1]], writes=[bkey])
                        P.op("act", lambda e: e.activation(out=bb[2][:, 0:512], in_=X[0][:], func=AF.Copy), reads=[kX[0]], writes=[kb[2]])
                        P.op("dve", lambda e: e.tensor_copy(out=bb[2][:, 512:1024], in_=X[1][:]), reads=[kX[1]], writes=[kb[2]])

                    def s_logits():
                        for hd in range(4):
                            bank, bkey = X[hd // 2], kX[hd // 2]
                            co = (hd % 2) * 256
                            for hf in range(2):
                                m = 2 * hd + hf
                                P.op("pe", lambda e, m=m, hf=hf, bank=bank, co=co: e.matmul(bank[:, co:co + 256], lhsT=bb[2][:, m * 128:(m + 1) * 128], rhs=ckT[:, m, :], start=(hf == 0), stop=(hf == 1)),
                                     reads=[kb[2], "ckT"], writes=[bkey])

                    def s_softmax():
                        P.op("dve", lambda e: e.memset(sm[:, 8:16], 0.0), writes=[ksm])
                        for n in range(2):
                            P.op("dve", lambda e, n=n: e.tensor_reduce(out=sm[:, 2 * n:2 * n + 2], in_=X[n][:].rearrange("p (a b) -> p a b", a=2), axis=AX.X, op=ALU.max), reads=[kX[n], ksm], writes=[ksm])
                        P.op("dve", lambda e: e.tensor_scalar(out=sm[:, 4:8], in0=sm[:, 0:4], scalar1=-1.0 / 16.0, scalar2=None, op0=ALU.mult), reads=[ksm], writes=[ksm])
                        for hd in range(4):
                            bank, bkey = X[hd // 2], kX[hd // 2]
                            co = (hd % 2) * 256
                            P.op("act", lambda e, hd=hd, bank=bank, co=co: e.activation(out=Cc[:, hd * 256:(hd + 1) * 256], in_=bank[:, co:co + 256], func=AF.Exp, scale=1.0 / 16.0, bias=sm[:, 4 + hd:5 + hd], accum_out=sm[:, 8 + hd:9 + hd]),
                                 reads=[bkey, ksm], writes=[kC, ksm])
                        P.op("dve", lambda e: e.reciprocal(out=sm[:, 12:16], in_=sm[:, 8:12]), reads=[ksm], writes=[ksm])
                        for hd in range(4):
                            eng = "dve"
                            P.op(eng, lambda e, hd=hd: e.tensor_scalar(out=bb[3][:, hd * 256:(hd + 1) * 256], in0=Cc[:, hd * 256:(hd + 1) * 256], scalar1=sm[:, 12 + hd:13 + hd], scalar2=None, op0=ALU.mult),
                                 reads=[kC, ksm], writes=[kb[3]])

                    def s_tr_p():
                        tr(bb[3], kb[3], bb[0], kb[0], "act")

                    def s_att():
                        for m in range(8):
                            hd = m // 2
                            bank, bkey = X[m // 4], kX[m // 4]
                            co = (m % 4) * 128
                            for mc in range(2):
                                P.op("pe", lambda e, m=m, mc=mc, hd=hd, bank=bank, co=co: e.matmul(bank[:, co:co + 128], lhsT=cv[:, mc, m * 128:(m + 1) * 128], rhs=bb[0][:, (2 * hd + mc) * 128:(2 * hd + mc + 1) * 128], start=(mc == 0), stop=(mc == 1)),
                                     reads=["cv", kb[0]], writes=[bkey])
                        P.op("act", lambda e: e.activation(out=bb[1][:, 0:512], in_=X[0][:], func=AF.Copy), reads=[kX[0]], writes=[kb[1]])
                        P.op("dve", lambda e: e.tensor_copy(out=bb[1][:, 512:1024], in_=X[1][:]), reads=[kX[1]], writes=[kb[1]])

                    def s_co():
                        mm16(lambda k: bb[1][:, k * 128:(k + 1) * 128], kb[1], wts["wco"], "wco")
                        for n in range(2):
                            cs_ = slice(n * 512, (n + 1) * 512)
                            P.op("dve", lambda e, n=n, cs_=cs_: e.tensor_tensor(out=x2[:, cs_], in0=X[n][:], in1=x1[:, cs_], op=ALU.add), reads=[kX[n], kx1], writes=[kx2])
                        P.op("sp", lambda e: e.dma_start(out=X2d[r0:r0 + 128, :], in_=x2[:]), reads=[kx2], writes=["X2d"], dma=True)

                    return [s_gn1, s_gn2, s_tr_r, s_ret, s_four, s_merge, s_tr_m, s_mix, s_norm, s_tr_x, s_hq, s_logits, s_softmax, s_tr_p, s_att, s_co]

                NST = 16
                import os as _os
                SK = int(_os.environ.get("P4SKEW", "8"))
                p4a_load(0)
                active = []
                nxt = 0
                tick = 0
                while nxt < NM or active:
                    admit = (tick % NST == 0) or (tick % NST == SK)
                    if nxt < NM and admit:
                        if nxt + 1 < NM:
                            p4a_load(nxt + 1)
                        active.append([p4a_stages(nxt), 0])
                        nxt += 1
                    for a_ in active:
                        a_[0][a_[1]]()
                        a_[1] += 1
                    active = [a_ for a_ in active if a_[1] < NST]
                    tick += 1
                P.flush()

            with contextlib.ExitStack() as es:
                sb = lambda n, s, d: es.enter_context(nc.sbuf_tensor(n, s, d))
                ps = lambda n, s, d: es.enter_context(nc.psum_tensor(n, s, d))
                wup = sb("wup", [128, 8, DFF], BF16)
                wdn = sb("wdn", [128, 32, D], BF16)
                for k in range(8):
                    P.op("pool", lambda e, k=k: e.dma_start(out=wup[:, k, :], in_=w_up[k * 128:(k + 1) * 128, :]), writes=["wup"], dma=True)
                for kq in range(8):
                    P.op("pool", lambda e, kq=kq: e.dma_start(out=wdn[:, 4 * kq:4 * kq + 4, :], in_=w_dn[kq * 512:(kq + 1) * 512, :].rearrange("(k p) n -> p k n", p=128)), writes=["wdn"], dma=True)
                nmlp = sb("nmlp_s", [128, D], F32)
                nfin = sb("nfin_s", [128, D], F32)
                P.op("sp", lambda e: e.dma_start(out=nmlp[:], in_=nmlp_d[:, :]), writes=["nmlp"], dma=True)
                P.op("sp", lambda e: e.dma_start(out=nfin[:], in_=nfin_d[:, :]), writes=["nfin"], dma=True)
                xg = [sb("xg%d" % i, [128, 2, D], F32) for i in range(2)]
                xn = [sb("xn5_%d" % i, [128, D], BF16) for i in range(2)]
                xnT = [sb("xnT5_%d" % i, [128, 8, 256], BF16) for i in range(2)]
                hT = sb("hT", [128, 32, 256], BF16)
                sq = [sb("sq%d" % i, [128, 256], F32) for i in range(2)]
                ot = [sb("ot%d" % i, [128, D], F32) for i in range(2)]
                st = [sb("st5_%d" % i, [128, 8], F32) for i in range(3)]
                pT = ps("pT5", [128, D], BF16)
                pu = [ps("pu%d" % i, [128, 512], F32) for i in range(4)]
                pd = [ps("pd%d" % i, [128, 512], F32) for i in range(2)]

                def p4b_load(gi):
                    sl = gi % 2
                    P.op("sp", lambda e: e.dma_start(out=xg[sl][:], in_=X2d[gi * 256:(gi + 1) * 256, :].rearrange("(t p) d -> p t d", p=128)), writes=["xg%d" % sl], dma=True)

                oc = [0]

                def p4b_norm(gi):
                    sl = gi % 2
                    gk = "xg%d" % sl
                    for t in range(2):
                        rstd_ops(st[t], "st5_%d" % t, xg[sl][:, t, :], gk, xn[t][:], "xn5_%d" % t)
                        P.op("dve", lambda e, t=t: e.scalar_tensor_tensor(out=xn[t][:], in0=xg[sl][:, t, :], scalar=st[t][:, 2:3], in1=nmlp[:], op0=ALU.mult, op1=ALU.mult), reads=[gk, "st5_%d" % t, "nmlp"], writes=["xn5_%d" % t])

                def p4b_tr(gi):
                    sl = gi % 2
                    for t in range(2):
                        transposes8(xn[t], "xn5_%d" % t, pT, "pT5")
                        P.op("act", lambda e, t=t: e.activation(out=xnT[sl][:, :, t * 128:(t + 1) * 128], in_=pT[:].rearrange("p (k c) -> p k c", k=8), func=AF.Copy), reads=["pT5"], writes=["xnT5_%d" % sl])

                def p4b_up(gi, f0, f1, hooks=None):
                    sl = gi % 2
                    for f in range(f0, f1):
                        if hooks and f in hooks:
                            hooks[f]()
                        bank, bkey = pu[f % 4], "pu%d" % (f % 4)
                        for k in range(8):
                            P.op("pe", lambda e, f=f, k=k, bank=bank: e.matmul(bank[:, 0:256], lhsT=wup[:, k, f * 128:(f + 1) * 128], rhs=xnT[sl][:, k, :], start=(k == 0), stop=(k == 7)), reads=["wup", "xnT5_%d" % sl], writes=[bkey])
                        sqt, sqk = sq[f % 2], "sq%d" % (f % 2)
                        P.op("act", lambda e, bank=bank, sqt=sqt: e.activation(out=sqt[:], in_=bank[:, 0:256], func=AF.Square), reads=[bkey], writes=[sqk])
                        P.op("dve", lambda e, f=f, bank=bank, sqt=sqt: e.scalar_tensor_tensor(out=hT[:, f, :], in0=bank[:, 0:256], scalar=0.0, in1=sqt[:], op0=ALU.is_gt, op1=ALU.mult), reads=[bkey, sqk], writes=["hT"])

                def p4b_down(gi):
                    sl = gi % 2
                    gk = "xg%d" % sl
                    epi = []
                    for t in range(2):
                        for n in range(2):
                            for f in range(32):
                                P.op("pe", lambda e, t=t, n=n, f=f: e.matmul(pd[n][:], lhsT=hT[:, f, t * 128:(t + 1) * 128], rhs=wdn[:, f, n * 512:(n + 1) * 512], start=(f == 0), stop=(f == 31)), reads=["hT", "wdn"], writes=["pd%d" % n])
                            cs_ = slice(n * 512, (n + 1) * 512)
                            P.op("dve", lambda e, t=t, n=n, cs_=cs_: e.tensor_tensor(out=xg[sl][:, t, cs_], in0=pd[n][:], in1=xg[sl][:, t, cs_], op=ALU.add), reads=["pd%d" % n, gk], writes=[gk])

                        def fin(t=t):
                            osl = oc[0] % 2
                            oc[0] += 1
                            ok = "ot%d" % osl
                            rstd_ops(st[2], "st5_2", xg[sl][:, t, :], gk, ot[osl][:], ok)
                            P.op("dve", lambda e: e.scalar_tensor_tensor(out=ot[osl][:], in0=xg[sl][:, t, :], scalar=st[2][:, 2:3], in1=nfin[:], op0=ALU.mult, op1=ALU.mult), reads=[gk, "st5_2", "nfin"], writes=[ok])
                            r0 = gi * 256 + t * 128
                            P.op("sp", lambda e: e.dma_start(out=y_out[r0:r0 + 128, :], in_=ot[osl][:]), reads=[ok], writes=["y_out"], dma=True)
                        fin()
                    return epi

                NG = NM // 2
                p4b_load(0)
                p4b_norm(0)
                p4b_tr(0)
                pend = []
                for gi in range(NG):
                    if gi + 1 < NG:
                        p4b_load(gi + 1)
                    hk = {6: pend[0]} if pend else None
                    p4b_up(gi, 0, 16, hk)
                    if gi + 1 < NG:
                        p4b_norm(gi + 1)
                    p4b_up(gi, 16, 32)
                    if gi + 1 < NG:
                        p4b_tr(gi + 1)
                    pend = p4b_down(gi)
                for f_ in pend:
                    f_()
                P.flush()
        P.flush(final=True)
    return nc, P


def _other_chunks(h):
    return np.arange(0, 64) if h == 1 else np.arange(127, 63, -1)


def _consts(h):
    bf = ml_dtypes.bfloat16
    c = {}
    c["ident"] = np.eye(128, dtype=np.float32).astype(bf)
    inv = (np.float32(10000.0) ** (-(np.arange(0, 256, 2, dtype=np.float32)) / np.float32(256))).astype(np.float32)

    def rot(pos):
        ang = (pos.astype(np.float32)[:, None] * inv[None, :]).astype(np.float32).astype(np.float64)
        return np.concatenate([np.cos(ang), np.sin(ang)], axis=1).astype(np.float32)
    pos_m = h * 8192 + np.arange(8192)
    oc = _other_chunks(h)
    pos_o = (oc[:, None] * 128 + np.arange(128)[None, :]).reshape(-1)
    c["rotm"] = rot(pos_m)
    c["roto"] = rot(pos_o)
    dk = np.zeros((128, DKW), np.float32)
    j = np.arange(128)[:, None].astype(np.float64)
    i = np.arange(128)[None, :].astype(np.float64)
    dk[:, O_E0F:O_E0F + 128] = np.maximum(i - j, 0)
    dk[:, O_MF:O_MF + 128] = (i >= j)
    dk[:, O_E0B:O_E0B + 128] = np.maximum(j - i, 0)
    dk[:, O_MB:O_MB + 128] = (j > i)
    dk[:, O_XF:O_XF + 128] = i + 1
    dk[:, O_XB:O_XB + 128] = 128 - i
    zf = 127 - j
    zb = j
    dk[:, O_ZF:O_ZF + 256] = zf
    dk[:, O_ZB:O_ZB + 256] = zb
    dk[:, O_ZO:O_ZO + 256] = zf if h == 1 else zb
    dk[:, O_MSKF] = 1.0 if h == 1 else 0.0
    dk[:, O_MSKB] = 1.0 if h == 0 else 0.0
    dk[:, O_EPS6] = 1e-6
    dk[:, O_EPS5] = 1e-5
    dk[:, O_ONE] = 1.0
    dk[:, O_NH:O_NH + 8] = -0.5
    c["dk"] = dk
    a = np.concatenate([64 * h + np.arange(64), oc]).astype(np.float64)
    d = np.arange(128, dtype=np.float64)
    th = 2 * np.pi * ((a[:, None] * d[None, :]) % 128) / 128.0
    fcs = np.zeros((128, 2, 256), np.float64)
    fcs[:, 0, :128] = np.cos(th)
    fcs[:, 0, 128:] = -np.sin(th)
    fcs[:, 1, :128] = np.sin(th)
    fcs[:, 1, 128:] = np.cos(th)
    c["fcs"] = fcs.astype(np.float32).astype(bf)
    b = np.arange(128, dtype=np.int64)[:, None, None]
    dd = np.arange(128, dtype=np.int64)[None, :, None]
    cg = (64 * h + np.arange(64, dtype=np.int64))[None, None, :]
    num = (cg * b * 128 + dd * b) % 16384
    th2 = 2 * np.pi * num.astype(np.float64) / 16384.0
    mt = np.zeros((128, 128, 2, 64), np.float64)
    mt[:, :, 0, :] = np.cos(th2)
    mt[:, :, 1, :] = np.sin(th2)
    c["mt"] = mt.astype(np.float32).astype(bf)
    ch = (np.arange(2)[None, :, None] * 128 + np.arange(128)[:, None, None]).astype(np.int64)
    jj = np.arange(256, dtype=np.int64)[None, None, :]
    th3 = 2 * np.pi * ((ch * jj) % 256).astype(np.float64) / 256.0
    cs = np.concatenate([np.cos(th3), -np.sin(th3)], axis=2)
    c["cs"] = cs.astype(np.float32).astype(bf)
    return c


_CACHE = {}


def _rep(v):
    return np.ascontiguousarray(np.broadcast_to(np.asarray(v, np.float32).reshape(1, -1), (128, v.size)))


def make_in_maps(inp):
    seqs = [(inp["x_prompt"][0], inp["mem_prompt"][0]), (inp["x_prompt"][1], inp["mem_prompt"][1]), (inp["x_sample"][0], inp["mem_sample"][0])]
    shared = {
        "w_in": np.ascontiguousarray(inp["w_in"][0]), "w_ro": np.ascontiguousarray(inp["w_ret_out"][0]),
        "w_4": np.ascontiguousarray(inp["w_four_out"][0]), "w_mx": np.ascontiguousarray(inp["w_mix_out"][0]),
        "w_cq": np.ascontiguousarray(inp["w_cq"][0]), "w_ck": np.ascontiguousarray(inp["w_ck"][0]),
        "w_cv": np.ascontiguousarray(inp["w_cv"][0]), "w_co": np.ascontiguousarray(inp["w_co"][0]),
        "w_up": np.ascontiguousarray(inp["w_up"][0]), "w_dn": np.ascontiguousarray(inp["w_down"][0]),
        "nmix": _rep(inp["norm_mix_w"][0]), "nca": _rep(inp["norm_ca_w"][0]), "nmem": _rep(inp["norm_mem_w"][0]),
        "nmlp": _rep(inp["norm_mlp_w"][0]), "nfin": _rep(inp["norm_final_w"]), "gnw": _rep(inp["ret_gn_w"][0]),
    }
    consts = [_consts(0), _consts(1)]
    in_maps = []
    for core in range(8):
        si = min(core // 2, 2)
        h = core % 2
        x, mem = seqs[si]
        x = np.asarray(x, np.float32)
        xm = np.ascontiguousarray(x[h * 8192:(h + 1) * 8192])
        oc = _other_chunks(h)
        xo = np.ascontiguousarray(x.reshape(128, 128, D)[oc].reshape(8192, D))
        df = np.asarray(inp["ret_decay_fwd"][0], np.float32)
        db = np.asarray(inp["ret_decay_bwd"][0], np.float32)
        do = df if h == 1 else db
        m = dict(shared)
        m.update(consts[h])
        m.update({"xm": xm, "xo": xo, "mem": np.ascontiguousarray(np.asarray(mem, np.float32)),
                  "dec": _rep(np.concatenate([df, db, do]))})
        in_maps.append(m)
    return in_maps


def kernel(**inputs):
    inp = {k: np.asarray(v) for k, v in inputs.items()}
    if "nc" not in _CACHE:
        _CACHE["nc"] = build()[0]
    nc = _CACHE["nc"]
    in_maps = make_in_maps(inp)
    res = run_bass_kernel_spmd(nc, in_maps, core_ids=list(range(8)))
    ys = [np.asarray(r["y"], np.float32) for r in res.results]
    y_prompt = np.stack([np.concatenate([ys[0], ys[1]], 0), np.concatenate([ys[2], ys[3]], 0)], 0)
    y_sample = np.concatenate([ys[4], ys[5]], 0)[None]
    return (y_prompt, y_sample)
```

```python
import contextlib
import numpy as np
import ml_dtypes
import concourse.bass as bass
import concourse.mybir as mybir
from concourse.bass_utils import run_bass_kernel_spmd

F32 = mybir.dt.float32
BF16 = mybir.dt.bfloat16
AF = mybir.ActivationFunctionType
ALU = mybir.AluOpType
AX = mybir.AxisListType

D = 1024
SEQ = 16384
NM = 64
NO = 64
INW = 7168
DFF = 4096
NMEM = 256


class Prog:
    NPOOL = 8

    def __init__(self, nc):
        self.nc = nc
        self.eng = {"pe": nc.tensor, "act": nc.scalar, "dve": nc.vector, "pool": nc.gpsimd, "sp": nc.sync}
        self.sems = {}
        self.cnt = {}
        self.known = {e: {} for e in self.eng}
        self.carry = {e: {} for e in self.eng}
        self.dma_rr = {e: 0 for e in self.eng}
        self.n_inst = 0
        self._reset()

    def _reset(self):
        self.ops = []
        self.lastw = {}
        self.lastr = {}
        self.last_on_sem = {}

    def getsem(self, name):
        if name not in self.sems:
            self.sems[name] = self.nc.alloc_semaphore(name=name)
        return self.sems[name]

    def op(self, eng, fn, reads=(), writes=(), dma=False, ndma=1):
        idx = len(self.ops)
        deps = set()
        for k in reads:
            deps.update(self.lastw.get(k, {}).values())
        for k in writes:
            deps.update(self.lastw.get(k, {}).values())
            deps.update(self.lastr.get(k, {}).values())
        semname = None
        if dma:
            semname = "d_%s_%d" % (eng, self.dma_rr[eng] % self.NPOOL)
            self.dma_rr[eng] += 1
            if semname in self.last_on_sem:
                deps.add(self.last_on_sem[semname])
            self.last_on_sem[semname] = idx
        self.ops.append(dict(eng=eng, fn=fn, deps=deps, dma=dma, semname=semname, ndma=ndma, waited=False, tok=None))
        ek = (eng, semname)
        for k in reads:
            self.lastr.setdefault(k, {})[ek] = idx
        for k in writes:
            self.lastw[k] = {ek: idx}
            self.lastr[k] = {}
        return idx

    def flush(self, final=False):
        ops = self.ops
        last_eng = {}
        for i, o in enumerate(ops):
            if not o["dma"]:
                last_eng[o["eng"]] = i
            for d in o["deps"]:
                od = ops[d]
                if (not od["dma"]) and od["eng"] == o["eng"] == "pe":
                    continue
                od["waited"] = True
        for i in last_eng.values():
            ops[i]["waited"] = True
        for o in ops:
            if o["dma"]:
                o["waited"] = True
        for o in ops:
            if not o["waited"]:
                continue
            if o["dma"]:
                nm = o["semname"]
                self.cnt[nm] = self.cnt.get(nm, 0) + 16 * o["ndma"]
            else:
                nm = "e_" + o["eng"]
                self.cnt[nm] = self.cnt.get(nm, 0) + 1
            o["tok"] = (nm, self.cnt[nm])
        for o in ops:
            e = o["eng"]
            engobj = self.eng[e]
            need = dict(self.carry[e])
            self.carry[e] = {}
            for d in o["deps"]:
                od = ops[d]
                if od["tok"] is None:
                    continue
                if (not od["dma"]) and od["eng"] == e == "pe":
                    continue
                nm, v = od["tok"]
                need[nm] = max(need.get(nm, 0), v)
            for nm, v in need.items():
                if self.known[e].get(nm, 0) >= v:
                    continue
                engobj.wait_ge(self.getsem(nm), v)
                self.known[e][nm] = v
                self.n_inst += 1
            r = o["fn"](engobj)
            self.n_inst += 1
            if o["tok"] is not None:
                nm, v = o["tok"]
                if o["dma"]:
                    insts = r if isinstance(r, (list, tuple)) else [r]
                    assert len(insts) == o["ndma"], (len(insts), o["ndma"])
                    for ins in insts:
                        ins.then_inc(self.getsem(nm), 16)
                else:
                    r.then_inc(self.getsem(nm), 1)
        allc = dict(self.cnt)
        for e in self.eng:
            self.carry[e] = dict(allc)
        if final:
            engobj = self.eng["sp"]
            for nm, v in allc.items():
                if self.known["sp"].get(nm, 0) < v:
                    engobj.wait_ge(self.getsem(nm), v)
                    self.known["sp"][nm] = v
        self._reset()


O_E0F, O_MF, O_E0B, O_MB, O_XF, O_XB = 0, 128, 256, 384, 512, 640
O_ZF, O_ZB, O_ZO = 768, 1024, 1280
O_MSKF, O_MSKB, O_EPS6, O_EPS5, O_ONE = 1536, 1537, 1538, 1539, 1540
O_NH = 1544
DKW = 1552


def build(dbg=False, phases=("p1", "p2", "p3", "p4")):
    nc = bass.Bass("TRN2", target_bir_lowering=False)

    def din(name, shape, dt=F32):
        return nc.dram_tensor(name, shape, dt, kind="ExternalInput").ap()

    def dscr(name, shape, dt):
        return nc.dram_tensor(name, shape, dt, kind="ExternalOutput" if dbg else "Internal").ap()

    xm = din("xm", [NM * 128, D])
    xo = din("xo", [NO * 128, D])
    mem = din("mem", [NMEM, D])
    w_in = din("w_in", [D, INW])
    w_ro = din("w_ro", [D, D])
    w_4 = din("w_4", [D, D])
    w_mx = din("w_mx", [D, D])
    w_cq = din("w_cq", [D, D])
    w_ck = din("w_ck", [D, D])
    w_cv = din("w_cv", [D, D])
    w_co = din("w_co", [D, D])
    w_up = din("w_up", [D, DFF])
    w_dn = din("w_dn", [DFF, D])
    nmix_d = din("nmix", [128, D])
    nca_d = din("nca", [128, D])
    nmem_d = din("nmem", [128, D])
    nmlp_d = din("nmlp", [128, D])
    nfin_d = din("nfin", [128, D])
    gnw_d = din("gnw", [128, D])
    dec_d = din("dec", [128, 12])
    dk_d = din("dk", [128, DKW])
    ident_d = din("ident", [128, 128], BF16)
    rotm_d = din("rotm", [NM * 128, 256])
    roto_d = din("roto", [NO * 128, 256])
    fcs_d = din("fcs", [128, 2, 256], BF16)
    mt_d = din("mt", [128, 128, 2, 64], BF16)
    cs_d = din("cs", [128, 2, 512], BF16)
    y_out = nc.dram_tensor("y", [NM * 128, D], F32, kind="ExternalOutput").ap()

    NT = (NM + NO) * 128
    Qd = dscr("Qd", [NM * 128, D], BF16)
    Kd = dscr("Kd", [NM * 128, D], BF16)
    Vd = dscr("Vd", [NM * 128, D], BF16)
    Gd = dscr("Gd", [NM * 128, D], BF16)
    GRd = dscr("GRd", [NM * 128, D], BF16)
    GFd = dscr("GFd", [NM * 128, D], BF16)
    KOd = dscr("KOd", [NO * 128, D], BF16)
    VOd = dscr("VOd", [NO * 128, D], BF16)
    Zd = dscr("Zd", [4, 2, 2, NT, 128], BF16)
    YFd = dscr("YFd", [NM * 128, D], F32)
    YBd = dscr("YBd", [NM * 128, D], F32)
    UFd = dscr("UFd", [D, NM * 128], BF16)
    X2d = dscr("X2d", [NM * 128, D], F32)

    P = Prog(nc)
    with contextlib.ExitStack() as gs:
        ident = gs.enter_context(nc.sbuf_tensor("ident_s", [128, 128], BF16))
        dk = gs.enter_context(nc.sbuf_tensor("dk_s", [128, DKW], F32))
        ckT = gs.enter_context(nc.sbuf_tensor("ckT", [128, 8, NMEM], BF16))
        cv = gs.enter_context(nc.sbuf_tensor("cv", [128, 2, D], BF16))
        P.op("sp", lambda e: e.dma_start(out=ident[:], in_=ident_d[:, :]), writes=["ident"], dma=True)
        P.op("sp", lambda e: e.dma_start(out=dk[:], in_=dk_d[:, :]), writes=["dk"], dma=True)
        eps6 = dk[:, O_EPS6:O_EPS6 + 1]
        eps5 = dk[:, O_EPS5:O_EPS5 + 1]
        one_ap = dk[:, O_ONE:O_ONE + 1]

        def rstd_ops(st, key, x_ap, xkey, junk, jkey):
            P.op("dve", lambda e: e.memset(st[:], 0.0), writes=[key])
            P.op("act", lambda e: e.activation(out=junk, in_=x_ap, func=AF.Square, accum_out=st[:, 0:1]),
                 reads=[xkey, key], writes=[jkey, key])
            P.op("act", lambda e: e.activation(out=st[:, 1:2], in_=st[:, 0:1], func=AF.Sqrt, scale=1.0 / D, bias=eps6),
                 reads=[key, "dk"], writes=[key])
            P.op("dve", lambda e: e.reciprocal(out=st[:, 2:3], in_=st[:, 1:2]), reads=[key], writes=[key])

        def rstd_pow(st, key, x_ap, xkey, junk, jkey):
            P.op("dve", lambda e: e.memset(st[:], 0.0), writes=[key])
            P.op("act", lambda e: e.activation(out=junk, in_=x_ap, func=AF.Square, accum_out=st[:, 0:1]),
                 reads=[xkey, key], writes=[jkey, key])
            P.op("dve", lambda e: e.tensor_scalar(out=st[:, 1:2], in0=st[:, 0:1], scalar1=1.0 / D, scalar2=eps6, op0=ALU.mult, op1=ALU.add), reads=[key, "dk"], writes=[key])
            P.op("act", lambda e: e.activation(out=st[:, 3:4], in_=st[:, 1:2], func=AF.Ln), reads=[key], writes=[key])
            P.op("act", lambda e: e.activation(out=st[:, 2:3], in_=st[:, 3:4], func=AF.Exp, scale=-0.5), reads=[key], writes=[key])

        def transposes8(src, skey, pT, pkey):
            for k in range(8):
                P.op("pe", lambda e, k=k: e.transpose(out=pT[:, k * 128:(k + 1) * 128], in_=src[:, k * 128:(k + 1) * 128], identity=ident[:]),
                     reads=[skey, "ident"], writes=[pkey])

        if "p1" in phases:
            with contextlib.ExitStack() as es:
                sb = lambda n, s, d: es.enter_context(nc.sbuf_tensor(n, s, d))
                ps = lambda n, s, d: es.enter_context(nc.psum_tensor(n, s, d))
                win = sb("win", [128, 8, INW], BF16)
                nmix = sb("nmix_s", [128, D], F32)
                css = sb("css", [128, 2, 512], BF16)
                xs = [sb("xs%d" % i, [128, D], F32) for i in range(2)]
                rt = [sb("rt%d" % i, [128, 256], F32) for i in range(3)]
                st_ = [sb("st1_%d" % i, [128, 8], F32) for i in range(2)]
                xn_ = [sb("xn1_%d" % i, [128, D], BF16) for i in range(2)]
                xnT_ = [sb("xnT1_%d" % i, [128, D], BF16) for i in range(2)]
                qkf = [sb("qkf%d" % i, [128, D], F32) for i in range(2)]
                tmp = [sb("tmp%d" % i, [128, 4, 128], F32) for i in range(4)]
                outs = {nm: [sb("o_%s%d" % (nm, i), [128, D], BF16) for i in range(2)] for nm in ("q", "k", "v")}
                outs.update({nm: [sb("o_%s0" % nm, [128, D], BF16)] for nm in ("g", "gr", "gf")})
                ub = sb("ub", [128, D], BF16)
                uT = sb("uT", [128, D], BF16)
                zt = [sb("zt%d" % i, [128, 4, 512], BF16) for i in range(2)]
                pT = ps("pT1", [128, D], BF16)
                pT2 = ps("pT1b", [128, D], BF16)
                pm = [ps("pm1_%d" % i, [128, 512], F32) for i in range(6)]
                for k in range(8):
                    P.op("pool", lambda e, k=k: e.dma_start(out=win[:, k, :], in_=w_in[k * 128:(k + 1) * 128, :]),
                         writes=["win%d" % k], dma=True)
                P.op("sp", lambda e: e.dma_start(out=nmix[:], in_=nmix_d[:, :]), writes=["nmix"], dma=True)
                P.op("sp", lambda e: e.dma_start(out=css[:], in_=cs_d[:, :, :]), writes=["css"], dma=True)
                bi = [0]

                def p1_load_x(ti):
                    mine = ti < NM
                    sl = ti % 2
                    src = xm if mine else xo
                    r0 = (ti if mine else ti - NM) * 128
                    P.op("sp", lambda e: e.dma_start(out=xs[sl][:], in_=src[r0:r0 + 128, :]), writes=["xs%d" % sl], dma=True)

                def p1_load_rt(ti):
                    mine = ti < NM
                    sl = ti % 3
                    rsrc = rotm_d if mine else roto_d
                    r0 = (ti if mine else ti - NM) * 128
                    P.op("sp", lambda e: e.dma_start(out=rt[sl][:], in_=rsrc[r0:r0 + 128, :]), writes=["rt%d" % sl], dma=True)

                def rotary(srcf, skey, dst, dkey, sl):
                    X = srcf[:].rearrange("p (h t f) -> p h t f", h=4, t=2)
                    O = dst[:].rearrange("p (h t f) -> p h t f", h=4, t=2)
                    cosb = rt[sl][:, 0:128].rearrange("p (o f) -> p o f", o=1).broadcast_to([128, 4, 128])
                    sinb = rt[sl][:, 128:256].rearrange("p (o f) -> p o f", o=1).broadcast_to([128, 4, 128])
                    rk = "rt%d" % sl
                    P.op("pool", lambda e: e.tensor_tensor(out=tmp[0][:], in0=X[:, :, 0, :], in1=cosb, op=ALU.mult), reads=[skey, rk], writes=["tmp0"])
                    P.op("dve", lambda e: e.tensor_tensor(out=tmp[1][:], in0=X[:, :, 1, :], in1=sinb, op=ALU.mult), reads=[skey, rk], writes=["tmp1"])
                    P.op("dve", lambda e: e.tensor_tensor(out=O[:, :, 0, :], in0=tmp[0][:], in1=tmp[1][:], op=ALU.subtract), reads=["tmp0", "tmp1"], writes=[dkey])
                    P.op("pool", lambda e: e.tensor_tensor(out=tmp[2][:], in0=X[:, :, 1, :], in1=cosb, op=ALU.mult), reads=[skey, rk], writes=["tmp2"])
                    P.op("dve", lambda e: e.tensor_tensor(out=tmp[3][:], in0=X[:, :, 0, :], in1=sinb, op=ALU.mult), reads=[skey, rk], writes=["tmp3"])
                    P.op("pool", lambda e: e.tensor_tensor(out=O[:, :, 1, :], in0=tmp[2][:], in1=tmp[3][:], op=ALU.add), reads=["tmp2", "tmp3", dkey], writes=[dkey])

                def p1_A_stages(ti):
                    sl = ti % 2
                    xk = "xs%d" % sl
                    st, xn, xnT = st_[sl], xn_[sl], xnT_[sl]
                    ks_, kn_, kt_ = "st1_%d" % sl, "xn1_%d" % sl, "xnT1_%d" % sl

                    def a0():
                        P.op("dve", lambda e: e.memset(st[:], 0.0), writes=[ks_])
                        P.op("act", lambda e: e.activation(out=xn[:], in_=xs[sl][:], func=AF.Square, accum_out=st[:, 0:1]), reads=[xk, ks_], writes=[kn_, ks_])

                    def a1():
                        P.op("act", lambda e: e.activation(out=st[:, 1:2], in_=st[:, 0:1], func=AF.Sqrt, scale=1.0 / D, bias=eps6), reads=[ks_, "dk"], writes=[ks_])

                    def a2():
                        P.op("dve", lambda e: e.reciprocal(out=st[:, 2:3], in_=st[:, 1:2]), reads=[ks_], writes=[ks_])

                    def a3():
                        P.op("dve", lambda e: e.scalar_tensor_tensor(out=xn[:], in0=xs[sl][:], scalar=st[:, 2:3], in1=nmix[:], op0=ALU.mult, op1=ALU.mult),
                             reads=[xk, ks_, "nmix"], writes=[kn_])

                    def a4():
                        transposes8(xn, kn_, pT, "pT1")

                    def a5():
                        P.op("dve", lambda e: e.tensor_copy(out=xnT[:], in_=pT[:]), reads=["pT1"], writes=[kt_])
                    return [a0, a1, a2, a3, a4, a5]

                def p1_tail_stages(ti):
                    zr0 = ti * 128
                    zsl = ti % 2

                    def t0():
                        transposes8(ub, "ub", pT2, "pT1b")
                        P.op("act", lambda e: e.activation(out=uT[:], in_=pT2[:], func=AF.Copy), reads=["pT1b"], writes=["uT"])

                    def t1():
                        for g in range(4):
                            b = bi[0] % 6
                            bi[0] += 1
                            bank, bkey = pm[b], "pm1_%d" % b
                            for kk in range(2):
                                P.op("pe", lambda e, g=g, kk=kk, bank=bank: e.matmul(bank[:], lhsT=uT[:, (2 * g + kk) * 128:(2 * g + kk + 1) * 128], rhs=css[:, kk, :], start=(kk == 0), stop=(kk == 1)),
                                     reads=["uT", "css"], writes=[bkey])
                            if g % 2 == 0:
                                P.op("act", lambda e, g=g, bank=bank: e.activation(out=zt[zsl][:, g, :], in_=bank[:], func=AF.Copy), reads=[bkey], writes=["zt%d" % zsl])
                            else:
                                P.op("dve", lambda e, g=g, bank=bank: e.tensor_copy(out=zt[zsl][:, g, :], in_=bank[:]), reads=[bkey], writes=["zt%d" % zsl])
                        for g in range(4):
                            P.op("sp", lambda e, g=g: e.dma_start(
                                out=Zd[g, :, :, zr0:zr0 + 128, :].rearrange("ri hf t c -> t (ri hf) c"),
                                in_=zt[zsl][:, g, :].rearrange("p (rh c) -> p rh c", rh=4)),
                                reads=["zt%d" % zsl], writes=["Zd"], dma=True)
                    return [t0, t1]

                def p1_compute(ti):
                    mine = ti < NM
                    sl = ti % 2
                    r0 = (ti if mine else ti - NM) * 128
                    zr0 = ti * 128
                    xnT = xnT_[sl]
                    kt_ = "xnT1_%d" % sl
                    slices = list(range(14)) if mine else [2, 3, 4, 5, 8, 9]
                    nxt_st = p1_A_stages(ti + 1) if ti + 1 < NM + NO else []
                    hook = ({1: 0, 2: 1, 3: 2, 5: 3, 8: 4, 10: 5} if mine else {0: 0, 1: 1, 2: 2, 3: 3, 4: 4, 5: 5})
                    prev_tail = p1_tail_stages(ti - 1) if ti > 0 else []
                    thook = ({4: 0, 7: 1} if mine else {1: 0, 3: 1})
                    for si_, n in enumerate(slices):
                        if si_ in hook and nxt_st:
                            nxt_st[hook[si_]]()
                        if si_ in thook and prev_tail:
                            prev_tail[thook[si_]]()
                        b = bi[0] % 6
                        bi[0] += 1
                        bank, bkey = pm[b], "pm1_%d" % b
                        for k in range(8):
                            P.op("pe", lambda e, k=k, n=n, bank=bank: e.matmul(bank[:], lhsT=xnT[:, k * 128:(k + 1) * 128], rhs=win[:, k, n * 512:(n + 1) * 512], start=(k == 0), stop=(k == 7)),
                                 reads=[kt_, "win%d" % k], writes=[bkey])
                        hf = n % 2
                        cs_ = slice(hf * 512, (hf + 1) * 512)
                        if n in (0, 1):
                            P.op("act", lambda e, bank=bank, cs_=cs_: e.activation(out=qkf[0][:, cs_], in_=bank[:], func=AF.Copy, scale=1.0 / 16.0), reads=[bkey], writes=["qkf0"])
                            if n == 1:
                                rotary(qkf[0], "qkf0", outs["q"][sl], "o_q%d" % sl, ti % 3)
                        elif n in (2, 3):
                            P.op("act", lambda e, bank=bank, cs_=cs_: e.activation(out=qkf[1][:, cs_], in_=bank[:], func=AF.Copy), reads=[bkey], writes=["qkf1"])
                            if n == 3:
                                rotary(qkf[1], "qkf1", outs["k"][sl], "o_k%d" % sl, ti % 3)
                        else:
                            nm_, eng = {4: ("v", "dve"), 5: ("v", "dve"), 6: ("g", "act"), 7: ("g", "act"), 8: ("u", "dve"), 9: ("u", "dve"),
                                        10: ("gr", "act"), 11: ("gr", "act"), 12: ("gf", "dve"), 13: ("gf", "dve")}[n]
                            if nm_ == "u":
                                dst, dkey = ub, "ub"
                            elif nm_ == "v":
                                dst, dkey = outs["v"][sl], "o_v%d" % sl
                            else:
                                dst, dkey = outs[nm_][0], "o_%s0" % nm_
                            if eng == "act":
                                P.op("act", lambda e, bank=bank, cs_=cs_, dst=dst: e.activation(out=dst[:, cs_], in_=bank[:], func=AF.Copy), reads=[bkey], writes=[dkey])
                            else:
                                P.op("dve", lambda e, bank=bank, cs_=cs_, dst=dst: e.tensor_copy(out=dst[:, cs_], in_=bank[:]), reads=[bkey], writes=[dkey])
                    if mine:
                        for nm_, dd in (("q", Qd), ("k", Kd), ("v", Vd)):
                            P.op("sp", lambda e, nm_=nm_, dd=dd: e.dma_start(out=dd[r0:r0 + 128, :], in_=outs[nm_][sl][:]), reads=["o_%s%d" % (nm_, sl)], writes=["dram_" + nm_], dma=True)
                        for nm_, dd in (("g", Gd), ("gr", GRd), ("gf", GFd)):
                            P.op("sp", lambda e, nm_=nm_, dd=dd: e.dma_start(out=dd[r0:r0 + 128, :], in_=outs[nm_][0][:]), reads=["o_%s0" % nm_], writes=["dram_" + nm_], dma=True)
                    else:
                        for nm_, dd in (("k", KOd), ("v", VOd)):
                            P.op("sp", lambda e, nm_=nm_, dd=dd: e.dma_start(out=dd[r0:r0 + 128, :], in_=outs[nm_][sl][:]), reads=["o_%s%d" % (nm_, sl)], writes=["dram_o" + nm_], dma=True)

                p1_load_x(0)
                p1_load_rt(0)
                p1_load_x(1)
                for f_ in p1_A_stages(0):
                    f_()
                for ti in range(NM + NO):
                    if ti + 2 < NM + NO:
                        p1_load_x(ti + 2)
                    if ti + 1 < NM + NO:
                        p1_load_rt(ti + 1)
                    p1_compute(ti)
                for f_ in p1_tail_stages(NM + NO - 1):
                    f_()
                P.flush()

        if "p2" in phases:
            with contextlib.ExitStack() as es:
                sb = lambda n, s, d: es.enter_context(nc.sbuf_tensor(n, s, d))
                ps = lambda n, s, d: es.enter_context(nc.psum_tensor(n, s, d))
                dec = sb("dec_s", [128, 12], F32)
                lg = sb("lg", [128, 12], F32)
                gch = sb("gch", [128, 12], F32)
                DT = [sb("DT%d" % i, [128, 4, 128], F32) for i in range(2)]
                xi = [sb("xi%d" % i, [128, 8, 128], F32) for i in range(2)]
                zeta = [sb("zeta%d" % i, [128, 4, 256], F32) for i in range(3)]
                tmpd = sb("tmpd", [128, 128], F32)
                S = [sb("S%d" % i, [128, 8, 256], F32) for i in range(3)]
                Sbf = [sb("Sbf%d" % i, [128, 8, 256], BF16) for i in range(2)]
                qs = [[sb("q2_%d%d" % (d_, i), [128, D], BF16) for i in range(2)] for d_ in range(2)]
                ks = [[sb("k2_%d%d" % (d_, i), [128, D], BF16) for i in range(2)] for d_ in range(2)]
                vs = [[sb("v2_%d%d" % (d_, i), [128, D], BF16) for i in range(2)] for d_ in range(2)]
                qT = [sb("qT%d" % i, [128, D], BF16) for i in range(2)]
                qxT = [sb("qxT%d" % i, [128, D], BF16) for i in range(2)]
                kT = [sb("kT%d" % i, [128, D], BF16) for i in range(2)]
                kz = [sb("kz%d" % i, [128, D], BF16) for i in range(2)]
                PT = [sb("PT%d" % i, [128, 512], BF16) for i in range(2)]
                ysb = [sb("ysb%d" % i, [128, D], F32) for i in range(2)]
                pTt = [ps("pTt%d" % i, [128, D], BF16) for i in range(2)]
                pS = [ps("pS%d" % i, [128, 512], F32) for i in range(2)]
                py = [ps("py%d" % i, [128, 512], F32) for i in range(2)]
                pst = [ps("pst%d" % i, [128, 512], F32) for i in range(2)]

                P.op("sp", lambda e: e.dma_start(out=dec[:], in_=dec_d[:, :]), writes=["dec"], dma=True)
                P.op("act", lambda e: e.activation(out=lg[:], in_=dec[:], func=AF.Exp, scale=-1.0), reads=["dec"], writes=["lg"])
                P.op("act", lambda e: e.activation(out=lg[:], in_=lg[:], func=AF.Ln, bias=one_ap, scale=1.0), reads=["lg", "dk"], writes=["lg"])
                P.op("dve", lambda e: e.tensor_scalar(out=lg[:], in0=lg[:], scalar1=-1.0, scalar2=None, op0=ALU.mult), reads=["lg"], writes=["lg"])
                P.op("act", lambda e: e.activation(out=gch[:], in_=lg[:], func=AF.Exp, scale=128.0), reads=["lg"], writes=["gch"])
                for d_ in range(2):
                    oe, om, ox = (O_E0F, O_MF, O_XF) if d_ == 0 else (O_E0B, O_MB, O_XB)
                    for hd in range(4):
                        col = 4 * d_ + hd
                        P.op("act", lambda e, oe=oe, col=col: e.activation(out=tmpd[:], in_=dk[:, oe:oe + 128], func=AF.Exp, scale=lg[:, col:col + 1]), reads=["dk", "lg"], writes=["tmpd"])
                        P.op("dve", lambda e, om=om, d_=d_, hd=hd: e.tensor_tensor(out=DT[d_][:, hd, :], in0=tmpd[:], in1=dk[:, om:om + 128], op=ALU.mult), reads=["tmpd", "dk"], writes=["DT%d" % d_])
                        for hf in range(2):
                            P.op("act", lambda e, ox=ox, col=col, d_=d_, hd=hd, hf=hf: e.activation(out=xi[d_][:, 2 * hd + hf, :], in_=dk[:, ox:ox + 128], func=AF.Exp, scale=lg[:, col:col + 1]), reads=["dk", "lg"], writes=["xi%d" % d_])
                for z_, oz in enumerate((O_ZF, O_ZB, O_ZO)):
                    for hd in range(4):
                        col = 4 * z_ + hd
                        P.op("act", lambda e, z_=z_, oz=oz, hd=hd, col=col: e.activation(out=zeta[z_][:, hd, :], in_=dk[:, oz:oz + 256], func=AF.Exp, scale=lg[:, col:col + 1]), reads=["dk", "lg"], writes=["zeta%d" % z_])
                P.op("dve", lambda e: e.memset(S[2][:], 0.0), writes=["S2"])
                import os
                P2STOP = int(os.environ.get("P2STOP", "9"))

                def state_head(si, kzt, kzkey, vt, vkey, gcol0, bank, bkey, hd):
                    for hf in range(2):
                        P.op("pe", lambda e, hf=hf: e.matmul(bank[:, hf * 256:(hf + 1) * 256], lhsT=kzt[:, hd * 256 + hf * 128:hd * 256 + hf * 128 + 128], rhs=vt[:, hd * 256:(hd + 1) * 256], start=True, stop=True),
                             reads=[kzkey, vkey], writes=[bkey])
                    sview = S[si][:, 2 * hd:2 * hd + 2, :].rearrange("p a b -> p (a b)")
                    P.op("dve", lambda e: e.scalar_tensor_tensor(out=sview, in0=sview, scalar=gch[:, gcol0 + hd:gcol0 + hd + 1], in1=bank[:], op0=ALU.mult, op1=ALU.add),
                         reads=["S%d" % si, "gch", bkey], writes=["S%d" % si])

                def state_update(si, kzt, kzkey, vt, vkey, gcol0, bset):
                    for hd in range(4):
                        bank, bkey = ((pst[bset], "pst%d" % bset) if hd % 2 == 0 else (py[bset], "py%d" % bset))
                        state_head(si, kzt, kzkey, vt, vkey, gcol0, bank, bkey, hd)

                def p2a_load(j):
                    sl = j % 2
                    P.op("sp", lambda e: e.dma_start(out=ks[0][sl][:], in_=KOd[j * 128:(j + 1) * 128, :]), writes=["k2_0%d" % sl], dma=True)
                    P.op("sp", lambda e: e.dma_start(out=vs[0][sl][:], in_=VOd[j * 128:(j + 1) * 128, :]), writes=["v2_0%d" % sl], dma=True)
                if P2STOP >= 2:
                    p2a_load(0)
                for j in range(NO if P2STOP >= 2 else 0):
                    if j + 1 < NO:
                        p2a_load(j + 1)
                    sl = j % 2
                    bs_ = j % 2
                    P.op("pool", lambda e, sl=sl, bs_=bs_: e.tensor_tensor(out=kz[bs_][:], in0=ks[0][sl][:], in1=zeta[2][:].rearrange("p a b -> p (a b)"), op=ALU.mult), reads=["k2_0%d" % sl, "zeta2"], writes=["kz%d" % bs_])
                    state_update(2, kz[bs_], "kz%d" % bs_, vs[0][sl], "v2_0%d" % sl, 8, bs_)
                P.op("dve", lambda e: e.tensor_scalar(out=S[0][:], in0=S[2][:], scalar1=dk[:, O_MSKF:O_MSKF + 1], scalar2=None, op0=ALU.mult), reads=["S2", "dk"], writes=["S0"])
                P.op("dve", lambda e: e.tensor_scalar(out=S[1][:], in0=S[2][:], scalar1=dk[:, O_MSKB:O_MSKB + 1], scalar2=None, op0=ALU.mult), reads=["S2", "dk"], writes=["S1"])
                for d_ in range(2):
                    P.op("act", lambda e, d_=d_: e.activation(out=Sbf[d_][:], in_=S[d_][:], func=AF.Copy), reads=["S%d" % d_], writes=["Sbf%d" % d_])
                P.flush()

                def p2_load(d_, c, sl):
                    r0 = c * 128
                    P.op("sp", lambda e: e.dma_start(out=qs[d_][sl][:], in_=Qd[r0:r0 + 128, :]), writes=["q2_%d%d" % (d_, sl)], dma=True)
                    P.op("sp", lambda e: e.dma_start(out=ks[d_][sl][:], in_=Kd[r0:r0 + 128, :]), writes=["k2_%d%d" % (d_, sl)], dma=True)
                    P.op("sp", lambda e: e.dma_start(out=vs[d_][sl][:], in_=Vd[r0:r0 + 128, :]), writes=["v2_%d%d" % (d_, sl)], dma=True)

                def p2_stages(d_, c, sl):
                    q_, k_, v_ = qs[d_][sl], ks[d_][sl], vs[d_][sl]
                    qk_, kk_, vk_ = "q2_%d%d" % (d_, sl), "k2_%d%d" % (d_, sl), "v2_%d%d" % (d_, sl)
                    ds = str(d_)
                    T_, kT_ = pTt[d_], "pTt" + ds

                    def s0():
                        transposes8(q_, qk_, T_, kT_)
                        P.op("act", lambda e: e.activation(out=qT[d_][:], in_=T_[:], func=AF.Copy), reads=[kT_], writes=["qT" + ds])
                        P.op("dve", lambda e: e.tensor_tensor(out=qxT[d_][:], in0=qT[d_][:], in1=xi[d_][:].rearrange("p a b -> p (a b)"), op=ALU.mult), reads=["qT" + ds, "xi" + ds], writes=["qxT" + ds])

                    def s1():
                        transposes8(k_, kk_, T_, kT_)
                        P.op("act", lambda e: e.activation(out=kT[d_][:], in_=T_[:], func=AF.Copy), reads=[kT_], writes=["kT" + ds])
                        P.op("pool", lambda e: e.tensor_tensor(out=kz[d_][:], in0=k_[:], in1=zeta[d_][:].rearrange("p a b -> p (a b)"), op=ALU.mult), reads=[kk_, "zeta" + ds], writes=["kz" + ds])

                    def s2():
                        for hd in range(4):
                            for hf in range(2):
                                m = 2 * hd + hf
                                P.op("pe", lambda e, hd=hd, hf=hf, m=m: e.matmul(pS[d_][:, hd * 128:(hd + 1) * 128], lhsT=kT[d_][:, m * 128:(m + 1) * 128], rhs=qT[d_][:, m * 128:(m + 1) * 128], start=(hf == 0), stop=(hf == 1)),
                                     reads=["kT" + ds, "qT" + ds], writes=["pS" + ds])
                        P.op("dve", lambda e: e.tensor_tensor(out=PT[d_][:], in0=pS[d_][:], in1=DT[d_][:].rearrange("p a b -> p (a b)"), op=ALU.mult), reads=["pS" + ds, "DT" + ds], writes=["PT" + ds])

                    def ystage(half):
                        bank, bkey = py[d_], "py" + ds
                        for hd in (2 * half, 2 * half + 1):
                            co = (hd % 2) * 256
                            P.op("pe", lambda e, hd=hd, co=co: e.matmul(bank[:, co:co + 256], lhsT=PT[d_][:, hd * 128:(hd + 1) * 128], rhs=v_[:, hd * 256:(hd + 1) * 256], start=True, stop=False),
                                 reads=["PT" + ds, vk_], writes=[bkey])
                            for hf in range(2):
                                m = 2 * hd + hf
                                P.op("pe", lambda e, m=m, hf=hf, co=co: e.matmul(bank[:, co:co + 256], lhsT=qxT[d_][:, m * 128:(m + 1) * 128], rhs=Sbf[d_][:, m, :], start=False, stop=(hf == 1)),
                                     reads=["qxT" + ds, "Sbf" + ds], writes=[bkey])
                        if half == 0:
                            P.op("act", lambda e: e.activation(out=ysb[d_][:, 0:512], in_=bank[:], func=AF.Copy), reads=[bkey], writes=["ysb" + ds])
                        else:
                            P.op("dve", lambda e: e.tensor_copy(out=ysb[d_][:, 512:1024], in_=bank[:]), reads=[bkey], writes=["ysb" + ds])
                            yd = YFd if d_ == 0 else YBd
                            P.op("sp", lambda e: e.dma_start(out=yd[c * 128:(c + 1) * 128, :], in_=ysb[d_][:]), reads=["ysb" + ds], writes=["dram_y" + ds], dma=True)

                    def sh(hd):
                        bank, bkey = ((pst[d_], "pst" + ds) if hd % 2 == 0 else (py[d_], "py" + ds))
                        state_head(d_, kz[d_], "kz" + ds, v_, vk_, 4 * d_, bank, bkey, hd)
                        if hd % 2 == 1:
                            hh = hd // 2
                            P.op("act", lambda e: e.activation(out=Sbf[d_][:, 4 * hh:4 * hh + 4, :], in_=S[d_][:, 4 * hh:4 * hh + 4, :], func=AF.Copy), reads=["S" + ds], writes=["Sbf" + ds])

                    return [s0, s1, s2, lambda: ystage(0), lambda: ystage(1), lambda: sh(0), lambda: sh(1), lambda: sh(2), lambda: sh(3)]

                if P2STOP >= 3:
                    p2_load(0, 0, 0)
                    p2_load(1, NM - 1, 0)
                for t in range(NM if P2STOP >= 3 else 0):
                    sl = t % 2
                    if t + 1 < NM:
                        p2_load(0, t + 1, 1 - sl)
                        p2_load(1, NM - 2 - t, 1 - sl)
                    sf_ = p2_stages(0, t, sl)
                    sb_ = p2_stages(1, NM - 1 - t, sl)
                    for a_, b_ in zip(sf_, sb_):
                        a_()
                        b_()
                P.flush()

        if "p3" in phases:
            with contextlib.ExitStack() as es:
                sb = lambda n, s, d: es.enter_context(nc.sbuf_tensor(n, s, d))
                ps = lambda n, s, d: es.enter_context(nc.psum_tensor(n, s, d))
                ZL = sb("ZL", [128, 2, 128, 128], BF16)
                T = sb("Tt", [128, 128, 256], BF16)
                mts = sb("mts", [128, 128, 2, 64], BF16)
                ufs = sb("ufs", [128, 64, 128], BF16)
                fcs = sb("fcs_s", [128, 2, 256], BF16)
                pb = [ps("pb%d" % i, [128, 512], F32) for i in range(4)]
                pc = [ps("pc%d" % i, [128, 512], F32) for i in range(4)]
                P.op("sp", lambda e: e.dma_start(out=fcs[:], in_=fcs_d[:, :, :]), writes=["fcs"], dma=True)
                for dq in range(4):
                    P.op("sp", lambda e, dq=dq: e.dma_start(out=mts[:, dq * 32:(dq + 1) * 32, :, :], in_=mt_d[:, dq * 32:(dq + 1) * 32, :, :]), writes=["mts"], dma=True)
                for s in range(8):
                    for ri in range(2):
                        for hb in range(2):
                            P.op("sp", lambda e, s=s, ri=ri, hb=hb: e.dma_start(
                                out=ZL[:, ri, hb * 64:(hb + 1) * 64, :],
                                in_=Zd[s // 2, ri, s % 2, :, :].rearrange("(l b) c -> l b c", b=128)[:, hb * 64:(hb + 1) * 64, :]),
                                reads=["Zd"], writes=["ZL"], dma=True)
                    for cp in range(64):
                        bank, bkey = pb[cp % 4], "pb%d" % (cp % 4)
                        for q_ in range(2):
                            ch = 2 * cp + q_
                            P.op("pe", lambda e, ch=ch, q_=q_, bank=bank: e.matmul(bank[:, q_ * 256:(q_ + 1) * 256], lhsT=ZL[:, 0, :, ch], rhs=fcs[:, 0, :], start=True, stop=False),
                                 reads=["ZL", "fcs"], writes=[bkey])
                            P.op("pe", lambda e, ch=ch, q_=q_, bank=bank: e.matmul(bank[:, q_ * 256:(q_ + 1) * 256], lhsT=ZL[:, 1, :, ch], rhs=fcs[:, 1, :], start=False, stop=True),
                                 reads=["ZL", "fcs"], writes=[bkey])
                        tv = T[:, 2 * cp:2 * cp + 2, :].rearrange("p a b -> p (a b)")
                        if cp % 2 == 0:
                            P.op("act", lambda e, bank=bank, tv=tv: e.activation(out=tv, in_=bank[:], func=AF.Copy), reads=[bkey], writes=["Tt"])
                        else:
                            P.op("dve", lambda e, bank=bank, tv=tv: e.tensor_copy(out=tv, in_=bank[:]), reads=[bkey], writes=["Tt"])
                    for db in range(16):
                        bank, bkey = pc[db % 4], "pc%d" % (db % 4)
                        for dd in range(8):
                            d_ = db * 8 + dd
                            P.op("pe", lambda e, d_=d_, dd=dd, bank=bank: e.matmul(bank[:, dd * 64:(dd + 1) * 64], lhsT=T[:, :, d_], rhs=mts[:, d_, 0, :], start=True, stop=False),
                                 reads=["Tt", "mts"], writes=[bkey])
                            P.op("pe", lambda e, d_=d_, dd=dd, bank=bank: e.matmul(bank[:, dd * 64:(dd + 1) * 64], lhsT=T[:, :, 128 + d_], rhs=mts[:, d_, 1, :], start=False, stop=True),
                                 reads=["Tt", "mts"], writes=[bkey])
                        ov = ufs[:, :, db * 8:(db + 1) * 8]
                        iv = bank[:].rearrange("p (dd c) -> p c dd", dd=8)
                        if db % 2 == 0:
                            P.op("act", lambda e, ov=ov, iv=iv: e.activation(out=ov, in_=iv, func=AF.Copy, scale=1.0 / 2048.0), reads=[bkey], writes=["ufs"])
                        else:
                            P.op("dve", lambda e, ov=ov, iv=iv: e.tensor_scalar(out=ov, in0=iv, scalar1=1.0 / 2048.0, scalar2=None, op0=ALU.mult), reads=[bkey], writes=["ufs"])
                    P.op("sp", lambda e, s=s: e.dma_start(out=UFd[s * 128:(s + 1) * 128, :], in_=ufs[:].rearrange("p c d -> p (c d)")), reads=["ufs"], writes=["UFd"], dma=True)
                P.flush()

        if "p4" in phases or "p4a" in phases:
            with contextlib.ExitStack() as es:
                sb = lambda n, s, d: es.enter_context(nc.sbuf_tensor(n, s, d))
                ps = lambda n, s, d: es.enter_context(nc.psum_tensor(n, s, d))
                wck = sb("wck", [128, 8, D], BF16)
                wcv = sb("wcv", [128, 8, D], BF16)
                nmem = sb("nmem_s", [128, D], F32)
                ms = [sb("ms%d" % i, [128, D], F32) for i in range(2)]
                st = sb("st0", [128, 8], F32)
                mn = sb("mn", [128, D], BF16)
                mnT = sb("mnT", [128, 8, NMEM], BF16)
                pT = ps("pT0", [128, D], BF16)
                pm = [ps("pm0_%d" % i, [128, 512], F32) for i in range(4)]
                for wsb, wd, nm_ in ((wck, w_ck, "wck"), (wcv, w_cv, "wcv")):
                    for kq in range(4):
                        P.op("pool", lambda e, wsb=wsb, wd=wd, kq=kq: e.dma_start(out=wsb[:, 2 * kq:2 * kq + 2, :], in_=wd[kq * 256:(kq + 1) * 256, :].rearrange("(k p) n -> p k n", p=128)), writes=[nm_], dma=True)
                P.op("sp", lambda e: e.dma_start(out=nmem[:], in_=nmem_d[:, :]), writes=["nmem"], dma=True)
                for t in range(2):
                    P.op("sp", lambda e, t=t: e.dma_start(out=ms[t][:], in_=mem[t * 128:(t + 1) * 128, :]), writes=["ms%d" % t], dma=True)
                for t in range(2):
                    rstd_ops(st, "st0", ms[t][:], "ms%d" % t, mn[:], "mn")
                    P.op("dve", lambda e, t=t: e.scalar_tensor_tensor(out=mn[:], in0=ms[t][:], scalar=st[:, 2:3], in1=nmem[:], op0=ALU.mult, op1=ALU.mult), reads=["ms%d" % t, "st0", "nmem"], writes=["mn"])
                    transposes8(mn, "mn", pT, "pT0")
                    P.op("dve", lambda e, t=t: e.tensor_copy(out=mnT[:, :, t * 128:(t + 1) * 128], in_=pT[:].rearrange("p (k c) -> p k c", k=8)), reads=["pT0"], writes=["mnT"])
                bi0 = 0
                for m in range(8):
                    bank, bkey = pm[bi0 % 4], "pm0_%d" % (bi0 % 4)
                    bi0 += 1
                    for k in range(8):
                        P.op("pe", lambda e, m=m, k=k, bank=bank: e.matmul(bank[:, 0:NMEM], lhsT=wck[:, k, m * 128:(m + 1) * 128], rhs=mnT[:, k, :], start=(k == 0), stop=(k == 7)), reads=["wck", "mnT"], writes=[bkey])
                    P.op("act", lambda e, m=m, bank=bank: e.activation(out=ckT[:, m, :], in_=bank[:, 0:NMEM], func=AF.Copy), reads=[bkey], writes=["ckT"])
                for t in range(2):
                    for n in range(2):
                        bank, bkey = pm[bi0 % 4], "pm0_%d" % (bi0 % 4)
                        bi0 += 1
                        for k in range(8):
                            P.op("pe", lambda e, t=t, n=n, k=k, bank=bank: e.matmul(bank[:], lhsT=mnT[:, k, t * 128:(t + 1) * 128], rhs=wcv[:, k, n * 512:(n + 1) * 512], start=(k == 0), stop=(k == 7)), reads=["wcv", "mnT"], writes=[bkey])
                        P.op("dve", lambda e, t=t, n=n, bank=bank: e.tensor_copy(out=cv[:, t, n * 512:(n + 1) * 512], in_=bank[:]), reads=[bkey], writes=["cv"])
                P.flush()

            with contextlib.ExitStack() as es:
                sb = lambda n, s, d: es.enter_context(nc.sbuf_tensor(n, s, d))
                ps = lambda n, s, d: es.enter_context(nc.psum_tensor(n, s, d))
                wts = {}
                for nm_, wd in (("wro", w_ro), ("w4", w_4), ("wmx", w_mx), ("wcq", w_cq), ("wco", w_co)):
                    wts[nm_] = sb(nm_, [128, 8, D], BF16)
                    for kq in range(4):
                        P.op("pool", lambda e, nm_=nm_, wd=wd, kq=kq: e.dma_start(out=wts[nm_][:, 2 * kq:2 * kq + 2, :], in_=wd[kq * 256:(kq + 1) * 256, :].rearrange("(k p) n -> p k n", p=128)), writes=[nm_], dma=True)
                gnw = sb("gnw_s", [128, D], F32)
                nca = sb("nca_s", [128, D], F32)
                P.op("sp", lambda e: e.dma_start(out=gnw[:], in_=gnw_d[:, :]), writes=["gnw"], dma=True)
                P.op("sp", lambda e: e.dma_start(out=nca[:], in_=nca_d[:, :]), writes=["nca"], dma=True)
                yf = [sb("yf%d" % i, [128, D], F32) for i in range(2)]
                yb = [sb("yb%d" % i, [128, D], F32) for i in range(2)]
                xt = [sb("xt%d" % i, [128, D], F32) for i in range(2)]
                gt = [sb("gt%d" % i, [128, D], BF16) for i in range(2)]
                grt = [sb("grt%d" % i, [128, D], BF16) for i in range(2)]
                gft = [sb("gft%d" % i, [128, D], BF16) for i in range(2)]
                uft = [sb("uft%d" % i, [128, 8, 128], BF16) for i in range(2)]
                SETS = []
                for i in range(2):
                    SETS.append(dict(
                        A=sb("A4_%d" % i, [128, D], F32), B=sb("B4_%d" % i, [128, D], F32), C=sb("C4_%d" % i, [128, D], F32),
                        x1=sb("x1_%d" % i, [128, D], F32), x2=sb("x2_%d" % i, [128, D], F32),
                        b=[sb("b4_%d_%d" % (i, j), [128, D], BF16) for j in range(4)],
                        bs=sb("bs4_%d" % i, [128, 4, 6], F32), mv=sb("mv4_%d" % i, [128, 4, 2], F32),
                        sm=sb("sm4_%d" % i, [128, 16], F32), st=sb("st4_%d" % i, [128, 8], F32),
                        T=ps("pT4_%d" % i, [128, D], BF16), X=[ps("pX%d_%d" % (j, i), [128, 512], F32) for j in range(2)],
                        Y=ps("pY_%d" % i, [128, 512], F32)))
                UFv = UFd.rearrange("(s p) t -> p s t", p=128)

                def p4a_load(c):
                    sl = c % 2
                    r0 = c * 128
                    for tl, dd, nm_ in ((yf, YFd, "yf"), (yb, YBd, "yb"), (gt, Gd, "gt")):
                        P.op("sp", lambda e, tl=tl, dd=dd: e.dma_start(out=tl[sl][:], in_=dd[r0:r0 + 128, :]), writes=["%s%d" % (nm_, sl)], dma=True)
                    P.op("sp", lambda e: e.dma_start(out=uft[sl][:], in_=UFv[:, :, r0:r0 + 128]), writes=["uft%d" % sl], dma=True)
                    for tl, dd, nm_ in ((grt, GRd, "grt"), (gft, GFd, "gft"), (xt, xm, "xt")):
                        P.op("sp", lambda e, tl=tl, dd=dd: e.dma_start(out=tl[sl][:], in_=dd[r0:r0 + 128, :]), writes=["%s%d" % (nm_, sl)], dma=True)

                def p4a_stages(c):
                    i = c % 2
                    sl = i
                    S_ = SETS[i]
                    A, B, Cc, x1, x2, bb, bs, mv, sm, st, T_, X, Y = (S_[k_] for k_ in ("A", "B", "C", "x1", "x2", "b", "bs", "mv", "sm", "st", "T", "X", "Y"))
                    s_ = str(i)
                    kA, kB, kC, kx1, kx2, kbs, kmv, ksm, kst, kT, kY = ("A4_" + s_, "B4_" + s_, "C4_" + s_, "x1_" + s_, "x2_" + s_, "bs4_" + s_, "mv4_" + s_, "sm4_" + s_, "st4_" + s_, "pT4_" + s_, "pY_" + s_)
                    kX = ["pX0_" + s_, "pX1_" + s_]
                    kb = ["b4_%s_%d" % (s_, j) for j in range(4)]
                    r0 = c * 128

                    def mm16(lhs_fn, lkey, w, wkey):
                        for n in range(2):
                            for k in range(8):
                                P.op("pe", lambda e, n=n, k=k: e.matmul(X[n][:], lhsT=lhs_fn(k), rhs=w[:, k, n * 512:(n + 1) * 512], start=(k == 0), stop=(k == 7)),
                                     reads=[lkey, wkey], writes=[kX[n]])

                    def tr(src, skey, dst, dkey, eng):
                        for k in range(8):
                            P.op("pe", lambda e, k=k: e.transpose(out=T_[:, k * 128:(k + 1) * 128], in_=src[:, k * 128:(k + 1) * 128], identity=ident[:]), reads=[skey, "ident"], writes=[kT])
                        if eng == "act":
                            P.op("act", lambda e: e.activation(out=dst[:], in_=T_[:], func=AF.Copy), reads=[kT], writes=[dkey])
                        else:
                            P.op("dve", lambda e: e.tensor_copy(out=dst[:], in_=T_[:]), reads=[kT], writes=[dkey])

                    def s_gn1():
                        P.op("dve", lambda e: e.tensor_tensor(out=A[:], in0=yf[sl][:], in1=yb[sl][:], op=ALU.add), reads=["yf" + s_, "yb" + s_], writes=[kA])
                        for hd in range(4):
                            P.op("dve", lambda e, hd=hd: e.bn_stats(out=bs[:, hd, :], in_=A[:, hd * 256:(hd + 1) * 256]), reads=[kA], writes=[kbs])
                            P.op("dve", lambda e, hd=hd: e.bn_aggr(out=mv[:, hd, :], in_=bs[:, hd, :]), reads=[kbs], writes=[kmv])
                        P.op("dve", lambda e: e.tensor_scalar(out=sm[:, 0:4], in0=mv[:, :, 1], scalar1=eps5, scalar2=None, op0=ALU.add), reads=[kmv, "dk"], writes=[ksm])
                        P.op("act", lambda e: e.activation(out=sm[:, 0:4], in_=sm[:, 0:4], func=AF.Ln), reads=[ksm], writes=[ksm])
                        P.op("act", lambda e: e.activation(out=sm[:, 4:8], in_=sm[:, 0:4], func=AF.Exp, scale=-0.5), reads=[ksm], writes=[ksm])
                        P.op("act", lambda e: e.activation(out=Cc[:], in_=gt[sl][:], func=AF.Sigmoid), reads=["gt" + s_], writes=[kC])
                        P.op("dve", lambda e: e.tensor_tensor(out=Cc[:], in0=Cc[:], in1=gt[sl][:], op=ALU.mult), reads=[kC, "gt" + s_], writes=[kC])

                    def s_gn2():
                        for hd in range(4):
                            P.op("dve", lambda e, hd=hd: e.tensor_scalar(out=B[:, hd * 256:(hd + 1) * 256], in0=A[:, hd * 256:(hd + 1) * 256], scalar1=mv[:, hd, 0:1], scalar2=sm[:, 4 + hd:5 + hd], op0=ALU.subtract, op1=ALU.mult),
                                 reads=[kA, kmv, ksm], writes=[kB])
                        P.op("dve", lambda e: e.tensor_tensor(out=B[:], in0=B[:], in1=gnw[:], op=ALU.mult), reads=[kB, "gnw"], writes=[kB])
                        P.op("dve", lambda e: e.tensor_tensor(out=bb[0][:], in0=B[:], in1=Cc[:], op=ALU.mult), reads=[kB, kC], writes=[kb[0]])
                        P.op("act", lambda e: e.activation(out=A[:], in_=grt[sl][:], func=AF.Sigmoid), reads=["grt" + s_], writes=[kA])
                        P.op("act", lambda e: e.activation(out=B[:], in_=gft[sl][:], func=AF.Sigmoid), reads=["gft" + s_], writes=[kB])

                    def s_tr_r():
                        tr(bb[0], kb[0], bb[1], kb[1], "act")

                    def s_ret():
                        mm16(lambda k: bb[1][:, k * 128:(k + 1) * 128], kb[1], wts["wro"], "wro")

                    def s_four():
                        for n in range(2):
                            cs_ = slice(n * 512, (n + 1) * 512)
                            for k in range(8):
                                P.op("pe", lambda e, n=n, k=k: e.matmul(Y[:], lhsT=uft[sl][:, k, :], rhs=wts["w4"][:, k, n * 512:(n + 1) * 512], start=(k == 0), stop=(k == 7)), reads=["uft" + s_, "w4"], writes=[kY])
                            P.op("dve", lambda e, cs_=cs_: e.tensor_tensor(out=B[:, cs_], in0=Y[:], in1=B[:, cs_], op=ALU.mult), reads=[kY, kB], writes=[kB])

                    def s_merge():
                        for n in range(2):
                            cs_ = slice(n * 512, (n + 1) * 512)
                            P.op("dve", lambda e, n=n, cs_=cs_: e.tensor_tensor(out=A[:, cs_], in0=X[n][:], in1=A[:, cs_], op=ALU.mult), reads=[kX[n], kA], writes=[kA])
                        P.op("dve", lambda e: e.tensor_tensor(out=bb[2][:], in0=A[:], in1=B[:], op=ALU.add), reads=[kA, kB], writes=[kb[2]])

                    def s_tr_m():
                        tr(bb[2], kb[2], bb[3], kb[3], "act")

                    def s_mix():
                        mm16(lambda k: bb[3][:, k * 128:(k + 1) * 128], kb[3], wts["wmx"], "wmx")
                        for n in range(2):
                            cs_ = slice(n * 512, (n + 1) * 512)
                            P.op("dve", lambda e, n=n, cs_=cs_: e.tensor_tensor(out=x1[:, cs_], in0=X[n][:], in1=xt[sl][:, cs_], op=ALU.add), reads=[kX[n], "xt" + s_], writes=[kx1])

                    def s_norm():
                        rstd_pow(st, kst, x1[:], kx1, Cc[:], kC)
                        P.op("dve", lambda e: e.scalar_tensor_tensor(out=bb[0][:], in0=x1[:], scalar=st[:, 2:3], in1=nca[:], op0=ALU.mult, op1=ALU.mult), reads=[kx1, kst, "nca"], writes=[kb[0]])

                    def s_tr_x():
                        tr(bb[0], kb[0], bb[1], kb[1], "dve")

                    def s_hq():
                        for m in range(8):
                            bank, bkey = X[m // 4], kX[m // 4]
                            co = (m % 4) * 128
                            for k in range(8):
                                P.op("pe", lambda e, m=m, k=k, bank=bank, co=co: e.matmul(bank[:, co:co + 128], lhsT=wts["wcq"][:, k, m * 128:(m + 1) * 128], rhs=bb[1][:, k * 128:(k + 1) * 128], start=(k == 0), stop=(k == 7)),
                                     reads=["wcq", kb[1]], writes=[bkey])
                        P.op("act", lambda e: e.activation(out=bb[2][:, 0:512], in_=X[0][:], func=AF.Copy), reads=[kX[0]], writes=[kb[2]])
                        P.op("dve", lambda e: e.tensor_copy(out=bb[2][:, 512:1024], in_=X[1][:]), reads=[kX[1]], writes=[kb[2]])

                    def s_logits():
                        for hd in range(4):
                            bank, bkey = X[hd // 2], kX[hd // 2]
                            co = (hd % 2) * 256
                            for hf in range(2):
                                m = 2 * hd + hf
                                P.op("pe", lambda e, m=m, hf=hf, bank=bank, co=co: e.matmul(bank[:, co:co + 256], lhsT=bb[2][:, m * 128:(m + 1) * 128], rhs=ckT[:, m, :], start=(hf == 0), stop=(hf == 1)),
                                     reads=[kb[2], "ckT"], writes=[bkey])

                    def s_softmax():
                        P.op("dve", lambda e: e.memset(sm[:, 8:16], 0.0), writes=[ksm])
                        for n in range(2):
                            P.op("dve", lambda e, n=n: e.tensor_reduce(out=sm[:, 2 * n:2 * n + 2], in_=X[n][:].rearrange("p (a b) -> p a b", a=2), axis=AX.X, op=ALU.max), reads=[kX[n], ksm], writes=[ksm])
                        P.op("dve", lambda e: e.tensor_scalar(out=sm[:, 4:8], in0=sm[:, 0:4], scalar1=-1.0 / 16.0, scalar2=None, op0=ALU.mult), reads=[ksm], writes=[ksm])
                        for hd in range(4):
                            bank, bkey = X[hd // 2], kX[hd // 2]
                            co = (hd % 2) * 256
                            P.op("act", lambda e, hd=hd, bank=bank, co=co: e.activation(out=Cc[:, hd * 256:(hd + 1) * 256], in_=bank[:, co:co + 256], func=AF.Exp, scale=1.0 / 16.0, bias=sm[:, 4 + hd:5 + hd], accum_out=sm[:, 8 + hd:9 + hd]),
                                 reads=[bkey, ksm], writes=[kC, ksm])
                        P.op("dve", lambda e: e.reciprocal(out=sm[:, 12:16], in_=sm[:, 8:12]), reads=[ksm], writes=[ksm])
                        for hd in range(4):
                            eng = "dve"
                            P.op(eng, lambda e, hd=hd: e.tensor_scalar(out=bb[3][:, hd * 256:(hd + 1) * 256], in0=Cc[:, hd * 256:(hd + 1) * 256], scalar1=sm[:, 12 + hd:13 + hd], scalar2=None, op0=ALU.mult),
                                 reads=[kC, ksm], writes=[kb[3]])

                    def s_tr_p():
                        tr(bb[3], kb[3], bb[0], kb[0], "act")

                    def s_att():
                        for m in range(8):
                            hd = m // 2
                            bank, bkey = X[m // 4], kX[m // 4]
                            co = (m % 4) * 128
                            for mc in range(2):
                                P.op("pe", lambda e, m=m, mc=mc, hd=hd, bank=bank, co=co: e.matmul(bank[:, co:co + 128], lhsT=cv[:, mc, m * 128:(m + 1) * 128], rhs=bb[0][:, (2 * hd + mc) * 128:(2 * hd + mc + 1) * 128], start=(mc == 0), stop=(mc == 1)),
                                     reads=["cv", kb[0]], writes=[bkey])
                        P.op("act", lambda e: e.activation(out=bb[1][:, 0:512], in_=X[0][:], func=AF.Copy), reads=[kX[0]], writes=[kb[1]])
                        P.op("dve", lambda e: e.tensor_copy(out=bb[1][:, 512:1024], in_=X[1][:]), reads=[kX[1]], writes=[kb[1]])

                    def s_co():
                        mm16(lambda k: bb[1][:, k * 128:(k + 1) * 128], kb[1], wts["wco"], "wco")
                        for n in range(2):
                            cs_ = slice(n * 512, (n + 1) * 512)
                            P.op("dve", lambda e, n=n, cs_=cs_: e.tensor_tensor(out=x2[:, cs_], in0=X[n][:], in1=x1[:, cs_], op=ALU.add), reads=[kX[n], kx1], writes=[kx2])
                        P.op("sp", lambda e: e.dma_start(out=X2d[r0:r0 + 128, :], in_=x2[:]), reads=[kx2], writes=["X2d"], dma=True)

                    return [s_gn1, s_gn2, s_tr_r, s_ret, s_four, s_merge, s_tr_m, s_mix, s_norm, s_tr_x, s_hq, s_logits, s_softmax, s_tr_p, s_att, s_co]

                NST = 16
                import os as _os
                SK = int(_os.environ.get("P4SKEW", "8"))
                p4a_load(0)
                active = []
                nxt = 0
                tick = 0
                while nxt < NM or active:
                    admit = (tick % NST == 0) or (tick % NST == SK)
                    if nxt < NM and admit:
                        if nxt + 1 < NM:
                            p4a_load(nxt + 1)
                        active.append([p4a_stages(nxt), 0])
                        nxt += 1
                    for a_ in active:
                        a_[0][a_[1]]()
                        a_[1] += 1
                    active = [a_ for a_ in active if a_[1] < NST]
                    tick += 1
                P.flush()

            with contextlib.ExitStack() as es:
                sb = lambda n, s, d: es.enter_context(nc.sbuf_tensor(n, s, d))
                ps = lambda n, s, d: es.enter_context(nc.psum_tensor(n, s, d))
                wup = sb("wup", [128, 8, DFF], BF16)
                wdn = sb("wdn", [128, 32, D], BF16)
                for k in range(8):
                    P.op("pool", lambda e, k=k: e.dma_start(out=wup[:, k, :], in_=w_up[k * 128:(k + 1) * 128, :]), writes=["wup"], dma=True)
                for kq in range(8):
                    P.op("pool", lambda e, kq=kq: e.dma_start(out=wdn[:, 4 * kq:4 * kq + 4, :], in_=w_dn[kq * 512:(kq + 1) * 512, :].rearrange("(k p) n -> p k n", p=128)), writes=["wdn"], dma=True)
                nmlp = sb("nmlp_s", [128, D], F32)
                nfin = sb("nfin_s", [128, D], F32)
                P.op("sp", lambda e: e.dma_start(out=nmlp[:], in_=nmlp_d[:, :]), writes=["nmlp"], dma=True)
                P.op("sp", lambda e: e.dma_start(out=nfin[:], in_=nfin_d[:, :]), writes=["nfin"], dma=True)
                xg = [sb("xg%d" % i, [128, 2, D], F32) for i in range(2)]
                xn = [sb("xn5_%d" % i, [128, D], BF16) for i in range(2)]
                xnT = [sb("xnT5_%d" % i, [128, 8, 256], BF16) for i in range(2)]
                hT = sb("hT", [128, 32, 256], BF16)
                sq = [sb("sq%d" % i, [128, 256], F32) for i in range(2)]
                ot = [sb("ot%d" % i, [128, D], F32) for i in range(2)]
                st = [sb("st5_%d" % i, [128, 8], F32) for i in range(3)]
                pT = ps("pT5", [128, D], BF16)
                pu = [ps("pu%d" % i, [128, 512], F32) for i in range(4)]
                pd = [ps("pd%d" % i, [128, 512], F32) for i in range(2)]

                def p4b_load(gi):
                    sl = gi % 2
                    P.op("sp", lambda e: e.dma_start(out=xg[sl][:], in_=X2d[gi * 256:(gi + 1) * 256, :].rearrange("(t p) d -> p t d", p=128)), writes=["xg%d" % sl], dma=True)

                oc = [0]

                def p4b_norm(gi):
                    sl = gi % 2
                    gk = "xg%d" % sl
                    for t in range(2):
                        rstd_ops(st[t], "st5_%d" % t, xg[sl][:, t, :], gk, xn[t][:], "xn5_%d" % t)
                        P.op("dve", lambda e, t=t: e.scalar_tensor_tensor(out=xn[t][:], in0=xg[sl][:, t, :], scalar=st[t][:, 2:3], in1=nmlp[:], op0=ALU.mult, op1=ALU.mult), reads=[gk, "st5_%d" % t, "nmlp"], writes=["xn5_%d" % t])

                def p4b_tr(gi):
                    sl = gi % 2
                    for t in range(2):
                        transposes8(xn[t], "xn5_%d" % t, pT, "pT5")
                        P.op("act", lambda e, t=t: e.activation(out=xnT[sl][:, :, t * 128:(t + 1) * 128], in_=pT[:].rearrange("p (k c) -> p k c", k=8), func=AF.Copy), reads=["pT5"], writes=["xnT5_%d" % sl])

                def p4b_up(gi, f0, f1, hooks=None):
                    sl = gi % 2
                    for f in range(f0, f1):
                        if hooks and f in hooks:
                            hooks[f]()
                        bank, bkey = pu[f % 4], "pu%d" % (f % 4)
                        for k in range(8):
                            P.op("pe", lambda e, f=f, k=k, bank=bank: e.matmul(bank[:, 0:256], lhsT=wup[:, k, f * 128:(f + 1) * 128], rhs=xnT[sl][:, k, :], start=(k == 0), stop=(k == 7)), reads=["wup", "xnT5_%d" % sl], writes=[bkey])
                        sqt, sqk = sq[f % 2], "sq%d" % (f % 2)
                        P.op("act", lambda e, bank=bank, sqt=sqt: e.activation(out=sqt[:], in_=bank[:, 0:256], func=AF.Square), reads=[bkey], writes=[sqk])
                        P.op("dve", lambda e, f=f, bank=bank, sqt=sqt: e.scalar_tensor_tensor(out=hT[:, f, :], in0=bank[:, 0:256], scalar=0.0, in1=sqt[:], op0=ALU.is_gt, op1=ALU.mult), reads=[bkey, sqk], writes=["hT"])

                def p4b_down(gi):
                    sl = gi % 2
                    gk = "xg%d" % sl
                    epi = []
                    for t in range(2):
                        for n in range(2):
                            for f in range(32):
                                P.op("pe", lambda e, t=t, n=n, f=f: e.matmul(pd[n][:], lhsT=hT[:, f, t * 128:(t + 1) * 128], rhs=wdn[:, f, n * 512:(n + 1) * 512], start=(f == 0), stop=(f == 31)), reads=["hT", "wdn"], writes=["pd%d" % n])
                            cs_ = slice(n * 512, (n + 1) * 512)
                            P.op("dve", lambda e, t=t, n=n, cs_=cs_: e.tensor_tensor(out=xg[sl][:, t, cs_], in0=pd[n][:], in1=xg[sl][:, t, cs_], op=ALU.add), reads=["pd%d" % n, gk], writes=[gk])

                        def fin(t=t):
                            osl = oc[0] % 2
                            oc[0] += 1
                            ok = "ot%d" % osl
                            rstd_ops(st[2], "st5_2", xg[sl][:, t, :], gk, ot[osl][:], ok)
                            P.op("dve", lambda e: e.scalar_tensor_tensor(out=ot[osl][:], in0=xg[sl][:, t, :], scalar=st[2][:, 2:3], in1=nfin[:], op0=ALU.mult, op1=ALU.mult), reads=[gk, "st5_2", "nfin"], writes=[ok])
                            r0 = gi * 256 + t * 128
                            P.op("sp", lambda e: e.dma_start(out=y_out[r0:r0 + 128, :], in_=ot[osl][:]), reads=[ok], writes=["y_out"], dma=True)
                        fin()
                    return epi

                NG = NM // 2
                p4b_load(0)
                p4b_norm(0)
                p4b_tr(0)
                pend = []
                for gi in range(NG):
                    if gi + 1 < NG:
                        p4b_load(gi + 1)
                    hk = {6: pend[0]} if pend else None
                    p4b_up(gi, 0, 16, hk)
                    if gi + 1 < NG:
                        p4b_norm(gi + 1)
                    p4b_up(gi, 16, 32)
                    if gi + 1 < NG:
                        p4b_tr(gi + 1)
                    pend = p4b_down(gi)
                for f_ in pend:
                    f_()
                P.flush()
        P.flush(final=True)
    return nc, P


def _other_chunks(h):
    return np.arange(0, 64) if h == 1 else np.arange(127, 63, -1)


def _consts(h):
    bf = ml_dtypes.bfloat16
    c = {}
    c["ident"] = np.eye(128, dtype=np.float32).astype(bf)
    inv = (np.float32(10000.0) ** (-(np.arange(0, 256, 2, dtype=np.float32)) / np.float32(256))).astype(np.float32)

    def rot(pos):
        ang = (pos.astype(np.float32)[:, None] * inv[None, :]).astype(np.float32).astype(np.float64)
        return np.concatenate([np.cos(ang), np.sin(ang)], axis=1).astype(np.float32)
    pos_m = h * 8192 + np.arange(8192)
    oc = _other_chunks(h)
    pos_o = (oc[:, None] * 128 + np.arange(128)[None, :]).reshape(-1)
    c["rotm"] = rot(pos_m)
    c["roto"] = rot(pos_o)
    dk = np.zeros((128, DKW), np.float32)
    j = np.arange(128)[:, None].astype(np.float64)
    i = np.arange(128)[None, :].astype(np.float64)
    dk[:, O_E0F:O_E0F + 128] = np.maximum(i - j, 0)
    dk[:, O_MF:O_MF + 128] = (i >= j)
    dk[:, O_E0B:O_E0B + 128] = np.maximum(j - i, 0)
    dk[:, O_MB:O_MB + 128] = (j > i)
    dk[:, O_XF:O_XF + 128] = i + 1
    dk[:, O_XB:O_XB + 128] = 128 - i
    zf = 127 - j
    zb = j
    dk[:, O_ZF:O_ZF + 256] = zf
    dk[:, O_ZB:O_ZB + 256] = zb
    dk[:, O_ZO:O_ZO + 256] = zf if h == 1 else zb
    dk[:, O_MSKF] = 1.0 if h == 1 else 0.0
    dk[:, O_MSKB] = 1.0 if h == 0 else 0.0
    dk[:, O_EPS6] = 1e-6
    dk[:, O_EPS5] = 1e-5
    dk[:, O_ONE] = 1.0
    dk[:, O_NH:O_NH + 8] = -0.5
    c["dk"] = dk
    a = np.concatenate([64 * h + np.arange(64), oc]).astype(np.float64)
    d = np.arange(128, dtype=np.float64)
    th = 2 * np.pi * ((a[:, None] * d[None, :]) % 128) / 128.0
    fcs = np.zeros((128, 2, 256), np.float64)
    fcs[:, 0, :128] = np.cos(th)
    fcs[:, 0, 128:] = -np.sin(th)
    fcs[:, 1, :128] = np.sin(th)
    fcs[:, 1, 128:] = np.cos(th)
    c["fcs"] = fcs.astype(np.float32).astype(bf)
    b = np.arange(128, dtype=np.int64)[:, None, None]
    dd = np.arange(128, dtype=np.int64)[None, :, None]
    cg = (64 * h + np.arange(64, dtype=np.int64))[None, None, :]
    num = (cg * b * 128 + dd * b) % 16384
    th2 = 2 * np.pi * num.astype(np.float64) / 16384.0
    mt = np.zeros((128, 128, 2, 64), np.float64)
    mt[:, :, 0, :] = np.cos(th2)
    mt[:, :, 1, :] = np.sin(th2)
    c["mt"] = mt.astype(np.float32).astype(bf)
    ch = (np.arange(2)[None, :, None] * 128 + np.arange(128)[:, None, None]).astype(np.int64)
    jj = np.arange(256, dtype=np.int64)[None, None, :]
    th3 = 2 * np.pi * ((ch * jj) % 256).astype(np.float64) / 256.0
    cs = np.concatenate([np.cos(th3), -np.sin(th3)], axis=2)
    c["cs"] = cs.astype(np.float32).astype(bf)
    return c


_CACHE = {}


def _rep(v):
    return np.ascontiguousarray(np.broadcast_to(np.asarray(v, np.float32).reshape(1, -1), (128, v.size)))


def make_in_maps(inp):
    seqs = [(inp["x_prompt"][0], inp["mem_prompt"][0]), (inp["x_prompt"][1], inp["mem_prompt"][1]), (inp["x_sample"][0], inp["mem_sample"][0])]
    shared = {
        "w_in": np.ascontiguousarray(inp["w_in"][0]), "w_ro": np.ascontiguousarray(inp["w_ret_out"][0]),
        "w_4": np.ascontiguousarray(inp["w_four_out"][0]), "w_mx": np.ascontiguousarray(inp["w_mix_out"][0]),
        "w_cq": np.ascontiguousarray(inp["w_cq"][0]), "w_ck": np.ascontiguousarray(inp["w_ck"][0]),
        "w_cv": np.ascontiguousarray(inp["w_cv"][0]), "w_co": np.ascontiguousarray(inp["w_co"][0]),
        "w_up": np.ascontiguousarray(inp["w_up"][0]), "w_dn": np.ascontiguousarray(inp["w_down"][0]),
        "nmix": _rep(inp["norm_mix_w"][0]), "nca": _rep(inp["norm_ca_w"][0]), "nmem": _rep(inp["norm_mem_w"][0]),
        "nmlp": _rep(inp["norm_mlp_w"][0]), "nfin": _rep(inp["norm_final_w"]), "gnw": _rep(inp["ret_gn_w"][0]),
    }
    consts = [_consts(0), _consts(1)]
    in_maps = []
    for core in range(8):
        si = min(core // 2, 2)
        h = core % 2
        x, mem = seqs[si]
        x = np.asarray(x, np.float32)
        xm = np.ascontiguousarray(x[h * 8192:(h + 1) * 8192])
        oc = _other_chunks(h)
        xo = np.ascontiguousarray(x.reshape(128, 128, D)[oc].reshape(8192, D))
        df = np.asarray(inp["ret_decay_fwd"][0], np.float32)
        db = np.asarray(inp["ret_decay_bwd"][0], np.float32)
        do = df if h == 1 else db
        m = dict(shared)
        m.update(consts[h])
        m.update({"xm": xm, "xo": xo, "mem": np.ascontiguousarray(np.asarray(mem, np.float32)),
                  "dec": _rep(np.concatenate([df, db, do]))})
        in_maps.append(m)
    return in_maps


def kernel(**inputs):
    inp = {k: np.asarray(v) for k, v in inputs.items()}
    if "nc" not in _CACHE:
        _CACHE["nc"] = build()[0]
    nc = _CACHE["nc"]
    in_maps = make_in_maps(inp)
    res = run_bass_kernel_spmd(nc, in_maps, core_ids=list(range(8)))
    ys = [np.asarray(r["y"], np.float32) for r in res.results]
    y_prompt = np.stack([np.concatenate([ys[0], ys[1]], 0), np.concatenate([ys[2], ys[3]], 0)], 0)
    y_sample = np.concatenate([ys[4], ys[5]], 0)[None]
    return (y_prompt, y_sample)
```

```python
import contextlib
import numpy as np
import ml_dtypes
import concourse.bass as bass
import concourse.mybir as mybir
from concourse.bass_utils import run_bass_kernel_spmd

F32 = mybir.dt.float32
BF16 = mybir.dt.bfloat16
AF = mybir.ActivationFunctionType
ALU = mybir.AluOpType
AX = mybir.AxisListType

D = 1024
SEQ = 16384
NM = 64
NO = 64
INW = 7168
DFF = 4096
NMEM = 256


class Prog:
    NPOOL = 8

    def __init__(self, nc):
        self.nc = nc
        self.eng = {"pe": nc.tensor, "act": nc.scalar, "dve": nc.vector, "pool": nc.gpsimd, "sp": nc.sync}
        self.sems = {}
        self.cnt = {}
        self.known = {e: {} for e in self.eng}
        self.carry = {e: {} for e in self.eng}
        self.dma_rr = {e: 0 for e in self.eng}
        self.n_inst = 0
        self._reset()

    def _reset(self):
        self.ops = []
        self.lastw = {}
        self.lastr = {}
        self.last_on_sem = {}

    def getsem(self, name):
        if name not in self.sems:
            self.sems[name] = self.nc.alloc_semaphore(name=name)
        return self.sems[name]

    def op(self, eng, fn, reads=(), writes=(), dma=False, ndma=1):
        idx = len(self.ops)
        deps = set()
        for k in reads:
            deps.update(self.lastw.get(k, {}).values())
        for k in writes:
            deps.update(self.lastw.get(k, {}).values())
            deps.update(self.lastr.get(k, {}).values())
        semname = None
        if dma:
            semname = "d_%s_%d" % (eng, self.dma_rr[eng] % self.NPOOL)
            self.dma_rr[eng] += 1
            if semname in self.last_on_sem:
                deps.add(self.last_on_sem[semname])
            self.last_on_sem[semname] = idx
        self.ops.append(dict(eng=eng, fn=fn, deps=deps, dma=dma, semname=semname, ndma=ndma, waited=False, tok=None))
        ek = (eng, semname)
        for k in reads:
            self.lastr.setdefault(k, {})[ek] = idx
        for k in writes:
            self.lastw[k] = {ek: idx}
            self.lastr[k] = {}
        return idx

    def flush(self, final=False):
        ops = self.ops
        last_eng = {}
        for i, o in enumerate(ops):
            if not o["dma"]:
                last_eng[o["eng"]] = i
            for d in o["deps"]:
                od = ops[d]
                if (not od["dma"]) and od["eng"] == o["eng"] == "pe":
                    continue
                od["waited"] = True
        for i in last_eng.values():
            ops[i]["waited"] = True
        for o in ops:
            if o["dma"]:
                o["waited"] = True
        for o in ops:
            if not o["waited"]:
                continue
            if o["dma"]:
                nm = o["semname"]
                self.cnt[nm] = self.cnt.get(nm, 0) + 16 * o["ndma"]
            else:
                nm = "e_" + o["eng"]
                self.cnt[nm] = self.cnt.get(nm, 0) + 1
            o["tok"] = (nm, self.cnt[nm])
        for o in ops:
            e = o["eng"]
            engobj = self.eng[e]
            need = dict(self.carry[e])
            self.carry[e] = {}
            for d in o["deps"]:
                od = ops[d]
                if od["tok"] is None:
                    continue
                if (not od["dma"]) and od["eng"] == e == "pe":
                    continue
                nm, v = od["tok"]
                need[nm] = max(need.get(nm, 0), v)
            for nm, v in need.items():
                if self.known[e].get(nm, 0) >= v:
                    continue
                engobj.wait_ge(self.getsem(nm), v)
                self.known[e][nm] = v
                self.n_inst += 1
            r = o["fn"](engobj)
            self.n_inst += 1
            if o["tok"] is not None:
                nm, v = o["tok"]
                if o["dma"]:
                    insts = r if isinstance(r, (list, tuple)) else [r]
                    assert len(insts) == o["ndma"], (len(insts), o["ndma"])
                    for ins in insts:
                        ins.then_inc(self.getsem(nm), 16)
                else:
                    r.then_inc(self.getsem(nm), 1)
        allc = dict(self.cnt)
        for e in self.eng:
            self.carry[e] = dict(allc)
        if final:
            engobj = self.eng["sp"]
            for nm, v in allc.items():
                if self.known["sp"].get(nm, 0) < v:
                    engobj.wait_ge(self.getsem(nm), v)
                    self.known["sp"][nm] = v
        self._reset()


O_E0F, O_MF, O_E0B, O_MB, O_XF, O_XB = 0, 128, 256, 384, 512, 640
O_ZF, O_ZB, O_ZO = 768, 1024, 1280
O_MSKF, O_MSKB, O_EPS6, O_EPS5, O_ONE = 1536, 1537, 1538, 1539, 1540
O_NH = 1544
DKW = 1552


def build(dbg=False, phases=("p1", "p2", "p3", "p4")):
    nc = bass.Bass("TRN2", target_bir_lowering=False)

    def din(name, shape, dt=F32):
        return nc.dram_tensor(name, shape, dt, kind="ExternalInput").ap()

    def dscr(name, shape, dt):
        return nc.dram_tensor(name, shape, dt, kind="ExternalOutput" if dbg else "Internal").ap()

    xm = din("xm", [NM * 128, D])
    xo = din("xo", [NO * 128, D])
    mem = din("mem", [NMEM, D])
    w_in = din("w_in", [D, INW])
    w_ro = din("w_ro", [D, D])
    w_4 = din("w_4", [D, D])
    w_mx = din("w_mx", [D, D])
    w_cq = din("w_cq", [D, D])
    w_ck = din("w_ck", [D, D])
    w_cv = din("w_cv", [D, D])
    w_co = din("w_co", [D, D])
    w_up = din("w_up", [D, DFF])
    w_dn = din("w_dn", [DFF, D])
    nmix_d = din("nmix", [128, D])
    nca_d = din("nca", [128, D])
    nmem_d = din("nmem", [128, D])
    nmlp_d = din("nmlp", [128, D])
    nfin_d = din("nfin", [128, D])
    gnw_d = din("gnw", [128, D])
    dec_d = din("dec", [128, 12])
    dk_d = din("dk", [128, DKW])
    ident_d = din("ident", [128, 128], BF16)
    rotm_d = din("rotm", [NM * 128, 256])
    roto_d = din("roto", [NO * 128, 256])
    fcs_d = din("fcs", [128, 2, 256], BF16)
    mt_d = din("mt", [128, 128, 2, 64], BF16)
    cs_d = din("cs", [128, 2, 512], BF16)
    y_out = nc.dram_tensor("y", [NM * 128, D], F32, kind="ExternalOutput").ap()

    NT = (NM + NO) * 128
    Qd = dscr("Qd", [NM * 128, D], BF16)
    Kd = dscr("Kd", [NM * 128, D], BF16)
    Vd = dscr("Vd", [NM * 128, D], BF16)
    Gd = dscr("Gd", [NM * 128, D], BF16)
    GRd = dscr("GRd", [NM * 128, D], BF16)
    GFd = dscr("GFd", [NM * 128, D], BF16)
    KOd = dscr("KOd", [NO * 128, D], BF16)
    VOd = dscr("VOd", [NO * 128, D], BF16)
    Zd = dscr("Zd", [4, 2, 2, NT, 128], BF16)
    YFd = dscr("YFd", [NM * 128, D], F32)
    YBd = dscr("YBd", [NM * 128, D], F32)
    UFd = dscr("UFd", [D, NM * 128], BF16)
    X2d = dscr("X2d", [NM * 128, D], F32)

    P = Prog(nc)
    with contextlib.ExitStack() as gs:
        ident = gs.enter_context(nc.sbuf_tensor("ident_s", [128, 128], BF16))
        dk = gs.enter_context(nc.sbuf_tensor("dk_s", [128, DKW], F32))
        ckT = gs.enter_context(nc.sbuf_tensor("ckT", [128, 8, NMEM], BF16))
        cv = gs.enter_context(nc.sbuf_tensor("cv", [128, 2, D], BF16))
        P.op("sp", lambda e: e.dma_start(out=ident[:], in_=ident_d[:, :]), writes=["ident"], dma=True)
        P.op("sp", lambda e: e.dma_start(out=dk[:], in_=dk_d[:, :]), writes=["dk"], dma=True)
        eps6 = dk[:, O_EPS6:O_EPS6 + 1]
        eps5 = dk[:, O_EPS5:O_EPS5 + 1]
        one_ap = dk[:, O_ONE:O_ONE + 1]

        def rstd_ops(st, key, x_ap, xkey, junk, jkey):
            P.op("dve", lambda e: e.memset(st[:], 0.0), writes=[key])
            P.op("act", lambda e: e.activation(out=junk, in_=x_ap, func=AF.Square, accum_out=st[:, 0:1]),
                 reads=[xkey, key], writes=[jkey, key])
            P.op("act", lambda e: e.activation(out=st[:, 1:2], in_=st[:, 0:1], func=AF.Sqrt, scale=1.0 / D, bias=eps6),
                 reads=[key, "dk"], writes=[key])
            P.op("dve", lambda e: e.reciprocal(out=st[:, 2:3], in_=st[:, 1:2]), reads=[key], writes=[key])

        def rstd_pow(st, key, x_ap, xkey, junk, jkey):
            P.op("dve", lambda e: e.memset(st[:], 0.0), writes=[key])
            P.op("act", lambda e: e.activation(out=junk, in_=x_ap, func=AF.Square, accum_out=st[:, 0:1]),
                 reads=[xkey, key], writes=[jkey, key])
            P.op("dve", lambda e: e.tensor_scalar(out=st[:, 1:2], in0=st[:, 0:1], scalar1=1.0 / D, scalar2=eps6, op0=ALU.mult, op1=ALU.add), reads=[key, "dk"], writes=[key])
            P.op("act", lambda e: e.activation(out=st[:, 3:4], in_=st[:, 1:2], func=AF.Ln), reads=[key], writes=[key])
            P.op("act", lambda e: e.activation(out=st[:, 2:3], in_=st[:, 3:4], func=AF.Exp, scale=-0.5), reads=[key], writes=[key])

        def transposes8(src, skey, pT, pkey):
            for k in range(8):
                P.op("pe", lambda e, k=k: e.transpose(out=pT[:, k * 128:(k + 1) * 128], in_=src[:, k * 128:(k + 1) * 128], identity=ident[:]),
                     reads=[skey, "ident"], writes=[pkey])

        if "p1" in phases:
            with contextlib.ExitStack() as es:
                sb = lambda n, s, d: es.enter_context(nc.sbuf_tensor(n, s, d))
                ps = lambda n, s, d: es.enter_context(nc.psum_tensor(n, s, d))
                win = sb("win", [128, 8, INW], BF16)
                nmix = sb("nmix_s", [128, D], F32)
                css = sb("css", [128, 2, 512], BF16)
                xs = [sb("xs%d" % i, [128, D], F32) for i in range(2)]
                rt = [sb("rt%d" % i, [128, 256], F32) for i in range(3)]
                st_ = [sb("st1_%d" % i, [128, 8], F32) for i in range(2)]
                xn_ = [sb("xn1_%d" % i, [128, D], BF16) for i in range(2)]
                xnT_ = [sb("xnT1_%d" % i, [128, D], BF16) for i in range(2)]
                qkf = [sb("qkf%d" % i, [128, D], F32) for i in range(2)]
                tmp = [sb("tmp%d" % i, [128, 4, 128], F32) for i in range(4)]
                outs = {nm: [sb("o_%s%d" % (nm, i), [128, D], BF16) for i in range(2)] for nm in ("q", "k", "v")}
                outs.update({nm: [sb("o_%s0" % nm, [128, D], BF16)] for nm in ("g", "gr", "gf")})
                ub = sb("ub", [128, D], BF16)
                uT = sb("uT", [128, D], BF16)
                zt = [sb("zt%d" % i, [128, 4, 512], BF16) for i in range(2)]
                pT = ps("pT1", [128, D], BF16)
                pT2 = ps("pT1b", [128, D], BF16)
                pm = [ps("pm1_%d" % i, [128, 512], F32) for i in range(6)]
                for k in range(8):
                    P.op("pool", lambda e, k=k: e.dma_start(out=win[:, k, :], in_=w_in[k * 128:(k + 1) * 128, :]),
                         writes=["win%d" % k], dma=True)
                P.op("sp", lambda e: e.dma_start(out=nmix[:], in_=nmix_d[:, :]), writes=["nmix"], dma=True)
                P.op("sp", lambda e: e.dma_start(out=css[:], in_=cs_d[:, :, :]), writes=["css"], dma=True)
                bi = [0]

                def p1_load_x(ti):
                    mine = ti < NM
                    sl = ti % 2
                    src = xm if mine else xo
                    r0 = (ti if mine else ti - NM) * 128
                    P.op("sp", lambda e: e.dma_start(out=xs[sl][:], in_=src[r0:r0 + 128, :]), writes=["xs%d" % sl], dma=True)

                def p1_load_rt(ti):
                    mine = ti < NM
                    sl = ti % 3
                    rsrc = rotm_d if mine else roto_d
                    r0 = (ti if mine else ti - NM) * 128
                    P.op("sp", lambda e: e.dma_start(out=rt[sl][:], in_=rsrc[r0:r0 + 128, :]), writes=["rt%d" % sl], dma=True)

                def rotary(srcf, skey, dst, dkey, sl):
                    X = srcf[:].rearrange("p (h t f) -> p h t f", h=4, t=2)
                    O = dst[:].rearrange("p (h t f) -> p h t f", h=4, t=2)
                    cosb = rt[sl][:, 0:128].rearrange("p (o f) -> p o f", o=1).broadcast_to([128, 4, 128])
                    sinb = rt[sl][:, 128:256].rearrange("p (o f) -> p o f", o=1).broadcast_to([128, 4, 128])
                    rk = "rt%d" % sl
                    P.op("pool", lambda e: e.tensor_tensor(out=tmp[0][:], in0=X[:, :, 0, :], in1=cosb, op=ALU.mult), reads=[skey, rk], writes=["tmp0"])
                    P.op("dve", lambda e: e.tensor_tensor(out=tmp[1][:], in0=X[:, :, 1, :], in1=sinb, op=ALU.mult), reads=[skey, rk], writes=["tmp1"])
                    P.op("dve", lambda e: e.tensor_tensor(out=O[:, :, 0, :], in0=tmp[0][:], in1=tmp[1][:], op=ALU.subtract), reads=["tmp0", "tmp1"], writes=[dkey])
                    P.op("pool", lambda e: e.tensor_tensor(out=tmp[2][:], in0=X[:, :, 1, :], in1=cosb, op=ALU.mult), reads=[skey, rk], writes=["tmp2"])
                    P.op("dve", lambda e: e.tensor_tensor(out=tmp[3][:], in0=X[:, :, 0, :], in1=sinb, op=ALU.mult), reads=[skey, rk], writes=["tmp3"])
                    P.op("pool", lambda e: e.tensor_tensor(out=O[:, :, 1, :], in0=tmp[2][:], in1=tmp[3][:], op=ALU.add), reads=["tmp2", "tmp3", dkey], writes=[dkey])

                def p1_A_stages(ti):
                    sl = ti % 2
                    xk = "xs%d" % sl
                    st, xn, xnT = st_[sl], xn_[sl], xnT_[sl]
                    ks_, kn_, kt_ = "st1_%d" % sl, "xn1_%d" % sl, "xnT1_%d" % sl

                    def a0():
                        P.op("dve", lambda e: e.memset(st[:], 0.0), writes=[ks_])
                        P.op("act", lambda e: e.activation(out=xn[:], in_=xs[sl][:], func=AF.Square, accum_out=st[:, 0:1]), reads=[xk, ks_], writes=[kn_, ks_])

                    def a1():
                        P.op("act", lambda e: e.activation(out=st[:, 1:2], in_=st[:, 0:1], func=AF.Sqrt, scale=1.0 / D, bias=eps6), reads=[ks_, "dk"], writes=[ks_])

                    def a2():
                        P.op("dve", lambda e: e.reciprocal(out=st[:, 2:3], in_=st[:, 1:2]), reads=[ks_], writes=[ks_])

                    def a3():
                        P.op("dve", lambda e: e.scalar_tensor_tensor(out=xn[:], in0=xs[sl][:], scalar=st[:, 2:3], in1=nmix[:], op0=ALU.mult, op1=ALU.mult),
                             reads=[xk, ks_, "nmix"], writes=[kn_])

                    def a4():
                        transposes8(xn, kn_, pT, "pT1")

                    def a5():
                        P.op("dve", lambda e: e.tensor_copy(out=xnT[:], in_=pT[:]), reads=["pT1"], writes=[kt_])
                    return [a0, a1, a2, a3, a4, a5]

                def p1_tail_stages(ti):
                    zr0 = ti * 128
                    zsl = ti % 2

                    def t0():
                        transposes8(ub, "ub", pT2, "pT1b")
                        P.op("act", lambda e: e.activation(out=uT[:], in_=pT2[:], func=AF.Copy), reads=["pT1b"], writes=["uT"])

                    def t1():
                        for g in range(4):
                            b = bi[0] % 6
                            bi[0] += 1
                            bank, bkey = pm[b], "pm1_%d" % b
                            for kk in range(2):
                                P.op("pe", lambda e, g=g, kk=kk, bank=bank: e.matmul(bank[:], lhsT=uT[:, (2 * g + kk) * 128:(2 * g + kk + 1) * 128], rhs=css[:, kk, :], start=(kk == 0), stop=(kk == 1)),
                                     reads=["uT", "css"], writes=[bkey])
                            if g % 2 == 0:
                                P.op("act", lambda e, g=g, bank=bank: e.activation(out=zt[zsl][:, g, :], in_=bank[:], func=AF.Copy), reads=[bkey], writes=["zt%d" % zsl])
                            else:
                                P.op("dve", lambda e, g=g, bank=bank: e.tensor_copy(out=zt[zsl][:, g, :], in_=bank[:]), reads=[bkey], writes=["zt%d" % zsl])
                        for g in range(4):
                            P.op("sp", lambda e, g=g: e.dma_start(
                                out=Zd[g, :, :, zr0:zr0 + 128, :].rearrange("ri hf t c -> t (ri hf) c"),
                                in_=zt[zsl][:, g, :].rearrange("p (rh c) -> p rh c", rh=4)),
                                reads=["zt%d" % zsl], writes=["Zd"], dma=True)
                    return [t0, t1]

                def p1_compute(ti):
                    mine = ti < NM
                    sl = ti % 2
                    r0 = (ti if mine else ti - NM) * 128
                    zr0 = ti * 128
                    xnT = xnT_[sl]
                    kt_ = "xnT1_%d" % sl
                    slices = list(range(14)) if mine else [2, 3, 4, 5, 8, 9]
                    nxt_st = p1_A_stages(ti + 1) if ti + 1 < NM + NO else []
                    hook = ({1: 0, 2: 1, 3: 2, 5: 3, 8: 4, 10: 5} if mine else {0: 0, 1: 1, 2: 2, 3: 3, 4: 4, 5: 5})
                    prev_tail = p1_tail_stages(ti - 1) if ti > 0 else []
                    thook = ({4: 0, 7: 1} if mine else {1: 0, 3: 1})
                    for si_, n in enumerate(slices):
                        if si_ in hook and nxt_st:
                            nxt_st[hook[si_]]()
                        if si_ in thook and prev_tail:
                            prev_tail[thook[si_]]()
                        b = bi[0] % 6
                        bi[0] += 1
                        bank, bkey = pm[b], "pm1_%d" % b
                        for k in range(8):
                            P.op("pe", lambda e, k=k, n=n, bank=bank: e.matmul(bank[:], lhsT=xnT[:, k * 128:(k + 1) * 128], rhs=win[:, k, n * 512:(n + 1) * 512], start=(k == 0), stop=(k == 7)),
                                 reads=[kt_, "win%d" % k], writes=[bkey])
                        hf = n % 2
                        cs_ = slice(hf * 512, (hf + 1) * 512)
                        if n in (0, 1):
                            P.op("act", lambda e, bank=bank, cs_=cs_: e.activation(out=qkf[0][:, cs_], in_=bank[:], func=AF.Copy, scale=1.0 / 16.0), reads=[bkey], writes=["qkf0"])
                            if n == 1:
                                rotary(qkf[0], "qkf0", outs["q"][sl], "o_q%d" % sl, ti % 3)
                        elif n in (2, 3):
                            P.op("act", lambda e, bank=bank, cs_=cs_: e.activation(out=qkf[1][:, cs_], in_=bank[:], func=AF.Copy), reads=[bkey], writes=["qkf1"])
                            if n == 3:
                                rotary(qkf[1], "qkf1", outs["k"][sl], "o_k%d" % sl, ti % 3)
                        else:
                            nm_, eng = {4: ("v", "dve"), 5: ("v", "dve"), 6: ("g", "act"), 7: ("g", "act"), 8: ("u", "dve"), 9: ("u", "dve"),
                                        10: ("gr", "act"), 11: ("gr", "act"), 12: ("gf", "dve"), 13: ("gf", "dve")}[n]
                            if nm_ == "u":
                                dst, dkey = ub, "ub"
                            elif nm_ == "v":
                                dst, dkey = outs["v"][sl], "o_v%d" % sl
                            else:
                                dst, dkey = outs[nm_][0], "o_%s0" % nm_
                            if eng == "act":
                                P.op("act", lambda e, bank=bank, cs_=cs_, dst=dst: e.activation(out=dst[:, cs_], in_=bank[:], func=AF.Copy), reads=[bkey], writes=[dkey])
                            else:
                                P.op("dve", lambda e, bank=bank, cs_=cs_, dst=dst: e.tensor_copy(out=dst[:, cs_], in_=bank[:]), reads=[bkey], writes=[dkey])
                    if mine:
                        for nm_, dd in (("q", Qd), ("k", Kd), ("v", Vd)):
                            P.op("sp", lambda e, nm_=nm_, dd=dd: e.dma_start(out=dd[r0:r0 + 128, :], in_=outs[nm_][sl][:]), reads=["o_%s%d" % (nm_, sl)], writes=["dram_" + nm_], dma=True)
                        for nm_, dd in (("g", Gd), ("gr", GRd), ("gf", GFd)):
                            P.op("sp", lambda e, nm_=nm_, dd=dd: e.dma_start(out=dd[r0:r0 + 128, :], in_=outs[nm_][0][:]), reads=["o_%s0" % nm_], writes=["dram_" + nm_], dma=True)
                    else:
                        for nm_, dd in (("k", KOd), ("v", VOd)):
                            P.op("sp", lambda e, nm_=nm_, dd=dd: e.dma_start(out=dd[r0:r0 + 128, :], in_=outs[nm_][sl][:]), reads=["o_%s%d" % (nm_, sl)], writes=["dram_o" + nm_], dma=True)

                p1_load_x(0)
                p1_load_rt(0)
                p1_load_x(1)
                for f_ in p1_A_stages(0):
                    f_()
                for ti in range(NM + NO):
                    if ti + 2 < NM + NO:
                        p1_load_x(ti + 2)
                    if ti + 1 < NM + NO:
                        p1_load_rt(ti + 1)
                    p1_compute(ti)
                for f_ in p1_tail_stages(NM + NO - 1):
                    f_()
                P.flush()

        if "p2" in phases:
            with contextlib.ExitStack() as es:
                sb = lambda n, s, d: es.enter_context(nc.sbuf_tensor(n, s, d))
                ps = lambda n, s, d: es.enter_context(nc.psum_tensor(n, s, d))
                dec = sb("dec_s", [128, 12], F32)
                lg = sb("lg", [128, 12], F32)
                gch = sb("gch", [128, 12], F32)
                DT = [sb("DT%d" % i, [128, 4, 128], F32) for i in range(2)]
                xi = [sb("xi%d" % i, [128, 8, 128], F32) for i in range(2)]
                zeta = [sb("zeta%d" % i, [128, 4, 256], F32) for i in range(3)]
                tmpd = sb("tmpd", [128, 128], F32)
                S = [sb("S%d" % i, [128, 8, 256], F32) for i in range(3)]
                Sbf = [sb("Sbf%d" % i, [128, 8, 256], BF16) for i in range(2)]
                qs = [[sb("q2_%d%d" % (d_, i), [128, D], BF16) for i in range(2)] for d_ in range(2)]
                ks = [[sb("k2_%d%d" % (d_, i), [128, D], BF16) for i in range(2)] for d_ in range(2)]
                vs = [[sb("v2_%d%d" % (d_, i), [128, D], BF16) for i in range(2)] for d_ in range(2)]
                qT = [sb("qT%d" % i, [128, D], BF16) for i in range(2)]
                qxT = [sb("qxT%d" % i, [128, D], BF16) for i in range(2)]
                kT = [sb("kT%d" % i, [128, D], BF16) for i in range(2)]
                kz = [sb("kz%d" % i, [128, D], BF16) for i in range(2)]
                PT = [sb("PT%d" % i, [128, 512], BF16) for i in range(2)]
                ysb = [sb("ysb%d" % i, [128, D], F32) for i in range(2)]
                pTt = [ps("pTt%d" % i, [128, D], BF16) for i in range(2)]
                pS = [ps("pS%d" % i, [128, 512], F32) for i in range(2)]
                py = [ps("py%d" % i, [128, 512], F32) for i in range(2)]
                pst = [ps("pst%d" % i, [128, 512], F32) for i in range(2)]

                P.op("sp", lambda e: e.dma_start(out=dec[:], in_=dec_d[:, :]), writes=["dec"], dma=True)
                P.op("act", lambda e: e.activation(out=lg[:], in_=dec[:], func=AF.Exp, scale=-1.0), reads=["dec"], writes=["lg"])
                P.op("act", lambda e: e.activation(out=lg[:], in_=lg[:], func=AF.Ln, bias=one_ap, scale=1.0), reads=["lg", "dk"], writes=["lg"])
                P.op("dve", lambda e: e.tensor_scalar(out=lg[:], in0=lg[:], scalar1=-1.0, scalar2=None, op0=ALU.mult), reads=["lg"], writes=["lg"])
                P.op("act", lambda e: e.activation(out=gch[:], in_=lg[:], func=AF.Exp, scale=128.0), reads=["lg"], writes=["gch"])
                for d_ in range(2):
                    oe, om, ox = (O_E0F, O_MF, O_XF) if d_ == 0 else (O_E0B, O_MB, O_XB)
                    for hd in range(4):
                        col = 4 * d_ + hd
                        P.op("act", lambda e, oe=oe, col=col: e.activation(out=tmpd[:], in_=dk[:, oe:oe + 128], func=AF.Exp, scale=lg[:, col:col + 1]), reads=["dk", "lg"], writes=["tmpd"])
                        P.op("dve", lambda e, om=om, d_=d_, hd=hd: e.tensor_tensor(out=DT[d_][:, hd, :], in0=tmpd[:], in1=dk[:, om:om + 128], op=ALU.mult), reads=["tmpd", "dk"], writes=["DT%d" % d_])
                        for hf in range(2):
                            P.op("act", lambda e, ox=ox, col=col, d_=d_, hd=hd, hf=hf: e.activation(out=xi[d_][:, 2 * hd + hf, :], in_=dk[:, ox:ox + 128], func=AF.Exp, scale=lg[:, col:col + 1]), reads=["dk", "lg"], writes=["xi%d" % d_])
                for z_, oz in enumerate((O_ZF, O_ZB, O_ZO)):
                    for hd in range(4):
                        col = 4 * z_ + hd
                        P.op("act", lambda e, z_=z_, oz=oz, hd=hd, col=col: e.activation(out=zeta[z_][:, hd, :], in_=dk[:, oz:oz + 256], func=AF.Exp, scale=lg[:, col:col + 1]), reads=["dk", "lg"], writes=["zeta%d" % z_])
                P.op("dve", lambda e: e.memset(S[2][:], 0.0), writes=["S2"])
                import os
                P2STOP = int(os.environ.get("P2STOP", "9"))

                def state_head(si, kzt, kzkey, vt, vkey, gcol0, bank, bkey, hd):
                    for hf in range(2):
                        P.op("pe", lambda e, hf=hf: e.matmul(bank[:, hf * 256:(hf + 1) * 256], lhsT=kzt[:, hd * 256 + hf * 128:hd * 256 + hf * 128 + 128], rhs=vt[:, hd * 256:(hd + 1) * 256], start=True, stop=True),
                             reads=[kzkey, vkey], writes=[bkey])
                    sview = S[si][:, 2 * hd:2 * hd + 2, :].rearrange("p a b -> p (a b)")
                    P.op("dve", lambda e: e.scalar_tensor_tensor(out=sview, in0=sview, scalar=gch[:, gcol0 + hd:gcol0 + hd + 1], in1=bank[:], op0=ALU.mult, op1=ALU.add),
                         reads=["S%d" % si, "gch", bkey], writes=["S%d" % si])

                def state_update(si, kzt, kzkey, vt, vkey, gcol0, bset):
                    for hd in range(4):
                        bank, bkey = ((pst[bset], "pst%d" % bset) if hd % 2 == 0 else (py[bset], "py%d" % bset))
                        state_head(si, kzt, kzkey, vt, vkey, gcol0, bank, bkey, hd)

                def p2a_load(j):
                    sl = j % 2
                    P.op("sp", lambda e: e.dma_start(out=ks[0][sl][:], in_=KOd[j * 128:(j + 1) * 128, :]), writes=["k2_0%d" % sl], dma=True)
                    P.op("sp", lambda e: e.dma_start(out=vs[0][sl][:], in_=VOd[j * 128:(j + 1) * 128, :]), writes=["v2_0%d" % sl], dma=True)
                if P2STOP >= 2:
                    p2a_load(0)
                for j in range(NO if P2STOP >= 2 else 0):
                    if j + 1 < NO:
                        p2a_load(j + 1)
                    sl = j % 2
                    bs_ = j % 2
                    P.op("pool", lambda e, sl=sl, bs_=bs_: e.tensor_tensor(out=kz[bs_][:], in0=ks[0][sl][:], in1=zeta[2][:].rearrange("p a b -> p (a b)"), op=ALU.mult), reads=["k2_0%d" % sl, "zeta2"], writes=["kz%d" % bs_])
                    state_update(2, kz[bs_], "kz%d" % bs_, vs[0][sl], "v2_0%d" % sl, 8, bs_)
                P.op("dve", lambda e: e.tensor_scalar(out=S[0][:], in0=S[2][:], scalar1=dk[:, O_MSKF:O_MSKF + 1], scalar2=None, op0=ALU.mult), reads=["S2", "dk"], writes=["S0"])
                P.op("dve", lambda e: e.tensor_scalar(out=S[1][:], in0=S[2][:], scalar1=dk[:, O_MSKB:O_MSKB + 1], scalar2=None, op0=ALU.mult), reads=["S2", "dk"], writes=["S1"])
                for d_ in range(2):
                    P.op("act", lambda e, d_=d_: e.activation(out=Sbf[d_][:], in_=S[d_][:], func=AF.Copy), reads=["S%d" % d_], writes=["Sbf%d" % d_])
                P.flush()

                def p2_load(d_, c, sl):
                    r0 = c * 128
                    P.op("sp", lambda e: e.dma_start(out=qs[d_][sl][:], in_=Qd[r0:r0 + 128, :]), writes=["q2_%d%d" % (d_, sl)], dma=True)
                    P.op("sp", lambda e: e.dma_start(out=ks[d_][sl][:], in_=Kd[r0:r0 + 128, :]), writes=["k2_%d%d" % (d_, sl)], dma=True)
                    P.op("sp", lambda e: e.dma_start(out=vs[d_][sl][:], in_=Vd[r0:r0 + 128, :]), writes=["v2_%d%d" % (d_, sl)], dma=True)

                def p2_stages(d_, c, sl):
                    q_, k_, v_ = qs[d_][sl], ks[d_][sl], vs[d_][sl]
                    qk_, kk_, vk_ = "q2_%d%d" % (d_, sl), "k2_%d%d" % (d_, sl), "v2_%d%d" % (d_, sl)
                    ds = str(d_)
                    T_, kT_ = pTt[d_], "pTt" + ds

                    def s0():
                        transposes8(q_, qk_, T_, kT_)
                        P.op("act", lambda e: e.activation(out=qT[d_][:], in_=T_[:], func=AF.Copy), reads=[kT_], writes=["qT" + ds])
                        P.op("dve", lambda e: e.tensor_tensor(out=qxT[d_][:], in0=qT[d_][:], in1=xi[d_][:].rearrange("p a b -> p (a b)"), op=ALU.mult), reads=["qT" + ds, "xi" + ds], writes=["qxT" + ds])

                    def s1():
                        transposes8(k_, kk_, T_, kT_)
                        P.op("act", lambda e: e.activation(out=kT[d_][:], in_=T_[:], func=AF.Copy), reads=[kT_], writes=["kT" + ds])
                        P.op("pool", lambda e: e.tensor_tensor(out=kz[d_][:], in0=k_[:], in1=zeta[d_][:].rearrange("p a b -> p (a b)"), op=ALU.mult), reads=[kk_, "zeta" + ds], writes=["kz" + ds])

                    def s2():
                        for hd in range(4):
                            for hf in range(2):
                                m = 2 * hd + hf
                                P.op("pe", lambda e, hd=hd, hf=hf, m=m: e.matmul(pS[d_][:, hd * 128:(hd + 1) * 128], lhsT=kT[d_][:, m * 128:(m + 1) * 128], rhs=qT[d_][:, m * 128:(m + 1) * 128], start=(hf == 0), stop=(hf == 1)),
                                     reads=["kT" + ds, "qT" + ds], writes=["pS" + ds])
                        P.op("dve", lambda e: e.tensor_tensor(out=PT[d_][:], in0=pS[d_][:], in1=DT[d_][:].rearrange("p a b -> p (a b)"), op=ALU.mult), reads=["pS" + ds, "DT" + ds], writes=["PT" + ds])

                    def ystage(half):
                        bank, bkey = py[d_], "py" + ds
                        for hd in (2 * half, 2 * half + 1):
                            co = (hd % 2) * 256
                            P.op("pe", lambda e, hd=hd, co=co: e.matmul(bank[:, co:co + 256], lhsT=PT[d_][:, hd * 128:(hd + 1) * 128], rhs=v_[:, hd * 256:(hd + 1) * 256], start=True, stop=False),
                                 reads=["PT" + ds, vk_], writes=[bkey])
                            for hf in range(2):
                                m = 2 * hd + hf
                                P.op("pe", lambda e, m=m, hf=hf, co=co: e.matmul(bank[:, co:co + 256], lhsT=qxT[d_][:, m * 128:(m + 1) * 128], rhs=Sbf[d_][:, m, :], start=False, stop=(hf == 1)),
                                     reads=["qxT" + ds, "Sbf" + ds], writes=[bkey])
                        if half == 0:
                            P.op("act", lambda e: e.activation(out=ysb[d_][:, 0:512], in_=bank[:], func=AF.Copy), reads=[bkey], writes=["ysb" + ds])
                        else:
                            P.op("dve", lambda e: e.tensor_copy(out=ysb[d_][:, 512:1024], in_=bank[:]), reads=[bkey], writes=["ysb" + ds])
                            yd = YFd if d_ == 0 else YBd
                            P.op("sp", lambda e: e.dma_start(out=yd[c * 128:(c + 1) * 128, :], in_=ysb[d_][:]), reads=["ysb" + ds], writes=["dram_y" + ds], dma=True)

                    def sh(hd):
                        bank, bkey = ((pst[d_], "pst" + ds) if hd % 2 == 0 else (py[d_], "py" + ds))
                        state_head(d_, kz[d_], "kz" + ds, v_, vk_, 4 * d_, bank, bkey, hd)
                        if hd % 2 == 1:
                            hh = hd // 2
                            P.op("act", lambda e: e.activation(out=Sbf[d_][:, 4 * hh:4 * hh + 4, :], in_=S[d_][:, 4 * hh:4 * hh + 4, :], func=AF.Copy), reads=["S" + ds], writes=["Sbf" + ds])

                    return [s0, s1, s2, lambda: ystage(0), lambda: ystage(1), lambda: sh(0), lambda: sh(1), lambda: sh(2), lambda: sh(3)]

                if P2STOP >= 3:
                    p2_load(0, 0, 0)
                    p2_load(1, NM - 1, 0)
                for t in range(NM if P2STOP >= 3 else 0):
                    sl = t % 2
                    if t + 1 < NM:
                        p2_load(0, t + 1, 1 - sl)
                        p2_load(1, NM - 2 - t, 1 - sl)
                    sf_ = p2_stages(0, t, sl)
                    sb_ = p2_stages(1, NM - 1 - t, sl)
                    for a_, b_ in zip(sf_, sb_):
                        a_()
                        b_()
                P.flush()

        if "p3" in phases:
            with contextlib.ExitStack() as es:
                sb = lambda n, s, d: es.enter_context(nc.sbuf_tensor(n, s, d))
                ps = lambda n, s, d: es.enter_context(nc.psum_tensor(n, s, d))
                ZL = sb("ZL", [128, 2, 128, 128], BF16)
                T = sb("Tt", [128, 128, 256], BF16)
                mts = sb("mts", [128, 128, 2, 64], BF16)
                ufs = sb("ufs", [128, 64, 128], BF16)
                fcs = sb("fcs_s", [128, 2, 256], BF16)
                pb = [ps("pb%d" % i, [128, 512], F32) for i in range(4)]
                pc = [ps("pc%d" % i, [128, 512], F32) for i in range(4)]
                P.op("sp", lambda e: e.dma_start(out=fcs[:], in_=fcs_d[:, :, :]), writes=["fcs"], dma=True)
                for dq in range(4):
                    P.op("sp", lambda e, dq=dq: e.dma_start(out=mts[:, dq * 32:(dq + 1) * 32, :, :], in_=mt_d[:, dq * 32:(dq + 1) * 32, :, :]), writes=["mts"], dma=True)
                for s in range(8):
                    for ri in range(2):
                        for hb in range(2):
                            P.op("sp", lambda e, s=s, ri=ri, hb=hb: e.dma_start(
                                out=ZL[:, ri, hb * 64:(hb + 1) * 64, :],
                                in_=Zd[s // 2, ri, s % 2, :, :].rearrange("(l b) c -> l b c", b=128)[:, hb * 64:(hb + 1) * 64, :]),
                                reads=["Zd"], writes=["ZL"], dma=True)
                    for cp in range(64):
                        bank, bkey = pb[cp % 4], "pb%d" % (cp % 4)
                        for q_ in range(2):
                            ch = 2 * cp + q_
                            P.op("pe", lambda e, ch=ch, q_=q_, bank=bank: e.matmul(bank[:, q_ * 256:(q_ + 1) * 256], lhsT=ZL[:, 0, :, ch], rhs=fcs[:, 0, :], start=True, stop=False),
                                 reads=["ZL", "fcs"], writes=[bkey])
                            P.op("pe", lambda e, ch=ch, q_=q_, bank=bank: e.matmul(bank[:, q_ * 256:(q_ + 1) * 256], lhsT=ZL[:, 1, :, ch], rhs=fcs[:, 1, :], start=False, stop=True),
                                 reads=["ZL", "fcs"], writes=[bkey])
                        tv = T[:, 2 * cp:2 * cp + 2, :].rearrange("p a b -> p (a b)")
                        if cp % 2 == 0:
                            P.op("act", lambda e, bank=bank, tv=tv: e.activation(out=tv, in_=bank[:], func=AF.Copy), reads=[bkey], writes=["Tt"])
                        else:
                            P.op("dve", lambda e, bank=bank, tv=tv: e.tensor_copy(out=tv, in_=bank[:]), reads=[bkey], writes=["Tt"])
                    for db in range(16):
                        bank, bkey = pc[db % 4], "pc%d" % (db % 4)
                        for dd in range(8):
                            d_ = db * 8 + dd
                            P.op("pe", lambda e, d_=d_, dd=dd, bank=bank: e.matmul(bank[:, dd * 64:(dd + 1) * 64], lhsT=T[:, :, d_], rhs=mts[:, d_, 0, :], start=True, stop=False),
                                 reads=["Tt", "mts"], writes=[bkey])
                            P.op("pe", lambda e, d_=d_, dd=dd, bank=bank: e.matmul(bank[:, dd * 64:(dd + 1) * 64], lhsT=T[:, :, 128 + d_], rhs=mts[:, d_, 1, :], start=False, stop=True),
                                 reads=["Tt", "mts"], writes=[bkey])
                        ov = ufs[:, :, db * 8:(db + 1) * 8]
                        iv = bank[:].rearrange("p (dd c) -> p c dd", dd=8)
                        if db % 2 == 0:
                            P.op("act", lambda e, ov=ov, iv=iv: e.activation(out=ov, in_=iv, func=AF.Copy, scale=1.0 / 2048.0), reads=[bkey], writes=["ufs"])
                        else:
                            P.op("dve", lambda e, ov=ov, iv=iv: e.tensor_scalar(out=ov, in0=iv, scalar1=1.0 / 2048.0, scalar2=None, op0=ALU.mult), reads=[bkey], writes=["ufs"])
                    P.op("sp", lambda e, s=s: e.dma_start(out=UFd[s * 128:(s + 1) * 128, :], in_=ufs[:].rearrange("p c d -> p (c d)")), reads=["ufs"], writes=["UFd"], dma=True)
                P.flush()

        if "p4" in phases or "p4a" in phases:
            with contextlib.ExitStack() as es:
                sb = lambda n, s, d: es.enter_context(nc.sbuf_tensor(n, s, d))
                ps = lambda n, s, d: es.enter_context(nc.psum_tensor(n, s, d))
                wck = sb("wck", [128, 8, D], BF16)
                wcv = sb("wcv", [128, 8, D], BF16)
                nmem = sb("nmem_s", [128, D], F32)
                ms = [sb("ms%d" % i, [128, D], F32) for i in range(2)]
                st = sb("st0", [128, 8], F32)
                mn = sb("mn", [128, D], BF16)
                mnT = sb("mnT", [128, 8, NMEM], BF16)
                pT = ps("pT0", [128, D], BF16)
                pm = [ps("pm0_%d" % i, [128, 512], F32) for i in range(4)]
                for wsb, wd, nm_ in ((wck, w_ck, "wck"), (wcv, w_cv, "wcv")):
                    for kq in range(4):
                        P.op("pool", lambda e, wsb=wsb, wd=wd, kq=kq: e.dma_start(out=wsb[:, 2 * kq:2 * kq + 2, :], in_=wd[kq * 256:(kq + 1) * 256, :].rearrange("(k p) n -> p k n", p=128)), writes=[nm_], dma=True)
                P.op("sp", lambda e: e.dma_start(out=nmem[:], in_=nmem_d[:, :]), writes=["nmem"], dma=True)
                for t in range(2):
                    P.op("sp", lambda e, t=t: e.dma_start(out=ms[t][:], in_=mem[t * 128:(t + 1) * 128, :]), writes=["ms%d" % t], dma=True)
                for t in range(2):
                    rstd_ops(st, "st0", ms[t][:], "ms%d" % t, mn[:], "mn")
                    P.op("dve", lambda e, t=t: e.scalar_tensor_tensor(out=mn[:], in0=ms[t][:], scalar=st[:, 2:3], in1=nmem[:], op0=ALU.mult, op1=ALU.mult), reads=["ms%d" % t, "st0", "nmem"], writes=["mn"])
                    transposes8(mn, "mn", pT, "pT0")
                    P.op("dve", lambda e, t=t: e.tensor_copy(out=mnT[:, :, t * 128:(t + 1) * 128], in_=pT[:].rearrange("p (k c) -> p k c", k=8)), reads=["pT0"], writes=["mnT"])
                bi0 = 0
                for m in range(8):
                    bank, bkey = pm[bi0 % 4], "pm0_%d" % (bi0 % 4)
                    bi0 += 1
                    for k in range(8):
                        P.op("pe", lambda e, m=m, k=k, bank=bank: e.matmul(bank[:, 0:NMEM], lhsT=wck[:, k, m * 128:(m + 1) * 128], rhs=mnT[:, k, :], start=(k == 0), stop=(k == 7)), reads=["wck", "mnT"], writes=[bkey])
                    P.op("act", lambda e, m=m, bank=bank: e.activation(out=ckT[:, m, :], in_=bank[:, 0:NMEM], func=AF.Copy), reads=[bkey], writes=["ckT"])
                for t in range(2):
                    for n in range(2):
                        bank, bkey = pm[bi0 % 4], "pm0_%d" % (bi0 % 4)
                        bi0 += 1
                        for k in range(8):
                            P.op("pe", lambda e, t=t, n=n, k=k, bank=bank: e.matmul(bank[:], lhsT=mnT[:, k, t * 128:(t + 1) * 128], rhs=wcv[:, k, n * 512:(n + 1) * 512], start=(k == 0), stop=(k == 7)), reads=["wcv", "mnT"], writes=[bkey])
                        P.op("dve", lambda e, t=t, n=n, bank=bank: e.tensor_copy(out=cv[:, t, n * 512:(n + 1) * 512], in_=bank[:]), reads=[bkey], writes=["cv"])
                P.flush()

            with contextlib.ExitStack() as es:
                sb = lambda n, s, d: es.enter_context(nc.sbuf_tensor(n, s, d))
                ps = lambda n, s, d: es.enter_context(nc.psum_tensor(n, s, d))
                wts = {}
                for nm_, wd in (("wro", w_ro), ("w4", w_4), ("wmx", w_mx), ("wcq", w_cq), ("wco", w_co)):
                    wts[nm_] = sb(nm_, [128, 8, D], BF16)
                    for kq in range(4):
                        P.op("pool", lambda e, nm_=nm_, wd=wd, kq=kq: e.dma_start(out=wts[nm_][:, 2 * kq:2 * kq + 2, :], in_=wd[kq * 256:(kq + 1) * 256, :].rearrange("(k p) n -> p k n", p=128)), writes=[nm_], dma=True)
                gnw = sb("gnw_s", [128, D], F32)
                nca = sb("nca_s", [128, D], F32)
                P.op("sp", lambda e: e.dma_start(out=gnw[:], in_=gnw_d[:, :]), writes=["gnw"], dma=True)
                P.op("sp", lambda e: e.dma_start(out=nca[:], in_=nca_d[:, :]), writes=["nca"], dma=True)
                yf = [sb("yf%d" % i, [128, D], F32) for i in range(2)]
                yb = [sb("yb%d" % i, [128, D], F32) for i in range(2)]
                xt = [sb("xt%d" % i, [128, D], F32) for i in range(2)]
                gt = [sb("gt%d" % i, [128, D], BF16) for i in range(2)]
                grt = [sb("grt%d" % i, [128, D], BF16) for i in range(2)]
                gft = [sb("gft%d" % i, [128, D], BF16) for i in range(2)]
                uft = [sb("uft%d" % i, [128, 8, 128], BF16) for i in range(2)]
                SETS = []
                for i in range(2):
                    SETS.append(dict(
                        A=sb("A4_%d" % i, [128, D], F32), B=sb("B4_%d" % i, [128, D], F32), C=sb("C4_%d" % i, [128, D], F32),
                        x1=sb("x1_%d" % i, [128, D], F32), x2=sb("x2_%d" % i, [128, D], F32),
                        b=[sb("b4_%d_%d" % (i, j), [128, D], BF16) for j in range(4)],
                        bs=sb("bs4_%d" % i, [128, 4, 6], F32), mv=sb("mv4_%d" % i, [128, 4, 2], F32),
                        sm=sb("sm4_%d" % i, [128, 16], F32), st=sb("st4_%d" % i, [128, 8], F32),
                        T=ps("pT4_%d" % i, [128, D], BF16), X=[ps("pX%d_%d" % (j, i), [128, 512], F32) for j in range(2)],
                        Y=ps("pY_%d" % i, [128, 512], F32)))
                UFv = UFd.rearrange("(s p) t -> p s t", p=128)

                def p4a_load(c):
                    sl = c % 2
                    r0 = c * 128
                    for tl, dd, nm_ in ((yf, YFd, "yf"), (yb, YBd, "yb"), (gt, Gd, "gt")):
                        P.op("sp", lambda e, tl=tl, dd=dd: e.dma_start(out=tl[sl][:], in_=dd[r0:r0 + 128, :]), writes=["%s%d" % (nm_, sl)], dma=True)
                    P.op("sp", lambda e: e.dma_start(out=uft[sl][:], in_=UFv[:, :, r0:r0 + 128]), writes=["uft%d" % sl], dma=True)
                    for tl, dd, nm_ in ((grt, GRd, "grt"), (gft, GFd, "gft"), (xt, xm, "xt")):
                        P.op("sp", lambda e, tl=tl, dd=dd: e.dma_start(out=tl[sl][:], in_=dd[r0:r0 + 128, :]), writes=["%s%d" % (nm_, sl)], dma=True)

                def p4a_stages(c):
                    i = c % 2
                    sl = i
                    S_ = SETS[i]
                    A, B, Cc, x1, x2, bb, bs, mv, sm, st, T_, X, Y = (S_[k_] for k_ in ("A", "B", "C", "x1", "x2", "b", "bs", "mv", "sm", "st", "T", "X", "Y"))
                    s_ = str(i)
                    kA, kB, kC, kx1, kx2, kbs, kmv, ksm, kst, kT, kY = ("A4_" + s_, "B4_" + s_, "C4_" + s_, "x1_" + s_, "x2_" + s_, "bs4_" + s_, "mv4_" + s_, "sm4_" + s_, "st4_" + s_, "pT4_" + s_, "pY_" + s_)
                    kX = ["pX0_" + s_, "pX1_" + s_]
                    kb = ["b4_%s_%d" % (s_, j) for j in range(4)]
                    r0 = c * 128

                    def mm16(lhs_fn, lkey, w, wkey):
                        for n in range(2):
                            for k in range(8):
                                P.op("pe", lambda e, n=n, k=k: e.matmul(X[n][:], lhsT=lhs_fn(k), rhs=w[:, k, n * 512:(n + 1) * 512], start=(k == 0), stop=(k == 7)),
                                     reads=[lkey, wkey], writes=[kX[n]])

                    def tr(src, skey, dst, dkey, eng):
                        for k in range(8):
                            P.op("pe", lambda e, k=k: e.transpose(out=T_[:, k * 128:(k + 1) * 128], in_=src[:, k * 128:(k + 1) * 128], identity=ident[:]), reads=[skey, "ident"], writes=[kT])
                        if eng == "act":
                            P.op("act", lambda e: e.activation(out=dst[:], in_=T_[:], func=AF.Copy), reads=[kT], writes=[dkey])
                        else:
                            P.op("dve", lambda e: e.tensor_copy(out=dst[:], in_=T_[:]), reads=[kT], writes=[dkey])

                    def s_gn1():
                        P.op("dve", lambda e: e.tensor_tensor(out=A[:], in0=yf[sl][:], in1=yb[sl][:], op=ALU.add), reads=["yf" + s_, "yb" + s_], writes=[kA])
                        for hd in range(4):
                            P.op("dve", lambda e, hd=hd: e.bn_stats(out=bs[:, hd, :], in_=A[:, hd * 256:(hd + 1) * 256]), reads=[kA], writes=[kbs])
                            P.op("dve", lambda e, hd=hd: e.bn_aggr(out=mv[:, hd, :], in_=bs[:, hd, :]), reads=[kbs], writes=[kmv])
                        P.op("dve", lambda e: e.tensor_scalar(out=sm[:, 0:4], in0=mv[:, :, 1], scalar1=eps5, scalar2=None, op0=ALU.add), reads=[kmv, "dk"], writes=[ksm])
                        P.op("act", lambda e: e.activation(out=sm[:, 0:4], in_=sm[:, 0:4], func=AF.Ln), reads=[ksm], writes=[ksm])
                        P.op("act", lambda e: e.activation(out=sm[:, 4:8], in_=sm[:, 0:4], func=AF.Exp, scale=-0.5), reads=[ksm], writes=[ksm])
                        P.op("act", lambda e: e.activation(out=Cc[:], in_=gt[sl][:], func=AF.Sigmoid), reads=["gt" + s_], writes=[kC])
                        P.op("dve", lambda e: e.tensor_tensor(out=Cc[:], in0=Cc[:], in1=gt[sl][:], op=ALU.mult), reads=[kC, "gt" + s_], writes=[kC])

                    def s_gn2():
                        for hd in range(4):
                            P.op("dve", lambda e, hd=hd: e.tensor_scalar(out=B[:, hd * 256:(hd + 1) * 256], in0=A[:, hd * 256:(hd + 1) * 256], scalar1=mv[:, hd, 0:1], scalar2=sm[:, 4 + hd:5 + hd], op0=ALU.subtract, op1=ALU.mult),
                                 reads=[kA, kmv, ksm], writes=[kB])
                        P.op("dve", lambda e: e.tensor_tensor(out=B[:], in0=B[:], in1=gnw[:], op=ALU.mult), reads=[kB, "gnw"], writes=[kB])
                        P.op("dve", lambda e: e.tensor_tensor(out=bb[0][:], in0=B[:], in1=Cc[:], op=ALU.mult), reads=[kB, kC], writes=[kb[0]])
                        P.op("act", lambda e: e.activation(out=A[:], in_=grt[sl][:], func=AF.Sigmoid), reads=["grt" + s_], writes=[kA])
                        P.op("act", lambda e: e.activation(out=B[:], in_=gft[sl][:], func=AF.Sigmoid), reads=["gft" + s_], writes=[kB])

                    def s_tr_r():
                        tr(bb[0], kb[0], bb[1], kb[1], "act")

                    def s_ret():
                        mm16(lambda k: bb[1][:, k * 128:(k + 1) * 128], kb[1], wts["wro"], "wro")

                    def s_four():
                        for n in range(2):
                            cs_ = slice(n * 512, (n + 1) * 512)
                            for k in range(8):
                                P.op("pe", lambda e, n=n, k=k: e.matmul(Y[:], lhsT=uft[sl][:, k, :], rhs=wts["w4"][:, k, n * 512:(n + 1) * 512], start=(k == 0), stop=(k == 7)), reads=["uft" + s_, "w4"], writes=[kY])
                            P.op("dve", lambda e, cs_=cs_: e.tensor_tensor(out=B[:, cs_], in0=Y[:], in1=B[:, cs_], op=ALU.mult), reads=[kY, kB], writes=[kB])

                    def s_merge():
                        for n in range(2):
                            cs_ = slice(n * 512, (n + 1) * 512)
                            P.op("dve", lambda e, n=n, cs_=cs_: e.tensor_tensor(out=A[:, cs_], in0=X[n][:], in1=A[:, cs_], op=ALU.mult), reads=[kX[n], kA], writes=[kA])
                        P.op("dve", lambda e: e.tensor_tensor(out=bb[2][:], in0=A[:], in1=B[:], op=ALU.add), reads=[kA, kB], writes=[kb[2]])

                    def s_tr_m():
                        tr(bb[2], kb[2], bb[3], kb[3], "act")

                    def s_mix():
                        mm16(lambda k: bb[3][:, k * 128:(k + 1) * 128], kb[3], wts["wmx"], "wmx")
                        for n in range(2):
                            cs_ = slice(n * 512, (n + 1) * 512)
                            P.op("dve", lambda e, n=n, cs_=cs_: e.tensor_tensor(out=x1[:, cs_], in0=X[n][:], in1=xt[sl][:, cs_], op=ALU.add), reads=[kX[n], "xt" + s_], writes=[kx1])

                    def s_norm():
                        rstd_pow(st, kst, x1[:], kx1, Cc[:], kC)
                        P.op("dve", lambda e: e.scalar_tensor_tensor(out=bb[0][:], in0=x1[:], scalar=st[:, 2:3], in1=nca[:], op0=ALU.mult, op1=ALU.mult), reads=[kx1, kst, "nca"], writes=[kb[0]])

                    def s_tr_x():
                        tr(bb[0], kb[0], bb[1], kb[1], "dve")

                    def s_hq():
                        for m in range(8):
                            bank, bkey = X[m // 4], kX[m // 4]
                            co = (m % 4) * 128
                            for k in range(8):
                                P.op("pe", lambda e, m=m, k=k, bank=bank, co=co: e.matmul(bank[:, co:co + 128], lhsT=wts["wcq"][:, k, m * 128:(m + 1) * 128], rhs=bb[1][:, k * 128:(k + 1) * 128], start=(k == 0), stop=(k == 7)),
                                     reads=["wcq", kb[1]], writes=[bkey])
                        P.op("act", lambda e: e.activation(out=bb[2][:, 0:512], in_=X[0][:], func=AF.Copy), reads=[kX[0]], writes=[kb[2]])
                        P.op("dve", lambda e: e.tensor_copy(out=bb[2][:, 512:1024], in_=X[1][:]), reads=[kX[1]], writes=[kb[2]])

                    def s_logits():
                        for hd in range(4):
                            bank, bkey = X[hd // 2], kX[hd // 2]
                            co = (hd % 2) * 256
                            for hf in range(2):
                                m = 2 * hd + hf
                                P.op("pe", lambda e, m=m, hf=hf, bank=bank, co=co: e.matmul(bank[:, co:co + 256], lhsT=bb[2][:, m * 128:(m + 1) * 128], rhs=ckT[:, m, :], start=(hf == 0), stop=(hf == 1)),
                                     reads=[kb[2], "ckT"], writes=[bkey])

                    def s_softmax():
                        P.op("dve", lambda e: e.memset(sm[:, 8:16], 0.0), writes=[ksm])
                        for n in range(2):
                            P.op("dve", lambda e, n=n: e.tensor_reduce(out=sm[:, 2 * n:2 * n + 2], in_=X[n][:].rearrange("p (a b) -> p a b", a=2), axis=AX.X, op=ALU.max), reads=[kX[n], ksm], writes=[ksm])
                        P.op("dve", lambda e: e.tensor_scalar(out=sm[:, 4:8], in0=sm[:, 0:4], scalar1=-1.0 / 16.0, scalar2=None, op0=ALU.mult), reads=[ksm], writes=[ksm])
                        for hd in range(4):
                            bank, bkey = X[hd // 2], kX[hd // 2]
                            co = (hd % 2) * 256
                            P.op("act", lambda e, hd=hd, bank=bank, co=co: e.activation(out=Cc[:, hd * 256:(hd + 1) * 256], in_=bank[:, co:co + 256], func=AF.Exp, scale=1.0 / 16.0, bias=sm[:, 4 + hd:5 + hd], accum_out=sm[:, 8 + hd:9 + hd]),
                                 reads=[bkey, ksm], writes=[kC, ksm])
                        P.op("dve", lambda e: e.reciprocal(out=sm[:, 12:16], in_=sm[:, 8:12]), reads=[ksm], writes=[ksm])
                        for hd in range(4):
                            eng = "dve"
                            P.op(eng, lambda e, hd=hd: e.tensor_scalar(out=bb[3][:, hd * 256:(hd + 1) * 256], in0=Cc[:, hd * 256:(hd + 1) * 256], scalar1=sm[:, 12 + hd:13 + hd], scalar2=None, op0=ALU.mult),
                                 reads=[kC, ksm], writes=[kb[3]])

                    def s_tr_p():
                        tr(bb[3], kb[3], bb[0], kb[0], "act")

                    def s_att():
                        for m in range(8):
                            hd = m // 2
                            bank, bkey = X[m // 4], kX[m // 4]
                            co = (m % 4) * 128
                            for mc in range(2):
                                P.op("pe", lambda e, m=m, mc=mc, hd=hd, bank=bank, co=co: e.matmul(bank[:, co:co + 128], lhsT=cv[:, mc, m * 128:(m + 1) * 128], rhs=bb[0][:, (2 * hd + mc) * 128:(2 * hd + mc + 1) * 128], start=(mc == 0), stop=(mc == 1)),
                                     reads=["cv", kb[0]], writes=[bkey])
                        P.op("act", lambda e: e.activation(out=bb[1][:, 0:512], in_=X[0][:], func=AF.Copy), reads=[kX[0]], writes=[kb[1]])
                        P.op("dve", lambda e: e.tensor_copy(out=bb[1][:, 512:1024], in_=X[1][:]), reads=[kX[1]], writes=[kb[1]])

                    def s_co():
                        mm16(lambda k: bb[1][:, k * 128:(k + 1) * 128], kb[1], wts["wco"], "wco")
                        for n in range(2):
                            cs_ = slice(n * 512, (n + 1) * 512)
                            P.op("dve", lambda e, n=n, cs_=cs_: e.tensor_tensor(out=x2[:, cs_], in0=X[n][:], in1=x1[:, cs_], op=ALU.add), reads=[kX[n], kx1], writes=[kx2])
                        P.op("sp", lambda e: e.dma_start(out=X2d[r0:r0 + 128, :], in_=x2[:]), reads=[kx2], writes=["X2d"], dma=True)

                    return [s_gn1, s_gn2, s_tr_r, s_ret, s_four, s_merge, s_tr_m, s_mix, s_norm, s_tr_x, s_hq, s_logits, s_softmax, s_tr_p, s_att, s_co]

                NST = 16
                import os as _os
                SK = int(_os.environ.get("P4SKEW", "8"))
                p4a_load(0)
                active = []
                nxt = 0
                tick = 0
                while nxt < NM or active:
                    admit = (tick % NST == 0) or (tick % NST == SK)
                    if nxt < NM and admit:
                        if nxt + 1 < NM:
                            p4a_load(nxt + 1)
                        active.append([p4a_stages(nxt), 0])
                        nxt += 1
                    for a_ in active:
                        a_[0][a_[1]]()
                        a_[1] += 1
                    active = [a_ for a_ in active if a_[1] < NST]
                    tick += 1
                P.flush()

        if "p4" in phases or "p4b" in phases:
            with contextlib.ExitStack() as es:
                sb = lambda n, s, d: es.enter_context(nc.sbuf_tensor(n, s, d))
                ps = lambda n, s, d: es.enter_context(nc.psum_tensor(n, s, d))
                wup = sb("wup", [128, 8, DFF], BF16)
                wdn = sb("wdn", [128, 32, D], BF16)
                for k in range(8):
                    P.op("pool", lambda e, k=k: e.dma_start(out=wup[:, k, :], in_=w_up[k * 128:(k + 1) * 128, :]), writes=["wup"], dma=True)
                for kq in range(8):
                    P.op("pool", lambda e, kq=kq: e.dma_start(out=wdn[:, 4 * kq:4 * kq + 4, :], in_=w_dn[kq * 512:(kq + 1) * 512, :].rearrange("(k p) n -> p k n", p=128)), writes=["wdn"], dma=True)
                nmlp = sb("nmlp_s", [128, D], F32)
                nfin = sb("nfin_s", [128, D], F32)
                P.op("sp", lambda e: e.dma_start(out=nmlp[:], in_=nmlp_d[:, :]), writes=["nmlp"], dma=True)
                P.op("sp", lambda e: e.dma_start(out=nfin[:], in_=nfin_d[:, :]), writes=["nfin"], dma=True)
                xg = [sb("xg%d" % i, [128, 2, D], F32) for i in range(2)]
                xn = [sb("xn5_%d" % i, [128, D], BF16) for i in range(2)]
                xnT = [sb("xnT5_%d" % i, [128, 8, 256], BF16) for i in range(2)]
                hT = sb("hT", [128, 32, 256], BF16)
                sq = [sb("sq%d" % i, [128, 256], F32) for i in range(2)]
                ot = [sb("ot%d" % i, [128, D], F32) for i in range(2)]
                st = [sb("st5_%d" % i, [128, 8], F32) for i in range(3)]
                pT = ps("pT5", [128, D], BF16)
                pu = [ps("pu%d" % i, [128, 512], F32) for i in range(5)]
                pd = [ps("pd%d" % i, [128, 512], F32) for i in range(2)]

                def p4b_load(gi):
                    sl = gi % 2
                    P.op("sp", lambda e: e.dma_start(out=xg[sl][:], in_=X2d[gi * 256:(gi + 1) * 256, :].rearrange("(t p) d -> p t d", p=128)), writes=["xg%d" % sl], dma=True)

                oc = [0]

                def p4b_norm_stages(gi):
                    sl = gi % 2
                    gk = "xg%d" % sl
                    out = []
                    for t in range(2):
                        stt_, kst_, xnt_, kxn_ = st[t], "st5_%d" % t, xn[t], "xn5_%d" % t

                        def n0(t=t, stt_=stt_, kst_=kst_, xnt_=xnt_, kxn_=kxn_):
                            P.op("dve", lambda e: e.memset(stt_[:], 0.0), writes=[kst_])
                            P.op("act", lambda e: e.activation(out=xnt_[:], in_=xg[sl][:, t, :], func=AF.Square, accum_out=stt_[:, 0:1]), reads=[gk, kst_], writes=[kxn_, kst_])

                        def n1(stt_=stt_, kst_=kst_):
                            P.op("act", lambda e: e.activation(out=stt_[:, 1:2], in_=stt_[:, 0:1], func=AF.Sqrt, scale=1.0 / D, bias=eps6), reads=[kst_, "dk"], writes=[kst_])

                        def n2(stt_=stt_, kst_=kst_):
                            P.op("dve", lambda e: e.reciprocal(out=stt_[:, 2:3], in_=stt_[:, 1:2]), reads=[kst_], writes=[kst_])

                        def n3(t=t, stt_=stt_, kst_=kst_, xnt_=xnt_, kxn_=kxn_):
                            P.op("dve", lambda e: e.scalar_tensor_tensor(out=xnt_[:], in0=xg[sl][:, t, :], scalar=stt_[:, 2:3], in1=nmlp[:], op0=ALU.mult, op1=ALU.mult), reads=[gk, kst_, "nmlp"], writes=[kxn_])
                        out += [n0, n1, n2, n3]
                    return out

                def p4b_tr(gi):
                    sl = gi % 2
                    for t in range(2):
                        transposes8(xn[t], "xn5_%d" % t, pT, "pT5")
                        P.op("act", lambda e, t=t: e.activation(out=xnT[sl][:, :, t * 128:(t + 1) * 128], in_=pT[:].rearrange("p (k c) -> p k c", k=8), func=AF.Copy), reads=["pT5"], writes=["xnT5_%d" % sl])

                def p4b_up(gi, f0, f1, hooks=None):
                    sl = gi % 2
                    for f in range(f0, f1):
                        if hooks and f in hooks:
                            hooks[f]()
                        bank, bkey = pu[f % 5], "pu%d" % (f % 5)
                        for k in range(8):
                            P.op("pe", lambda e, f=f, k=k, bank=bank: e.matmul(bank[:, 0:256], lhsT=wup[:, k, f * 128:(f + 1) * 128], rhs=xnT[sl][:, k, :], start=(k == 0), stop=(k == 7)), reads=["wup", "xnT5_%d" % sl], writes=[bkey])
                        sqt, sqk = sq[f % 2], "sq%d" % (f % 2)
                        if f % 2 == 0:
                            P.op("act", lambda e, bank=bank, sqt=sqt: e.activation(out=sqt[:], in_=bank[:, 0:256], func=AF.Relu), reads=[bkey], writes=[sqk])
                            P.op("act", lambda e, f=f, sqt=sqt: e.activation(out=hT[:, f, :], in_=sqt[:], func=AF.Square), reads=[sqk], writes=["hT"])
                        else:
                            P.op("dve", lambda e, bank=bank, sqt=sqt: e.tensor_scalar(out=sqt[:], in0=bank[:, 0:256], scalar1=0.0, scalar2=None, op0=ALU.max), reads=[bkey], writes=[sqk])
                            P.op("dve", lambda e, f=f, sqt=sqt: e.tensor_tensor(out=hT[:, f, :], in0=sqt[:], in1=sqt[:], op=ALU.mult), reads=[sqk], writes=["hT"])

                def p4b_down(gi):
                    sl = gi % 2
                    gk = "xg%d" % sl
                    epi = []
                    for t in range(2):
                        for n in range(2):
                            for f in range(32):
                                P.op("pe", lambda e, t=t, n=n, f=f: e.matmul(pd[n][:], lhsT=hT[:, f, t * 128:(t + 1) * 128], rhs=wdn[:, f, n * 512:(n + 1) * 512], start=(f == 0), stop=(f == 31)), reads=["hT", "wdn"], writes=["pd%d" % n])
                            cs_ = slice(n * 512, (n + 1) * 512)
                            P.op("dve", lambda e, t=t, n=n, cs_=cs_: e.tensor_tensor(out=xg[sl][:, t, cs_], in0=pd[n][:], in1=xg[sl][:, t, cs_], op=ALU.add), reads=["pd%d" % n, gk], writes=[gk])

                        def fin(t=t):
                            osl = oc[0] % 2
                            oc[0] += 1
                            ok = "ot%d" % osl
                            rstd_ops(st[2], "st5_2", xg[sl][:, t, :], gk, ot[osl][:], ok)
                            P.op("dve", lambda e: e.scalar_tensor_tensor(out=ot[osl][:], in0=xg[sl][:, t, :], scalar=st[2][:, 2:3], in1=nfin[:], op0=ALU.mult, op1=ALU.mult), reads=[gk, "st5_2", "nfin"], writes=[ok])
                            r0 = gi * 256 + t * 128
                            P.op("sp", lambda e: e.dma_start(out=y_out[r0:r0 + 128, :], in_=ot[osl][:]), reads=[ok], writes=["y_out"], dma=True)
                        fin()
                    return epi

                NG = NM // 2
                p4b_load(0)
                for f_ in p4b_norm_stages(0):
                    f_()
                p4b_tr(0)
                for gi in range(NG):
                    if gi + 1 < NG:
                        p4b_load(gi + 1)
                        ns_ = p4b_norm_stages(gi + 1)
                        hk = {9: ns_[0], 11: ns_[1], 13: ns_[2], 15: ns_[3], 18: ns_[4], 20: ns_[5], 22: ns_[6], 24: ns_[7]}
                    else:
                        hk = None
                    p4b_up(gi, 0, 32, hk)
                    if gi + 1 < NG:
                        p4b_tr(gi + 1)
                    p4b_down(gi)
                P.flush()
        P.flush(final=True)
    return nc, P


def _other_chunks(h):
    return np.arange(0, 64) if h == 1 else np.arange(127, 63, -1)


def _consts(h):
    bf = ml_dtypes.bfloat16
    c = {}
    c["ident"] = np.eye(128, dtype=np.float32).astype(bf)
    inv = (np.float32(10000.0) ** (-(np.arange(0, 256, 2, dtype=np.float32)) / np.float32(256))).astype(np.float32)

    def rot(pos):
        ang = (pos.astype(np.float32)[:, None] * inv[None, :]).astype(np.float32).astype(np.float64)
        return np.concatenate([np.cos(ang), np.sin(ang)], axis=1).astype(np.float32)
    pos_m = h * 8192 + np.arange(8192)
    oc = _other_chunks(h)
    pos_o = (oc[:, None] * 128 + np.arange(128)[None, :]).reshape(-1)
    c["rotm"] = rot(pos_m)
    c["roto"] = rot(pos_o)
    dk = np.zeros((128, DKW), np.float32)
    j = np.arange(128)[:, None].astype(np.float64)
    i = np.arange(128)[None, :].astype(np.float64)
    dk[:, O_E0F:O_E0F + 128] = np.maximum(i - j, 0)
    dk[:, O_MF:O_MF + 128] = (i >= j)
    dk[:, O_E0B:O_E0B + 128] = np.maximum(j - i, 0)
    dk[:, O_MB:O_MB + 128] = (j > i)
    dk[:, O_XF:O_XF + 128] = i + 1
    dk[:, O_XB:O_XB + 128] = 128 - i
    zf = 127 - j
    zb = j
    dk[:, O_ZF:O_ZF + 256] = zf
    dk[:, O_ZB:O_ZB + 256] = zb
    dk[:, O_ZO:O_ZO + 256] = zf if h == 1 else zb
    dk[:, O_MSKF] = 1.0 if h == 1 else 0.0
    dk[:, O_MSKB] = 1.0 if h == 0 else 0.0
    dk[:, O_EPS6] = 1e-6
    dk[:, O_EPS5] = 1e-5
    dk[:, O_ONE] = 1.0
    dk[:, O_NH:O_NH + 8] = -0.5
    c["dk"] = dk
    a = np.concatenate([64 * h + np.arange(64), oc]).astype(np.float64)
    d = np.arange(128, dtype=np.float64)
    th = 2 * np.pi * ((a[:, None] * d[None, :]) % 128) / 128.0
    fcs = np.zeros((128, 2, 256), np.float64)
    fcs[:, 0, :128] = np.cos(th)
    fcs[:, 0, 128:] = -np.sin(th)
    fcs[:, 1, :128] = np.sin(th)
    fcs[:, 1, 128:] = np.cos(th)
    c["fcs"] = fcs.astype(np.float32).astype(bf)
    b = np.arange(128, dtype=np.int64)[:, None, None]
    dd = np.arange(128, dtype=np.int64)[None, :, None]
    cg = (64 * h + np.arange(64, dtype=np.int64))[None, None, :]
    num = (cg * b * 128 + dd * b) % 16384
    th2 = 2 * np.pi * num.astype(np.float64) / 16384.0
    mt = np.zeros((128, 128, 2, 64), np.float64)
    mt[:, :, 0, :] = np.cos(th2)
    mt[:, :, 1, :] = np.sin(th2)
    c["mt"] = mt.astype(np.float32).astype(bf)
    ch = (np.arange(2)[None, :, None] * 128 + np.arange(128)[:, None, None]).astype(np.int64)
    jj = np.arange(256, dtype=np.int64)[None, None, :]
    th3 = 2 * np.pi * ((ch * jj) % 256).astype(np.float64) / 256.0
    cs = np.concatenate([np.cos(th3), -np.sin(th3)], axis=2)
    c["cs"] = cs.astype(np.float32).astype(bf)
    return c


_CACHE = {}


def _rep(v):
    return np.ascontiguousarray(np.broadcast_to(np.asarray(v, np.float32).reshape(1, -1), (128, v.size)))


def make_in_maps(inp):
    seqs = [(inp["x_prompt"][0], inp["mem_prompt"][0]), (inp["x_prompt"][1], inp["mem_prompt"][1]), (inp["x_sample"][0], inp["mem_sample"][0])]
    shared = {
        "w_in": np.ascontiguousarray(inp["w_in"][0]), "w_ro": np.ascontiguousarray(inp["w_ret_out"][0]),
        "w_4": np.ascontiguousarray(inp["w_four_out"][0]), "w_mx": np.ascontiguousarray(inp["w_mix_out"][0]),
        "w_cq": np.ascontiguousarray(inp["w_cq"][0]), "w_ck": np.ascontiguousarray(inp["w_ck"][0]),
        "w_cv": np.ascontiguousarray(inp["w_cv"][0]), "w_co": np.ascontiguousarray(inp["w_co"][0]),
        "w_up": np.ascontiguousarray(inp["w_up"][0]), "w_dn": np.ascontiguousarray(inp["w_down"][0]),
        "nmix": _rep(inp["norm_mix_w"][0]), "nca": _rep(inp["norm_ca_w"][0]), "nmem": _rep(inp["norm_mem_w"][0]),
        "nmlp": _rep(inp["norm_mlp_w"][0]), "nfin": _rep(inp["norm_final_w"]), "gnw": _rep(inp["ret_gn_w"][0]),
    }
    consts = [_consts(0), _consts(1)]
    in_maps = []
    for core in range(8):
        si = min(core // 2, 2)
        h = core % 2
        x, mem = seqs[si]
        x = np.asarray(x, np.float32)
        xm = np.ascontiguousarray(x[h * 8192:(h + 1) * 8192])
        oc = _other_chunks(h)
        xo = np.ascontiguousarray(x.reshape(128, 128, D)[oc].reshape(8192, D))
        df = np.asarray(inp["ret_decay_fwd"][0], np.float32)
        db = np.asarray(inp["ret_decay_bwd"][0], np.float32)
        do = df if h == 1 else db
        m = dict(shared)
        m.update(consts[h])
        m.update({"xm": xm, "xo": xo, "mem": np.ascontiguousarray(np.asarray(mem, np.float32)),
                  "dec": _rep(np.concatenate([df, db, do]))})
        in_maps.append(m)
    return in_maps


def kernel(**inputs):
    inp = {k: np.asarray(v) for k, v in inputs.items()}
    if "nc" not in _CACHE:
        _CACHE["nc"] = build()[0]
    nc = _CACHE["nc"]
    in_maps = make_in_maps(inp)
    res = run_bass_kernel_spmd(nc, in_maps, core_ids=list(range(8)))
    ys = [np.asarray(r["y"], np.float32) for r in res.results]
    y_prompt = np.stack([np.concatenate([ys[0], ys[1]], 0), np.concatenate([ys[2], ys[3]], 0)], 0)
    y_sample = np.concatenate([ys[4], ys[5]], 0)[None]
    return (y_prompt, y_sample)
```

```python
import contextlib
import numpy as np
import ml_dtypes
import concourse.bass as bass
import concourse.mybir as mybir
from concourse.bass_utils import run_bass_kernel_spmd

F32 = mybir.dt.float32
BF16 = mybir.dt.bfloat16
AF = mybir.ActivationFunctionType
ALU = mybir.AluOpType
AX = mybir.AxisListType

D = 1024
SEQ = 16384
NM = 64
NO = 64
INW = 7168
DFF = 4096
NMEM = 256


class Prog:
    NPOOL = 8

    def __init__(self, nc):
        self.nc = nc
        self.eng = {"pe": nc.tensor, "act": nc.scalar, "dve": nc.vector, "pool": nc.gpsimd, "sp": nc.sync}
        self.sems = {}
        self.cnt = {}
        self.known = {e: {} for e in self.eng}
        self.carry = {e: {} for e in self.eng}
        self.dma_rr = {e: 0 for e in self.eng}
        self.n_inst = 0
        self._reset()

    def _reset(self):
        self.ops = []
        self.lastw = {}
        self.lastr = {}
        self.last_on_sem = {}

    def getsem(self, name):
        if name not in self.sems:
            self.sems[name] = self.nc.alloc_semaphore(name=name)
        return self.sems[name]

    def op(self, eng, fn, reads=(), writes=(), dma=False, ndma=1):
        idx = len(self.ops)
        deps = set()
        for k in reads:
            deps.update(self.lastw.get(k, {}).values())
        for k in writes:
            deps.update(self.lastw.get(k, {}).values())
            deps.update(self.lastr.get(k, {}).values())
        semname = None
        if dma:
            semname = "d_%s_%d" % (eng, self.dma_rr[eng] % self.NPOOL)
            self.dma_rr[eng] += 1
            if semname in self.last_on_sem:
                deps.add(self.last_on_sem[semname])
            self.last_on_sem[semname] = idx
        self.ops.append(dict(eng=eng, fn=fn, deps=deps, dma=dma, semname=semname, ndma=ndma, waited=False, tok=None))
        ek = (eng, semname)
        for k in reads:
            self.lastr.setdefault(k, {})[ek] = idx
        for k in writes:
            self.lastw[k] = {ek: idx}
            self.lastr[k] = {}
        return idx

    def flush(self, final=False):
        ops = self.ops
        last_eng = {}
        for i, o in enumerate(ops):
            if not o["dma"]:
                last_eng[o["eng"]] = i
            for d in o["deps"]:
                od = ops[d]
                if (not od["dma"]) and od["eng"] == o["eng"] == "pe":
                    continue
                od["waited"] = True
        for i in last_eng.values():
            ops[i]["waited"] = True
        for o in ops:
            if o["dma"]:
                o["waited"] = True
        for o in ops:
            if not o["waited"]:
                continue
            if o["dma"]:
                nm = o["semname"]
                self.cnt[nm] = self.cnt.get(nm, 0) + 16 * o["ndma"]
            else:
                nm = "e_" + o["eng"]
                self.cnt[nm] = self.cnt.get(nm, 0) + 1
            o["tok"] = (nm, self.cnt[nm])
        for o in ops:
            e = o["eng"]
            engobj = self.eng[e]
            need = dict(self.carry[e])
            self.carry[e] = {}
            for d in o["deps"]:
                od = ops[d]
                if od["tok"] is None:
                    continue
                if (not od["dma"]) and od["eng"] == e == "pe":
                    continue
                nm, v = od["tok"]
                need[nm] = max(need.get(nm, 0), v)
            for nm, v in need.items():
                if self.known[e].get(nm, 0) >= v:
                    continue
                engobj.wait_ge(self.getsem(nm), v)
                self.known[e][nm] = v
                self.n_inst += 1
            r = o["fn"](engobj)
            self.n_inst += 1
            if o["tok"] is not None:
                nm, v = o["tok"]
                if o["dma"]:
                    insts = r if isinstance(r, (list, tuple)) else [r]
                    assert len(insts) == o["ndma"], (len(insts), o["ndma"])
                    for ins in insts:
                        ins.then_inc(self.getsem(nm), 16)
                else:
                    r.then_inc(self.getsem(nm), 1)
        allc = dict(self.cnt)
        for e in self.eng:
            self.carry[e] = dict(allc)
        if final:
            engobj = self.eng["sp"]
            for nm, v in allc.items():
                if self.known["sp"].get(nm, 0) < v:
                    engobj.wait_ge(self.getsem(nm), v)
                    self.known["sp"][nm] = v
        self._reset()


O_E0F, O_MF, O_E0B, O_MB, O_XF, O_XB = 0, 128, 256, 384, 512, 640
O_ZF, O_ZB, O_ZO = 768, 1024, 1280
O_MSKF, O_MSKB, O_EPS6, O_EPS5, O_ONE = 1536, 1537, 1538, 1539, 1540
O_NH = 1544
DKW = 1552


def build(dbg=False, phases=("p1", "p2", "p3", "p4")):
    nc = bass.Bass("TRN2", target_bir_lowering=False)

    def din(name, shape, dt=F32):
        return nc.dram_tensor(name, shape, dt, kind="ExternalInput").ap()

    def dscr(name, shape, dt):
        return nc.dram_tensor(name, shape, dt, kind="ExternalOutput" if dbg else "Internal").ap()

    xm = din("xm", [NM * 128, D])
    xo = din("xo", [NO * 128, D])
    mem = din("mem", [NMEM, D])
    w_in = din("w_in", [D, INW])
    w_ro = din("w_ro", [D, D])
    w_4 = din("w_4", [D, D])
    w_mx = din("w_mx", [D, D])
    w_cq = din("w_cq", [D, D])
    w_ck = din("w_ck", [D, D])
    w_cv = din("w_cv", [D, D])
    w_co = din("w_co", [D, D])
    w_up = din("w_up", [D, DFF])
    w_dn = din("w_dn", [DFF, D])
    nmix_d = din("nmix", [128, D])
    nca_d = din("nca", [128, D])
    nmem_d = din("nmem", [128, D])
    nmlp_d = din("nmlp", [128, D])
    nfin_d = din("nfin", [128, D])
    gnw_d = din("gnw", [128, D])
    dec_d = din("dec", [128, 12])
    dk_d = din("dk", [128, DKW])
    ident_d = din("ident", [128, 128], BF16)
    rotm_d = din("rotm", [NM * 128, 256])
    roto_d = din("roto", [NO * 128, 256])
    fcs_d = din("fcs", [128, 2, 256], BF16)
    mt_d = din("mt", [128, 128, 2, 64], BF16)
    cs_d = din("cs", [128, 2, 512], BF16)
    y_out = nc.dram_tensor("y", [NM * 128, D], F32, kind="ExternalOutput").ap()

    NT = (NM + NO) * 128
    Qd = dscr("Qd", [NM * 128, D], BF16)
    Kd = dscr("Kd", [NM * 128, D], BF16)
    Vd = dscr("Vd", [NM * 128, D], BF16)
    Gd = dscr("Gd", [NM * 128, D], BF16)
    GRd = dscr("GRd", [NM * 128, D], BF16)
    GFd = dscr("GFd", [NM * 128, D], BF16)
    KOd = dscr("KOd", [NO * 128, D], BF16)
    VOd = dscr("VOd", [NO * 128, D], BF16)
    Zd = dscr("Zd", [4, 2, 2, NT, 128], BF16)
    YFd = dscr("YFd", [NM * 128, D], F32)
    YBd = dscr("YBd", [NM * 128, D], F32)
    UFd = dscr("UFd", [D, NM * 128], BF16)
    X2d = dscr("X2d", [NM * 128, D], F32)

    P = Prog(nc)
    with contextlib.ExitStack() as gs:
        ident = gs.enter_context(nc.sbuf_tensor("ident_s", [128, 128], BF16))
        dk = gs.enter_context(nc.sbuf_tensor("dk_s", [128, DKW], F32))
        ckT = gs.enter_context(nc.sbuf_tensor("ckT", [128, 8, NMEM], BF16))
        cv = gs.enter_context(nc.sbuf_tensor("cv", [128, 2, D], BF16))
        P.op("sp", lambda e: e.dma_start(out=ident[:], in_=ident_d[:, :]), writes=["ident"], dma=True)
        P.op("sp", lambda e: e.dma_start(out=dk[:], in_=dk_d[:, :]), writes=["dk"], dma=True)
        eps6 = dk[:, O_EPS6:O_EPS6 + 1]
        eps5 = dk[:, O_EPS5:O_EPS5 + 1]
        one_ap = dk[:, O_ONE:O_ONE + 1]

        def rstd_ops(st, key, x_ap, xkey, junk, jkey):
            P.op("dve", lambda e: e.memset(st[:], 0.0), writes=[key])
            P.op("act", lambda e: e.activation(out=junk, in_=x_ap, func=AF.Square, accum_out=st[:, 0:1]),
                 reads=[xkey, key], writes=[jkey, key])
            P.op("act", lambda e: e.activation(out=st[:, 1:2], in_=st[:, 0:1], func=AF.Sqrt, scale=1.0 / D, bias=eps6),
                 reads=[key, "dk"], writes=[key])
            P.op("dve", lambda e: e.reciprocal(out=st[:, 2:3], in_=st[:, 1:2]), reads=[key], writes=[key])

        def rstd_pow(st, key, x_ap, xkey, junk, jkey):
            P.op("dve", lambda e: e.memset(st[:], 0.0), writes=[key])
            P.op("act", lambda e: e.activation(out=junk, in_=x_ap, func=AF.Square, accum_out=st[:, 0:1]),
                 reads=[xkey, key], writes=[jkey, key])
            P.op("dve", lambda e: e.tensor_scalar(out=st[:, 1:2], in0=st[:, 0:1], scalar1=1.0 / D, scalar2=eps6, op0=ALU.mult, op1=ALU.add), reads=[key, "dk"], writes=[key])
            P.op("act", lambda e: e.activation(out=st[:, 3:4], in_=st[:, 1:2], func=AF.Ln), reads=[key], writes=[key])
            P.op("act", lambda e: e.activation(out=st[:, 2:3], in_=st[:, 3:4], func=AF.Exp, scale=-0.5), reads=[key], writes=[key])

        def transposes8(src, skey, pT, pkey):
            for k in range(8):
                P.op("pe", lambda e, k=k: e.transpose(out=pT[:, k * 128:(k + 1) * 128], in_=src[:, k * 128:(k + 1) * 128], identity=ident[:]),
                     reads=[skey, "ident"], writes=[pkey])

        if "p1" in phases:
            with contextlib.ExitStack() as es:
                sb = lambda n, s, d: es.enter_context(nc.sbuf_tensor(n, s, d))
                ps = lambda n, s, d: es.enter_context(nc.psum_tensor(n, s, d))
                win = sb("win", [128, 8, INW], BF16)
                nmix = sb("nmix_s", [128, D], F32)
                css = sb("css", [128, 2, 512], BF16)
                xs = [sb("xs%d" % i, [128, D], F32) for i in range(2)]
                rt = [sb("rt%d" % i, [128, 256], F32) for i in range(3)]
                st_ = [sb("st1_%d" % i, [128, 8], F32) for i in range(2)]
                xn_ = [sb("xn1_%d" % i, [128, D], BF16) for i in range(2)]
                xnT_ = [sb("xnT1_%d" % i, [128, D], BF16) for i in range(2)]
                qkf = [sb("qkf%d" % i, [128, D], F32) for i in range(2)]
                tmp = [sb("tmp%d" % i, [128, 4, 128], F32) for i in range(4)]
                outs = {nm: [sb("o_%s%d" % (nm, i), [128, D], BF16) for i in range(2)] for nm in ("q", "k", "v")}
                outs.update({nm: [sb("o_%s0" % nm, [128, D], BF16)] for nm in ("g", "gr", "gf")})
                ub = sb("ub", [128, D], BF16)
                uT = sb("uT", [128, D], BF16)
                zt = [sb("zt%d" % i, [128, 4, 512], BF16) for i in range(2)]
                pT = ps("pT1", [128, D], BF16)
                pT2 = ps("pT1b", [128, D], BF16)
                pm = [ps("pm1_%d" % i, [128, 512], F32) for i in range(6)]
                for k in range(8):
                    P.op("pool", lambda e, k=k: e.dma_start(out=win[:, k, :], in_=w_in[k * 128:(k + 1) * 128, :]),
                         writes=["win%d" % k], dma=True)
                P.op("sp", lambda e: e.dma_start(out=nmix[:], in_=nmix_d[:, :]), writes=["nmix"], dma=True)
                P.op("sp", lambda e: e.dma_start(out=css[:], in_=cs_d[:, :, :]), writes=["css"], dma=True)
                bi = [0]

                def p1_load_x(ti):
                    mine = ti < NM
                    sl = ti % 2
                    src = xm if mine else xo
                    r0 = (ti if mine else ti - NM) * 128
                    P.op("sp", lambda e: e.dma_start(out=xs[sl][:], in_=src[r0:r0 + 128, :]), writes=["xs%d" % sl], dma=True)

                def p1_load_rt(ti):
                    mine = ti < NM
                    sl = ti % 3
                    rsrc = rotm_d if mine else roto_d
                    r0 = (ti if mine else ti - NM) * 128
                    P.op("sp", lambda e: e.dma_start(out=rt[sl][:], in_=rsrc[r0:r0 + 128, :]), writes=["rt%d" % sl], dma=True)

                def rotary(srcf, skey, dst, dkey, sl):
                    X = srcf[:].rearrange("p (h t f) -> p h t f", h=4, t=2)
                    O = dst[:].rearrange("p (h t f) -> p h t f", h=4, t=2)
                    cosb = rt[sl][:, 0:128].rearrange("p (o f) -> p o f", o=1).broadcast_to([128, 4, 128])
                    sinb = rt[sl][:, 128:256].rearrange("p (o f) -> p o f", o=1).broadcast_to([128, 4, 128])
                    rk = "rt%d" % sl
                    P.op("pool", lambda e: e.tensor_tensor(out=tmp[0][:], in0=X[:, :, 0, :], in1=cosb, op=ALU.mult), reads=[skey, rk], writes=["tmp0"])
                    P.op("dve", lambda e: e.tensor_tensor(out=tmp[1][:], in0=X[:, :, 1, :], in1=sinb, op=ALU.mult), reads=[skey, rk], writes=["tmp1"])
                    P.op("dve", lambda e: e.tensor_tensor(out=O[:, :, 0, :], in0=tmp[0][:], in1=tmp[1][:], op=ALU.subtract), reads=["tmp0", "tmp1"], writes=[dkey])
                    P.op("pool", lambda e: e.tensor_tensor(out=tmp[2][:], in0=X[:, :, 1, :], in1=cosb, op=ALU.mult), reads=[skey, rk], writes=["tmp2"])
                    P.op("dve", lambda e: e.tensor_tensor(out=tmp[3][:], in0=X[:, :, 0, :], in1=sinb, op=ALU.mult), reads=[skey, rk], writes=["tmp3"])
                    P.op("pool", lambda e: e.tensor_tensor(out=O[:, :, 1, :], in0=tmp[2][:], in1=tmp[3][:], op=ALU.add), reads=["tmp2", "tmp3", dkey], writes=[dkey])

                def p1_A_stages(ti):
                    sl = ti % 2
                    xk = "xs%d" % sl
                    st, xn, xnT = st_[sl], xn_[sl], xnT_[sl]
                    ks_, kn_, kt_ = "st1_%d" % sl, "xn1_%d" % sl, "xnT1_%d" % sl

                    def a0():
                        P.op("dve", lambda e: e.memset(st[:], 0.0), writes=[ks_])
                        P.op("act", lambda e: e.activation(out=xn[:], in_=xs[sl][:], func=AF.Square, accum_out=st[:, 0:1]), reads=[xk, ks_], writes=[kn_, ks_])

                    def a1():
                        P.op("act", lambda e: e.activation(out=st[:, 1:2], in_=st[:, 0:1], func=AF.Sqrt, scale=1.0 / D, bias=eps6), reads=[ks_, "dk"], writes=[ks_])

                    def a2():
                        P.op("dve", lambda e: e.reciprocal(out=st[:, 2:3], in_=st[:, 1:2]), reads=[ks_], writes=[ks_])

                    def a3():
                        P.op("dve", lambda e: e.scalar_tensor_tensor(out=xn[:], in0=xs[sl][:], scalar=st[:, 2:3], in1=nmix[:], op0=ALU.mult, op1=ALU.mult),
                             reads=[xk, ks_, "nmix"], writes=[kn_])

                    def a4():
                        transposes8(xn, kn_, pT, "pT1")

                    def a5():
                        P.op("dve", lambda e: e.tensor_copy(out=xnT[:], in_=pT[:]), reads=["pT1"], writes=[kt_])
                    return [a0, a1, a2, a3, a4, a5]

                def p1_tail_stages(ti):
                    zr0 = ti * 128
                    zsl = ti % 2

                    def t0():
                        transposes8(ub, "ub", pT2, "pT1b")
                        P.op("act", lambda e: e.activation(out=uT[:], in_=pT2[:], func=AF.Copy), reads=["pT1b"], writes=["uT"])

                    def t1():
                        for g in range(4):
                            b = bi[0] % 6
                            bi[0] += 1
                            bank, bkey = pm[b], "pm1_%d" % b
                            for kk in range(2):
                                P.op("pe", lambda e, g=g, kk=kk, bank=bank: e.matmul(bank[:], lhsT=uT[:, (2 * g + kk) * 128:(2 * g + kk + 1) * 128], rhs=css[:, kk, :], start=(kk == 0), stop=(kk == 1)),
                                     reads=["uT", "css"], writes=[bkey])
                            if g % 2 == 0:
                                P.op("act", lambda e, g=g, bank=bank: e.activation(out=zt[zsl][:, g, :], in_=bank[:], func=AF.Copy), reads=[bkey], writes=["zt%d" % zsl])
                            else:
                                P.op("dve", lambda e, g=g, bank=bank: e.tensor_copy(out=zt[zsl][:, g, :], in_=bank[:]), reads=[bkey], writes=["zt%d" % zsl])
                        for g in range(4):
                            P.op("sp", lambda e, g=g: e.dma_start(
                                out=Zd[g, :, :, zr0:zr0 + 128, :].rearrange("ri hf t c -> t (ri hf) c"),
                                in_=zt[zsl][:, g, :].rearrange("p (rh c) -> p rh c", rh=4)),
                                reads=["zt%d" % zsl], writes=["Zd"], dma=True)
                    return [t0, t1]

                def p1_compute(ti):
                    mine = ti < NM
                    sl = ti % 2
                    r0 = (ti if mine else ti - NM) * 128
                    zr0 = ti * 128
                    xnT = xnT_[sl]
                    kt_ = "xnT1_%d" % sl
                    slices = list(range(14)) if mine else [2, 3, 4, 5, 8, 9]
                    nxt_st = p1_A_stages(ti + 1) if ti + 1 < NM + NO else []
                    hook = ({1: 0, 2: 1, 3: 2, 5: 3, 8: 4, 10: 5} if mine else {0: 0, 1: 1, 2: 2, 3: 3, 4: 4, 5: 5})
                    prev_tail = p1_tail_stages(ti - 1) if ti > 0 else []
                    thook = ({4: 0, 7: 1} if mine else {1: 0, 3: 1})
                    for si_, n in enumerate(slices):
                        if si_ in hook and nxt_st:
                            nxt_st[hook[si_]]()
                        if si_ in thook and prev_tail:
                            prev_tail[thook[si_]]()
                        b = bi[0] % 6
                        bi[0] += 1
                        bank, bkey = pm[b], "pm1_%d" % b
                        for k in range(8):
                            P.op("pe", lambda e, k=k, n=n, bank=bank: e.matmul(bank[:], lhsT=xnT[:, k * 128:(k + 1) * 128], rhs=win[:, k, n * 512:(n + 1) * 512], start=(k == 0), stop=(k == 7)),
                                 reads=[kt_, "win%d" % k], writes=[bkey])
                        hf = n % 2
                        cs_ = slice(hf * 512, (hf + 1) * 512)
                        if n in (0, 1):
                            P.op("act", lambda e, bank=bank, cs_=cs_: e.activation(out=qkf[0][:, cs_], in_=bank[:], func=AF.Copy, scale=1.0 / 16.0), reads=[bkey], writes=["qkf0"])
                            if n == 1:
                                rotary(qkf[0], "qkf0", outs["q"][sl], "o_q%d" % sl, ti % 3)
                        elif n in (2, 3):
                            P.op("act", lambda e, bank=bank, cs_=cs_: e.activation(out=qkf[1][:, cs_], in_=bank[:], func=AF.Copy), reads=[bkey], writes=["qkf1"])
                            if n == 3:
                                rotary(qkf[1], "qkf1", outs["k"][sl], "o_k%d" % sl, ti % 3)
                        else:
                            nm_, eng = {4: ("v", "dve"), 5: ("v", "dve"), 6: ("g", "act"), 7: ("g", "act"), 8: ("u", "dve"), 9: ("u", "dve"),
                                        10: ("gr", "act"), 11: ("gr", "act"), 12: ("gf", "dve"), 13: ("gf", "dve")}[n]
                            if nm_ == "u":
                                dst, dkey = ub, "ub"
                            elif nm_ == "v":
                                dst, dkey = outs["v"][sl], "o_v%d" % sl
                            else:
                                dst, dkey = outs[nm_][0], "o_%s0" % nm_
                            if eng == "act":
                                P.op("act", lambda e, bank=bank, cs_=cs_, dst=dst: e.activation(out=dst[:, cs_], in_=bank[:], func=AF.Copy), reads=[bkey], writes=[dkey])
                            else:
                                P.op("dve", lambda e, bank=bank, cs_=cs_, dst=dst: e.tensor_copy(out=dst[:, cs_], in_=bank[:]), reads=[bkey], writes=[dkey])
                    if mine:
                        for nm_, dd in (("q", Qd), ("k", Kd), ("v", Vd)):
                            P.op("sp", lambda e, nm_=nm_, dd=dd: e.dma_start(out=dd[r0:r0 + 128, :], in_=outs[nm_][sl][:]), reads=["o_%s%d" % (nm_, sl)], writes=["dram_" + nm_], dma=True)
                        for nm_, dd in (("g", Gd), ("gr", GRd), ("gf", GFd)):
                            P.op("sp", lambda e, nm_=nm_, dd=dd: e.dma_start(out=dd[r0:r0 + 128, :], in_=outs[nm_][0][:]), reads=["o_%s0" % nm_], writes=["dram_" + nm_], dma=True)
                    else:
                        for nm_, dd in (("k", KOd), ("v", VOd)):
                            P.op("sp", lambda e, nm_=nm_, dd=dd: e.dma_start(out=dd[r0:r0 + 128, :], in_=outs[nm_][sl][:]), reads=["o_%s%d" % (nm_, sl)], writes=["dram_o" + nm_], dma=True)

                p1_load_x(0)
                p1_load_rt(0)
                p1_load_x(1)
                for f_ in p1_A_stages(0):
                    f_()
                for ti in range(NM + NO):
                    if ti + 2 < NM + NO:
                        p1_load_x(ti + 2)
                    if ti + 1 < NM + NO:
                        p1_load_rt(ti + 1)
                    p1_compute(ti)
                for f_ in p1_tail_stages(NM + NO - 1):
                    f_()
                P.flush()

        if "p2" in phases:
            with contextlib.ExitStack() as es:
                sb = lambda n, s, d: es.enter_context(nc.sbuf_tensor(n, s, d))
                ps = lambda n, s, d: es.enter_context(nc.psum_tensor(n, s, d))
                dec = sb("dec_s", [128, 12], F32)
                lg = sb("lg", [128, 12], F32)
                gch = sb("gch", [128, 12], F32)
                DT = [sb("DT%d" % i, [128, 4, 128], F32) for i in range(2)]
                xi = [sb("xi%d" % i, [128, 8, 128], F32) for i in range(2)]
                zeta = [sb("zeta%d" % i, [128, 4, 256], F32) for i in range(3)]
                tmpd = sb("tmpd", [128, 128], F32)
                S = [sb("S%d" % i, [128, 8, 256], F32) for i in range(3)]
                Sbf = [sb("Sbf%d" % i, [128, 8, 256], BF16) for i in range(2)]
                qs = [[sb("q2_%d%d" % (d_, i), [128, D], BF16) for i in range(2)] for d_ in range(2)]
                ks = [[sb("k2_%d%d" % (d_, i), [128, D], BF16) for i in range(2)] for d_ in range(2)]
                vs = [[sb("v2_%d%d" % (d_, i), [128, D], BF16) for i in range(2)] for d_ in range(2)]
                qT = [sb("qT%d" % i, [128, D], BF16) for i in range(2)]
                qxT = [sb("qxT%d" % i, [128, D], BF16) for i in range(2)]
                kT = [sb("kT%d" % i, [128, D], BF16) for i in range(2)]
                kz = [sb("kz%d" % i, [128, D], BF16) for i in range(2)]
                PT = [sb("PT%d" % i, [128, 512], BF16) for i in range(2)]
                ysb = [sb("ysb%d" % i, [128, D], F32) for i in range(2)]
                pTt = [ps("pTt%d" % i, [128, D], BF16) for i in range(2)]
                pS = [ps("pS%d" % i, [128, 512], F32) for i in range(2)]
                py = [ps("py%d" % i, [128, 512], F32) for i in range(2)]
                pst = [ps("pst%d" % i, [128, 512], F32) for i in range(2)]

                P.op("sp", lambda e: e.dma_start(out=dec[:], in_=dec_d[:, :]), writes=["dec"], dma=True)
                P.op("act", lambda e: e.activation(out=lg[:], in_=dec[:], func=AF.Exp, scale=-1.0), reads=["dec"], writes=["lg"])
                P.op("act", lambda e: e.activation(out=lg[:], in_=lg[:], func=AF.Ln, bias=one_ap, scale=1.0), reads=["lg", "dk"], writes=["lg"])
                P.op("dve", lambda e: e.tensor_scalar(out=lg[:], in0=lg[:], scalar1=-1.0, scalar2=None, op0=ALU.mult), reads=["lg"], writes=["lg"])
                P.op("act", lambda e: e.activation(out=gch[:], in_=lg[:], func=AF.Exp, scale=128.0), reads=["lg"], writes=["gch"])
                for d_ in range(2):
                    oe, om, ox = (O_E0F, O_MF, O_XF) if d_ == 0 else (O_E0B, O_MB, O_XB)
                    for hd in range(4):
                        col = 4 * d_ + hd
                        P.op("act", lambda e, oe=oe, col=col: e.activation(out=tmpd[:], in_=dk[:, oe:oe + 128], func=AF.Exp, scale=lg[:, col:col + 1]), reads=["dk", "lg"], writes=["tmpd"])
                        P.op("dve", lambda e, om=om, d_=d_, hd=hd: e.tensor_tensor(out=DT[d_][:, hd, :], in0=tmpd[:], in1=dk[:, om:om + 128], op=ALU.mult), reads=["tmpd", "dk"], writes=["DT%d" % d_])
                        for hf in range(2):
                            P.op("act", lambda e, ox=ox, col=col, d_=d_, hd=hd, hf=hf: e.activation(out=xi[d_][:, 2 * hd + hf, :], in_=dk[:, ox:ox + 128], func=AF.Exp, scale=lg[:, col:col + 1]), reads=["dk", "lg"], writes=["xi%d" % d_])
                for z_, oz in enumerate((O_ZF, O_ZB, O_ZO)):
                    for hd in range(4):
                        col = 4 * z_ + hd
                        P.op("act", lambda e, z_=z_, oz=oz, hd=hd, col=col: e.activation(out=zeta[z_][:, hd, :], in_=dk[:, oz:oz + 256], func=AF.Exp, scale=lg[:, col:col + 1]), reads=["dk", "lg"], writes=["zeta%d" % z_])
                P.op("dve", lambda e: e.memset(S[2][:], 0.0), writes=["S2"])
                import os
                P2STOP = int(os.environ.get("P2STOP", "9"))

                def state_head(si, kzt, kzkey, vt, vkey, gcol0, bank, bkey, hd):
                    for hf in range(2):
                        P.op("pe", lambda e, hf=hf: e.matmul(bank[:, hf * 256:(hf + 1) * 256], lhsT=kzt[:, hd * 256 + hf * 128:hd * 256 + hf * 128 + 128], rhs=vt[:, hd * 256:(hd + 1) * 256], start=True, stop=True),
                             reads=[kzkey, vkey], writes=[bkey])
                    sview = S[si][:, 2 * hd:2 * hd + 2, :].rearrange("p a b -> p (a b)")
                    P.op("dve", lambda e: e.scalar_tensor_tensor(out=sview, in0=sview, scalar=gch[:, gcol0 + hd:gcol0 + hd + 1], in1=bank[:], op0=ALU.mult, op1=ALU.add),
                         reads=["S%d" % si, "gch", bkey], writes=["S%d" % si])

                def state_update(si, kzt, kzkey, vt, vkey, gcol0, bset):
                    for hd in range(4):
                        bank, bkey = ((pst[bset], "pst%d" % bset) if hd % 2 == 0 else (py[bset], "py%d" % bset))
                        state_head(si, kzt, kzkey, vt, vkey, gcol0, bank, bkey, hd)

                def p2a_load(j):
                    sl = j % 2
                    P.op("sp", lambda e: e.dma_start(out=ks[0][sl][:], in_=KOd[j * 128:(j + 1) * 128, :]), writes=["k2_0%d" % sl], dma=True)
                    P.op("sp", lambda e: e.dma_start(out=vs[0][sl][:], in_=VOd[j * 128:(j + 1) * 128, :]), writes=["v2_0%d" % sl], dma=True)
                if P2STOP >= 2:
                    p2a_load(0)
                for j in range(NO if P2STOP >= 2 else 0):
                    if j + 1 < NO:
                        p2a_load(j + 1)
                    sl = j % 2
                    bs_ = j % 2
                    P.op("pool", lambda e, sl=sl, bs_=bs_: e.tensor_tensor(out=kz[bs_][:], in0=ks[0][sl][:], in1=zeta[2][:].rearrange("p a b -> p (a b)"), op=ALU.mult), reads=["k2_0%d" % sl, "zeta2"], writes=["kz%d" % bs_])
                    state_update(2, kz[bs_], "kz%d" % bs_, vs[0][sl], "v2_0%d" % sl, 8, bs_)
                P.op("dve", lambda e: e.tensor_scalar(out=S[0][:], in0=S[2][:], scalar1=dk[:, O_MSKF:O_MSKF + 1], scalar2=None, op0=ALU.mult), reads=["S2", "dk"], writes=["S0"])
                P.op("dve", lambda e: e.tensor_scalar(out=S[1][:], in0=S[2][:], scalar1=dk[:, O_MSKB:O_MSKB + 1], scalar2=None, op0=ALU.mult), reads=["S2", "dk"], writes=["S1"])
                for d_ in range(2):
                    P.op("act", lambda e, d_=d_: e.activation(out=Sbf[d_][:], in_=S[d_][:], func=AF.Copy), reads=["S%d" % d_], writes=["Sbf%d" % d_])
                P.flush()

                def p2_load(d_, c, sl):
                    r0 = c * 128
                    P.op("sp", lambda e: e.dma_start(out=qs[d_][sl][:], in_=Qd[r0:r0 + 128, :]), writes=["q2_%d%d" % (d_, sl)], dma=True)
                    P.op("sp", lambda e: e.dma_start(out=ks[d_][sl][:], in_=Kd[r0:r0 + 128, :]), writes=["k2_%d%d" % (d_, sl)], dma=True)
                    P.op("sp", lambda e: e.dma_start(out=vs[d_][sl][:], in_=Vd[r0:r0 + 128, :]), writes=["v2_%d%d" % (d_, sl)], dma=True)

                def p2_stages(d_, c, sl):
                    q_, k_, v_ = qs[d_][sl], ks[d_][sl], vs[d_][sl]
                    qk_, kk_, vk_ = "q2_%d%d" % (d_, sl), "k2_%d%d" % (d_, sl), "v2_%d%d" % (d_, sl)
                    ds = str(d_)
                    T_, kT_ = pTt[d_], "pTt" + ds

                    def s0():
                        transposes8(q_, qk_, T_, kT_)
                        P.op("act", lambda e: e.activation(out=qT[d_][:], in_=T_[:], func=AF.Copy), reads=[kT_], writes=["qT" + ds])
                        P.op("dve", lambda e: e.tensor_tensor(out=qxT[d_][:], in0=qT[d_][:], in1=xi[d_][:].rearrange("p a b -> p (a b)"), op=ALU.mult), reads=["qT" + ds, "xi" + ds], writes=["qxT" + ds])

                    def s1():
                        transposes8(k_, kk_, T_, kT_)
                        P.op("act", lambda e: e.activation(out=kT[d_][:], in_=T_[:], func=AF.Copy), reads=[kT_], writes=["kT" + ds])
                        P.op("pool", lambda e: e.tensor_tensor(out=kz[d_][:], in0=k_[:], in1=zeta[d_][:].rearrange("p a b -> p (a b)"), op=ALU.mult), reads=[kk_, "zeta" + ds], writes=["kz" + ds])

                    def s2():
                        for hd in range(4):
                            for hf in range(2):
                                m = 2 * hd + hf
                                P.op("pe", lambda e, hd=hd, hf=hf, m=m: e.matmul(pS[d_][:, hd * 128:(hd + 1) * 128], lhsT=kT[d_][:, m * 128:(m + 1) * 128], rhs=qT[d_][:, m * 128:(m + 1) * 128], start=(hf == 0), stop=(hf == 1)),
                                     reads=["kT" + ds, "qT" + ds], writes=["pS" + ds])
                        P.op("dve", lambda e: e.tensor_tensor(out=PT[d_][:], in0=pS[d_][:], in1=DT[d_][:].rearrange("p a b -> p (a b)"), op=ALU.mult), reads=["pS" + ds, "DT" + ds], writes=["PT" + ds])

                    def ystage(half):
                        bank, bkey = py[d_], "py" + ds
                        for hd in (2 * half, 2 * half + 1):
                            co = (hd % 2) * 256
                            P.op("pe", lambda e, hd=hd, co=co: e.matmul(bank[:, co:co + 256], lhsT=PT[d_][:, hd * 128:(hd + 1) * 128], rhs=v_[:, hd * 256:(hd + 1) * 256], start=True, stop=False),
                                 reads=["PT" + ds, vk_], writes=[bkey])
                            for hf in range(2):
                                m = 2 * hd + hf
                                P.op("pe", lambda e, m=m, hf=hf, co=co: e.matmul(bank[:, co:co + 256], lhsT=qxT[d_][:, m * 128:(m + 1) * 128], rhs=Sbf[d_][:, m, :], start=False, stop=(hf == 1)),
                                     reads=["qxT" + ds, "Sbf" + ds], writes=[bkey])
                        if half == 0:
                            P.op("act", lambda e: e.activation(out=ysb[d_][:, 0:512], in_=bank[:], func=AF.Copy), reads=[bkey], writes=["ysb" + ds])
                        else:
                            P.op("dve", lambda e: e.tensor_copy(out=ysb[d_][:, 512:1024], in_=bank[:]), reads=[bkey], writes=["ysb" + ds])
                            yd = YFd if d_ == 0 else YBd
                            P.op("sp", lambda e: e.dma_start(out=yd[c * 128:(c + 1) * 128, :], in_=ysb[d_][:]), reads=["ysb" + ds], writes=["dram_y" + ds], dma=True)

                    def sh(hd):
                        bank, bkey = ((pst[d_], "pst" + ds) if hd % 2 == 0 else (py[d_], "py" + ds))
                        state_head(d_, kz[d_], "kz" + ds, v_, vk_, 4 * d_, bank, bkey, hd)
                        if hd % 2 == 1:
                            hh = hd // 2
                            P.op("act", lambda e: e.activation(out=Sbf[d_][:, 4 * hh:4 * hh + 4, :], in_=S[d_][:, 4 * hh:4 * hh + 4, :], func=AF.Copy), reads=["S" + ds], writes=["Sbf" + ds])

                    return [s0, s1, s2, lambda: ystage(0), lambda: ystage(1), lambda: sh(0), lambda: sh(1), lambda: sh(2), lambda: sh(3)]

                if P2STOP >= 3:
                    p2_load(0, 0, 0)
                    p2_load(1, NM - 1, 0)
                for t in range(NM if P2STOP >= 3 else 0):
                    sl = t % 2
                    if t + 1 < NM:
                        p2_load(0, t + 1, 1 - sl)
                        p2_load(1, NM - 2 - t, 1 - sl)
                    sf_ = p2_stages(0, t, sl)
                    sb_ = p2_stages(1, NM - 1 - t, sl)
                    for a_, b_ in zip(sf_, sb_):
                        a_()
                        b_()
                P.flush()

        if "p3" in phases:
            with contextlib.ExitStack() as es:
                sb = lambda n, s, d: es.enter_context(nc.sbuf_tensor(n, s, d))
                ps = lambda n, s, d: es.enter_context(nc.psum_tensor(n, s, d))
                ZL = sb("ZL", [128, 2, 128, 128], BF16)
                T = sb("Tt", [128, 128, 256], BF16)
                mts = sb("mts", [128, 128, 2, 64], BF16)
                ufs = sb("ufs", [128, 64, 128], BF16)
                fcs = sb("fcs_s", [128, 2, 256], BF16)
                pb = [ps("pb%d" % i, [128, 512], F32) for i in range(4)]
                pc = [ps("pc%d" % i, [128, 512], F32) for i in range(4)]
                P.op("sp", lambda e: e.dma_start(out=fcs[:], in_=fcs_d[:, :, :]), writes=["fcs"], dma=True)
                for dq in range(4):
                    P.op("sp", lambda e, dq=dq: e.dma_start(out=mts[:, dq * 32:(dq + 1) * 32, :, :], in_=mt_d[:, dq * 32:(dq + 1) * 32, :, :]), writes=["mts"], dma=True)
                for s in range(8):
                    for ri in range(2):
                        for hb in range(2):
                            P.op("sp", lambda e, s=s, ri=ri, hb=hb: e.dma_start(
                                out=ZL[:, ri, hb * 64:(hb + 1) * 64, :],
                                in_=Zd[s // 2, ri, s % 2, :, :].rearrange("(l b) c -> l b c", b=128)[:, hb * 64:(hb + 1) * 64, :]),
                                reads=["Zd"], writes=["ZL"], dma=True)
                    for cp in range(64):
                        bank, bkey = pb[cp % 4], "pb%d" % (cp % 4)
                        for q_ in range(2):
                            ch = 2 * cp + q_
                            P.op("pe", lambda e, ch=ch, q_=q_, bank=bank: e.matmul(bank[:, q_ * 256:(q_ + 1) * 256], lhsT=ZL[:, 0, :, ch], rhs=fcs[:, 0, :], start=True, stop=False),
                                 reads=["ZL", "fcs"], writes=[bkey])
                            P.op("pe", lambda e, ch=ch, q_=q_, bank=bank: e.matmul(bank[:, q_ * 256:(q_ + 1) * 256], lhsT=ZL[:, 1, :, ch], rhs=fcs[:, 1, :], start=False, stop=True),
                                 reads=["ZL", "fcs"], writes=[bkey])
                        tv = T[:, 2 * cp:2 * cp + 2, :].rearrange("p a b -> p (a b)")
                        if cp % 2 == 0:
                            P.op("act", lambda e, bank=bank, tv=tv: e.activation(out=tv, in_=bank[:], func=AF.Copy), reads=[bkey], writes=["Tt"])
                        else:
                            P.op("dve", lambda e, bank=bank, tv=tv: e.tensor_copy(out=tv, in_=bank[:]), reads=[bkey], writes=["Tt"])
                    for db in range(16):
                        bank, bkey = pc[db % 4], "pc%d" % (db % 4)
                        for dd in range(8):
                            d_ = db * 8 + dd
                            P.op("pe", lambda e, d_=d_, dd=dd, bank=bank: e.matmul(bank[:, dd * 64:(dd + 1) * 64], lhsT=T[:, :, d_], rhs=mts[:, d_, 0, :], start=True, stop=False),
                                 reads=["Tt", "mts"], writes=[bkey])
                            P.op("pe", lambda e, d_=d_, dd=dd, bank=bank: e.matmul(bank[:, dd * 64:(dd + 1) * 64], lhsT=T[:, :, 128 + d_], rhs=mts[:, d_, 1, :], start=False, stop=True),
                                 reads=["Tt", "mts"], writes=[bkey])
                        ov = ufs[:, :, db * 8:(db + 1) * 8]
                        iv = bank[:].rearrange("p (dd c) -> p c dd", dd=8)
                        if db % 2 == 0:
                            P.op("act", lambda e, ov=ov, iv=iv: e.activation(out=ov, in_=iv, func=AF.Copy, scale=1.0 / 2048.0), reads=[bkey], writes=["ufs"])
                        else:
                            P.op("dve", lambda e, ov=ov, iv=iv: e.tensor_scalar(out=ov, in0=iv, scalar1=1.0 / 2048.0, scalar2=None, op0=ALU.mult), reads=[bkey], writes=["ufs"])
                    P.op("sp", lambda e, s=s: e.dma_start(out=UFd[s * 128:(s + 1) * 128, :], in_=ufs[:].rearrange("p c d -> p (c d)")), reads=["ufs"], writes=["UFd"], dma=True)
                P.flush()

        if "p4" in phases or "p4a" in phases:
            with contextlib.ExitStack() as es:
                sb = lambda n, s, d: es.enter_context(nc.sbuf_tensor(n, s, d))
                ps = lambda n, s, d: es.enter_context(nc.psum_tensor(n, s, d))
                wck = sb("wck", [128, 8, D], BF16)
                wcv = sb("wcv", [128, 8, D], BF16)
                nmem = sb("nmem_s", [128, D], F32)
                ms = [sb("ms%d" % i, [128, D], F32) for i in range(2)]
                st = sb("st0", [128, 8], F32)
                mn = sb("mn", [128, D], BF16)
                mnT = sb("mnT", [128, 8, NMEM], BF16)
                pT = ps("pT0", [128, D], BF16)
                pm = [ps("pm0_%d" % i, [128, 512], F32) for i in range(4)]
                for wsb, wd, nm_ in ((wck, w_ck, "wck"), (wcv, w_cv, "wcv")):
                    for kq in range(4):
                        P.op("pool", lambda e, wsb=wsb, wd=wd, kq=kq: e.dma_start(out=wsb[:, 2 * kq:2 * kq + 2, :], in_=wd[kq * 256:(kq + 1) * 256, :].rearrange("(k p) n -> p k n", p=128)), writes=[nm_], dma=True)
                P.op("sp", lambda e: e.dma_start(out=nmem[:], in_=nmem_d[:, :]), writes=["nmem"], dma=True)
                for t in range(2):
                    P.op("sp", lambda e, t=t: e.dma_start(out=ms[t][:], in_=mem[t * 128:(t + 1) * 128, :]), writes=["ms%d" % t], dma=True)
                for t in range(2):
                    rstd_ops(st, "st0", ms[t][:], "ms%d" % t, mn[:], "mn")
                    P.op("dve", lambda e, t=t: e.scalar_tensor_tensor(out=mn[:], in0=ms[t][:], scalar=st[:, 2:3], in1=nmem[:], op0=ALU.mult, op1=ALU.mult), reads=["ms%d" % t, "st0", "nmem"], writes=["mn"])
                    transposes8(mn, "mn", pT, "pT0")
                    P.op("dve", lambda e, t=t: e.tensor_copy(out=mnT[:, :, t * 128:(t + 1) * 128], in_=pT[:].rearrange("p (k c) -> p k c", k=8)), reads=["pT0"], writes=["mnT"])
                bi0 = 0
                for m in range(8):
                    bank, bkey = pm[bi0 % 4], "pm0_%d" % (bi0 % 4)
                    bi0 += 1
                    for k in range(8):
                        P.op("pe", lambda e, m=m, k=k, bank=bank: e.matmul(bank[:, 0:NMEM], lhsT=wck[:, k, m * 128:(m + 1) * 128], rhs=mnT[:, k, :], start=(k == 0), stop=(k == 7)), reads=["wck", "mnT"], writes=[bkey])
                    P.op("act", lambda e, m=m, bank=bank: e.activation(out=ckT[:, m, :], in_=bank[:, 0:NMEM], func=AF.Copy), reads=[bkey], writes=["ckT"])
                for t in range(2):
                    for n in range(2):
                        bank, bkey = pm[bi0 % 4], "pm0_%d" % (bi0 % 4)
                        bi0 += 1
                        for k in range(8):
                            P.op("pe", lambda e, t=t, n=n, k=k, bank=bank: e.matmul(bank[:], lhsT=mnT[:, k, t * 128:(t + 1) * 128], rhs=wcv[:, k, n * 512:(n + 1) * 512], start=(k == 0), stop=(k == 7)), reads=["wcv", "mnT"], writes=[bkey])
                        P.op("dve", lambda e, t=t, n=n, bank=bank: e.tensor_copy(out=cv[:, t, n * 512:(n + 1) * 512], in_=bank[:]), reads=[bkey], writes=["cv"])
                P.flush()

            with contextlib.ExitStack() as es:
                sb = lambda n, s, d: es.enter_context(nc.sbuf_tensor(n, s, d))
                ps = lambda n, s, d: es.enter_context(nc.psum_tensor(n, s, d))
                wts = {}
                for nm_, wd in (("wro", w_ro), ("w4", w_4), ("wmx", w_mx), ("wcq", w_cq), ("wco", w_co)):
                    wts[nm_] = sb(nm_, [128, 8, D], BF16)
                    for kq in range(4):
                        P.op("pool", lambda e, nm_=nm_, wd=wd, kq=kq: e.dma_start(out=wts[nm_][:, 2 * kq:2 * kq + 2, :], in_=wd[kq * 256:(kq + 1) * 256, :].rearrange("(k p) n -> p k n", p=128)), writes=[nm_], dma=True)
                gnw = sb("gnw_s", [128, D], F32)
                nca = sb("nca_s", [128, D], F32)
                P.op("sp", lambda e: e.dma_start(out=gnw[:], in_=gnw_d[:, :]), writes=["gnw"], dma=True)
                P.op("sp", lambda e: e.dma_start(out=nca[:], in_=nca_d[:, :]), writes=["nca"], dma=True)
                yf = [sb("yf%d" % i, [128, D], F32) for i in range(2)]
                yb = [sb("yb%d" % i, [128, D], F32) for i in range(2)]
                xt = [sb("xt%d" % i, [128, D], F32) for i in range(2)]
                gt = [sb("gt%d" % i, [128, D], BF16) for i in range(2)]
                grt = [sb("grt%d" % i, [128, D], BF16) for i in range(2)]
                gft = [sb("gft%d" % i, [128, D], BF16) for i in range(2)]
                uft = [sb("uft%d" % i, [128, 8, 128], BF16) for i in range(2)]
                SETS = []
                for i in range(2):
                    SETS.append(dict(
                        A=sb("A4_%d" % i, [128, D], F32), B=sb("B4_%d" % i, [128, D], F32), C=sb("C4_%d" % i, [128, D], F32),
                        x1=sb("x1_%d" % i, [128, D], F32), x2=sb("x2_%d" % i, [128, D], F32),
                        b=[sb("b4_%d_%d" % (i, j), [128, D], BF16) for j in range(4)],
                        bs=sb("bs4_%d" % i, [128, 4, 6], F32), mv=sb("mv4_%d" % i, [128, 4, 2], F32),
                        sm=sb("sm4_%d" % i, [128, 16], F32), st=sb("st4_%d" % i, [128, 8], F32),
                        T=ps("pT4_%d" % i, [128, D], BF16), X=[ps("pX%d_%d" % (j, i), [128, 512], F32) for j in range(2)],
                        Y=ps("pY_%d" % i, [128, 512], F32)))
                UFv = UFd.rearrange("(s p) t -> p s t", p=128)

                def p4a_load(c):
                    sl = c % 2
                    r0 = c * 128
                    for tl, dd, nm_ in ((yf, YFd, "yf"), (yb, YBd, "yb"), (gt, Gd, "gt")):
                        P.op("sp", lambda e, tl=tl, dd=dd: e.dma_start(out=tl[sl][:], in_=dd[r0:r0 + 128, :]), writes=["%s%d" % (nm_, sl)], dma=True)
                    P.op("sp", lambda e: e.dma_start(out=uft[sl][:], in_=UFv[:, :, r0:r0 + 128]), writes=["uft%d" % sl], dma=True)
                    for tl, dd, nm_ in ((grt, GRd, "grt"), (gft, GFd, "gft"), (xt, xm, "xt")):
                        P.op("sp", lambda e, tl=tl, dd=dd: e.dma_start(out=tl[sl][:], in_=dd[r0:r0 + 128, :]), writes=["%s%d" % (nm_, sl)], dma=True)

                def p4a_stages(c):
                    i = c % 2
                    sl = i
                    S_ = SETS[i]
                    A, B, Cc, x1, x2, bb, bs, mv, sm, st, T_, X, Y = (S_[k_] for k_ in ("A", "B", "C", "x1", "x2", "b", "bs", "mv", "sm", "st", "T", "X", "Y"))
                    s_ = str(i)
                    kA, kB, kC, kx1, kx2, kbs, kmv, ksm, kst, kT, kY = ("A4_" + s_, "B4_" + s_, "C4_" + s_, "x1_" + s_, "x2_" + s_, "bs4_" + s_, "mv4_" + s_, "sm4_" + s_, "st4_" + s_, "pT4_" + s_, "pY_" + s_)
                    kX = ["pX0_" + s_, "pX1_" + s_]
                    kb = ["b4_%s_%d" % (s_, j) for j in range(4)]
                    r0 = c * 128

                    def mm16(lhs_fn, lkey, w, wkey):
                        for n in range(2):
                            for k in range(8):
                                P.op("pe", lambda e, n=n, k=k: e.matmul(X[n][:], lhsT=lhs_fn(k), rhs=w[:, k, n * 512:(n + 1) * 512], start=(k == 0), stop=(k == 7)),
                                     reads=[lkey, wkey], writes=[kX[n]])

                    def tr(src, skey, dst, dkey, eng):
                        for k in range(8):
                            P.op("pe", lambda e, k=k: e.transpose(out=T_[:, k * 128:(k + 1) * 128], in_=src[:, k * 128:(k + 1) * 128], identity=ident[:]), reads=[skey, "ident"], writes=[kT])
                        if eng == "act":
                            P.op("act", lambda e: e.activation(out=dst[:], in_=T_[:], func=AF.Copy), reads=[kT], writes=[dkey])
                        else:
                            P.op("dve", lambda e: e.tensor_copy(out=dst[:], in_=T_[:]), reads=[kT], writes=[dkey])

                    def dve(fn, r, w):
                        P.op("dve", fn, reads=r, writes=w)

                    def act(fn, r, w):
                        P.op("act", fn, reads=r, writes=w)

                    def i_gn1a():
                        dve(lambda e: e.tensor_tensor(out=A[:], in0=yf[sl][:], in1=yb[sl][:], op=ALU.add), ["yf" + s_, "yb" + s_], [kA])
                        for hd in range(4):
                            dve(lambda e, hd=hd: e.bn_stats(out=bs[:, hd, :], in_=A[:, hd * 256:(hd + 1) * 256]), [kA], [kbs])
                            dve(lambda e, hd=hd: e.bn_aggr(out=mv[:, hd, :], in_=bs[:, hd, :]), [kbs], [kmv])
                        dve(lambda e: e.tensor_scalar(out=sm[:, 0:4], in0=mv[:, :, 1], scalar1=eps5, scalar2=None, op0=ALU.add), [kmv, "dk"], [ksm])
                        act(lambda e: e.activation(out=Cc[:], in_=gt[sl][:], func=AF.Sigmoid), ["gt" + s_], [kC])

                    def i_gn1b():
                        act(lambda e: e.activation(out=sm[:, 0:4], in_=sm[:, 0:4], func=AF.Ln), [ksm], [ksm])
                        act(lambda e: e.activation(out=sm[:, 4:8], in_=sm[:, 0:4], func=AF.Exp, scale=-0.5), [ksm], [ksm])

                    def i_gn2():
                        for hd in range(4):
                            dve(lambda e, hd=hd: e.tensor_scalar(out=B[:, hd * 256:(hd + 1) * 256], in0=A[:, hd * 256:(hd + 1) * 256], scalar1=mv[:, hd, 0:1], scalar2=sm[:, 4 + hd:5 + hd], op0=ALU.subtract, op1=ALU.mult),
                                [kA, kmv, ksm], [kB])
                        dve(lambda e: e.tensor_tensor(out=Cc[:], in0=Cc[:], in1=gt[sl][:], op=ALU.mult), [kC, "gt" + s_], [kC])
                        dve(lambda e: e.tensor_tensor(out=B[:], in0=B[:], in1=gnw[:], op=ALU.mult), [kB, "gnw"], [kB])
                        dve(lambda e: e.tensor_tensor(out=bb[0][:], in0=B[:], in1=Cc[:], op=ALU.mult), [kB, kC], [kb[0]])

                    def trp(src, skey):
                        for k in range(8):
                            P.op("pe", lambda e, k=k: e.transpose(out=T_[:, k * 128:(k + 1) * 128], in_=src[:, k * 128:(k + 1) * 128], identity=ident[:]), reads=[skey, "ident"], writes=[kT])

                    def i_tr_r():
                        trp(bb[0], kb[0])
                        act(lambda e: e.activation(out=A[:], in_=grt[sl][:], func=AF.Sigmoid), ["grt" + s_], [kA])
                        act(lambda e: e.activation(out=B[:], in_=gft[sl][:], func=AF.Sigmoid), ["gft" + s_], [kB])

                    def i_tr_r_ev():
                        act(lambda e: e.activation(out=bb[1][:], in_=T_[:], func=AF.Copy), [kT], [kb[1]])

                    def i_ret():
                        mm16(lambda k: bb[1][:, k * 128:(k + 1) * 128], kb[1], wts["wro"], "wro")
                        for k in range(8):
                            P.op("pe", lambda e, k=k: e.matmul(Y[:], lhsT=uft[sl][:, k, :], rhs=wts["w4"][:, k, 0:512], start=(k == 0), stop=(k == 7)), reads=["uft" + s_, "w4"], writes=[kY])

                    def i_m_a():
                        dve(lambda e: e.tensor_tensor(out=A[:, 0:512], in0=X[0][:], in1=A[:, 0:512], op=ALU.mult), [kX[0], kA], [kA])
                        dve(lambda e: e.tensor_tensor(out=B[:, 0:512], in0=Y[:], in1=B[:, 0:512], op=ALU.mult), [kY, kB], [kB])
                        dve(lambda e: e.tensor_tensor(out=A[:, 512:1024], in0=X[1][:], in1=A[:, 512:1024], op=ALU.mult), [kX[1], kA], [kA])

                    def i_four1():
                        for k in range(8):
                            P.op("pe", lambda e, k=k: e.matmul(X[0][:], lhsT=uft[sl][:, k, :], rhs=wts["w4"][:, k, 512:1024], start=(k == 0), stop=(k == 7)), reads=["uft" + s_, "w4"], writes=[kX[0]])

                    def i_m_b():
                        dve(lambda e: e.tensor_tensor(out=B[:, 512:1024], in0=X[0][:], in1=B[:, 512:1024], op=ALU.mult), [kX[0], kB], [kB])
                        dve(lambda e: e.tensor_tensor(out=bb[2][:], in0=A[:], in1=B[:], op=ALU.add), [kA, kB], [kb[2]])

                    def i_tr_m():
                        trp(bb[2], kb[2])

                    def i_tr_m_ev():
                        act(lambda e: e.activation(out=bb[3][:], in_=T_[:], func=AF.Copy), [kT], [kb[3]])

                    def i_mix():
                        mm16(lambda k: bb[3][:, k * 128:(k + 1) * 128], kb[3], wts["wmx"], "wmx")

                    def i_x1():
                        for n in range(2):
                            cs_ = slice(n * 512, (n + 1) * 512)
                            dve(lambda e, n=n, cs_=cs_: e.tensor_tensor(out=x1[:, cs_], in0=X[n][:], in1=xt[sl][:, cs_], op=ALU.add), [kX[n], "xt" + s_], [kx1])
                        dve(lambda e: e.memset(st[:], 0.0), [], [kst])

                    def i_n_a():
                        act(lambda e: e.activation(out=Cc[:], in_=x1[:], func=AF.Square, accum_out=st[:, 0:1]), [kx1, kst], [kC, kst])

                    def i_n_b():
                        dve(lambda e: e.tensor_scalar(out=st[:, 1:2], in0=st[:, 0:1], scalar1=1.0 / D, scalar2=eps6, op0=ALU.mult, op1=ALU.add), [kst, "dk"], [kst])

                    def i_n_c():
                        act(lambda e: e.activation(out=st[:, 3:4], in_=st[:, 1:2], func=AF.Ln), [kst], [kst])
                        act(lambda e: e.activation(out=st[:, 2:3], in_=st[:, 3:4], func=AF.Exp, scale=-0.5), [kst], [kst])

                    def i_n_d():
                        dve(lambda e: e.scalar_tensor_tensor(out=bb[0][:], in0=x1[:], scalar=st[:, 2:3], in1=nca[:], op0=ALU.mult, op1=ALU.mult), [kx1, kst, "nca"], [kb[0]])

                    def i_tr_x():
                        trp(bb[0], kb[0])

                    def i_tr_x_ev():
                        dve(lambda e: e.tensor_copy(out=bb[1][:], in_=T_[:]), [kT], [kb[1]])

                    def i_hq():
                        for m in range(8):
                            bank, bkey = X[m // 4], kX[m // 4]
                            co = (m % 4) * 128
                            for k in range(8):
                                P.op("pe", lambda e, m=m, k=k, bank=bank, co=co: e.matmul(bank[:, co:co + 128], lhsT=wts["wcq"][:, k, m * 128:(m + 1) * 128], rhs=bb[1][:, k * 128:(k + 1) * 128], start=(k == 0), stop=(k == 7)),
                                     reads=["wcq", kb[1]], writes=[bkey])

                    def i_hq_ev():
                        act(lambda e: e.activation(out=bb[2][:, 0:512], in_=X[0][:], func=AF.Copy), [kX[0]], [kb[2]])
                        dve(lambda e: e.tensor_copy(out=bb[2][:, 512:1024], in_=X[1][:]), [kX[1]], [kb[2]])

                    def i_logits():
                        for hd in range(4):
                            bank, bkey = X[hd // 2], kX[hd // 2]
                            co = (hd % 2) * 256
                            for hf in range(2):
                                m = 2 * hd + hf
                                P.op("pe", lambda e, m=m, hf=hf, bank=bank, co=co: e.matmul(bank[:, co:co + 256], lhsT=bb[2][:, m * 128:(m + 1) * 128], rhs=ckT[:, m, :], start=(hf == 0), stop=(hf == 1)),
                                     reads=[kb[2], "ckT"], writes=[bkey])

                    def i_sm_a():
                        dve(lambda e: e.memset(sm[:, 8:16], 0.0), [], [ksm])
                        for n in range(2):
                            dve(lambda e, n=n: e.tensor_reduce(out=sm[:, 2 * n:2 * n + 2], in_=X[n][:].rearrange("p (a b) -> p a b", a=2), axis=AX.X, op=ALU.max), [kX[n], ksm], [ksm])
                        dve(lambda e: e.tensor_scalar(out=sm[:, 4:8], in0=sm[:, 0:4], scalar1=-1.0 / 16.0, scalar2=None, op0=ALU.mult), [ksm], [ksm])

                    def i_sm_b():
                        for hd in range(4):
                            bank, bkey = X[hd // 2], kX[hd // 2]
                            co = (hd % 2) * 256
                            act(lambda e, hd=hd, bank=bank, co=co: e.activation(out=Cc[:, hd * 256:(hd + 1) * 256], in_=bank[:, co:co + 256], func=AF.Exp, scale=1.0 / 16.0, bias=sm[:, 4 + hd:5 + hd], accum_out=sm[:, 8 + hd:9 + hd]),
                                [bkey, ksm], [kC, ksm])

                    def i_sm_c():
                        dve(lambda e: e.reciprocal(out=sm[:, 12:16], in_=sm[:, 8:12]), [ksm], [ksm])
                        for hd in range(4):
                            dve(lambda e, hd=hd: e.tensor_scalar(out=bb[3][:, hd * 256:(hd + 1) * 256], in0=Cc[:, hd * 256:(hd + 1) * 256], scalar1=sm[:, 12 + hd:13 + hd], scalar2=None, op0=ALU.mult),
                                [kC, ksm], [kb[3]])

                    def i_tr_p():
                        trp(bb[3], kb[3])

                    def i_tr_p_ev():
                        act(lambda e: e.activation(out=bb[0][:], in_=T_[:], func=AF.Copy), [kT], [kb[0]])

                    def i_att():
                        for m in range(8):
                            hd = m // 2
                            bank, bkey = X[m // 4], kX[m // 4]
                            co = (m % 4) * 128
                            for mc in range(2):
                                P.op("pe", lambda e, m=m, mc=mc, hd=hd, bank=bank, co=co: e.matmul(bank[:, co:co + 128], lhsT=cv[:, mc, m * 128:(m + 1) * 128], rhs=bb[0][:, (2 * hd + mc) * 128:(2 * hd + mc + 1) * 128], start=(mc == 0), stop=(mc == 1)),
                                     reads=["cv", kb[0]], writes=[bkey])

                    def i_att_ev():
                        act(lambda e: e.activation(out=bb[1][:, 0:512], in_=X[0][:], func=AF.Copy), [kX[0]], [kb[1]])
                        dve(lambda e: e.tensor_copy(out=bb[1][:, 512:1024], in_=X[1][:]), [kX[1]], [kb[1]])

                    def i_co():
                        mm16(lambda k: bb[1][:, k * 128:(k + 1) * 128], kb[1], wts["wco"], "wco")

                    def i_x2():
                        for n in range(2):
                            cs_ = slice(n * 512, (n + 1) * 512)
                            dve(lambda e, n=n, cs_=cs_: e.tensor_tensor(out=x2[:, cs_], in0=X[n][:], in1=x1[:, cs_], op=ALU.add), [kX[n], kx1], [kx2])
                        P.op("sp", lambda e: e.dma_start(out=X2d[r0:r0 + 128, :], in_=x2[:]), reads=[kx2], writes=["X2d"], dma=True)

                    def i_nop():
                        pass

                    return [i_gn1a, i_gn1b, i_gn2, i_tr_r, i_tr_r_ev, i_ret, i_m_a, i_four1, i_m_b, i_tr_m, i_tr_m_ev, i_mix, i_x1, i_n_a, i_n_b, i_n_c,
                            i_n_d, i_tr_x, i_tr_x_ev, i_hq, i_hq_ev, i_logits, i_sm_a, i_sm_b, i_sm_c, i_tr_p, i_tr_p_ev, i_att, i_att_ev, i_co, i_x2, i_nop]

                NST = 32
                import os as _os
                SK = int(_os.environ.get("P4SKEW", "16"))
                p4a_load(0)
                active = []
                nxt = 0
                tick = 0
                while nxt < NM or active:
                    admit = (tick % NST == 0) or (tick % NST == SK)
                    if nxt < NM and admit:
                        if nxt + 1 < NM:
                            p4a_load(nxt + 1)
                        active.append([p4a_stages(nxt), 0])
                        nxt += 1
                    for a_ in active:
                        a_[0][a_[1]]()
                        a_[1] += 1
                    active = [a_ for a_ in active if a_[1] < NST]
                    tick += 1
                P.flush()

        if "p4" in phases or "p4b" in phases:
            with contextlib.ExitStack() as es:
                sb = lambda n, s, d: es.enter_context(nc.sbuf_tensor(n, s, d))
                ps = lambda n, s, d: es.enter_context(nc.psum_tensor(n, s, d))
                wup = sb("wup", [128, 8, DFF], BF16)
                wdn = sb("wdn", [128, 32, D], BF16)
                for k in range(8):
                    P.op("pool", lambda e, k=k: e.dma_start(out=wup[:, k, :], in_=w_up[k * 128:(k + 1) * 128, :]), writes=["wup"], dma=True)
                for kq in range(8):
                    P.op("pool", lambda e, kq=kq: e.dma_start(out=wdn[:, 4 * kq:4 * kq + 4, :], in_=w_dn[kq * 512:(kq + 1) * 512, :].rearrange("(k p) n -> p k n", p=128)), writes=["wdn"], dma=True)
                nmlp = sb("nmlp_s", [128, D], F32)
                nfin = sb("nfin_s", [128, D], F32)
                P.op("sp", lambda e: e.dma_start(out=nmlp[:], in_=nmlp_d[:, :]), writes=["nmlp"], dma=True)
                P.op("sp", lambda e: e.dma_start(out=nfin[:], in_=nfin_d[:, :]), writes=["nfin"], dma=True)
                xg = [sb("xg%d" % i, [128, 2, D], F32) for i in range(2)]
                xn = [sb("xn5_%d" % i, [128, D], BF16) for i in range(2)]
                xnT = [sb("xnT5_%d" % i, [128, 8, 256], BF16) for i in range(2)]
                hT = sb("hT", [128, 32, 256], BF16)
                sq = [sb("sq%d" % i, [128, 256], F32) for i in range(2)]
                ot = [sb("ot%d" % i, [128, D], F32) for i in range(2)]
                st = [sb("st5_%d" % i, [128, 8], F32) for i in range(3)]
                pT = ps("pT5", [128, D], BF16)
                pu = [ps("pu%d" % i, [128, 512], F32) for i in range(5)]
                pd = [ps("pd%d" % i, [128, 512], F32) for i in range(2)]

                def p4b_load(gi):
                    sl = gi % 2
                    P.op("sp", lambda e: e.dma_start(out=xg[sl][:], in_=X2d[gi * 256:(gi + 1) * 256, :].rearrange("(t p) d -> p t d", p=128)), writes=["xg%d" % sl], dma=True)

                oc = [0]

                def p4b_norm_stages(gi):
                    sl = gi % 2
                    gk = "xg%d" % sl
                    out = []
                    for t in range(2):
                        stt_, kst_, xnt_, kxn_ = st[t], "st5_%d" % t, xn[t], "xn5_%d" % t

                        def n0(t=t, stt_=stt_, kst_=kst_, xnt_=xnt_, kxn_=kxn_):
                            P.op("dve", lambda e: e.memset(stt_[:], 0.0), writes=[kst_])
                            P.op("act", lambda e: e.activation(out=xnt_[:], in_=xg[sl][:, t, :], func=AF.Square, accum_out=stt_[:, 0:1]), reads=[gk, kst_], writes=[kxn_, kst_])

                        def n1(stt_=stt_, kst_=kst_):
                            P.op("act", lambda e: e.activation(out=stt_[:, 1:2], in_=stt_[:, 0:1], func=AF.Sqrt, scale=1.0 / D, bias=eps6), reads=[kst_, "dk"], writes=[kst_])

                        def n2(stt_=stt_, kst_=kst_):
                            P.op("dve", lambda e: e.reciprocal(out=stt_[:, 2:3], in_=stt_[:, 1:2]), reads=[kst_], writes=[kst_])

                        def n3(t=t, stt_=stt_, kst_=kst_, xnt_=xnt_, kxn_=kxn_):
                            P.op("dve", lambda e: e.scalar_tensor_tensor(out=xnt_[:], in0=xg[sl][:, t, :], scalar=stt_[:, 2:3], in1=nmlp[:], op0=ALU.mult, op1=ALU.mult), reads=[gk, kst_, "nmlp"], writes=[kxn_])
                        out += [n0, n1, n2, n3]
                    return out

                def p4b_tr(gi):
                    sl = gi % 2
                    for t in range(2):
                        transposes8(xn[t], "xn5_%d" % t, pT, "pT5")
                        P.op("act", lambda e, t=t: e.activation(out=xnT[sl][:, :, t * 128:(t + 1) * 128], in_=pT[:].rearrange("p (k c) -> p k c", k=8), func=AF.Copy), reads=["pT5"], writes=["xnT5_%d" % sl])

                def p4b_up(gi, f0, f1, hooks=None):
                    sl = gi % 2
                    for f in range(f0, f1):
                        if hooks and f in hooks:
                            hooks[f]()
                        bank, bkey = pu[f % 5], "pu%d" % (f % 5)
                        for k in range(8):
                            P.op("pe", lambda e, f=f, k=k, bank=bank: e.matmul(bank[:, 0:256], lhsT=wup[:, k, f * 128:(f + 1) * 128], rhs=xnT[sl][:, k, :], start=(k == 0), stop=(k == 7)), reads=["wup", "xnT5_%d" % sl], writes=[bkey])
                        sqt, sqk = sq[f % 2], "sq%d" % (f % 2)
                        if f % 2 == 0:
                            P.op("act", lambda e, bank=bank, sqt=sqt: e.activation(out=sqt[:], in_=bank[:, 0:256], func=AF.Relu), reads=[bkey], writes=[sqk])
                            P.op("act", lambda e, f=f, sqt=sqt: e.activation(out=hT[:, f, :], in_=sqt[:], func=AF.Square), reads=[sqk], writes=["hT"])
                        else:
                            P.op("dve", lambda e, bank=bank, sqt=sqt: e.tensor_scalar(out=sqt[:], in0=bank[:, 0:256], scalar1=0.0, scalar2=None, op0=ALU.max), reads=[bkey], writes=[sqk])
                            P.op("dve", lambda e, f=f, sqt=sqt: e.tensor_tensor(out=hT[:, f, :], in0=sqt[:], in1=sqt[:], op=ALU.mult), reads=[sqk], writes=["hT"])

                def p4b_down(gi):
                    sl = gi % 2
                    gk = "xg%d" % sl
                    epi = []
                    for t in range(2):
                        for n in range(2):
                            for f in range(32):
                                P.op("pe", lambda e, t=t, n=n, f=f: e.matmul(pd[n][:], lhsT=hT[:, f, t * 128:(t + 1) * 128], rhs=wdn[:, f, n * 512:(n + 1) * 512], start=(f == 0), stop=(f == 31)), reads=["hT", "wdn"], writes=["pd%d" % n])
                            cs_ = slice(n * 512, (n + 1) * 512)
                            P.op("dve", lambda e, t=t, n=n, cs_=cs_: e.tensor_tensor(out=xg[sl][:, t, cs_], in0=pd[n][:], in1=xg[sl][:, t, cs_], op=ALU.add), reads=["pd%d" % n, gk], writes=[gk])

                        def fin(t=t):
                            osl = oc[0] % 2
                            oc[0] += 1
                            ok = "ot%d" % osl
                            rstd_ops(st[2], "st5_2", xg[sl][:, t, :], gk, ot[osl][:], ok)
                            P.op("dve", lambda e: e.scalar_tensor_tensor(out=ot[osl][:], in0=xg[sl][:, t, :], scalar=st[2][:, 2:3], in1=nfin[:], op0=ALU.mult, op1=ALU.mult), reads=[gk, "st5_2", "nfin"], writes=[ok])
                            r0 = gi * 256 + t * 128
                            P.op("sp", lambda e: e.dma_start(out=y_out[r0:r0 + 128, :], in_=ot[osl][:]), reads=[ok], writes=["y_out"], dma=True)
                        fin()
                    return epi

                NG = NM // 2
                p4b_load(0)
                for f_ in p4b_norm_stages(0):
                    f_()
                p4b_tr(0)
                for gi in range(NG):
                    if gi + 1 < NG:
                        p4b_load(gi + 1)
                        ns_ = p4b_norm_stages(gi + 1)
                        hk = {9: ns_[0], 11: ns_[1], 13: ns_[2], 15: ns_[3], 18: ns_[4], 20: ns_[5], 22: ns_[6], 24: ns_[7]}
                    else:
                        hk = None
                    p4b_up(gi, 0, 32, hk)
                    if gi + 1 < NG:
                        p4b_tr(gi + 1)
                    p4b_down(gi)
                P.flush()
        P.flush(final=True)
    return nc, P


def _other_chunks(h):
    return np.arange(0, 64) if h == 1 else np.arange(127, 63, -1)


def _consts(h):
    bf = ml_dtypes.bfloat16
    c = {}
    c["ident"] = np.eye(128, dtype=np.float32).astype(bf)
    inv = (np.float32(10000.0) ** (-(np.arange(0, 256, 2, dtype=np.float32)) / np.float32(256))).astype(np.float32)

    def rot(pos):
        ang = (pos.astype(np.float32)[:, None] * inv[None, :]).astype(np.float32).astype(np.float64)
        return np.concatenate([np.cos(ang), np.sin(ang)], axis=1).astype(np.float32)
    pos_m = h * 8192 + np.arange(8192)
    oc = _other_chunks(h)
    pos_o = (oc[:, None] * 128 + np.arange(128)[None, :]).reshape(-1)
    c["rotm"] = rot(pos_m)
    c["roto"] = rot(pos_o)
    dk = np.zeros((128, DKW), np.float32)
    j = np.arange(128)[:, None].astype(np.float64)
    i = np.arange(128)[None, :].astype(np.float64)
    dk[:, O_E0F:O_E0F + 128] = np.maximum(i - j, 0)
    dk[:, O_MF:O_MF + 128] = (i >= j)
    dk[:, O_E0B:O_E0B + 128] = np.maximum(j - i, 0)
    dk[:, O_MB:O_MB + 128] = (j > i)
    dk[:, O_XF:O_XF + 128] = i + 1
    dk[:, O_XB:O_XB + 128] = 128 - i
    zf = 127 - j
    zb = j
    dk[:, O_ZF:O_ZF + 256] = zf
    dk[:, O_ZB:O_ZB + 256] = zb
    dk[:, O_ZO:O_ZO + 256] = zf if h == 1 else zb
    dk[:, O_MSKF] = 1.0 if h == 1 else 0.0
    dk[:, O_MSKB] = 1.0 if h == 0 else 0.0
    dk[:, O_EPS6] = 1e-6
    dk[:, O_EPS5] = 1e-5
    dk[:, O_ONE] = 1.0
    dk[:, O_NH:O_NH + 8] = -0.5
    c["dk"] = dk
    a = np.concatenate([64 * h + np.arange(64), oc]).astype(np.float64)
    d = np.arange(128, dtype=np.float64)
    th = 2 * np.pi * ((a[:, None] * d[None, :]) % 128) / 128.0
    fcs = np.zeros((128, 2, 256), np.float64)
    fcs[:, 0, :128] = np.cos(th)
    fcs[:, 0, 128:] = -np.sin(th)
    fcs[:, 1, :128] = np.sin(th)
    fcs[:, 1, 128:] = np.cos(th)
    c["fcs"] = fcs.astype(np.float32).astype(bf)
    b = np.arange(128, dtype=np.int64)[:, None, None]
    dd = np.arange(128, dtype=np.int64)[None, :, None]
    cg = (64 * h + np.arange(64, dtype=np.int64))[None, None, :]
    num = (cg * b * 128 + dd * b) % 16384
    th2 = 2 * np.pi * num.astype(np.float64) / 16384.0
    mt = np.zeros((128, 128, 2, 64), np.float64)
    mt[:, :, 0, :] = np.cos(th2)
    mt[:, :, 1, :] = np.sin(th2)
    c["mt"] = mt.astype(np.float32).astype(bf)
    ch = (np.arange(2)[None, :, None] * 128 + np.arange(128)[:, None, None]).astype(np.int64)
    jj = np.arange(256, dtype=np.int64)[None, None, :]
    th3 = 2 * np.pi * ((ch * jj) % 256).astype(np.float64) / 256.0
    cs = np.concatenate([np.cos(th3), -np.sin(th3)], axis=2)
    c["cs"] = cs.astype(np.float32).astype(bf)
    return c


_CACHE = {}


def _rep(v):
    return np.ascontiguousarray(np.broadcast_to(np.asarray(v, np.float32).reshape(1, -1), (128, v.size)))


def make_in_maps(inp):
    seqs = [(inp["x_prompt"][0], inp["mem_prompt"][0]), (inp["x_prompt"][1], inp["mem_prompt"][1]), (inp["x_sample"][0], inp["mem_sample"][0])]
    shared = {
        "w_in": np.ascontiguousarray(inp["w_in"][0]), "w_ro": np.ascontiguousarray(inp["w_ret_out"][0]),
        "w_4": np.ascontiguousarray(inp["w_four_out"][0]), "w_mx": np.ascontiguousarray(inp["w_mix_out"][0]),
        "w_cq": np.ascontiguousarray(inp["w_cq"][0]), "w_ck": np.ascontiguousarray(inp["w_ck"][0]),
        "w_cv": np.ascontiguousarray(inp["w_cv"][0]), "w_co": np.ascontiguousarray(inp["w_co"][0]),
        "w_up": np.ascontiguousarray(inp["w_up"][0]), "w_dn": np.ascontiguousarray(inp["w_down"][0]),
        "nmix": _rep(inp["norm_mix_w"][0]), "nca": _rep(inp["norm_ca_w"][0]), "nmem": _rep(inp["norm_mem_w"][0]),
        "nmlp": _rep(inp["norm_mlp_w"][0]), "nfin": _rep(inp["norm_final_w"]), "gnw": _rep(inp["ret_gn_w"][0]),
    }
    consts = [_consts(0), _consts(1)]
    in_maps = []
    for core in range(8):
        si = min(core // 2, 2)
        h = core % 2
        x, mem = seqs[si]
        x = np.asarray(x, np.float32)
        xm = np.ascontiguousarray(x[h * 8192:(h + 1) * 8192])
        oc = _other_chunks(h)
        xo = np.ascontiguousarray(x.reshape(128, 128, D)[oc].reshape(8192, D))
        df = np.asarray(inp["ret_decay_fwd"][0], np.float32)
        db = np.asarray(inp["ret_decay_bwd"][0], np.float32)
        do = df if h == 1 else db
        m = dict(shared)
        m.update(consts[h])
        m.update({"xm": xm, "xo": xo, "mem": np.ascontiguousarray(np.asarray(mem, np.float32)),
                  "dec": _rep(np.concatenate([df, db, do]))})
        in_maps.append(m)
    return in_maps


def kernel(**inputs):
    inp = {k: np.asarray(v) for k, v in inputs.items()}
    if "nc" not in _CACHE:
        _CACHE["nc"] = build()[0]
    nc = _CACHE["nc"]
    in_maps = make_in_maps(inp)
    res = run_bass_kernel_spmd(nc, in_maps, core_ids=list(range(8)))
    ys = [np.asarray(r["y"], np.float32) for r in res.results]
    y_prompt = np.stack([np.concatenate([ys[0], ys[1]], 0), np.concatenate([ys[2], ys[3]], 0)], 0)
    y_sample = np.concatenate([ys[4], ys[5]], 0)[None]
    return (y_prompt, y_sample)
```

```python
import contextlib
import numpy as np
import ml_dtypes
import concourse.bass as bass
import concourse.mybir as mybir
from concourse.bass_utils import run_bass_kernel_spmd

F32 = mybir.dt.float32
BF16 = mybir.dt.bfloat16
AF = mybir.ActivationFunctionType
ALU = mybir.AluOpType
AX = mybir.AxisListType

D = 1024
SEQ = 16384
NM = 64
NO = 64
INW = 7168
DFF = 4096
NMEM = 256


class Prog:
    NPOOL = 8

    def __init__(self, nc):
        self.nc = nc
        self.eng = {"pe": nc.tensor, "act": nc.scalar, "dve": nc.vector, "pool": nc.gpsimd, "sp": nc.sync}
        self.sems = {}
        self.cnt = {}
        self.known = {e: {} for e in self.eng}
        self.carry = {e: {} for e in self.eng}
        self.dma_rr = {e: 0 for e in self.eng}
        self.n_inst = 0
        self._reset()

    def _reset(self):
        self.ops = []
        self.lastw = {}
        self.lastr = {}
        self.last_on_sem = {}

    def getsem(self, name):
        if name not in self.sems:
            self.sems[name] = self.nc.alloc_semaphore(name=name)
        return self.sems[name]

    def op(self, eng, fn, reads=(), writes=(), dma=False, ndma=1):
        idx = len(self.ops)
        deps = set()
        for k in reads:
            deps.update(self.lastw.get(k, {}).values())
        for k in writes:
            deps.update(self.lastw.get(k, {}).values())
            deps.update(self.lastr.get(k, {}).values())
        semname = None
        if dma:
            semname = "d_%s_%d" % (eng, self.dma_rr[eng] % self.NPOOL)
            self.dma_rr[eng] += 1
            if semname in self.last_on_sem:
                deps.add(self.last_on_sem[semname])
            self.last_on_sem[semname] = idx
        self.ops.append(dict(eng=eng, fn=fn, deps=deps, dma=dma, semname=semname, ndma=ndma, waited=False, tok=None))
        ek = (eng, semname)
        for k in reads:
            self.lastr.setdefault(k, {})[ek] = idx
        for k in writes:
            self.lastw[k] = {ek: idx}
            self.lastr[k] = {}
        return idx

    def flush(self, final=False):
        ops = self.ops
        last_eng = {}
        for i, o in enumerate(ops):
            if not o["dma"]:
                last_eng[o["eng"]] = i
            for d in o["deps"]:
                od = ops[d]
                if (not od["dma"]) and od["eng"] == o["eng"] == "pe":
                    continue
                od["waited"] = True
        for i in last_eng.values():
            ops[i]["waited"] = True
        for o in ops:
            if o["dma"]:
                o["waited"] = True
        for o in ops:
            if not o["waited"]:
                continue
            if o["dma"]:
                nm = o["semname"]
                self.cnt[nm] = self.cnt.get(nm, 0) + 16 * o["ndma"]
            else:
                nm = "e_" + o["eng"]
                self.cnt[nm] = self.cnt.get(nm, 0) + 1
            o["tok"] = (nm, self.cnt[nm])
        for o in ops:
            e = o["eng"]
            engobj = self.eng[e]
            need = dict(self.carry[e])
            self.carry[e] = {}
            for d in o["deps"]:
                od = ops[d]
                if od["tok"] is None:
                    continue
                if (not od["dma"]) and od["eng"] == e == "pe":
                    continue
                nm, v = od["tok"]
                need[nm] = max(need.get(nm, 0), v)
            for nm, v in need.items():
                if self.known[e].get(nm, 0) >= v:
                    continue
                engobj.wait_ge(self.getsem(nm), v)
                self.known[e][nm] = v
                self.n_inst += 1
            r = o["fn"](engobj)
            self.n_inst += 1
            if o["tok"] is not None:
                nm, v = o["tok"]
                if o["dma"]:
                    insts = r if isinstance(r, (list, tuple)) else [r]
                    assert len(insts) == o["ndma"], (len(insts), o["ndma"])
                    for ins in insts:
                        ins.then_inc(self.getsem(nm), 16)
                else:
                    r.then_inc(self.getsem(nm), 1)
        allc = dict(self.cnt)
        for e in self.eng:
            self.carry[e] = dict(allc)
        if final:
            engobj = self.eng["sp"]
            for nm, v in allc.items():
                if self.known["sp"].get(nm, 0) < v:
                    engobj.wait_ge(self.getsem(nm), v)
                    self.known["sp"][nm] = v
        self._reset()


O_E0F, O_MF, O_E0B, O_MB, O_XF, O_XB = 0, 128, 256, 384, 512, 640
O_ZF, O_ZB, O_ZO = 768, 1024, 1280
O_MSKF, O_MSKB, O_EPS6, O_EPS5, O_ONE = 1536, 1537, 1538, 1539, 1540
O_NH = 1544
DKW = 1552


def build(dbg=False, phases=("p1", "p2", "p3", "p4")):
    nc = bass.Bass("TRN2", target_bir_lowering=False)

    def din(name, shape, dt=F32):
        return nc.dram_tensor(name, shape, dt, kind="ExternalInput").ap()

    def dscr(name, shape, dt):
        return nc.dram_tensor(name, shape, dt, kind="ExternalOutput" if dbg else "Internal").ap()

    xm = din("xm", [NM * 128, D])
    xo = din("xo", [NO * 128, D])
    mem = din("mem", [NMEM, D])
    w_in = din("w_in", [D, INW])
    w_ro = din("w_ro", [D, D])
    w_4 = din("w_4", [D, D])
    w_mx = din("w_mx", [D, D])
    w_cq = din("w_cq", [D, D])
    w_ck = din("w_ck", [D, D])
    w_cv = din("w_cv", [D, D])
    w_co = din("w_co", [D, D])
    w_up = din("w_up", [D, DFF])
    w_dn = din("w_dn", [DFF, D])
    nmix_d = din("nmix", [128, D])
    nca_d = din("nca", [128, D])
    nmem_d = din("nmem", [128, D])
    nmlp_d = din("nmlp", [128, D])
    nfin_d = din("nfin", [128, D])
    gnw_d = din("gnw", [128, D])
    dec_d = din("dec", [128, 12])
    dk_d = din("dk", [128, DKW])
    ident_d = din("ident", [128, 128], BF16)
    rotm_d = din("rotm", [NM * 128, 256])
    roto_d = din("roto", [NO * 128, 256])
    fcs_d = din("fcs", [128, 2, 256], BF16)
    mt_d = din("mt", [128, 128, 2, 64], BF16)
    cs_d = din("cs", [128, 2, 512], BF16)
    y_out = nc.dram_tensor("y", [NM * 128, D], F32, kind="ExternalOutput").ap()

    NT = (NM + NO) * 128
    Qd = dscr("Qd", [NM * 128, D], BF16)
    Kd = dscr("Kd", [NM * 128, D], BF16)
    Vd = dscr("Vd", [NM * 128, D], BF16)
    Gd = dscr("Gd", [NM * 128, D], BF16)
    GRd = dscr("GRd", [NM * 128, D], BF16)
    GFd = dscr("GFd", [NM * 128, D], BF16)
    KOd = dscr("KOd", [NO * 128, D], BF16)
    VOd = dscr("VOd", [NO * 128, D], BF16)
    Zd = dscr("Zd", [4, 2, 2, NT, 128], BF16)
    YFd = dscr("YFd", [NM * 128, D], F32)
    YBd = dscr("YBd", [NM * 128, D], F32)
    UFd = dscr("UFd", [D, NM * 128], BF16)
    X2d = dscr("X2d", [NM * 128, D], F32)

    P = Prog(nc)
    with contextlib.ExitStack() as gs:
        ident = gs.enter_context(nc.sbuf_tensor("ident_s", [128, 128], BF16))
        dk = gs.enter_context(nc.sbuf_tensor("dk_s", [128, DKW], F32))
        ckT = gs.enter_context(nc.sbuf_tensor("ckT", [128, 8, NMEM], BF16))
        cv = gs.enter_context(nc.sbuf_tensor("cv", [128, 2, D], BF16))
        P.op("sp", lambda e: e.dma_start(out=ident[:], in_=ident_d[:, :]), writes=["ident"], dma=True)
        P.op("sp", lambda e: e.dma_start(out=dk[:], in_=dk_d[:, :]), writes=["dk"], dma=True)
        eps6 = dk[:, O_EPS6:O_EPS6 + 1]
        eps5 = dk[:, O_EPS5:O_EPS5 + 1]
        one_ap = dk[:, O_ONE:O_ONE + 1]

        def rstd_ops(st, key, x_ap, xkey, junk, jkey):
            P.op("dve", lambda e: e.memset(st[:], 0.0), writes=[key])
            P.op("act", lambda e: e.activation(out=junk, in_=x_ap, func=AF.Square, accum_out=st[:, 0:1]),
                 reads=[xkey, key], writes=[jkey, key])
            P.op("act", lambda e: e.activation(out=st[:, 1:2], in_=st[:, 0:1], func=AF.Sqrt, scale=1.0 / D, bias=eps6),
                 reads=[key, "dk"], writes=[key])
            P.op("dve", lambda e: e.reciprocal(out=st[:, 2:3], in_=st[:, 1:2]), reads=[key], writes=[key])

        def rstd_pow(st, key, x_ap, xkey, junk, jkey):
            P.op("dve", lambda e: e.memset(st[:], 0.0), writes=[key])
            P.op("act", lambda e: e.activation(out=junk, in_=x_ap, func=AF.Square, accum_out=st[:, 0:1]),
                 reads=[xkey, key], writes=[jkey, key])
            P.op("dve", lambda e: e.tensor_scalar(out=st[:, 1:2], in0=st[:, 0:1], scalar1=1.0 / D, scalar2=eps6, op0=ALU.mult, op1=ALU.add), reads=[key, "dk"], writes=[key])
            P.op("act", lambda e: e.activation(out=st[:, 3:4], in_=st[:, 1:2], func=AF.Ln), reads=[key], writes=[key])
            P.op("act", lambda e: e.activation(out=st[:, 2:3], in_=st[:, 3:4], func=AF.Exp, scale=-0.5), reads=[key], writes=[key])

        def transposes8(src, skey, pT, pkey):
            for k in range(8):
                P.op("pe", lambda e, k=k: e.transpose(out=pT[:, k * 128:(k + 1) * 128], in_=src[:, k * 128:(k + 1) * 128], identity=ident[:]),
                     reads=[skey, "ident"], writes=[pkey])

        if "p1" in phases:
            with contextlib.ExitStack() as es:
                sb = lambda n, s, d: es.enter_context(nc.sbuf_tensor(n, s, d))
                ps = lambda n, s, d: es.enter_context(nc.psum_tensor(n, s, d))
                win = sb("win", [128, 8, INW], BF16)
                nmix = sb("nmix_s", [128, D], F32)
                css = sb("css", [128, 2, 512], BF16)
                xs = [sb("xs%d" % i, [128, D], F32) for i in range(2)]
                rt = [sb("rt%d" % i, [128, 256], F32) for i in range(3)]
                st_ = [sb("st1_%d" % i, [128, 8], F32) for i in range(2)]
                xn_ = [sb("xn1_%d" % i, [128, D], BF16) for i in range(2)]
                xnT_ = [sb("xnT1_%d" % i, [128, D], BF16) for i in range(2)]
                qkf = [sb("qkf%d" % i, [128, D], F32) for i in range(2)]
                tmp = [sb("tmp%d" % i, [128, 4, 128], F32) for i in range(4)]
                outs = {nm: [sb("o_%s%d" % (nm, i), [128, D], BF16) for i in range(2)] for nm in ("q", "k", "v")}
                outs.update({nm: [sb("o_%s0" % nm, [128, D], BF16)] for nm in ("g", "gr", "gf")})
                ub = sb("ub", [128, D], BF16)
                uT = sb("uT", [128, D], BF16)
                zt = [sb("zt%d" % i, [128, 4, 512], BF16) for i in range(2)]
                pT = ps("pT1", [128, D], BF16)
                pT2 = ps("pT1b", [128, D], BF16)
                pm = [ps("pm1_%d" % i, [128, 512], F32) for i in range(6)]
                for k in range(8):
                    P.op("pool", lambda e, k=k: e.dma_start(out=win[:, k, :], in_=w_in[k * 128:(k + 1) * 128, :]),
                         writes=["win%d" % k], dma=True)
                P.op("sp", lambda e: e.dma_start(out=nmix[:], in_=nmix_d[:, :]), writes=["nmix"], dma=True)
                P.op("sp", lambda e: e.dma_start(out=css[:], in_=cs_d[:, :, :]), writes=["css"], dma=True)
                bi = [0]

                def p1_load_x(ti):
                    mine = ti < NM
                    sl = ti % 2
                    src = xm if mine else xo
                    r0 = (ti if mine else ti - NM) * 128
                    P.op("sp", lambda e: e.dma_start(out=xs[sl][:], in_=src[r0:r0 + 128, :]), writes=["xs%d" % sl], dma=True)

                def p1_load_rt(ti):
                    mine = ti < NM
                    sl = ti % 3
                    rsrc = rotm_d if mine else roto_d
                    r0 = (ti if mine else ti - NM) * 128
                    P.op("sp", lambda e: e.dma_start(out=rt[sl][:], in_=rsrc[r0:r0 + 128, :]), writes=["rt%d" % sl], dma=True)

                def rotary(srcf, skey, dst, dkey, sl):
                    X = srcf[:].rearrange("p (h t f) -> p h t f", h=4, t=2)
                    O = dst[:].rearrange("p (h t f) -> p h t f", h=4, t=2)
                    cosb = rt[sl][:, 0:128].rearrange("p (o f) -> p o f", o=1).broadcast_to([128, 4, 128])
                    sinb = rt[sl][:, 128:256].rearrange("p (o f) -> p o f", o=1).broadcast_to([128, 4, 128])
                    rk = "rt%d" % sl
                    P.op("pool", lambda e: e.tensor_tensor(out=tmp[0][:], in0=X[:, :, 0, :], in1=cosb, op=ALU.mult), reads=[skey, rk], writes=["tmp0"])
                    P.op("dve", lambda e: e.tensor_tensor(out=tmp[1][:], in0=X[:, :, 1, :], in1=sinb, op=ALU.mult), reads=[skey, rk], writes=["tmp1"])
                    P.op("dve", lambda e: e.tensor_tensor(out=O[:, :, 0, :], in0=tmp[0][:], in1=tmp[1][:], op=ALU.subtract), reads=["tmp0", "tmp1"], writes=[dkey])
                    P.op("pool", lambda e: e.tensor_tensor(out=tmp[2][:], in0=X[:, :, 1, :], in1=cosb, op=ALU.mult), reads=[skey, rk], writes=["tmp2"])
                    P.op("dve", lambda e: e.tensor_tensor(out=tmp[3][:], in0=X[:, :, 0, :], in1=sinb, op=ALU.mult), reads=[skey, rk], writes=["tmp3"])
                    P.op("pool", lambda e: e.tensor_tensor(out=O[:, :, 1, :], in0=tmp[2][:], in1=tmp[3][:], op=ALU.add), reads=["tmp2", "tmp3", dkey], writes=[dkey])

                def p1_A_stages(ti):
                    sl = ti % 2
                    xk = "xs%d" % sl
                    st, xn, xnT = st_[sl], xn_[sl], xnT_[sl]
                    ks_, kn_, kt_ = "st1_%d" % sl, "xn1_%d" % sl, "xnT1_%d" % sl

                    def a0():
                        P.op("dve", lambda e: e.memset(st[:], 0.0), writes=[ks_])
                        P.op("act", lambda e: e.activation(out=xn[:], in_=xs[sl][:], func=AF.Square, accum_out=st[:, 0:1]), reads=[xk, ks_], writes=[kn_, ks_])

                    def a1():
                        P.op("act", lambda e: e.activation(out=st[:, 1:2], in_=st[:, 0:1], func=AF.Sqrt, scale=1.0 / D, bias=eps6), reads=[ks_, "dk"], writes=[ks_])

                    def a2():
                        P.op("dve", lambda e: e.reciprocal(out=st[:, 2:3], in_=st[:, 1:2]), reads=[ks_], writes=[ks_])

                    def a3():
                        P.op("dve", lambda e: e.scalar_tensor_tensor(out=xn[:], in0=xs[sl][:], scalar=st[:, 2:3], in1=nmix[:], op0=ALU.mult, op1=ALU.mult),
                             reads=[xk, ks_, "nmix"], writes=[kn_])

                    def a4():
                        transposes8(xn, kn_, pT, "pT1")

                    def a5():
                        P.op("dve", lambda e: e.tensor_copy(out=xnT[:], in_=pT[:]), reads=["pT1"], writes=[kt_])
                    return [a0, a1, a2, a3, a4, a5]

                def p1_tail_stages(ti):
                    zr0 = ti * 128
                    zsl = ti % 2

                    def t0():
                        transposes8(ub, "ub", pT2, "pT1b")
                        P.op("act", lambda e: e.activation(out=uT[:], in_=pT2[:], func=AF.Copy), reads=["pT1b"], writes=["uT"])

                    def t1():
                        for g in range(4):
                            b = bi[0] % 6
                            bi[0] += 1
                            bank, bkey = pm[b], "pm1_%d" % b
                            for kk in range(2):
                                P.op("pe", lambda e, g=g, kk=kk, bank=bank: e.matmul(bank[:], lhsT=uT[:, (2 * g + kk) * 128:(2 * g + kk + 1) * 128], rhs=css[:, kk, :], start=(kk == 0), stop=(kk == 1)),
                                     reads=["uT", "css"], writes=[bkey])
                            if g % 2 == 0:
                                P.op("act", lambda e, g=g, bank=bank: e.activation(out=zt[zsl][:, g, :], in_=bank[:], func=AF.Copy), reads=[bkey], writes=["zt%d" % zsl])
                            else:
                                P.op("dve", lambda e, g=g, bank=bank: e.tensor_copy(out=zt[zsl][:, g, :], in_=bank[:]), reads=[bkey], writes=["zt%d" % zsl])
                        for g in range(4):
                            P.op("sp", lambda e, g=g: e.dma_start(
                                out=Zd[g, :, :, zr0:zr0 + 128, :].rearrange("ri hf t c -> t (ri hf) c"),
                                in_=zt[zsl][:, g, :].rearrange("p (rh c) -> p rh c", rh=4)),
                                reads=["zt%d" % zsl], writes=["Zd"], dma=True)
                    return [t0, t1]

                def p1_compute(ti):
                    mine = ti < NM
                    sl = ti % 2
                    r0 = (ti if mine else ti - NM) * 128
                    zr0 = ti * 128
                    xnT = xnT_[sl]
                    kt_ = "xnT1_%d" % sl
                    slices = list(range(14)) if mine else [2, 3, 4, 5, 8, 9]
                    nxt_st = p1_A_stages(ti + 1) if ti + 1 < NM + NO else []
                    hook = ({1: 0, 2: 1, 3: 2, 5: 3, 8: 4, 10: 5} if mine else {0: 0, 1: 1, 2: 2, 3: 3, 4: 4, 5: 5})
                    prev_tail = p1_tail_stages(ti - 1) if ti > 0 else []
                    thook = ({4: 0, 7: 1} if mine else {1: 0, 3: 1})
                    for si_, n in enumerate(slices):
                        if si_ in hook and nxt_st:
                            nxt_st[hook[si_]]()
                        if si_ in thook and prev_tail:
                            prev_tail[thook[si_]]()
                        b = bi[0] % 6
                        bi[0] += 1
                        bank, bkey = pm[b], "pm1_%d" % b
                        for k in range(8):
                            P.op("pe", lambda e, k=k, n=n, bank=bank: e.matmul(bank[:], lhsT=xnT[:, k * 128:(k + 1) * 128], rhs=win[:, k, n * 512:(n + 1) * 512], start=(k == 0), stop=(k == 7)),
                                 reads=[kt_, "win%d" % k], writes=[bkey])
                        hf = n % 2
                        cs_ = slice(hf * 512, (hf + 1) * 512)
                        if n in (0, 1):
                            P.op("act", lambda e, bank=bank, cs_=cs_: e.activation(out=qkf[0][:, cs_], in_=bank[:], func=AF.Copy, scale=1.0 / 16.0), reads=[bkey], writes=["qkf0"])
                            if n == 1:
                                rotary(qkf[0], "qkf0", outs["q"][sl], "o_q%d" % sl, ti % 3)
                        elif n in (2, 3):
                            P.op("act", lambda e, bank=bank, cs_=cs_: e.activation(out=qkf[1][:, cs_], in_=bank[:], func=AF.Copy), reads=[bkey], writes=["qkf1"])
                            if n == 3:
                                rotary(qkf[1], "qkf1", outs["k"][sl], "o_k%d" % sl, ti % 3)
                        else:
                            nm_, eng = {4: ("v", "dve"), 5: ("v", "dve"), 6: ("g", "act"), 7: ("g", "act"), 8: ("u", "dve"), 9: ("u", "dve"),
                                        10: ("gr", "act"), 11: ("gr", "act"), 12: ("gf", "dve"), 13: ("gf", "dve")}[n]
                            if nm_ == "u":
                                dst, dkey = ub, "ub"
                            elif nm_ == "v":
                                dst, dkey = outs["v"][sl], "o_v%d" % sl
                            else:
                                dst, dkey = outs[nm_][0], "o_%s0" % nm_
                            if eng == "act":
                                P.op("act", lambda e, bank=bank, cs_=cs_, dst=dst: e.activation(out=dst[:, cs_], in_=bank[:], func=AF.Copy), reads=[bkey], writes=[dkey])
                            else:
                                P.op("dve", lambda e, bank=bank, cs_=cs_, dst=dst: e.tensor_copy(out=dst[:, cs_], in_=bank[:]), reads=[bkey], writes=[dkey])
                    if mine:
                        for nm_, dd in (("q", Qd), ("k", Kd), ("v", Vd)):
                            P.op("sp", lambda e, nm_=nm_, dd=dd: e.dma_start(out=dd[r0:r0 + 128, :], in_=outs[nm_][sl][:]), reads=["o_%s%d" % (nm_, sl)], writes=["dram_" + nm_], dma=True)
                        for nm_, dd in (("g", Gd), ("gr", GRd), ("gf", GFd)):
                            P.op("sp", lambda e, nm_=nm_, dd=dd: e.dma_start(out=dd[r0:r0 + 128, :], in_=outs[nm_][0][:]), reads=["o_%s0" % nm_], writes=["dram_" + nm_], dma=True)
                    else:
                        for nm_, dd in (("k", KOd), ("v", VOd)):
                            P.op("sp", lambda e, nm_=nm_, dd=dd: e.dma_start(out=dd[r0:r0 + 128, :], in_=outs[nm_][sl][:]), reads=["o_%s%d" % (nm_, sl)], writes=["dram_o" + nm_], dma=True)

                p1_load_x(0)
                p1_load_rt(0)
                p1_load_x(1)
                for f_ in p1_A_stages(0):
                    f_()
                for ti in range(NM + NO):
                    if ti + 2 < NM + NO:
                        p1_load_x(ti + 2)
                    if ti + 1 < NM + NO:
                        p1_load_rt(ti + 1)
                    p1_compute(ti)
                for f_ in p1_tail_stages(NM + NO - 1):
                    f_()
                P.flush()

        if "p2" in phases:
            with contextlib.ExitStack() as es:
                sb = lambda n, s, d: es.enter_context(nc.sbuf_tensor(n, s, d))
                ps = lambda n, s, d: es.enter_context(nc.psum_tensor(n, s, d))
                dec = sb("dec_s", [128, 12], F32)
                lg = sb("lg", [128, 12], F32)
                gch = sb("gch", [128, 12], F32)
                DT = [sb("DT%d" % i, [128, 4, 128], F32) for i in range(2)]
                xi = [sb("xi%d" % i, [128, 8, 128], F32) for i in range(2)]
                zeta = [sb("zeta%d" % i, [128, 4, 256], F32) for i in range(3)]
                tmpd = sb("tmpd", [128, 128], F32)
                S = [sb("S%d" % i, [128, 8, 256], F32) for i in range(3)]
                Sbf = [sb("Sbf%d" % i, [128, 8, 256], BF16) for i in range(2)]
                qs = [[sb("q2_%d%d" % (d_, i), [128, D], BF16) for i in range(2)] for d_ in range(2)]
                ks = [[sb("k2_%d%d" % (d_, i), [128, D], BF16) for i in range(2)] for d_ in range(2)]
                vs = [[sb("v2_%d%d" % (d_, i), [128, D], BF16) for i in range(2)] for d_ in range(2)]
                qT = [sb("qT%d" % i, [128, D], BF16) for i in range(2)]
                qxT = [sb("qxT%d" % i, [128, D], BF16) for i in range(2)]
                kT = [sb("kT%d" % i, [128, D], BF16) for i in range(2)]
                kz = [sb("kz%d" % i, [128, D], BF16) for i in range(2)]
                PT = [sb("PT%d" % i, [128, 512], BF16) for i in range(2)]
                ysb = [sb("ysb%d" % i, [128, D], F32) for i in range(2)]
                pTt = [ps("pTt%d" % i, [128, D], BF16) for i in range(2)]
                pS = [ps("pS%d" % i, [128, 512], F32) for i in range(2)]
                py = [ps("py%d" % i, [128, 512], F32) for i in range(2)]
                pst = [ps("pst%d" % i, [128, 512], F32) for i in range(2)]

                P.op("sp", lambda e: e.dma_start(out=dec[:], in_=dec_d[:, :]), writes=["dec"], dma=True)
                P.op("act", lambda e: e.activation(out=lg[:], in_=dec[:], func=AF.Exp, scale=-1.0), reads=["dec"], writes=["lg"])
                P.op("act", lambda e: e.activation(out=lg[:], in_=lg[:], func=AF.Ln, bias=one_ap, scale=1.0), reads=["lg", "dk"], writes=["lg"])
                P.op("dve", lambda e: e.tensor_scalar(out=lg[:], in0=lg[:], scalar1=-1.0, scalar2=None, op0=ALU.mult), reads=["lg"], writes=["lg"])
                P.op("act", lambda e: e.activation(out=gch[:], in_=lg[:], func=AF.Exp, scale=128.0), reads=["lg"], writes=["gch"])
                for d_ in range(2):
                    oe, om, ox = (O_E0F, O_MF, O_XF) if d_ == 0 else (O_E0B, O_MB, O_XB)
                    for hd in range(4):
                        col = 4 * d_ + hd
                        P.op("act", lambda e, oe=oe, col=col: e.activation(out=tmpd[:], in_=dk[:, oe:oe + 128], func=AF.Exp, scale=lg[:, col:col + 1]), reads=["dk", "lg"], writes=["tmpd"])
                        P.op("dve", lambda e, om=om, d_=d_, hd=hd: e.tensor_tensor(out=DT[d_][:, hd, :], in0=tmpd[:], in1=dk[:, om:om + 128], op=ALU.mult), reads=["tmpd", "dk"], writes=["DT%d" % d_])
                        for hf in range(2):
                            P.op("act", lambda e, ox=ox, col=col, d_=d_, hd=hd, hf=hf: e.activation(out=xi[d_][:, 2 * hd + hf, :], in_=dk[:, ox:ox + 128], func=AF.Exp, scale=lg[:, col:col + 1]), reads=["dk", "lg"], writes=["xi%d" % d_])
                for z_, oz in enumerate((O_ZF, O_ZB, O_ZO)):
                    for hd in range(4):
                        col = 4 * z_ + hd
                        P.op("act", lambda e, z_=z_, oz=oz, hd=hd, col=col: e.activation(out=zeta[z_][:, hd, :], in_=dk[:, oz:oz + 256], func=AF.Exp, scale=lg[:, col:col + 1]), reads=["dk", "lg"], writes=["zeta%d" % z_])
                P.op("dve", lambda e: e.memset(S[2][:], 0.0), writes=["S2"])
                P2STOP = 9

                def state_head(si, kzt, kzkey, vt, vkey, gcol0, bank, bkey, hd):
                    for hf in range(2):
                        P.op("pe", lambda e, hf=hf: e.matmul(bank[:, hf * 256:(hf + 1) * 256], lhsT=kzt[:, hd * 256 + hf * 128:hd * 256 + hf * 128 + 128], rhs=vt[:, hd * 256:(hd + 1) * 256], start=True, stop=True),
                             reads=[kzkey, vkey], writes=[bkey])
                    sview = S[si][:, 2 * hd:2 * hd + 2, :].rearrange("p a b -> p (a b)")
                    P.op("dve", lambda e: e.scalar_tensor_tensor(out=sview, in0=sview, scalar=gch[:, gcol0 + hd:gcol0 + hd + 1], in1=bank[:], op0=ALU.mult, op1=ALU.add),
                         reads=["S%d" % si, "gch", bkey], writes=["S%d" % si])

                def state_update(si, kzt, kzkey, vt, vkey, gcol0, bset):
                    for hd in range(4):
                        bank, bkey = ((pst[bset], "pst%d" % bset) if hd % 2 == 0 else (py[bset], "py%d" % bset))
                        state_head(si, kzt, kzkey, vt, vkey, gcol0, bank, bkey, hd)

                def p2a_load(j):
                    sl = j % 2
                    P.op("sp", lambda e: e.dma_start(out=ks[0][sl][:], in_=KOd[j * 128:(j + 1) * 128, :]), writes=["k2_0%d" % sl], dma=True)
                    P.op("sp", lambda e: e.dma_start(out=vs[0][sl][:], in_=VOd[j * 128:(j + 1) * 128, :]), writes=["v2_0%d" % sl], dma=True)
                if P2STOP >= 2:
                    p2a_load(0)
                for j in range(NO if P2STOP >= 2 else 0):
                    if j + 1 < NO:
                        p2a_load(j + 1)
                    sl = j % 2
                    bs_ = j % 2
                    P.op("pool", lambda e, sl=sl, bs_=bs_: e.tensor_tensor(out=kz[bs_][:], in0=ks[0][sl][:], in1=zeta[2][:].rearrange("p a b -> p (a b)"), op=ALU.mult), reads=["k2_0%d" % sl, "zeta2"], writes=["kz%d" % bs_])
                    state_update(2, kz[bs_], "kz%d" % bs_, vs[0][sl], "v2_0%d" % sl, 8, bs_)
                P.op("dve", lambda e: e.tensor_scalar(out=S[0][:], in0=S[2][:], scalar1=dk[:, O_MSKF:O_MSKF + 1], scalar2=None, op0=ALU.mult), reads=["S2", "dk"], writes=["S0"])
                P.op("dve", lambda e: e.tensor_scalar(out=S[1][:], in0=S[2][:], scalar1=dk[:, O_MSKB:O_MSKB + 1], scalar2=None, op0=ALU.mult), reads=["S2", "dk"], writes=["S1"])
                for d_ in range(2):
                    P.op("act", lambda e, d_=d_: e.activation(out=Sbf[d_][:], in_=S[d_][:], func=AF.Copy), reads=["S%d" % d_], writes=["Sbf%d" % d_])
                P.flush()

                def p2_load(d_, c, sl):
                    r0 = c * 128
                    P.op("sp", lambda e: e.dma_start(out=qs[d_][sl][:], in_=Qd[r0:r0 + 128, :]), writes=["q2_%d%d" % (d_, sl)], dma=True)
                    P.op("sp", lambda e: e.dma_start(out=ks[d_][sl][:], in_=Kd[r0:r0 + 128, :]), writes=["k2_%d%d" % (d_, sl)], dma=True)
                    P.op("sp", lambda e: e.dma_start(out=vs[d_][sl][:], in_=Vd[r0:r0 + 128, :]), writes=["v2_%d%d" % (d_, sl)], dma=True)

                def p2_stages(d_, c, sl):
                    q_, k_, v_ = qs[d_][sl], ks[d_][sl], vs[d_][sl]
                    qk_, kk_, vk_ = "q2_%d%d" % (d_, sl), "k2_%d%d" % (d_, sl), "v2_%d%d" % (d_, sl)
                    ds = str(d_)
                    T_, kT_ = pTt[d_], "pTt" + ds

                    def s0():
                        transposes8(q_, qk_, T_, kT_)
                        P.op("act", lambda e: e.activation(out=qT[d_][:], in_=T_[:], func=AF.Copy), reads=[kT_], writes=["qT" + ds])
                        P.op("dve", lambda e: e.tensor_tensor(out=qxT[d_][:], in0=qT[d_][:], in1=xi[d_][:].rearrange("p a b -> p (a b)"), op=ALU.mult), reads=["qT" + ds, "xi" + ds], writes=["qxT" + ds])

                    def s1():
                        transposes8(k_, kk_, T_, kT_)
                        P.op("act", lambda e: e.activation(out=kT[d_][:], in_=T_[:], func=AF.Copy), reads=[kT_], writes=["kT" + ds])
                        P.op("pool", lambda e: e.tensor_tensor(out=kz[d_][:], in0=k_[:], in1=zeta[d_][:].rearrange("p a b -> p (a b)"), op=ALU.mult), reads=[kk_, "zeta" + ds], writes=["kz" + ds])

                    def s2():
                        for hd in range(4):
                            for hf in range(2):
                                m = 2 * hd + hf
                                P.op("pe", lambda e, hd=hd, hf=hf, m=m: e.matmul(pS[d_][:, hd * 128:(hd + 1) * 128], lhsT=kT[d_][:, m * 128:(m + 1) * 128], rhs=qT[d_][:, m * 128:(m + 1) * 128], start=(hf == 0), stop=(hf == 1)),
                                     reads=["kT" + ds, "qT" + ds], writes=["pS" + ds])
                        P.op("dve", lambda e: e.tensor_tensor(out=PT[d_][:], in0=pS[d_][:], in1=DT[d_][:].rearrange("p a b -> p (a b)"), op=ALU.mult), reads=["pS" + ds, "DT" + ds], writes=["PT" + ds])

                    def ystage(half):
                        bank, bkey = py[d_], "py" + ds
                        for hd in (2 * half, 2 * half + 1):
                            co = (hd % 2) * 256
                            P.op("pe", lambda e, hd=hd, co=co: e.matmul(bank[:, co:co + 256], lhsT=PT[d_][:, hd * 128:(hd + 1) * 128], rhs=v_[:, hd * 256:(hd + 1) * 256], start=True, stop=False),
                                 reads=["PT" + ds, vk_], writes=[bkey])
                            for hf in range(2):
                                m = 2 * hd + hf
                                P.op("pe", lambda e, m=m, hf=hf, co=co: e.matmul(bank[:, co:co + 256], lhsT=qxT[d_][:, m * 128:(m + 1) * 128], rhs=Sbf[d_][:, m, :], start=False, stop=(hf == 1)),
                                     reads=["qxT" + ds, "Sbf" + ds], writes=[bkey])
                        if half == 0:
                            P.op("act", lambda e: e.activation(out=ysb[d_][:, 0:512], in_=bank[:], func=AF.Copy), reads=[bkey], writes=["ysb" + ds])
                        else:
                            P.op("dve", lambda e: e.tensor_copy(out=ysb[d_][:, 512:1024], in_=bank[:]), reads=[bkey], writes=["ysb" + ds])
                            yd = YFd if d_ == 0 else YBd
                            P.op("sp", lambda e: e.dma_start(out=yd[c * 128:(c + 1) * 128, :], in_=ysb[d_][:]), reads=["ysb" + ds], writes=["dram_y" + ds], dma=True)

                    def sh(hd):
                        bank, bkey = ((pst[d_], "pst" + ds) if hd % 2 == 0 else (py[d_], "py" + ds))
                        state_head(d_, kz[d_], "kz" + ds, v_, vk_, 4 * d_, bank, bkey, hd)
                        if hd % 2 == 1:
                            hh = hd // 2
                            P.op("act", lambda e: e.activation(out=Sbf[d_][:, 4 * hh:4 * hh + 4, :], in_=S[d_][:, 4 * hh:4 * hh + 4, :], func=AF.Copy), reads=["S" + ds], writes=["Sbf" + ds])

                    return [s0, s1, s2, lambda: ystage(0), lambda: ystage(1), lambda: sh(0), lambda: sh(1), lambda: sh(2), lambda: sh(3)]

                if P2STOP >= 3:
                    p2_load(0, 0, 0)
                    p2_load(1, NM - 1, 0)
                for t in range(NM if P2STOP >= 3 else 0):
                    sl = t % 2
                    if t + 1 < NM:
                        p2_load(0, t + 1, 1 - sl)
                        p2_load(1, NM - 2 - t, 1 - sl)
                    sf_ = p2_stages(0, t, sl)
                    sb_ = p2_stages(1, NM - 1 - t, sl)
                    for a_, b_ in zip(sf_, sb_):
                        a_()
                        b_()
                P.flush()

        if "p3" in phases:
            with contextlib.ExitStack() as es:
                sb = lambda n, s, d: es.enter_context(nc.sbuf_tensor(n, s, d))
                ps = lambda n, s, d: es.enter_context(nc.psum_tensor(n, s, d))
                ZL = sb("ZL", [128, 2, 128, 128], BF16)
                T = sb("Tt", [128, 128, 256], BF16)
                mts = sb("mts", [128, 128, 2, 64], BF16)
                ufs = sb("ufs", [128, 64, 128], BF16)
                fcs = sb("fcs_s", [128, 2, 256], BF16)
                pb = [ps("pb%d" % i, [128, 512], F32) for i in range(4)]
                pc = [ps("pc%d" % i, [128, 512], F32) for i in range(4)]
                P.op("sp", lambda e: e.dma_start(out=fcs[:], in_=fcs_d[:, :, :]), writes=["fcs"], dma=True)
                for dq in range(4):
                    P.op("sp", lambda e, dq=dq: e.dma_start(out=mts[:, dq * 32:(dq + 1) * 32, :, :], in_=mt_d[:, dq * 32:(dq + 1) * 32, :, :]), writes=["mts"], dma=True)
                for s in range(8):
                    for ri in range(2):
                        for hb in range(2):
                            P.op("sp", lambda e, s=s, ri=ri, hb=hb: e.dma_start(
                                out=ZL[:, ri, hb * 64:(hb + 1) * 64, :],
                                in_=Zd[s // 2, ri, s % 2, :, :].rearrange("(l b) c -> l b c", b=128)[:, hb * 64:(hb + 1) * 64, :]),
                                reads=["Zd"], writes=["ZL"], dma=True)
                    for cp in range(64):
                        bank, bkey = pb[cp % 4], "pb%d" % (cp % 4)
                        for q_ in range(2):
                            ch = 2 * cp + q_
                            P.op("pe", lambda e, ch=ch, q_=q_, bank=bank: e.matmul(bank[:, q_ * 256:(q_ + 1) * 256], lhsT=ZL[:, 0, :, ch], rhs=fcs[:, 0, :], start=True, stop=False),
                                 reads=["ZL", "fcs"], writes=[bkey])
                            P.op("pe", lambda e, ch=ch, q_=q_, bank=bank: e.matmul(bank[:, q_ * 256:(q_ + 1) * 256], lhsT=ZL[:, 1, :, ch], rhs=fcs[:, 1, :], start=False, stop=True),
                                 reads=["ZL", "fcs"], writes=[bkey])
                        tv = T[:, 2 * cp:2 * cp + 2, :].rearrange("p a b -> p (a b)")
                        if cp % 2 == 0:
                            P.op("act", lambda e, bank=bank, tv=tv: e.activation(out=tv, in_=bank[:], func=AF.Copy), reads=[bkey], writes=["Tt"])
                        else:
                            P.op("dve", lambda e, bank=bank, tv=tv: e.tensor_copy(out=tv, in_=bank[:]), reads=[bkey], writes=["Tt"])
                    for db in range(16):
                        bank, bkey = pc[db % 4], "pc%d" % (db % 4)
                        for dd in range(8):
                            d_ = db * 8 + dd
                            P.op("pe", lambda e, d_=d_, dd=dd, bank=bank: e.matmul(bank[:, dd * 64:(dd + 1) * 64], lhsT=T[:, :, d_], rhs=mts[:, d_, 0, :], start=True, stop=False),
                                 reads=["Tt", "mts"], writes=[bkey])
                            P.op("pe", lambda e, d_=d_, dd=dd, bank=bank: e.matmul(bank[:, dd * 64:(dd + 1) * 64], lhsT=T[:, :, 128 + d_], rhs=mts[:, d_, 1, :], start=False, stop=True),
                                 reads=["Tt", "mts"], writes=[bkey])
                        ov = ufs[:, :, db * 8:(db + 1) * 8]
                        iv = bank[:].rearrange("p (dd c) -> p c dd", dd=8)
                        if db % 2 == 0:
                            P.op("act", lambda e, ov=ov, iv=iv: e.activation(out=ov, in_=iv, func=AF.Copy, scale=1.0 / 2048.0), reads=[bkey], writes=["ufs"])
                        else:
                            P.op("dve", lambda e, ov=ov, iv=iv: e.tensor_scalar(out=ov, in0=iv, scalar1=1.0 / 2048.0, scalar2=None, op0=ALU.mult), reads=[bkey], writes=["ufs"])
                    P.op("sp", lambda e, s=s: e.dma_start(out=UFd[s * 128:(s + 1) * 128, :], in_=ufs[:].rearrange("p c d -> p (c d)")), reads=["ufs"], writes=["UFd"], dma=True)
                P.flush()

        if "p4" in phases or "p4a" in phases:
            with contextlib.ExitStack() as es:
                sb = lambda n, s, d: es.enter_context(nc.sbuf_tensor(n, s, d))
                ps = lambda n, s, d: es.enter_context(nc.psum_tensor(n, s, d))
                wck = sb("wck", [128, 8, D], BF16)
                wcv = sb("wcv", [128, 8, D], BF16)
                nmem = sb("nmem_s", [128, D], F32)
                ms = [sb("ms%d" % i, [128, D], F32) for i in range(2)]
                st = sb("st0", [128, 8], F32)
                mn = sb("mn", [128, D], BF16)
                mnT = sb("mnT", [128, 8, NMEM], BF16)
                pT = ps("pT0", [128, D], BF16)
                pm = [ps("pm0_%d" % i, [128, 512], F32) for i in range(4)]
                for wsb, wd, nm_ in ((wck, w_ck, "wck"), (wcv, w_cv, "wcv")):
                    for kq in range(4):
                        P.op("pool", lambda e, wsb=wsb, wd=wd, kq=kq: e.dma_start(out=wsb[:, 2 * kq:2 * kq + 2, :], in_=wd[kq * 256:(kq + 1) * 256, :].rearrange("(k p) n -> p k n", p=128)), writes=[nm_], dma=True)
                P.op("sp", lambda e: e.dma_start(out=nmem[:], in_=nmem_d[:, :]), writes=["nmem"], dma=True)
                for t in range(2):
                    P.op("sp", lambda e, t=t: e.dma_start(out=ms[t][:], in_=mem[t * 128:(t + 1) * 128, :]), writes=["ms%d" % t], dma=True)
                for t in range(2):
                    rstd_ops(st, "st0", ms[t][:], "ms%d" % t, mn[:], "mn")
                    P.op("dve", lambda e, t=t: e.scalar_tensor_tensor(out=mn[:], in0=ms[t][:], scalar=st[:, 2:3], in1=nmem[:], op0=ALU.mult, op1=ALU.mult), reads=["ms%d" % t, "st0", "nmem"], writes=["mn"])
                    transposes8(mn, "mn", pT, "pT0")
                    P.op("dve", lambda e, t=t: e.tensor_copy(out=mnT[:, :, t * 128:(t + 1) * 128], in_=pT[:].rearrange("p (k c) -> p k c", k=8)), reads=["pT0"], writes=["mnT"])
                bi0 = 0
                for m in range(8):
                    bank, bkey = pm[bi0 % 4], "pm0_%d" % (bi0 % 4)
                    bi0 += 1
                    for k in range(8):
                        P.op("pe", lambda e, m=m, k=k, bank=bank: e.matmul(bank[:, 0:NMEM], lhsT=wck[:, k, m * 128:(m + 1) * 128], rhs=mnT[:, k, :], start=(k == 0), stop=(k == 7)), reads=["wck", "mnT"], writes=[bkey])
                    P.op("act", lambda e, m=m, bank=bank: e.activation(out=ckT[:, m, :], in_=bank[:, 0:NMEM], func=AF.Copy), reads=[bkey], writes=["ckT"])
                for t in range(2):
                    for n in range(2):
                        bank, bkey = pm[bi0 % 4], "pm0_%d" % (bi0 % 4)
                        bi0 += 1
                        for k in range(8):
                            P.op("pe", lambda e, t=t, n=n, k=k, bank=bank: e.matmul(bank[:], lhsT=mnT[:, k, t * 128:(t + 1) * 128], rhs=wcv[:, k, n * 512:(n + 1) * 512], start=(k == 0), stop=(k == 7)), reads=["wcv", "mnT"], writes=[bkey])
                        P.op("dve", lambda e, t=t, n=n, bank=bank: e.tensor_copy(out=cv[:, t, n * 512:(n + 1) * 512], in_=bank[:]), reads=[bkey], writes=["cv"])
                P.flush()

            with contextlib.ExitStack() as es:
                sb = lambda n, s, d: es.enter_context(nc.sbuf_tensor(n, s, d))
                ps = lambda n, s, d: es.enter_context(nc.psum_tensor(n, s, d))
                wts = {}
                for nm_, wd in (("wro", w_ro), ("w4", w_4), ("wmx", w_mx), ("wcq", w_cq), ("wco", w_co)):
                    wts[nm_] = sb(nm_, [128, 8, D], BF16)
                    for kq in range(4):
                        P.op("pool", lambda e, nm_=nm_, wd=wd, kq=kq: e.dma_start(out=wts[nm_][:, 2 * kq:2 * kq + 2, :], in_=wd[kq * 256:(kq + 1) * 256, :].rearrange("(k p) n -> p k n", p=128)), writes=[nm_], dma=True)
                gnw = sb("gnw_s", [128, D], F32)
                nca = sb("nca_s", [128, D], F32)
                P.op("sp", lambda e: e.dma_start(out=gnw[:], in_=gnw_d[:, :]), writes=["gnw"], dma=True)
                P.op("sp", lambda e: e.dma_start(out=nca[:], in_=nca_d[:, :]), writes=["nca"], dma=True)
                yf = [sb("yf%d" % i, [128, D], F32) for i in range(2)]
                yb = [sb("yb%d" % i, [128, D], F32) for i in range(2)]
                xt = [sb("xt%d" % i, [128, D], F32) for i in range(2)]
                gt = [sb("gt%d" % i, [128, D], BF16) for i in range(2)]
                grt = [sb("grt%d" % i, [128, D], BF16) for i in range(2)]
                gft = [sb("gft%d" % i, [128, D], BF16) for i in range(2)]
                uft = [sb("uft%d" % i, [128, 8, 128], BF16) for i in range(2)]
                SETS = []
                for i in range(2):
                    SETS.append(dict(
                        A=sb("A4_%d" % i, [128, D], F32), B=sb("B4_%d" % i, [128, D], F32), C=sb("C4_%d" % i, [128, D], F32),
                        x1=sb("x1_%d" % i, [128, D], F32), x2=sb("x2_%d" % i, [128, D], F32),
                        b=[sb("b4_%d_%d" % (i, j), [128, D], BF16) for j in range(4)],
                        bs=sb("bs4_%d" % i, [128, 4, 6], F32), mv=sb("mv4_%d" % i, [128, 4, 2], F32),
                        sm=sb("sm4_%d" % i, [128, 16], F32), st=sb("st4_%d" % i, [128, 8], F32),
                        T=ps("pT4_%d" % i, [128, D], BF16), X=[ps("pX%d_%d" % (j, i), [128, 512], F32) for j in range(2)],
                        Y=ps("pY_%d" % i, [128, 512], F32)))
                UFv = UFd.rearrange("(s p) t -> p s t", p=128)

                def p4a_load(c):
                    sl = c % 2
                    r0 = c * 128
                    for tl, dd, nm_ in ((yf, YFd, "yf"), (yb, YBd, "yb"), (gt, Gd, "gt")):
                        P.op("sp", lambda e, tl=tl, dd=dd: e.dma_start(out=tl[sl][:], in_=dd[r0:r0 + 128, :]), writes=["%s%d" % (nm_, sl)], dma=True)
                    P.op("sp", lambda e: e.dma_start(out=uft[sl][:], in_=UFv[:, :, r0:r0 + 128]), writes=["uft%d" % sl], dma=True)
                    for tl, dd, nm_ in ((grt, GRd, "grt"), (gft, GFd, "gft"), (xt, xm, "xt")):
                        P.op("sp", lambda e, tl=tl, dd=dd: e.dma_start(out=tl[sl][:], in_=dd[r0:r0 + 128, :]), writes=["%s%d" % (nm_, sl)], dma=True)

                def p4a_stages(c):
                    i = c % 2
                    sl = i
                    S_ = SETS[i]
                    A, B, Cc, x1, x2, bb, bs, mv, sm, st, T_, X, Y = (S_[k_] for k_ in ("A", "B", "C", "x1", "x2", "b", "bs", "mv", "sm", "st", "T", "X", "Y"))
                    s_ = str(i)
                    kA, kB, kC, kx1, kx2, kbs, kmv, ksm, kst, kT, kY = ("A4_" + s_, "B4_" + s_, "C4_" + s_, "x1_" + s_, "x2_" + s_, "bs4_" + s_, "mv4_" + s_, "sm4_" + s_, "st4_" + s_, "pT4_" + s_, "pY_" + s_)
                    kX = ["pX0_" + s_, "pX1_" + s_]
                    kb = ["b4_%s_%d" % (s_, j) for j in range(4)]
                    r0 = c * 128

                    def mm16(lhs_fn, lkey, w, wkey):
                        for n in range(2):
                            for k in range(8):
                                P.op("pe", lambda e, n=n, k=k: e.matmul(X[n][:], lhsT=lhs_fn(k), rhs=w[:, k, n * 512:(n + 1) * 512], start=(k == 0), stop=(k == 7)),
                                     reads=[lkey, wkey], writes=[kX[n]])

                    def tr(src, skey, dst, dkey, eng):
                        for k in range(8):
                            P.op("pe", lambda e, k=k: e.transpose(out=T_[:, k * 128:(k + 1) * 128], in_=src[:, k * 128:(k + 1) * 128], identity=ident[:]), reads=[skey, "ident"], writes=[kT])
                        if eng == "act":
                            P.op("act", lambda e: e.activation(out=dst[:], in_=T_[:], func=AF.Copy), reads=[kT], writes=[dkey])
                        else:
                            P.op("dve", lambda e: e.tensor_copy(out=dst[:], in_=T_[:]), reads=[kT], writes=[dkey])

                    def dve(fn, r, w):
                        P.op("dve", fn, reads=r, writes=w)

                    def act(fn, r, w):
                        P.op("act", fn, reads=r, writes=w)

                    def i_gn1a():
                        dve(lambda e: e.tensor_tensor(out=A[:], in0=yf[sl][:], in1=yb[sl][:], op=ALU.add), ["yf" + s_, "yb" + s_], [kA])
                        for hd in range(4):
                            dve(lambda e, hd=hd: e.bn_stats(out=bs[:, hd, :], in_=A[:, hd * 256:(hd + 1) * 256]), [kA], [kbs])
                            dve(lambda e, hd=hd: e.bn_aggr(out=mv[:, hd, :], in_=bs[:, hd, :]), [kbs], [kmv])
                        dve(lambda e: e.tensor_scalar(out=sm[:, 0:4], in0=mv[:, :, 1], scalar1=eps5, scalar2=None, op0=ALU.add), [kmv, "dk"], [ksm])
                        act(lambda e: e.activation(out=Cc[:], in_=gt[sl][:], func=AF.Sigmoid), ["gt" + s_], [kC])

                    def i_gn1b():
                        act(lambda e: e.activation(out=sm[:, 0:4], in_=sm[:, 0:4], func=AF.Ln), [ksm], [ksm])
                        act(lambda e: e.activation(out=sm[:, 4:8], in_=sm[:, 0:4], func=AF.Exp, scale=-0.5), [ksm], [ksm])

                    def i_gn2():
                        for hd in range(4):
                            dve(lambda e, hd=hd: e.tensor_scalar(out=B[:, hd * 256:(hd + 1) * 256], in0=A[:, hd * 256:(hd + 1) * 256], scalar1=mv[:, hd, 0:1], scalar2=sm[:, 4 + hd:5 + hd], op0=ALU.subtract, op1=ALU.mult),
                                [kA, kmv, ksm], [kB])
                        dve(lambda e: e.tensor_tensor(out=Cc[:], in0=Cc[:], in1=gt[sl][:], op=ALU.mult), [kC, "gt" + s_], [kC])
                        dve(lambda e: e.tensor_tensor(out=B[:], in0=B[:], in1=gnw[:], op=ALU.mult), [kB, "gnw"], [kB])
                        dve(lambda e: e.tensor_tensor(out=bb[0][:], in0=B[:], in1=Cc[:], op=ALU.mult), [kB, kC], [kb[0]])

                    def trp(src, skey):
                        for k in range(8):
                            P.op("pe", lambda e, k=k: e.transpose(out=T_[:, k * 128:(k + 1) * 128], in_=src[:, k * 128:(k + 1) * 128], identity=ident[:]), reads=[skey, "ident"], writes=[kT])

                    def i_tr_r():
                        trp(bb[0], kb[0])
                        act(lambda e: e.activation(out=A[:], in_=grt[sl][:], func=AF.Sigmoid), ["grt" + s_], [kA])
                        act(lambda e: e.activation(out=B[:], in_=gft[sl][:], func=AF.Sigmoid), ["gft" + s_], [kB])

                    def i_tr_r_ev():
                        act(lambda e: e.activation(out=bb[1][:], in_=T_[:], func=AF.Copy), [kT], [kb[1]])

                    def i_ret():
                        mm16(lambda k: bb[1][:, k * 128:(k + 1) * 128], kb[1], wts["wro"], "wro")
                        for k in range(8):
                            P.op("pe", lambda e, k=k: e.matmul(Y[:], lhsT=uft[sl][:, k, :], rhs=wts["w4"][:, k, 0:512], start=(k == 0), stop=(k == 7)), reads=["uft" + s_, "w4"], writes=[kY])

                    def i_m_a():
                        dve(lambda e: e.tensor_tensor(out=A[:, 0:512], in0=X[0][:], in1=A[:, 0:512], op=ALU.mult), [kX[0], kA], [kA])
                        dve(lambda e: e.tensor_tensor(out=B[:, 0:512], in0=Y[:], in1=B[:, 0:512], op=ALU.mult), [kY, kB], [kB])
                        dve(lambda e: e.tensor_tensor(out=A[:, 512:1024], in0=X[1][:], in1=A[:, 512:1024], op=ALU.mult), [kX[1], kA], [kA])

                    def i_four1():
                        for k in range(8):
                            P.op("pe", lambda e, k=k: e.matmul(X[0][:], lhsT=uft[sl][:, k, :], rhs=wts["w4"][:, k, 512:1024], start=(k == 0), stop=(k == 7)), reads=["uft" + s_, "w4"], writes=[kX[0]])

                    def i_m_b():
                        dve(lambda e: e.tensor_tensor(out=B[:, 512:1024], in0=X[0][:], in1=B[:, 512:1024], op=ALU.mult), [kX[0], kB], [kB])
                        dve(lambda e: e.tensor_tensor(out=bb[2][:], in0=A[:], in1=B[:], op=ALU.add), [kA, kB], [kb[2]])

                    def i_tr_m():
                        trp(bb[2], kb[2])

                    def i_tr_m_ev():
                        act(lambda e: e.activation(out=bb[3][:], in_=T_[:], func=AF.Copy), [kT], [kb[3]])

                    def i_mix():
                        mm16(lambda k: bb[3][:, k * 128:(k + 1) * 128], kb[3], wts["wmx"], "wmx")

                    def i_x1():
                        for n in range(2):
                            cs_ = slice(n * 512, (n + 1) * 512)
                            dve(lambda e, n=n, cs_=cs_: e.tensor_tensor(out=x1[:, cs_], in0=X[n][:], in1=xt[sl][:, cs_], op=ALU.add), [kX[n], "xt" + s_], [kx1])
                        dve(lambda e: e.memset(st[:], 0.0), [], [kst])

                    def i_n_a():
                        act(lambda e: e.activation(out=Cc[:], in_=x1[:], func=AF.Square, accum_out=st[:, 0:1]), [kx1, kst], [kC, kst])

                    def i_n_b():
                        dve(lambda e: e.tensor_scalar(out=st[:, 1:2], in0=st[:, 0:1], scalar1=1.0 / D, scalar2=eps6, op0=ALU.mult, op1=ALU.add), [kst, "dk"], [kst])

                    def i_n_c():
                        act(lambda e: e.activation(out=st[:, 3:4], in_=st[:, 1:2], func=AF.Ln), [kst], [kst])
                        act(lambda e: e.activation(out=st[:, 2:3], in_=st[:, 3:4], func=AF.Exp, scale=-0.5), [kst], [kst])

                    def i_n_d():
                        dve(lambda e: e.scalar_tensor_tensor(out=bb[0][:], in0=x1[:], scalar=st[:, 2:3], in1=nca[:], op0=ALU.mult, op1=ALU.mult), [kx1, kst, "nca"], [kb[0]])

                    def i_tr_x():
                        trp(bb[0], kb[0])

                    def i_tr_x_ev():
                        dve(lambda e: e.tensor_copy(out=bb[1][:], in_=T_[:]), [kT], [kb[1]])

                    def i_hq():
                        for m in range(8):
                            bank, bkey = X[m // 4], kX[m // 4]
                            co = (m % 4) * 128
                            for k in range(8):
                                P.op("pe", lambda e, m=m, k=k, bank=bank, co=co: e.matmul(bank[:, co:co + 128], lhsT=wts["wcq"][:, k, m * 128:(m + 1) * 128], rhs=bb[1][:, k * 128:(k + 1) * 128], start=(k == 0), stop=(k == 7)),
                                     reads=["wcq", kb[1]], writes=[bkey])

                    def i_hq_ev():
                        act(lambda e: e.activation(out=bb[2][:, 0:512], in_=X[0][:], func=AF.Copy), [kX[0]], [kb[2]])
                        dve(lambda e: e.tensor_copy(out=bb[2][:, 512:1024], in_=X[1][:]), [kX[1]], [kb[2]])

                    def i_logits():
                        for hd in range(4):
                            bank, bkey = X[hd // 2], kX[hd // 2]
                            co = (hd % 2) * 256
                            for hf in range(2):
                                m = 2 * hd + hf
                                P.op("pe", lambda e, m=m, hf=hf, bank=bank, co=co: e.matmul(bank[:, co:co + 256], lhsT=bb[2][:, m * 128:(m + 1) * 128], rhs=ckT[:, m, :], start=(hf == 0), stop=(hf == 1)),
                                     reads=[kb[2], "ckT"], writes=[bkey])

                    def i_sm_a():
                        dve(lambda e: e.memset(sm[:, 8:16], 0.0), [], [ksm])
                        for n in range(2):
                            dve(lambda e, n=n: e.tensor_reduce(out=sm[:, 2 * n:2 * n + 2], in_=X[n][:].rearrange("p (a b) -> p a b", a=2), axis=AX.X, op=ALU.max), [kX[n], ksm], [ksm])
                        dve(lambda e: e.tensor_scalar(out=sm[:, 4:8], in0=sm[:, 0:4], scalar1=-1.0 / 16.0, scalar2=None, op0=ALU.mult), [ksm], [ksm])

                    def i_sm_b():
                        for hd in range(4):
                            bank, bkey = X[hd // 2], kX[hd // 2]
                            co = (hd % 2) * 256
                            act(lambda e, hd=hd, bank=bank, co=co: e.activation(out=Cc[:, hd * 256:(hd + 1) * 256], in_=bank[:, co:co + 256], func=AF.Exp, scale=1.0 / 16.0, bias=sm[:, 4 + hd:5 + hd], accum_out=sm[:, 8 + hd:9 + hd]),
                                [bkey, ksm], [kC, ksm])

                    def i_sm_c():
                        dve(lambda e: e.reciprocal(out=sm[:, 12:16], in_=sm[:, 8:12]), [ksm], [ksm])
                        for hd in range(4):
                            dve(lambda e, hd=hd: e.tensor_scalar(out=bb[3][:, hd * 256:(hd + 1) * 256], in0=Cc[:, hd * 256:(hd + 1) * 256], scalar1=sm[:, 12 + hd:13 + hd], scalar2=None, op0=ALU.mult),
                                [kC, ksm], [kb[3]])

                    def i_tr_p():
                        trp(bb[3], kb[3])

                    def i_tr_p_ev():
                        act(lambda e: e.activation(out=bb[0][:], in_=T_[:], func=AF.Copy), [kT], [kb[0]])

                    def i_att():
                        for m in range(8):
                            hd = m // 2
                            bank, bkey = X[m // 4], kX[m // 4]
                            co = (m % 4) * 128
                            for mc in range(2):
                                P.op("pe", lambda e, m=m, mc=mc, hd=hd, bank=bank, co=co: e.matmul(bank[:, co:co + 128], lhsT=cv[:, mc, m * 128:(m + 1) * 128], rhs=bb[0][:, (2 * hd + mc) * 128:(2 * hd + mc + 1) * 128], start=(mc == 0), stop=(mc == 1)),
                                     reads=["cv", kb[0]], writes=[bkey])

                    def i_att_ev():
                        act(lambda e: e.activation(out=bb[1][:, 0:512], in_=X[0][:], func=AF.Copy), [kX[0]], [kb[1]])
                        dve(lambda e: e.tensor_copy(out=bb[1][:, 512:1024], in_=X[1][:]), [kX[1]], [kb[1]])

                    def i_co():
                        mm16(lambda k: bb[1][:, k * 128:(k + 1) * 128], kb[1], wts["wco"], "wco")

                    def i_x2():
                        for n in range(2):
                            cs_ = slice(n * 512, (n + 1) * 512)
                            dve(lambda e, n=n, cs_=cs_: e.tensor_tensor(out=x2[:, cs_], in0=X[n][:], in1=x1[:, cs_], op=ALU.add), [kX[n], kx1], [kx2])
                        P.op("sp", lambda e: e.dma_start(out=X2d[r0:r0 + 128, :], in_=x2[:]), reads=[kx2], writes=["X2d"], dma=True)

                    def i_nop():
                        pass

                    return [i_gn1a, i_gn1b, i_gn2, i_tr_r, i_tr_r_ev, i_ret, i_m_a, i_four1, i_m_b, i_tr_m, i_tr_m_ev, i_mix, i_x1, i_n_a, i_n_b, i_n_c,
                            i_n_d, i_tr_x, i_tr_x_ev, i_hq, i_hq_ev, i_logits, i_sm_a, i_sm_b, i_sm_c, i_tr_p, i_tr_p_ev, i_att, i_att_ev, i_co, i_x2, i_nop]

                NST = 32
                SK = NST // 2
                p4a_load(0)
                active = []
                nxt = 0
                tick = 0
                while nxt < NM or active:
                    admit = (tick % NST == 0) or (tick % NST == SK)
                    if nxt < NM and admit:
                        if nxt + 1 < NM:
                            p4a_load(nxt + 1)
                        active.append([p4a_stages(nxt), 0])
                        nxt += 1
                    for a_ in active:
                        a_[0][a_[1]]()
                        a_[1] += 1
                    active = [a_ for a_ in active if a_[1] < NST]
                    tick += 1
                P.flush()

        if "p4" in phases or "p4b" in phases:
            with contextlib.ExitStack() as es:
                sb = lambda n, s, d: es.enter_context(nc.sbuf_tensor(n, s, d))
                ps = lambda n, s, d: es.enter_context(nc.psum_tensor(n, s, d))
                wup = sb("wup", [128, 8, DFF], BF16)
                wdn = sb("wdn", [128, 32, D], BF16)
                for k in range(8):
                    P.op("pool", lambda e, k=k: e.dma_start(out=wup[:, k, :], in_=w_up[k * 128:(k + 1) * 128, :]), writes=["wup"], dma=True)
                for kq in range(8):
                    P.op("pool", lambda e, kq=kq: e.dma_start(out=wdn[:, 4 * kq:4 * kq + 4, :], in_=w_dn[kq * 512:(kq + 1) * 512, :].rearrange("(k p) n -> p k n", p=128)), writes=["wdn"], dma=True)
                nmlp = sb("nmlp_s", [128, D], F32)
                nfin = sb("nfin_s", [128, D], F32)
                P.op("sp", lambda e: e.dma_start(out=nmlp[:], in_=nmlp_d[:, :]), writes=["nmlp"], dma=True)
                P.op("sp", lambda e: e.dma_start(out=nfin[:], in_=nfin_d[:, :]), writes=["nfin"], dma=True)
                xg = [sb("xg%d" % i, [128, 2, D], F32) for i in range(2)]
                xn = [sb("xn5_%d" % i, [128, D], BF16) for i in range(2)]
                xnT = [sb("xnT5_%d" % i, [128, 8, 256], BF16) for i in range(2)]
                hT = sb("hT", [128, 32, 256], BF16)
                sq = [sb("sq%d" % i, [128, 256], F32) for i in range(2)]
                ot = [sb("ot%d" % i, [128, D], F32) for i in range(2)]
                st = [sb("st5_%d" % i, [128, 8], F32) for i in range(3)]
                pT = ps("pT5", [128, D], BF16)
                pu = [ps("pu%d" % i, [128, 512], F32) for i in range(5)]
                pd = [ps("pd%d" % i, [128, 512], F32) for i in range(2)]

                def p4b_load(gi):
                    sl = gi % 2
                    P.op("sp", lambda e: e.dma_start(out=xg[sl][:], in_=X2d[gi * 256:(gi + 1) * 256, :].rearrange("(t p) d -> p t d", p=128)), writes=["xg%d" % sl], dma=True)

                oc = [0]

                def p4b_norm_stages(gi):
                    sl = gi % 2
                    gk = "xg%d" % sl
                    out = []
                    for t in range(2):
                        stt_, kst_, xnt_, kxn_ = st[t], "st5_%d" % t, xn[t], "xn5_%d" % t

                        def n0(t=t, stt_=stt_, kst_=kst_, xnt_=xnt_, kxn_=kxn_):
                            P.op("dve", lambda e: e.memset(stt_[:], 0.0), writes=[kst_])
                            P.op("act", lambda e: e.activation(out=xnt_[:], in_=xg[sl][:, t, :], func=AF.Square, accum_out=stt_[:, 0:1]), reads=[gk, kst_], writes=[kxn_, kst_])

                        def n1(stt_=stt_, kst_=kst_):
                            P.op("act", lambda e: e.activation(out=stt_[:, 1:2], in_=stt_[:, 0:1], func=AF.Sqrt, scale=1.0 / D, bias=eps6), reads=[kst_, "dk"], writes=[kst_])

                        def n2(stt_=stt_, kst_=kst_):
                            P.op("dve", lambda e: e.reciprocal(out=stt_[:, 2:3], in_=stt_[:, 1:2]), reads=[kst_], writes=[kst_])

                        def n3(t=t, stt_=stt_, kst_=kst_, xnt_=xnt_, kxn_=kxn_):
                            P.op("dve", lambda e: e.scalar_tensor_tensor(out=xnt_[:], in0=xg[sl][:, t, :], scalar=stt_[:, 2:3], in1=nmlp[:], op0=ALU.mult, op1=ALU.mult), reads=[gk, kst_, "nmlp"], writes=[kxn_])
                        out += [n0, n1, n2, n3]
                    return out

                def p4b_tr(gi):
                    sl = gi % 2
                    for t in range(2):
                        transposes8(xn[t], "xn5_%d" % t, pT, "pT5")
                        P.op("act", lambda e, t=t: e.activation(out=xnT[sl][:, :, t * 128:(t + 1) * 128], in_=pT[:].rearrange("p (k c) -> p k c", k=8), func=AF.Copy), reads=["pT5"], writes=["xnT5_%d" % sl])

                def p4b_up(gi, f0, f1, hooks=None):
                    sl = gi % 2
                    for f in range(f0, f1):
                        if hooks and f in hooks:
                            hooks[f]()
                        bank, bkey = pu[f % 5], "pu%d" % (f % 5)
                        for k in range(8):
                            P.op("pe", lambda e, f=f, k=k, bank=bank: e.matmul(bank[:, 0:256], lhsT=wup[:, k, f * 128:(f + 1) * 128], rhs=xnT[sl][:, k, :], start=(k == 0), stop=(k == 7)), reads=["wup", "xnT5_%d" % sl], writes=[bkey])
                        sqt, sqk = sq[f % 2], "sq%d" % (f % 2)
                        if f % 2 == 0:
                            P.op("act", lambda e, bank=bank, sqt=sqt: e.activation(out=sqt[:], in_=bank[:, 0:256], func=AF.Relu), reads=[bkey], writes=[sqk])
                            P.op("act", lambda e, f=f, sqt=sqt: e.activation(out=hT[:, f, :], in_=sqt[:], func=AF.Square), reads=[sqk], writes=["hT"])
                        else:
                            P.op("dve", lambda e, bank=bank, sqt=sqt: e.tensor_scalar(out=sqt[:], in0=bank[:, 0:256], scalar1=0.0, scalar2=None, op0=ALU.max), reads=[bkey], writes=[sqk])
                            P.op("dve", lambda e, f=f, sqt=sqt: e.tensor_tensor(out=hT[:, f, :], in0=sqt[:], in1=sqt[:], op=ALU.mult), reads=[sqk], writes=["hT"])

                def p4b_down(gi):
                    sl = gi % 2
                    gk = "xg%d" % sl
                    epi = []
                    for t in range(2):
                        for n in range(2):
                            for f in range(32):
                                P.op("pe", lambda e, t=t, n=n, f=f: e.matmul(pd[n][:], lhsT=hT[:, f, t * 128:(t + 1) * 128], rhs=wdn[:, f, n * 512:(n + 1) * 512], start=(f == 0), stop=(f == 31)), reads=["hT", "wdn"], writes=["pd%d" % n])
                            cs_ = slice(n * 512, (n + 1) * 512)
                            P.op("dve", lambda e, t=t, n=n, cs_=cs_: e.tensor_tensor(out=xg[sl][:, t, cs_], in0=pd[n][:], in1=xg[sl][:, t, cs_], op=ALU.add), reads=["pd%d" % n, gk], writes=[gk])

                        def fin(t=t):
                            osl = oc[0] % 2
                            oc[0] += 1
                            ok = "ot%d" % osl
                            rstd_ops(st[2], "st5_2", xg[sl][:, t, :], gk, ot[osl][:], ok)
                            P.op("dve", lambda e: e.scalar_tensor_tensor(out=ot[osl][:], in0=xg[sl][:, t, :], scalar=st[2][:, 2:3], in1=nfin[:], op0=ALU.mult, op1=ALU.mult), reads=[gk, "st5_2", "nfin"], writes=[ok])
                            r0 = gi * 256 + t * 128
                            P.op("sp", lambda e: e.dma_start(out=y_out[r0:r0 + 128, :], in_=ot[osl][:]), reads=[ok], writes=["y_out"], dma=True)
                        fin()
                    return epi

                NG = NM // 2
                p4b_load(0)
                for f_ in p4b_norm_stages(0):
                    f_()
                p4b_tr(0)
                for gi in range(NG):
                    if gi + 1 < NG:
                        p4b_load(gi + 1)
                        ns_ = p4b_norm_stages(gi + 1)
                        hk = {9: ns_[0], 11: ns_[1], 13: ns_[2], 15: ns_[3], 18: ns_[4], 20: ns_[5], 22: ns_[6], 24: ns_[7]}
                    else:
                        hk = None
                    p4b_up(gi, 0, 32, hk)
                    if gi + 1 < NG:
                        p4b_tr(gi + 1)
                    p4b_down(gi)
                P.flush()
        P.flush(final=True)
    return nc, P


def _other_chunks(h):
    return np.arange(0, 64) if h == 1 else np.arange(127, 63, -1)


def _consts(h):
    bf = ml_dtypes.bfloat16
    c = {}
    c["ident"] = np.eye(128, dtype=np.float32).astype(bf)
    inv = (np.float32(10000.0) ** (-(np.arange(0, 256, 2, dtype=np.float32)) / np.float32(256))).astype(np.float32)

    def rot(pos):
        ang = (pos.astype(np.float32)[:, None] * inv[None, :]).astype(np.float32).astype(np.float64)
        return np.concatenate([np.cos(ang), np.sin(ang)], axis=1).astype(np.float32)
    pos_m = h * 8192 + np.arange(8192)
    oc = _other_chunks(h)
    pos_o = (oc[:, None] * 128 + np.arange(128)[None, :]).reshape(-1)
    c["rotm"] = rot(pos_m)
    c["roto"] = rot(pos_o)
    dk = np.zeros((128, DKW), np.float32)
    j = np.arange(128)[:, None].astype(np.float64)
    i = np.arange(128)[None, :].astype(np.float64)
    dk[:, O_E0F:O_E0F + 128] = np.maximum(i - j, 0)
    dk[:, O_MF:O_MF + 128] = (i >= j)
    dk[:, O_E0B:O_E0B + 128] = np.maximum(j - i, 0)
    dk[:, O_MB:O_MB + 128] = (j > i)
    dk[:, O_XF:O_XF + 128] = i + 1
    dk[:, O_XB:O_XB + 128] = 128 - i
    zf = 127 - j
    zb = j
    dk[:, O_ZF:O_ZF + 256] = zf
    dk[:, O_ZB:O_ZB + 256] = zb
    dk[:, O_ZO:O_ZO + 256] = zf if h == 1 else zb
    dk[:, O_MSKF] = 1.0 if h == 1 else 0.0
    dk[:, O_MSKB] = 1.0 if h == 0 else 0.0
    dk[:, O_EPS6] = 1e-6
    dk[:, O_EPS5] = 1e-5
    dk[:, O_ONE] = 1.0
    dk[:, O_NH:O_NH + 8] = -0.5
    c["dk"] = dk
    a = np.concatenate([64 * h + np.arange(64), oc]).astype(np.float64)
    d = np.arange(128, dtype=np.float64)
    th = 2 * np.pi * ((a[:, None] * d[None, :]) % 128) / 128.0
    fcs = np.zeros((128, 2, 256), np.float64)
    fcs[:, 0, :128] = np.cos(th)
    fcs[:, 0, 128:] = -np.sin(th)
    fcs[:, 1, :128] = np.sin(th)
    fcs[:, 1, 128:] = np.cos(th)
    c["fcs"] = fcs.astype(np.float32).astype(bf)
    b = np.arange(128, dtype=np.int64)[:, None, None]
    dd = np.arange(128, dtype=np.int64)[None, :, None]
    cg = (64 * h + np.arange(64, dtype=np.int64))[None, None, :]
    num = (cg * b * 128 + dd * b) % 16384
    th2 = 2 * np.pi * num.astype(np.float64) / 16384.0
    mt = np.zeros((128, 128, 2, 64), np.float64)
    mt[:, :, 0, :] = np.cos(th2)
    mt[:, :, 1, :] = np.sin(th2)
    c["mt"] = mt.astype(np.float32).astype(bf)
    ch = (np.arange(2)[None, :, None] * 128 + np.arange(128)[:, None, None]).astype(np.int64)
    jj = np.arange(256, dtype=np.int64)[None, None, :]
    th3 = 2 * np.pi * ((ch * jj) % 256).astype(np.float64) / 256.0
    cs = np.concatenate([np.cos(th3), -np.sin(th3)], axis=2)
    c["cs"] = cs.astype(np.float32).astype(bf)
    return c


_CACHE = {}


def _rep(v):
    return np.ascontiguousarray(np.broadcast_to(np.asarray(v, np.float32).reshape(1, -1), (128, v.size)))


def make_in_maps(inp):
    seqs = [(inp["x_prompt"][0], inp["mem_prompt"][0]), (inp["x_prompt"][1], inp["mem_prompt"][1]), (inp["x_sample"][0], inp["mem_sample"][0])]
    shared = {
        "w_in": np.ascontiguousarray(inp["w_in"][0]), "w_ro": np.ascontiguousarray(inp["w_ret_out"][0]),
        "w_4": np.ascontiguousarray(inp["w_four_out"][0]), "w_mx": np.ascontiguousarray(inp["w_mix_out"][0]),
        "w_cq": np.ascontiguousarray(inp["w_cq"][0]), "w_ck": np.ascontiguousarray(inp["w_ck"][0]),
        "w_cv": np.ascontiguousarray(inp["w_cv"][0]), "w_co": np.ascontiguousarray(inp["w_co"][0]),
        "w_up": np.ascontiguousarray(inp["w_up"][0]), "w_dn": np.ascontiguousarray(inp["w_down"][0]),
        "nmix": _rep(inp["norm_mix_w"][0]), "nca": _rep(inp["norm_ca_w"][0]), "nmem": _rep(inp["norm_mem_w"][0]),
        "nmlp": _rep(inp["norm_mlp_w"][0]), "nfin": _rep(inp["norm_final_w"]), "gnw": _rep(inp["ret_gn_w"][0]),
    }
    consts = [_consts(0), _consts(1)]
    in_maps = []
    for core in range(8):
        si = min(core // 2, 2)
        h = core % 2
        x, mem = seqs[si]
        x = np.asarray(x, np.float32)
        xm = np.ascontiguousarray(x[h * 8192:(h + 1) * 8192])
        oc = _other_chunks(h)
        xo = np.ascontiguousarray(x.reshape(128, 128, D)[oc].reshape(8192, D))
        df = np.asarray(inp["ret_decay_fwd"][0], np.float32)
        db = np.asarray(inp["ret_decay_bwd"][0], np.float32)
        do = df if h == 1 else db
        m = dict(shared)
        m.update(consts[h])
        m.update({"xm": xm, "xo": xo, "mem": np.ascontiguousarray(np.asarray(mem, np.float32)),
                  "dec": _rep(np.concatenate([df, db, do]))})
        in_maps.append(m)
    return in_maps


def kernel(**inputs):
    inp = {k: np.asarray(v) for k, v in inputs.items()}
    if "nc" not in _CACHE:
        _CACHE["nc"] = build()[0]
    nc = _CACHE["nc"]
    in_maps = make_in_maps(inp)
    res = run_bass_kernel_spmd(nc, in_maps, core_ids=list(range(8)))
    ys = [np.asarray(r["y"], np.float32) for r in res.results]
    y_prompt = np.stack([np.concatenate([ys[0], ys[1]], 0), np.concatenate([ys[2], ys[3]], 0)], 0)
    y_sample = np.concatenate([ys[4], ys[5]], 0)[None]
    return (y_prompt, y_sample)
```

```python
import contextlib
import numpy as np
import ml_dtypes
import concourse.bass as bass
import concourse.mybir as mybir
from concourse.bass_utils import run_bass_kernel_spmd

F32 = mybir.dt.float32
BF16 = mybir.dt.bfloat16
AF = mybir.ActivationFunctionType
ALU = mybir.AluOpType
AX = mybir.AxisListType

D = 1024
SEQ = 16384
NM = 64
NO = 64
INW = 7168
DFF = 4096
NMEM = 256


class Prog:
    NPOOL = 8

    def __init__(self, nc):
        self.nc = nc
        self.eng = {"pe": nc.tensor, "act": nc.scalar, "dve": nc.vector, "pool": nc.gpsimd, "sp": nc.sync}
        self.sems = {}
        self.cnt = {}
        self.known = {e: {} for e in self.eng}
        self.carry = {e: {} for e in self.eng}
        self.dma_rr = {e: 0 for e in self.eng}
        self.n_inst = 0
        self._reset()

    def _reset(self):
        self.ops = []
        self.lastw = {}
        self.lastr = {}
        self.last_on_sem = {}

    def getsem(self, name):
        if name not in self.sems:
            self.sems[name] = self.nc.alloc_semaphore(name=name)
        return self.sems[name]

    def op(self, eng, fn, reads=(), writes=(), dma=False, ndma=1):
        idx = len(self.ops)
        deps = set()
        for k in reads:
            deps.update(self.lastw.get(k, {}).values())
        for k in writes:
            deps.update(self.lastw.get(k, {}).values())
            deps.update(self.lastr.get(k, {}).values())
        semname = None
        if dma:
            semname = "d_%s_%d" % (eng, self.dma_rr[eng] % self.NPOOL)
            self.dma_rr[eng] += 1
            if semname in self.last_on_sem:
                deps.add(self.last_on_sem[semname])
            self.last_on_sem[semname] = idx
        self.ops.append(dict(eng=eng, fn=fn, deps=deps, dma=dma, semname=semname, ndma=ndma, waited=False, tok=None))
        ek = (eng, semname)
        for k in reads:
            self.lastr.setdefault(k, {})[ek] = idx
        for k in writes:
            self.lastw[k] = {ek: idx}
            self.lastr[k] = {}
        return idx

    def flush(self, final=False):
        ops = self.ops
        last_eng = {}
        for i, o in enumerate(ops):
            if not o["dma"]:
                last_eng[o["eng"]] = i
            for d in o["deps"]:
                od = ops[d]
                if (not od["dma"]) and od["eng"] == o["eng"] == "pe":
                    continue
                od["waited"] = True
        for i in last_eng.values():
            ops[i]["waited"] = True
        for o in ops:
            if o["dma"]:
                o["waited"] = True
        for o in ops:
            if not o["waited"]:
                continue
            if o["dma"]:
                nm = o["semname"]
                self.cnt[nm] = self.cnt.get(nm, 0) + 16 * o["ndma"]
            else:
                nm = "e_" + o["eng"]
                self.cnt[nm] = self.cnt.get(nm, 0) + 1
            o["tok"] = (nm, self.cnt[nm])
        for o in ops:
            e = o["eng"]
            engobj = self.eng[e]
            need = dict(self.carry[e])
            self.carry[e] = {}
            for d in o["deps"]:
                od = ops[d]
                if od["tok"] is None:
                    continue
                if (not od["dma"]) and od["eng"] == e == "pe":
                    continue
                nm, v = od["tok"]
                need[nm] = max(need.get(nm, 0), v)
            for nm, v in need.items():
                if self.known[e].get(nm, 0) >= v:
                    continue
                engobj.wait_ge(self.getsem(nm), v)
                self.known[e][nm] = v
                self.n_inst += 1
            r = o["fn"](engobj)
            self.n_inst += 1
            if o["tok"] is not None:
                nm, v = o["tok"]
                if o["dma"]:
                    insts = r if isinstance(r, (list, tuple)) else [r]
                    assert len(insts) == o["ndma"], (len(insts), o["ndma"])
                    for ins in insts:
                        ins.then_inc(self.getsem(nm), 16)
                else:
                    r.then_inc(self.getsem(nm), 1)
        allc = dict(self.cnt)
        for e in self.eng:
            self.carry[e] = dict(allc)
        if final:
            engobj = self.eng["sp"]
            for nm, v in allc.items():
                if self.known["sp"].get(nm, 0) < v:
                    engobj.wait_ge(self.getsem(nm), v)
                    self.known["sp"][nm] = v
        self._reset()


O_E0F, O_MF, O_E0B, O_MB, O_XF, O_XB = 0, 128, 256, 384, 512, 640
O_ZF, O_ZB, O_ZO = 768, 1024, 1280
O_MSKF, O_MSKB, O_EPS6, O_EPS5, O_ONE = 1536, 1537, 1538, 1539, 1540
O_NH = 1544
DKW = 1552


def build(dbg=False, phases=("p1", "p2", "p3", "p4")):
    nc = bass.Bass("TRN2", target_bir_lowering=False)

    def din(name, shape, dt=F32):
        return nc.dram_tensor(name, shape, dt, kind="ExternalInput").ap()

    def dscr(name, shape, dt):
        return nc.dram_tensor(name, shape, dt, kind="ExternalOutput" if dbg else "Internal").ap()

    xm = din("xm", [NM * 128, D])
    xo = din("xo", [NO * 128, D])
    mem = din("mem", [NMEM, D])
    w_in = din("w_in", [D, INW])
    w_ro = din("w_ro", [D, D])
    w_4 = din("w_4", [D, D])
    w_mx = din("w_mx", [D, D])
    w_cq = din("w_cq", [D, D])
    w_ck = din("w_ck", [D, D])
    w_cv = din("w_cv", [D, D])
    w_co = din("w_co", [D, D])
    w_up = din("w_up", [D, DFF])
    w_dn = din("w_dn", [DFF, D])
    nmix_d = din("nmix", [128, D])
    nca_d = din("nca", [128, D])
    nmem_d = din("nmem", [128, D])
    nmlp_d = din("nmlp", [128, D])
    nfin_d = din("nfin", [128, D])
    gnw_d = din("gnw", [128, D])
    dec_d = din("dec", [128, 12])
    dk_d = din("dk", [128, DKW])
    ident_d = din("ident", [128, 128], BF16)
    rotm_d = din("rotm", [NM * 128, 256])
    roto_d = din("roto", [NO * 128, 256])
    fcs_d = din("fcs", [128, 2, 256], BF16)
    mt_d = din("mt", [128, 128, 2, 64], BF16)
    cs_d = din("cs", [128, 2, 512], BF16)
    y_out = nc.dram_tensor("y", [NM * 128, D], F32, kind="ExternalOutput").ap()

    NT = (NM + NO) * 128
    Qd = dscr("Qd", [NM * 128, D], BF16)
    Kd = dscr("Kd", [NM * 128, D], BF16)
    Vd = dscr("Vd", [NM * 128, D], BF16)
    Gd = dscr("Gd", [NM * 128, D], BF16)
    GRd = dscr("GRd", [NM * 128, D], BF16)
    GFd = dscr("GFd", [NM * 128, D], BF16)
    KOd = dscr("KOd", [NO * 128, D], BF16)
    VOd = dscr("VOd", [NO * 128, D], BF16)
    Zd = dscr("Zd", [4, 2, 2, NT, 128], BF16)
    YFd = dscr("YFd", [NM * 128, D], F32)
    YBd = dscr("YBd", [NM * 128, D], F32)
    UFd = dscr("UFd", [D, NM * 128], BF16)
    X2d = dscr("X2d", [NM * 128, D], F32)

    P = Prog(nc)
    with contextlib.ExitStack() as gs:
        ident = gs.enter_context(nc.sbuf_tensor("ident_s", [128, 128], BF16))
        dk = gs.enter_context(nc.sbuf_tensor("dk_s", [128, DKW], F32))
        ckT = gs.enter_context(nc.sbuf_tensor("ckT", [128, 8, NMEM], BF16))
        cv = gs.enter_context(nc.sbuf_tensor("cv", [128, 2, D], BF16))
        P.op("sp", lambda e: e.dma_start(out=ident[:], in_=ident_d[:, :]), writes=["ident"], dma=True)
        P.op("sp", lambda e: e.dma_start(out=dk[:], in_=dk_d[:, :]), writes=["dk"], dma=True)
        eps6 = dk[:, O_EPS6:O_EPS6 + 1]
        eps5 = dk[:, O_EPS5:O_EPS5 + 1]
        one_ap = dk[:, O_ONE:O_ONE + 1]

        def rstd_ops(st, key, x_ap, xkey, junk, jkey):
            P.op("dve", lambda e: e.memset(st[:], 0.0), writes=[key])
            P.op("act", lambda e: e.activation(out=junk, in_=x_ap, func=AF.Square, accum_out=st[:, 0:1]),
                 reads=[xkey, key], writes=[jkey, key])
            P.op("act", lambda e: e.activation(out=st[:, 1:2], in_=st[:, 0:1], func=AF.Sqrt, scale=1.0 / D, bias=eps6),
                 reads=[key, "dk"], writes=[key])
            P.op("dve", lambda e: e.reciprocal(out=st[:, 2:3], in_=st[:, 1:2]), reads=[key], writes=[key])

        def rstd_pow(st, key, x_ap, xkey, junk, jkey):
            P.op("dve", lambda e: e.memset(st[:], 0.0), writes=[key])
            P.op("act", lambda e: e.activation(out=junk, in_=x_ap, func=AF.Square, accum_out=st[:, 0:1]),
                 reads=[xkey, key], writes=[jkey, key])
            P.op("dve", lambda e: e.tensor_scalar(out=st[:, 1:2], in0=st[:, 0:1], scalar1=1.0 / D, scalar2=eps6, op0=ALU.mult, op1=ALU.add), reads=[key, "dk"], writes=[key])
            P.op("act", lambda e: e.activation(out=st[:, 3:4], in_=st[:, 1:2], func=AF.Ln), reads=[key], writes=[key])
            P.op("act", lambda e: e.activation(out=st[:, 2:3], in_=st[:, 3:4], func=AF.Exp, scale=-0.5), reads=[key], writes=[key])

        def transposes8(src, skey, pT, pkey):
            for k in range(8):
                P.op("pe", lambda e, k=k: e.transpose(out=pT[:, k * 128:(k + 1) * 128], in_=src[:, k * 128:(k + 1) * 128], identity=ident[:]),
                     reads=[skey, "ident"], writes=[pkey])

        if "p1" in phases:
            with contextlib.ExitStack() as es:
                sb = lambda n, s, d: es.enter_context(nc.sbuf_tensor(n, s, d))
                ps = lambda n, s, d: es.enter_context(nc.psum_tensor(n, s, d))
                win = sb("win", [128, 8, INW], BF16)
                nmix = sb("nmix_s", [128, D], F32)
                css = sb("css", [128, 2, 512], BF16)
                xs = [sb("xs%d" % i, [128, D], F32) for i in range(2)]
                rt = [sb("rt%d" % i, [128, 256], F32) for i in range(3)]
                st_ = [sb("st1_%d" % i, [128, 8], F32) for i in range(2)]
                xn_ = [sb("xn1_%d" % i, [128, D], BF16) for i in range(2)]
                xnT_ = [sb("xnT1_%d" % i, [128, D], BF16) for i in range(2)]
                qkf = [sb("qkf%d" % i, [128, D], F32) for i in range(2)]
                tmp = [sb("tmp%d" % i, [128, 4, 128], F32) for i in range(4)]
                outs = {nm: [sb("o_%s%d" % (nm, i), [128, D], BF16) for i in range(2)] for nm in ("q", "k", "v")}
                outs.update({nm: [sb("o_%s0" % nm, [128, D], BF16)] for nm in ("g", "gr", "gf")})
                ub = sb("ub", [128, D], BF16)
                uT = sb("uT", [128, D], BF16)
                zt = [sb("zt%d" % i, [128, 4, 512], BF16) for i in range(2)]
                pT = ps("pT1", [128, D], BF16)
                pT2 = ps("pT1b", [128, D], BF16)
                pm = [ps("pm1_%d" % i, [128, 512], F32) for i in range(6)]
                for k in range(8):
                    P.op("pool", lambda e, k=k: e.dma_start(out=win[:, k, :], in_=w_in[k * 128:(k + 1) * 128, :]),
                         writes=["win%d" % k], dma=True)
                P.op("sp", lambda e: e.dma_start(out=nmix[:], in_=nmix_d[:, :]), writes=["nmix"], dma=True)
                P.op("sp", lambda e: e.dma_start(out=css[:], in_=cs_d[:, :, :]), writes=["css"], dma=True)
                bi = [0]

                def p1_load_x(ti):
                    mine = ti < NM
                    sl = ti % 2
                    src = xm if mine else xo
                    r0 = (ti if mine else ti - NM) * 128
                    P.op("sp", lambda e: e.dma_start(out=xs[sl][:], in_=src[r0:r0 + 128, :]), writes=["xs%d" % sl], dma=True)

                def p1_load_rt(ti):
                    mine = ti < NM
                    sl = ti % 3
                    rsrc = rotm_d if mine else roto_d
                    r0 = (ti if mine else ti - NM) * 128
                    P.op("sp", lambda e: e.dma_start(out=rt[sl][:], in_=rsrc[r0:r0 + 128, :]), writes=["rt%d" % sl], dma=True)

                def rotary(srcf, skey, dst, dkey, sl):
                    X = srcf[:].rearrange("p (h t f) -> p h t f", h=4, t=2)
                    O = dst[:].rearrange("p (h t f) -> p h t f", h=4, t=2)
                    cosb = rt[sl][:, 0:128].rearrange("p (o f) -> p o f", o=1).broadcast_to([128, 4, 128])
                    sinb = rt[sl][:, 128:256].rearrange("p (o f) -> p o f", o=1).broadcast_to([128, 4, 128])
                    rk = "rt%d" % sl
                    P.op("pool", lambda e: e.tensor_tensor(out=tmp[0][:], in0=X[:, :, 0, :], in1=cosb, op=ALU.mult), reads=[skey, rk], writes=["tmp0"])
                    P.op("dve", lambda e: e.tensor_tensor(out=tmp[1][:], in0=X[:, :, 1, :], in1=sinb, op=ALU.mult), reads=[skey, rk], writes=["tmp1"])
                    P.op("dve", lambda e: e.tensor_tensor(out=O[:, :, 0, :], in0=tmp[0][:], in1=tmp[1][:], op=ALU.subtract), reads=["tmp0", "tmp1"], writes=[dkey])
                    P.op("pool", lambda e: e.tensor_tensor(out=tmp[2][:], in0=X[:, :, 1, :], in1=cosb, op=ALU.mult), reads=[skey, rk], writes=["tmp2"])
                    P.op("dve", lambda e: e.tensor_tensor(out=tmp[3][:], in0=X[:, :, 0, :], in1=sinb, op=ALU.mult), reads=[skey, rk], writes=["tmp3"])
                    P.op("pool", lambda e: e.tensor_tensor(out=O[:, :, 1, :], in0=tmp[2][:], in1=tmp[3][:], op=ALU.add), reads=["tmp2", "tmp3", dkey], writes=[dkey])

                def p1_A_stages(ti):
                    sl = ti % 2
                    xk = "xs%d" % sl
                    st, xn, xnT = st_[sl], xn_[sl], xnT_[sl]
                    ks_, kn_, kt_ = "st1_%d" % sl, "xn1_%d" % sl, "xnT1_%d" % sl

                    def a0():
                        P.op("dve", lambda e: e.memset(st[:], 0.0), writes=[ks_])
                        P.op("act", lambda e: e.activation(out=xn[:], in_=xs[sl][:], func=AF.Square, accum_out=st[:, 0:1]), reads=[xk, ks_], writes=[kn_, ks_])

                    def a1():
                        P.op("act", lambda e: e.activation(out=st[:, 1:2], in_=st[:, 0:1], func=AF.Sqrt, scale=1.0 / D, bias=eps6), reads=[ks_, "dk"], writes=[ks_])

                    def a2():
                        P.op("dve", lambda e: e.reciprocal(out=st[:, 2:3], in_=st[:, 1:2]), reads=[ks_], writes=[ks_])

                    def a3():
                        P.op("dve", lambda e: e.scalar_tensor_tensor(out=xn[:], in0=xs[sl][:], scalar=st[:, 2:3], in1=nmix[:], op0=ALU.mult, op1=ALU.mult),
                             reads=[xk, ks_, "nmix"], writes=[kn_])

                    def a4():
                        transposes8(xn, kn_, pT, "pT1")

                    def a5():
                        P.op("dve", lambda e: e.tensor_copy(out=xnT[:], in_=pT[:]), reads=["pT1"], writes=[kt_])
                    return [a0, a1, a2, a3, a4, a5]

                def p1_tail_stages(ti):
                    zr0 = ti * 128
                    zsl = ti % 2

                    def t0():
                        transposes8(ub, "ub", pT2, "pT1b")
                        P.op("act", lambda e: e.activation(out=uT[:], in_=pT2[:], func=AF.Copy), reads=["pT1b"], writes=["uT"])

                    def t1():
                        for g in range(4):
                            b = bi[0] % 6
                            bi[0] += 1
                            bank, bkey = pm[b], "pm1_%d" % b
                            for kk in range(2):
                                P.op("pe", lambda e, g=g, kk=kk, bank=bank: e.matmul(bank[:], lhsT=uT[:, (2 * g + kk) * 128:(2 * g + kk + 1) * 128], rhs=css[:, kk, :], start=(kk == 0), stop=(kk == 1)),
                                     reads=["uT", "css"], writes=[bkey])
                            if g % 2 == 0:
                                P.op("act", lambda e, g=g, bank=bank: e.activation(out=zt[zsl][:, g, :], in_=bank[:], func=AF.Copy), reads=[bkey], writes=["zt%d" % zsl])
                            else:
                                P.op("dve", lambda e, g=g, bank=bank: e.tensor_copy(out=zt[zsl][:, g, :], in_=bank[:]), reads=[bkey], writes=["zt%d" % zsl])
                        for g in range(4):
                            P.op("sp", lambda e, g=g: e.dma_start(
                                out=Zd[g, :, :, zr0:zr0 + 128, :].rearrange("ri hf t c -> t (ri hf) c"),
                                in_=zt[zsl][:, g, :].rearrange("p (rh c) -> p rh c", rh=4)),
                                reads=["zt%d" % zsl], writes=["Zd"], dma=True)
                    return [t0, t1]

                def p1_compute(ti):
                    mine = ti < NM
                    sl = ti % 2
                    r0 = (ti if mine else ti - NM) * 128
                    zr0 = ti * 128
                    xnT = xnT_[sl]
                    kt_ = "xnT1_%d" % sl
                    slices = list(range(14)) if mine else [2, 3, 4, 5, 8, 9]
                    nxt_st = p1_A_stages(ti + 1) if ti + 1 < NM + NO else []
                    hook = ({1: (0,), 2: (1,), 3: (2,), 5: (3,), 8: (4,), 10: (5,)} if mine else {0: (0, 1), 1: (2,), 2: (3,), 4: (4,), 5: (5,)})
                    prev_tail = p1_tail_stages(ti - 1) if ti > 0 else []
                    thook = ({4: 0, 7: 1} if mine else {1: 0, 3: 1})
                    for si_, n in enumerate(slices):
                        if si_ in hook and nxt_st:
                            for h_ in hook[si_]:
                                nxt_st[h_]()
                        if si_ in thook and prev_tail:
                            prev_tail[thook[si_]]()
                        b = bi[0] % 6
                        bi[0] += 1
                        bank, bkey = pm[b], "pm1_%d" % b
                        for k in range(8):
                            P.op("pe", lambda e, k=k, n=n, bank=bank: e.matmul(bank[:], lhsT=xnT[:, k * 128:(k + 1) * 128], rhs=win[:, k, n * 512:(n + 1) * 512], start=(k == 0), stop=(k == 7)),
                                 reads=[kt_, "win%d" % k], writes=[bkey])
                        hf = n % 2
                        cs_ = slice(hf * 512, (hf + 1) * 512)
                        if n in (0, 1):
                            P.op("act", lambda e, bank=bank, cs_=cs_: e.activation(out=qkf[0][:, cs_], in_=bank[:], func=AF.Copy, scale=1.0 / 16.0), reads=[bkey], writes=["qkf0"])
                            if n == 1:
                                rotary(qkf[0], "qkf0", outs["q"][sl], "o_q%d" % sl, ti % 3)
                        elif n in (2, 3):
                            P.op("act", lambda e, bank=bank, cs_=cs_: e.activation(out=qkf[1][:, cs_], in_=bank[:], func=AF.Copy), reads=[bkey], writes=["qkf1"])
                            if n == 3:
                                rotary(qkf[1], "qkf1", outs["k"][sl], "o_k%d" % sl, ti % 3)
                        else:
                            nm_, eng = {4: ("v", "dve"), 5: ("v", "dve"), 6: ("g", "act"), 7: ("g", "act"), 8: ("u", "dve"), 9: ("u", "dve"),
                                        10: ("gr", "act"), 11: ("gr", "act"), 12: ("gf", "dve"), 13: ("gf", "dve")}[n]
                            if nm_ == "u":
                                dst, dkey = ub, "ub"
                            elif nm_ == "v":
                                dst, dkey = outs["v"][sl], "o_v%d" % sl
                            else:
                                dst, dkey = outs[nm_][0], "o_%s0" % nm_
                            if eng == "act":
                                P.op("act", lambda e, bank=bank, cs_=cs_, dst=dst: e.activation(out=dst[:, cs_], in_=bank[:], func=AF.Copy), reads=[bkey], writes=[dkey])
                            else:
                                P.op("dve", lambda e, bank=bank, cs_=cs_, dst=dst: e.tensor_copy(out=dst[:, cs_], in_=bank[:]), reads=[bkey], writes=[dkey])
                    if mine:
                        for nm_, dd in (("q", Qd), ("k", Kd), ("v", Vd)):
                            P.op("sp", lambda e, nm_=nm_, dd=dd: e.dma_start(out=dd[r0:r0 + 128, :], in_=outs[nm_][sl][:]), reads=["o_%s%d" % (nm_, sl)], writes=["dram_" + nm_], dma=True)
                        for nm_, dd in (("g", Gd), ("gr", GRd), ("gf", GFd)):
                            P.op("sp", lambda e, nm_=nm_, dd=dd: e.dma_start(out=dd[r0:r0 + 128, :], in_=outs[nm_][0][:]), reads=["o_%s0" % nm_], writes=["dram_" + nm_], dma=True)
                    else:
                        for nm_, dd in (("k", KOd), ("v", VOd)):
                            P.op("sp", lambda e, nm_=nm_, dd=dd: e.dma_start(out=dd[r0:r0 + 128, :], in_=outs[nm_][sl][:]), reads=["o_%s%d" % (nm_, sl)], writes=["dram_o" + nm_], dma=True)

                p1_load_x(0)
                p1_load_rt(0)
                p1_load_x(1)
                for f_ in p1_A_stages(0):
                    f_()
                for ti in range(NM + NO):
                    if ti + 2 < NM + NO:
                        p1_load_x(ti + 2)
                    if ti + 1 < NM + NO:
                        p1_load_rt(ti + 1)
                    p1_compute(ti)
                for f_ in p1_tail_stages(NM + NO - 1):
                    f_()
                P.flush()

        if "p2" in phases:
            with contextlib.ExitStack() as es:
                sb = lambda n, s, d: es.enter_context(nc.sbuf_tensor(n, s, d))
                ps = lambda n, s, d: es.enter_context(nc.psum_tensor(n, s, d))
                dec = sb("dec_s", [128, 12], F32)
                lg = sb("lg", [128, 12], F32)
                gch = sb("gch", [128, 12], F32)
                DT = [sb("DT%d" % i, [128, 4, 128], F32) for i in range(2)]
                xi = [sb("xi%d" % i, [128, 8, 128], F32) for i in range(2)]
                zeta = [sb("zeta%d" % i, [128, 4, 256], F32) for i in range(3)]
                tmpd = sb("tmpd", [128, 128], F32)
                S = [sb("S%d" % i, [128, 8, 256], F32) for i in range(3)]
                Sbf = [sb("Sbf%d" % i, [128, 8, 256], BF16) for i in range(2)]
                qs = [[sb("q2_%d%d" % (d_, i), [128, D], BF16) for i in range(2)] for d_ in range(2)]
                ks = [[sb("k2_%d%d" % (d_, i), [128, D], BF16) for i in range(2)] for d_ in range(2)]
                vs = [[sb("v2_%d%d" % (d_, i), [128, D], BF16) for i in range(2)] for d_ in range(2)]
                qT = [sb("qT%d" % i, [128, D], BF16) for i in range(2)]
                qxT = [sb("qxT%d" % i, [128, D], BF16) for i in range(2)]
                kT = [sb("kT%d" % i, [128, D], BF16) for i in range(2)]
                kz = [sb("kz%d" % i, [128, D], BF16) for i in range(2)]
                PT = [sb("PT%d" % i, [128, 512], BF16) for i in range(2)]
                ysb = [sb("ysb%d" % i, [128, D], F32) for i in range(2)]
                pTt = [ps("pTt%d" % i, [128, D], BF16) for i in range(2)]
                pS = [ps("pS%d" % i, [128, 512], F32) for i in range(2)]
                py = [ps("py%d" % i, [128, 512], F32) for i in range(2)]
                pst = [ps("pst%d" % i, [128, 512], F32) for i in range(2)]

                P.op("sp", lambda e: e.dma_start(out=dec[:], in_=dec_d[:, :]), writes=["dec"], dma=True)
                P.op("act", lambda e: e.activation(out=lg[:], in_=dec[:], func=AF.Exp, scale=-1.0), reads=["dec"], writes=["lg"])
                P.op("act", lambda e: e.activation(out=lg[:], in_=lg[:], func=AF.Ln, bias=one_ap, scale=1.0), reads=["lg", "dk"], writes=["lg"])
                P.op("dve", lambda e: e.tensor_scalar(out=lg[:], in0=lg[:], scalar1=-1.0, scalar2=None, op0=ALU.mult), reads=["lg"], writes=["lg"])
                P.op("act", lambda e: e.activation(out=gch[:], in_=lg[:], func=AF.Exp, scale=128.0), reads=["lg"], writes=["gch"])
                for d_ in range(2):
                    oe, om, ox = (O_E0F, O_MF, O_XF) if d_ == 0 else (O_E0B, O_MB, O_XB)
                    for hd in range(4):
                        col = 4 * d_ + hd
                        P.op("act", lambda e, oe=oe, col=col: e.activation(out=tmpd[:], in_=dk[:, oe:oe + 128], func=AF.Exp, scale=lg[:, col:col + 1]), reads=["dk", "lg"], writes=["tmpd"])
                        P.op("dve", lambda e, om=om, d_=d_, hd=hd: e.tensor_tensor(out=DT[d_][:, hd, :], in0=tmpd[:], in1=dk[:, om:om + 128], op=ALU.mult), reads=["tmpd", "dk"], writes=["DT%d" % d_])
                        for hf in range(2):
                            P.op("act", lambda e, ox=ox, col=col, d_=d_, hd=hd, hf=hf: e.activation(out=xi[d_][:, 2 * hd + hf, :], in_=dk[:, ox:ox + 128], func=AF.Exp, scale=lg[:, col:col + 1]), reads=["dk", "lg"], writes=["xi%d" % d_])
                for z_, oz in enumerate((O_ZF, O_ZB, O_ZO)):
                    for hd in range(4):
                        col = 4 * z_ + hd
                        P.op("act", lambda e, z_=z_, oz=oz, hd=hd, col=col: e.activation(out=zeta[z_][:, hd, :], in_=dk[:, oz:oz + 256], func=AF.Exp, scale=lg[:, col:col + 1]), reads=["dk", "lg"], writes=["zeta%d" % z_])
                P.op("dve", lambda e: e.memset(S[2][:], 0.0), writes=["S2"])
                P2STOP = 9

                def state_head(si, kzt, kzkey, vt, vkey, gcol0, bank, bkey, hd):
                    for hf in range(2):
                        P.op("pe", lambda e, hf=hf: e.matmul(bank[:, hf * 256:(hf + 1) * 256], lhsT=kzt[:, hd * 256 + hf * 128:hd * 256 + hf * 128 + 128], rhs=vt[:, hd * 256:(hd + 1) * 256], start=True, stop=True),
                             reads=[kzkey, vkey], writes=[bkey])
                    sview = S[si][:, 2 * hd:2 * hd + 2, :].rearrange("p a b -> p (a b)")
                    P.op("dve", lambda e: e.scalar_tensor_tensor(out=sview, in0=sview, scalar=gch[:, gcol0 + hd:gcol0 + hd + 1], in1=bank[:], op0=ALU.mult, op1=ALU.add),
                         reads=["S%d" % si, "gch", bkey], writes=["S%d" % si])

                def state_update(si, kzt, kzkey, vt, vkey, gcol0, bset):
                    for hd in range(4):
                        bank, bkey = ((pst[bset], "pst%d" % bset) if hd % 2 == 0 else (py[bset], "py%d" % bset))
                        state_head(si, kzt, kzkey, vt, vkey, gcol0, bank, bkey, hd)

                def p2a_load(j):
                    sl = j % 2
                    P.op("sp", lambda e: e.dma_start(out=ks[0][sl][:], in_=KOd[j * 128:(j + 1) * 128, :]), writes=["k2_0%d" % sl], dma=True)
                    P.op("sp", lambda e: e.dma_start(out=vs[0][sl][:], in_=VOd[j * 128:(j + 1) * 128, :]), writes=["v2_0%d" % sl], dma=True)
                if P2STOP >= 2:
                    p2a_load(0)
                for j in range(NO if P2STOP >= 2 else 0):
                    if j + 1 < NO:
                        p2a_load(j + 1)
                    sl = j % 2
                    bs_ = j % 2
                    P.op("pool", lambda e, sl=sl, bs_=bs_: e.tensor_tensor(out=kz[bs_][:], in0=ks[0][sl][:], in1=zeta[2][:].rearrange("p a b -> p (a b)"), op=ALU.mult), reads=["k2_0%d" % sl, "zeta2"], writes=["kz%d" % bs_])
                    state_update(2, kz[bs_], "kz%d" % bs_, vs[0][sl], "v2_0%d" % sl, 8, bs_)
                P.op("dve", lambda e: e.tensor_scalar(out=S[0][:], in0=S[2][:], scalar1=dk[:, O_MSKF:O_MSKF + 1], scalar2=None, op0=ALU.mult), reads=["S2", "dk"], writes=["S0"])
                P.op("dve", lambda e: e.tensor_scalar(out=S[1][:], in0=S[2][:], scalar1=dk[:, O_MSKB:O_MSKB + 1], scalar2=None, op0=ALU.mult), reads=["S2", "dk"], writes=["S1"])
                for d_ in range(2):
                    P.op("act", lambda e, d_=d_: e.activation(out=Sbf[d_][:], in_=S[d_][:], func=AF.Copy), reads=["S%d" % d_], writes=["Sbf%d" % d_])
                P.flush()

                def p2_load(d_, c, sl):
                    r0 = c * 128
                    P.op("sp", lambda e: e.dma_start(out=qs[d_][sl][:], in_=Qd[r0:r0 + 128, :]), writes=["q2_%d%d" % (d_, sl)], dma=True)
                    P.op("sp", lambda e: e.dma_start(out=ks[d_][sl][:], in_=Kd[r0:r0 + 128, :]), writes=["k2_%d%d" % (d_, sl)], dma=True)
                    P.op("sp", lambda e: e.dma_start(out=vs[d_][sl][:], in_=Vd[r0:r0 + 128, :]), writes=["v2_%d%d" % (d_, sl)], dma=True)

                def p2_stages(d_, c, sl):
                    q_, k_, v_ = qs[d_][sl], ks[d_][sl], vs[d_][sl]
                    qk_, kk_, vk_ = "q2_%d%d" % (d_, sl), "k2_%d%d" % (d_, sl), "v2_%d%d" % (d_, sl)
                    ds = str(d_)
                    T_, kT_ = pTt[d_], "pTt" + ds

                    def s0():
                        transposes8(q_, qk_, T_, kT_)
                        P.op("act", lambda e: e.activation(out=qT[d_][:], in_=T_[:], func=AF.Copy), reads=[kT_], writes=["qT" + ds])
                        P.op("dve", lambda e: e.tensor_tensor(out=qxT[d_][:], in0=qT[d_][:], in1=xi[d_][:].rearrange("p a b -> p (a b)"), op=ALU.mult), reads=["qT" + ds, "xi" + ds], writes=["qxT" + ds])

                    def s1():
                        transposes8(k_, kk_, T_, kT_)
                        P.op("act", lambda e: e.activation(out=kT[d_][:], in_=T_[:], func=AF.Copy), reads=[kT_], writes=["kT" + ds])
                        P.op("pool", lambda e: e.tensor_tensor(out=kz[d_][:], in0=k_[:], in1=zeta[d_][:].rearrange("p a b -> p (a b)"), op=ALU.mult), reads=[kk_, "zeta" + ds], writes=["kz" + ds])

                    def s2():
                        for hd in range(4):
                            for hf in range(2):
                                m = 2 * hd + hf
                                P.op("pe", lambda e, hd=hd, hf=hf, m=m: e.matmul(pS[d_][:, hd * 128:(hd + 1) * 128], lhsT=kT[d_][:, m * 128:(m + 1) * 128], rhs=qT[d_][:, m * 128:(m + 1) * 128], start=(hf == 0), stop=(hf == 1)),
                                     reads=["kT" + ds, "qT" + ds], writes=["pS" + ds])
                        P.op("dve", lambda e: e.tensor_tensor(out=PT[d_][:], in0=pS[d_][:], in1=DT[d_][:].rearrange("p a b -> p (a b)"), op=ALU.mult), reads=["pS" + ds, "DT" + ds], writes=["PT" + ds])

                    def ystage(half):
                        bank, bkey = py[d_], "py" + ds
                        for hd in (2 * half, 2 * half + 1):
                            co = (hd % 2) * 256
                            P.op("pe", lambda e, hd=hd, co=co: e.matmul(bank[:, co:co + 256], lhsT=PT[d_][:, hd * 128:(hd + 1) * 128], rhs=v_[:, hd * 256:(hd + 1) * 256], start=True, stop=False),
                                 reads=["PT" + ds, vk_], writes=[bkey])
                            for hf in range(2):
                                m = 2 * hd + hf
                                P.op("pe", lambda e, m=m, hf=hf, co=co: e.matmul(bank[:, co:co + 256], lhsT=qxT[d_][:, m * 128:(m + 1) * 128], rhs=Sbf[d_][:, m, :], start=False, stop=(hf == 1)),
                                     reads=["qxT" + ds, "Sbf" + ds], writes=[bkey])
                        if half == 0:
                            P.op("act", lambda e: e.activation(out=ysb[d_][:, 0:512], in_=bank[:], func=AF.Copy), reads=[bkey], writes=["ysb" + ds])
                        else:
                            P.op("dve", lambda e: e.tensor_copy(out=ysb[d_][:, 512:1024], in_=bank[:]), reads=[bkey], writes=["ysb" + ds])
                            yd = YFd if d_ == 0 else YBd
                            P.op("sp", lambda e: e.dma_start(out=yd[c * 128:(c + 1) * 128, :], in_=ysb[d_][:]), reads=["ysb" + ds], writes=["dram_y" + ds], dma=True)

                    def sh(hd):
                        bank, bkey = ((pst[d_], "pst" + ds) if hd % 2 == 0 else (py[d_], "py" + ds))
                        state_head(d_, kz[d_], "kz" + ds, v_, vk_, 4 * d_, bank, bkey, hd)
                        if hd % 2 == 1:
                            hh = hd // 2
                            P.op("act", lambda e: e.activation(out=Sbf[d_][:, 4 * hh:4 * hh + 4, :], in_=S[d_][:, 4 * hh:4 * hh + 4, :], func=AF.Copy), reads=["S" + ds], writes=["Sbf" + ds])

                    return [s0, s1, s2, lambda: ystage(0), lambda: ystage(1), lambda: sh(0), lambda: sh(1), lambda: sh(2), lambda: sh(3)]

                if P2STOP >= 3:
                    p2_load(0, 0, 0)
                    p2_load(1, NM - 1, 0)
                for t in range(NM if P2STOP >= 3 else 0):
                    sl = t % 2
                    if t + 1 < NM:
                        p2_load(0, t + 1, 1 - sl)
                        p2_load(1, NM - 2 - t, 1 - sl)
                    sf_ = p2_stages(0, t, sl)
                    sb_ = p2_stages(1, NM - 1 - t, sl)
                    for a_, b_ in zip(sf_, sb_):
                        a_()
                        b_()
                P.flush()

        if "p3" in phases:
            with contextlib.ExitStack() as es:
                sb = lambda n, s, d: es.enter_context(nc.sbuf_tensor(n, s, d))
                ps = lambda n, s, d: es.enter_context(nc.psum_tensor(n, s, d))
                ZL = sb("ZL", [128, 2, 128, 128], BF16)
                T = sb("Tt", [128, 256, 128], BF16)
                mts = sb("mts", [128, 128, 2, 64], BF16)
                ufs = sb("ufs", [128, 64, 128], BF16)
                fcs = sb("fcs_s", [128, 2, 256], BF16)
                pb = [ps("pb%d" % i, [128, 512], F32) for i in range(4)]
                pc = [ps("pc%d" % i, [128, 512], F32) for i in range(4)]
                P.op("sp", lambda e: e.dma_start(out=fcs[:], in_=fcs_d[:, :, :]), writes=["fcs"], dma=True)
                for dq in range(4):
                    P.op("sp", lambda e, dq=dq: e.dma_start(out=mts[:, dq * 32:(dq + 1) * 32, :, :], in_=mt_d[:, dq * 32:(dq + 1) * 32, :, :]), writes=["mts"], dma=True)
                for s in range(8):
                    for ri in range(2):
                        for hb in range(2):
                            P.op("sp", lambda e, s=s, ri=ri, hb=hb: e.dma_start(
                                out=ZL[:, ri, hb * 64:(hb + 1) * 64, :],
                                in_=Zd[s // 2, ri, s % 2, :, :].rearrange("(l b) c -> l b c", b=128)[:, hb * 64:(hb + 1) * 64, :]),
                                reads=["Zd"], writes=["ZL"], dma=True)
                    for cp in range(64):
                        bank, bkey = pb[cp % 4], "pb%d" % (cp % 4)
                        for q_ in range(2):
                            ch = 2 * cp + q_
                            P.op("pe", lambda e, ch=ch, q_=q_, bank=bank: e.matmul(bank[:, q_ * 256:(q_ + 1) * 256], lhsT=ZL[:, 0, :, ch], rhs=fcs[:, 0, :], start=True, stop=False),
                                 reads=["ZL", "fcs"], writes=[bkey])
                            P.op("pe", lambda e, ch=ch, q_=q_, bank=bank: e.matmul(bank[:, q_ * 256:(q_ + 1) * 256], lhsT=ZL[:, 1, :, ch], rhs=fcs[:, 1, :], start=False, stop=True),
                                 reads=["ZL", "fcs"], writes=[bkey])
                        tv = T[:, :, 2 * cp:2 * cp + 2]
                        bv = bank[:].rearrange("p (q d) -> p d q", q=2)
                        if cp % 2 == 0:
                            P.op("act", lambda e, bv=bv, tv=tv: e.activation(out=tv, in_=bv, func=AF.Copy), reads=[bkey], writes=["Tt"])
                        else:
                            P.op("dve", lambda e, bv=bv, tv=tv: e.tensor_copy(out=tv, in_=bv), reads=[bkey], writes=["Tt"])
                    for db in range(16):
                        bank, bkey = pc[db % 4], "pc%d" % (db % 4)
                        for dd in range(8):
                            d_ = db * 8 + dd
                            P.op("pe", lambda e, d_=d_, dd=dd, bank=bank: e.matmul(bank[:, dd * 64:(dd + 1) * 64], lhsT=T[:, d_, :], rhs=mts[:, d_, 0, :], start=True, stop=False),
                                 reads=["Tt", "mts"], writes=[bkey])
                            P.op("pe", lambda e, d_=d_, dd=dd, bank=bank: e.matmul(bank[:, dd * 64:(dd + 1) * 64], lhsT=T[:, 128 + d_, :], rhs=mts[:, d_, 1, :], start=False, stop=True),
                                 reads=["Tt", "mts"], writes=[bkey])
                        ov = ufs[:, :, db * 8:(db + 1) * 8]
                        iv = bank[:].rearrange("p (dd c) -> p c dd", dd=8)
                        if db % 2 == 0:
                            P.op("act", lambda e, ov=ov, iv=iv: e.activation(out=ov, in_=iv, func=AF.Copy, scale=1.0 / 2048.0), reads=[bkey], writes=["ufs"])
                        else:
                            P.op("dve", lambda e, ov=ov, iv=iv: e.tensor_scalar(out=ov, in0=iv, scalar1=1.0 / 2048.0, scalar2=None, op0=ALU.mult), reads=[bkey], writes=["ufs"])
                    P.op("sp", lambda e, s=s: e.dma_start(out=UFd[s * 128:(s + 1) * 128, :], in_=ufs[:].rearrange("p c d -> p (c d)")), reads=["ufs"], writes=["UFd"], dma=True)
                P.flush()

        if "p4" in phases or "p4a" in phases:
            with contextlib.ExitStack() as es:
                sb = lambda n, s, d: es.enter_context(nc.sbuf_tensor(n, s, d))
                ps = lambda n, s, d: es.enter_context(nc.psum_tensor(n, s, d))
                wck = sb("wck", [128, 8, D], BF16)
                wcv = sb("wcv", [128, 8, D], BF16)
                nmem = sb("nmem_s", [128, D], F32)
                ms = [sb("ms%d" % i, [128, D], F32) for i in range(2)]
                st = sb("st0", [128, 8], F32)
                mn = sb("mn", [128, D], BF16)
                mnT = sb("mnT", [128, 8, NMEM], BF16)
                pT = ps("pT0", [128, D], BF16)
                pm = [ps("pm0_%d" % i, [128, 512], F32) for i in range(4)]
                for wsb, wd, nm_ in ((wck, w_ck, "wck"), (wcv, w_cv, "wcv")):
                    for kq in range(4):
                        P.op("pool", lambda e, wsb=wsb, wd=wd, kq=kq: e.dma_start(out=wsb[:, 2 * kq:2 * kq + 2, :], in_=wd[kq * 256:(kq + 1) * 256, :].rearrange("(k p) n -> p k n", p=128)), writes=[nm_], dma=True)
                P.op("sp", lambda e: e.dma_start(out=nmem[:], in_=nmem_d[:, :]), writes=["nmem"], dma=True)
                for t in range(2):
                    P.op("sp", lambda e, t=t: e.dma_start(out=ms[t][:], in_=mem[t * 128:(t + 1) * 128, :]), writes=["ms%d" % t], dma=True)
                for t in range(2):
                    rstd_ops(st, "st0", ms[t][:], "ms%d" % t, mn[:], "mn")
                    P.op("dve", lambda e, t=t: e.scalar_tensor_tensor(out=mn[:], in0=ms[t][:], scalar=st[:, 2:3], in1=nmem[:], op0=ALU.mult, op1=ALU.mult), reads=["ms%d" % t, "st0", "nmem"], writes=["mn"])
                    transposes8(mn, "mn", pT, "pT0")
                    P.op("dve", lambda e, t=t: e.tensor_copy(out=mnT[:, :, t * 128:(t + 1) * 128], in_=pT[:].rearrange("p (k c) -> p k c", k=8)), reads=["pT0"], writes=["mnT"])
                bi0 = 0
                for m in range(8):
                    bank, bkey = pm[bi0 % 4], "pm0_%d" % (bi0 % 4)
                    bi0 += 1
                    for k in range(8):
                        P.op("pe", lambda e, m=m, k=k, bank=bank: e.matmul(bank[:, 0:NMEM], lhsT=wck[:, k, m * 128:(m + 1) * 128], rhs=mnT[:, k, :], start=(k == 0), stop=(k == 7)), reads=["wck", "mnT"], writes=[bkey])
                    P.op("act", lambda e, m=m, bank=bank: e.activation(out=ckT[:, m, :], in_=bank[:, 0:NMEM], func=AF.Copy), reads=[bkey], writes=["ckT"])
                for t in range(2):
                    for n in range(2):
                        bank, bkey = pm[bi0 % 4], "pm0_%d" % (bi0 % 4)
                        bi0 += 1
                        for k in range(8):
                            P.op("pe", lambda e, t=t, n=n, k=k, bank=bank: e.matmul(bank[:], lhsT=mnT[:, k, t * 128:(t + 1) * 128], rhs=wcv[:, k, n * 512:(n + 1) * 512], start=(k == 0), stop=(k == 7)), reads=["wcv", "mnT"], writes=[bkey])
                        P.op("dve", lambda e, t=t, n=n, bank=bank: e.tensor_copy(out=cv[:, t, n * 512:(n + 1) * 512], in_=bank[:]), reads=[bkey], writes=["cv"])
                P.flush()

            with contextlib.ExitStack() as es:
                sb = lambda n, s, d: es.enter_context(nc.sbuf_tensor(n, s, d))
                ps = lambda n, s, d: es.enter_context(nc.psum_tensor(n, s, d))
                wts = {}
                for nm_, wd in (("wro", w_ro), ("w4", w_4), ("wmx", w_mx), ("wcq", w_cq), ("wco", w_co)):
                    wts[nm_] = sb(nm_, [128, 8, D], BF16)
                    for kq in range(4):
                        P.op("pool", lambda e, nm_=nm_, wd=wd, kq=kq: e.dma_start(out=wts[nm_][:, 2 * kq:2 * kq + 2, :], in_=wd[kq * 256:(kq + 1) * 256, :].rearrange("(k p) n -> p k n", p=128)), writes=[nm_], dma=True)
                gnw = sb("gnw_s", [128, D], F32)
                nca = sb("nca_s", [128, D], F32)
                P.op("sp", lambda e: e.dma_start(out=gnw[:], in_=gnw_d[:, :]), writes=["gnw"], dma=True)
                P.op("sp", lambda e: e.dma_start(out=nca[:], in_=nca_d[:, :]), writes=["nca"], dma=True)
                yf = [sb("yf%d" % i, [128, D], F32) for i in range(2)]
                yb = [sb("yb%d" % i, [128, D], F32) for i in range(2)]
                xt = [sb("xt%d" % i, [128, D], F32) for i in range(2)]
                gt = [sb("gt%d" % i, [128, D], BF16) for i in range(2)]
                grt = [sb("grt%d" % i, [128, D], BF16) for i in range(2)]
                gft = [sb("gft%d" % i, [128, D], BF16) for i in range(2)]
                uft = [sb("uft%d" % i, [128, 8, 128], BF16) for i in range(2)]
                SETS = []
                for i in range(2):
                    SETS.append(dict(
                        A=sb("A4_%d" % i, [128, D], F32), B=sb("B4_%d" % i, [128, D], F32), C=sb("C4_%d" % i, [128, D], F32),
                        x1=sb("x1_%d" % i, [128, D], F32), x2=sb("x2_%d" % i, [128, D], F32),
                        b=[sb("b4_%d_%d" % (i, j), [128, D], BF16) for j in range(4)],
                        bs=sb("bs4_%d" % i, [128, 4, 6], F32), mv=sb("mv4_%d" % i, [128, 4, 2], F32),
                        sm=sb("sm4_%d" % i, [128, 16], F32), st=sb("st4_%d" % i, [128, 8], F32),
                        T=ps("pT4_%d" % i, [128, D], BF16), X=[ps("pX%d_%d" % (j, i), [128, 512], F32) for j in range(2)],
                        Y=ps("pY_%d" % i, [128, 512], F32)))
                UFv = UFd.rearrange("(s p) t -> p s t", p=128)

                def p4a_load(c):
                    sl = c % 2
                    r0 = c * 128
                    for tl, dd, nm_ in ((yf, YFd, "yf"), (yb, YBd, "yb"), (gt, Gd, "gt")):
                        P.op("sp", lambda e, tl=tl, dd=dd: e.dma_start(out=tl[sl][:], in_=dd[r0:r0 + 128, :]), writes=["%s%d" % (nm_, sl)], dma=True)
                    P.op("sp", lambda e: e.dma_start(out=uft[sl][:], in_=UFv[:, :, r0:r0 + 128]), writes=["uft%d" % sl], dma=True)
                    for tl, dd, nm_ in ((grt, GRd, "grt"), (gft, GFd, "gft"), (xt, xm, "xt")):
                        P.op("sp", lambda e, tl=tl, dd=dd: e.dma_start(out=tl[sl][:], in_=dd[r0:r0 + 128, :]), writes=["%s%d" % (nm_, sl)], dma=True)

                def p4a_stages(c):
                    i = c % 2
                    sl = i
                    S_ = SETS[i]
                    A, B, Cc, x1, x2, bb, bs, mv, sm, st, T_, X, Y = (S_[k_] for k_ in ("A", "B", "C", "x1", "x2", "b", "bs", "mv", "sm", "st", "T", "X", "Y"))
                    s_ = str(i)
                    kA, kB, kC, kx1, kx2, kbs, kmv, ksm, kst, kT, kY = ("A4_" + s_, "B4_" + s_, "C4_" + s_, "x1_" + s_, "x2_" + s_, "bs4_" + s_, "mv4_" + s_, "sm4_" + s_, "st4_" + s_, "pT4_" + s_, "pY_" + s_)
                    kX = ["pX0_" + s_, "pX1_" + s_]
                    kb = ["b4_%s_%d" % (s_, j) for j in range(4)]
                    r0 = c * 128

                    def mm16(lhs_fn, lkey, w, wkey):
                        for n in range(2):
                            for k in range(8):
                                P.op("pe", lambda e, n=n, k=k: e.matmul(X[n][:], lhsT=lhs_fn(k), rhs=w[:, k, n * 512:(n + 1) * 512], start=(k == 0), stop=(k == 7)),
                                     reads=[lkey, wkey], writes=[kX[n]])

                    def tr(src, skey, dst, dkey, eng):
                        for k in range(8):
                            P.op("pe", lambda e, k=k: e.transpose(out=T_[:, k * 128:(k + 1) * 128], in_=src[:, k * 128:(k + 1) * 128], identity=ident[:]), reads=[skey, "ident"], writes=[kT])
                        if eng == "act":
                            P.op("act", lambda e: e.activation(out=dst[:], in_=T_[:], func=AF.Copy), reads=[kT], writes=[dkey])
                        else:
                            P.op("dve", lambda e: e.tensor_copy(out=dst[:], in_=T_[:]), reads=[kT], writes=[dkey])

                    def dve(fn, r, w):
                        P.op("dve", fn, reads=r, writes=w)

                    def act(fn, r, w):
                        P.op("act", fn, reads=r, writes=w)

                    def i_gn1a():
                        dve(lambda e: e.tensor_tensor(out=A[:], in0=yf[sl][:], in1=yb[sl][:], op=ALU.add), ["yf" + s_, "yb" + s_], [kA])
                        for hd in range(4):
                            dve(lambda e, hd=hd: e.bn_stats(out=bs[:, hd, :], in_=A[:, hd * 256:(hd + 1) * 256]), [kA], [kbs])
                            dve(lambda e, hd=hd: e.bn_aggr(out=mv[:, hd, :], in_=bs[:, hd, :]), [kbs], [kmv])
                        dve(lambda e: e.tensor_scalar(out=sm[:, 0:4], in0=mv[:, :, 1], scalar1=eps5, scalar2=None, op0=ALU.add), [kmv, "dk"], [ksm])
                        act(lambda e: e.activation(out=Cc[:], in_=gt[sl][:], func=AF.Sigmoid), ["gt" + s_], [kC])

                    def i_gn1b():
                        act(lambda e: e.activation(out=sm[:, 0:4], in_=sm[:, 0:4], func=AF.Ln), [ksm], [ksm])
                        act(lambda e: e.activation(out=sm[:, 4:8], in_=sm[:, 0:4], func=AF.Exp, scale=-0.5), [ksm], [ksm])

                    def i_gn2():
                        for hd in range(4):
                            dve(lambda e, hd=hd: e.tensor_scalar(out=B[:, hd * 256:(hd + 1) * 256], in0=A[:, hd * 256:(hd + 1) * 256], scalar1=mv[:, hd, 0:1], scalar2=sm[:, 4 + hd:5 + hd], op0=ALU.subtract, op1=ALU.mult),
                                [kA, kmv, ksm], [kB])
                        dve(lambda e: e.tensor_tensor(out=Cc[:], in0=Cc[:], in1=gt[sl][:], op=ALU.mult), [kC, "gt" + s_], [kC])
                        dve(lambda e: e.tensor_tensor(out=B[:], in0=B[:], in1=gnw[:], op=ALU.mult), [kB, "gnw"], [kB])
                        dve(lambda e: e.tensor_tensor(out=bb[0][:], in0=B[:], in1=Cc[:], op=ALU.mult), [kB, kC], [kb[0]])

                    def trp(src, skey):
                        for k in range(8):
                            P.op("pe", lambda e, k=k: e.transpose(out=T_[:, k * 128:(k + 1) * 128], in_=src[:, k * 128:(k + 1) * 128], identity=ident[:]), reads=[skey, "ident"], writes=[kT])

                    def i_tr_r():
                        trp(bb[0], kb[0])
                        act(lambda e: e.activation(out=A[:], in_=grt[sl][:], func=AF.Sigmoid), ["grt" + s_], [kA])
                        act(lambda e: e.activation(out=B[:], in_=gft[sl][:], func=AF.Sigmoid), ["gft" + s_], [kB])

                    def i_tr_r_ev():
                        act(lambda e: e.activation(out=bb[1][:], in_=T_[:], func=AF.Copy), [kT], [kb[1]])

                    def i_ret():
                        mm16(lambda k: bb[1][:, k * 128:(k + 1) * 128], kb[1], wts["wro"], "wro")
                        for k in range(8):
                            P.op("pe", lambda e, k=k: e.matmul(Y[:], lhsT=uft[sl][:, k, :], rhs=wts["w4"][:, k, 0:512], start=(k == 0), stop=(k == 7)), reads=["uft" + s_, "w4"], writes=[kY])

                    def i_m_a():
                        dve(lambda e: e.tensor_tensor(out=A[:, 0:512], in0=X[0][:], in1=A[:, 0:512], op=ALU.mult), [kX[0], kA], [kA])
                        dve(lambda e: e.tensor_tensor(out=B[:, 0:512], in0=Y[:], in1=B[:, 0:512], op=ALU.mult), [kY, kB], [kB])
                        dve(lambda e: e.tensor_tensor(out=A[:, 512:1024], in0=X[1][:], in1=A[:, 512:1024], op=ALU.mult), [kX[1], kA], [kA])

                    def i_four1():
                        for k in range(8):
                            P.op("pe", lambda e, k=k: e.matmul(X[0][:], lhsT=uft[sl][:, k, :], rhs=wts["w4"][:, k, 512:1024], start=(k == 0), stop=(k == 7)), reads=["uft" + s_, "w4"], writes=[kX[0]])

                    def i_m_b():
                        dve(lambda e: e.tensor_tensor(out=B[:, 512:1024], in0=X[0][:], in1=B[:, 512:1024], op=ALU.mult), [kX[0], kB], [kB])
                        dve(lambda e: e.tensor_tensor(out=bb[2][:], in0=A[:], in1=B[:], op=ALU.add), [kA, kB], [kb[2]])

                    def i_tr_m():
                        trp(bb[2], kb[2])

                    def i_tr_m_ev():
                        act(lambda e: e.activation(out=bb[3][:], in_=T_[:], func=AF.Copy), [kT], [kb[3]])

                    def i_mix():
                        mm16(lambda k: bb[3][:, k * 128:(k + 1) * 128], kb[3], wts["wmx"], "wmx")

                    def i_x1():
                        for n in range(2):
                            cs_ = slice(n * 512, (n + 1) * 512)
                            dve(lambda e, n=n, cs_=cs_: e.tensor_tensor(out=x1[:, cs_], in0=X[n][:], in1=xt[sl][:, cs_], op=ALU.add), [kX[n], "xt" + s_], [kx1])
                        dve(lambda e: e.memset(st[:], 0.0), [], [kst])

                    def i_n_a():
                        act(lambda e: e.activation(out=Cc[:], in_=x1[:], func=AF.Square, accum_out=st[:, 0:1]), [kx1, kst], [kC, kst])

                    def i_n_b():
                        dve(lambda e: e.tensor_scalar(out=st[:, 1:2], in0=st[:, 0:1], scalar1=1.0 / D, scalar2=eps6, op0=ALU.mult, op1=ALU.add), [kst, "dk"], [kst])

                    def i_n_c():
                        act(lambda e: e.activation(out=st[:, 3:4], in_=st[:, 1:2], func=AF.Ln), [kst], [kst])
                        act(lambda e: e.activation(out=st[:, 2:3], in_=st[:, 3:4], func=AF.Exp, scale=-0.5), [kst], [kst])

                    def i_n_d():
                        dve(lambda e: e.scalar_tensor_tensor(out=bb[0][:], in0=x1[:], scalar=st[:, 2:3], in1=nca[:], op0=ALU.mult, op1=ALU.mult), [kx1, kst, "nca"], [kb[0]])

                    def i_tr_x():
                        trp(bb[0], kb[0])

                    def i_tr_x_ev():
                        dve(lambda e: e.tensor_copy(out=bb[1][:], in_=T_[:]), [kT], [kb[1]])

                    def i_hq():
                        for m in range(8):
                            bank, bkey = X[m // 4], kX[m // 4]
                            co = (m % 4) * 128
                            for k in range(8):
                                P.op("pe", lambda e, m=m, k=k, bank=bank, co=co: e.matmul(bank[:, co:co + 128], lhsT=wts["wcq"][:, k, m * 128:(m + 1) * 128], rhs=bb[1][:, k * 128:(k + 1) * 128], start=(k == 0), stop=(k == 7)),
                                     reads=["wcq", kb[1]], writes=[bkey])

                    def i_hq_ev():
                        act(lambda e: e.activation(out=bb[2][:, 0:512], in_=X[0][:], func=AF.Copy), [kX[0]], [kb[2]])
                        dve(lambda e: e.tensor_copy(out=bb[2][:, 512:1024], in_=X[1][:]), [kX[1]], [kb[2]])

                    def i_logits():
                        for hd in range(4):
                            bank, bkey = X[hd // 2], kX[hd // 2]
                            co = (hd % 2) * 256
                            for hf in range(2):
                                m = 2 * hd + hf
                                P.op("pe", lambda e, m=m, hf=hf, bank=bank, co=co: e.matmul(bank[:, co:co + 256], lhsT=bb[2][:, m * 128:(m + 1) * 128], rhs=ckT[:, m, :], start=(hf == 0), stop=(hf == 1)),
                                     reads=[kb[2], "ckT"], writes=[bkey])

                    def i_sm_a():
                        dve(lambda e: e.memset(sm[:, 8:16], 0.0), [], [ksm])
                        for n in range(2):
                            dve(lambda e, n=n: e.tensor_reduce(out=sm[:, 2 * n:2 * n + 2], in_=X[n][:].rearrange("p (a b) -> p a b", a=2), axis=AX.X, op=ALU.max), [kX[n], ksm], [ksm])
                        dve(lambda e: e.tensor_scalar(out=sm[:, 4:8], in0=sm[:, 0:4], scalar1=-1.0 / 16.0, scalar2=None, op0=ALU.mult), [ksm], [ksm])

                    def i_sm_b():
                        for hd in range(4):
                            bank, bkey = X[hd // 2], kX[hd // 2]
                            co = (hd % 2) * 256
                            act(lambda e, hd=hd, bank=bank, co=co: e.activation(out=Cc[:, hd * 256:(hd + 1) * 256], in_=bank[:, co:co + 256], func=AF.Exp, scale=1.0 / 16.0, bias=sm[:, 4 + hd:5 + hd], accum_out=sm[:, 8 + hd:9 + hd]),
                                [bkey, ksm], [kC, ksm])

                    def i_sm_c():
                        dve(lambda e: e.reciprocal(out=sm[:, 12:16], in_=sm[:, 8:12]), [ksm], [ksm])
                        for hd in range(4):
                            dve(lambda e, hd=hd: e.tensor_scalar(out=bb[3][:, hd * 256:(hd + 1) * 256], in0=Cc[:, hd * 256:(hd + 1) * 256], scalar1=sm[:, 12 + hd:13 + hd], scalar2=None, op0=ALU.mult),
                                [kC, ksm], [kb[3]])

                    def i_tr_p():
                        trp(bb[3], kb[3])

                    def i_tr_p_ev():
                        act(lambda e: e.activation(out=bb[0][:], in_=T_[:], func=AF.Copy), [kT], [kb[0]])

                    def i_att():
                        for m in range(8):
                            hd = m // 2
                            bank, bkey = X[m // 4], kX[m // 4]
                            co = (m % 4) * 128
                            for mc in range(2):
                                P.op("pe", lambda e, m=m, mc=mc, hd=hd, bank=bank, co=co: e.matmul(bank[:, co:co + 128], lhsT=cv[:, mc, m * 128:(m + 1) * 128], rhs=bb[0][:, (2 * hd + mc) * 128:(2 * hd + mc + 1) * 128], start=(mc == 0), stop=(mc == 1)),
                                     reads=["cv", kb[0]], writes=[bkey])

                    def i_att_ev():
                        act(lambda e: e.activation(out=bb[1][:, 0:512], in_=X[0][:], func=AF.Copy), [kX[0]], [kb[1]])
                        dve(lambda e: e.tensor_copy(out=bb[1][:, 512:1024], in_=X[1][:]), [kX[1]], [kb[1]])

                    def i_co():
                        mm16(lambda k: bb[1][:, k * 128:(k + 1) * 128], kb[1], wts["wco"], "wco")

                    def i_x2():
                        for n in range(2):
                            cs_ = slice(n * 512, (n + 1) * 512)
                            dve(lambda e, n=n, cs_=cs_: e.tensor_tensor(out=x2[:, cs_], in0=X[n][:], in1=x1[:, cs_], op=ALU.add), [kX[n], kx1], [kx2])
                        P.op("sp", lambda e: e.dma_start(out=X2d[r0:r0 + 128, :], in_=x2[:]), reads=[kx2], writes=["X2d"], dma=True)

                    def i_nop():
                        pass

                    return [i_gn1a, i_gn1b, i_gn2, i_tr_r, i_tr_r_ev, i_ret, i_m_a, i_four1, i_m_b, i_tr_m, i_tr_m_ev, i_mix, i_x1, i_n_a, i_n_b, i_n_c,
                            i_n_d, i_tr_x, i_tr_x_ev, i_hq, i_hq_ev, i_logits, i_sm_a, i_sm_b, i_sm_c, i_tr_p, i_tr_p_ev, i_att, i_att_ev, i_co, i_x2, i_nop]

                NST = 32
                SK = NST // 2
                p4a_load(0)
                active = []
                nxt = 0
                tick = 0
                while nxt < NM or active:
                    admit = (tick % NST == 0) or (tick % NST == SK)
                    if nxt < NM and admit:
                        if nxt + 1 < NM:
                            p4a_load(nxt + 1)
                        active.append([p4a_stages(nxt), 0])
                        nxt += 1
                    for a_ in active:
                        a_[0][a_[1]]()
                        a_[1] += 1
                    active = [a_ for a_ in active if a_[1] < NST]
                    tick += 1
                P.flush()

        if "p4" in phases or "p4b" in phases:
            with contextlib.ExitStack() as es:
                sb = lambda n, s, d: es.enter_context(nc.sbuf_tensor(n, s, d))
                ps = lambda n, s, d: es.enter_context(nc.psum_tensor(n, s, d))
                wup = sb("wup", [128, 8, DFF], BF16)
                wdn = sb("wdn", [128, 32, D], BF16)
                for k in range(8):
                    P.op("pool", lambda e, k=k: e.dma_start(out=wup[:, k, :], in_=w_up[k * 128:(k + 1) * 128, :]), writes=["wup"], dma=True)
                for kq in range(8):
                    P.op("pool", lambda e, kq=kq: e.dma_start(out=wdn[:, 4 * kq:4 * kq + 4, :], in_=w_dn[kq * 512:(kq + 1) * 512, :].rearrange("(k p) n -> p k n", p=128)), writes=["wdn"], dma=True)
                nmlp = sb("nmlp_s", [128, D], F32)
                nfin = sb("nfin_s", [128, D], F32)
                P.op("sp", lambda e: e.dma_start(out=nmlp[:], in_=nmlp_d[:, :]), writes=["nmlp"], dma=True)
                P.op("sp", lambda e: e.dma_start(out=nfin[:], in_=nfin_d[:, :]), writes=["nfin"], dma=True)
                xg = [sb("xg%d" % i, [128, 2, D], F32) for i in range(2)]
                xn = [sb("xn5_%d" % i, [128, D], BF16) for i in range(2)]
                xnT = [sb("xnT5_%d" % i, [128, 8, 256], BF16) for i in range(2)]
                hT = sb("hT", [128, 32, 256], BF16)
                sq = [sb("sq%d" % i, [128, 256], F32) for i in range(2)]
                ot = [sb("ot%d" % i, [128, D], F32) for i in range(2)]
                st = [sb("st5_%d" % i, [128, 8], F32) for i in range(3)]
                pT = ps("pT5", [128, D], BF16)
                pu = [ps("pu%d" % i, [128, 512], F32) for i in range(5)]
                pd = [ps("pd%d" % i, [128, 512], F32) for i in range(2)]

                def p4b_load(gi):
                    sl = gi % 2
                    P.op("sp", lambda e: e.dma_start(out=xg[sl][:], in_=X2d[gi * 256:(gi + 1) * 256, :].rearrange("(t p) d -> p t d", p=128)), writes=["xg%d" % sl], dma=True)

                oc = [0]

                def p4b_norm_stages(gi):
                    sl = gi % 2
                    gk = "xg%d" % sl
                    out = []
                    for t in range(2):
                        stt_, kst_, xnt_, kxn_ = st[t], "st5_%d" % t, xn[t], "xn5_%d" % t

                        def n0(t=t, stt_=stt_, kst_=kst_, xnt_=xnt_, kxn_=kxn_):
                            P.op("dve", lambda e: e.memset(stt_[:], 0.0), writes=[kst_])
                            P.op("act", lambda e: e.activation(out=xnt_[:], in_=xg[sl][:, t, :], func=AF.Square, accum_out=stt_[:, 0:1]), reads=[gk, kst_], writes=[kxn_, kst_])

                        def n1(stt_=stt_, kst_=kst_):
                            P.op("act", lambda e: e.activation(out=stt_[:, 1:2], in_=stt_[:, 0:1], func=AF.Sqrt, scale=1.0 / D, bias=eps6), reads=[kst_, "dk"], writes=[kst_])

                        def n2(stt_=stt_, kst_=kst_):
                            P.op("dve", lambda e: e.reciprocal(out=stt_[:, 2:3], in_=stt_[:, 1:2]), reads=[kst_], writes=[kst_])

                        def n3(t=t, stt_=stt_, kst_=kst_, xnt_=xnt_, kxn_=kxn_):
                            P.op("dve", lambda e: e.scalar_tensor_tensor(out=xnt_[:], in0=xg[sl][:, t, :], scalar=stt_[:, 2:3], in1=nmlp[:], op0=ALU.mult, op1=ALU.mult), reads=[gk, kst_, "nmlp"], writes=[kxn_])
                        out += [n0, n1, n2, n3]
                    return out

                def p4b_tr(gi):
                    sl = gi % 2
                    for t in range(2):
                        transposes8(xn[t], "xn5_%d" % t, pT, "pT5")
                        P.op("act", lambda e, t=t: e.activation(out=xnT[sl][:, :, t * 128:(t + 1) * 128], in_=pT[:].rearrange("p (k c) -> p k c", k=8), func=AF.Copy), reads=["pT5"], writes=["xnT5_%d" % sl])

                def p4b_up(gi, f0, f1, hooks=None):
                    sl = gi % 2
                    for f in range(f0, f1):
                        if hooks and f in hooks:
                            hooks[f]()
                        bank, bkey = pu[f % 5], "pu%d" % (f % 5)
                        for k in range(8):
                            P.op("pe", lambda e, f=f, k=k, bank=bank: e.matmul(bank[:, 0:256], lhsT=wup[:, k, f * 128:(f + 1) * 128], rhs=xnT[sl][:, k, :], start=(k == 0), stop=(k == 7)), reads=["wup", "xnT5_%d" % sl], writes=[bkey])
                        sqt, sqk = sq[f % 2], "sq%d" % (f % 2)
                        if f % 2 == 0:
                            P.op("act", lambda e, bank=bank, sqt=sqt: e.activation(out=sqt[:], in_=bank[:, 0:256], func=AF.Relu), reads=[bkey], writes=[sqk])
                            P.op("act", lambda e, f=f, sqt=sqt: e.activation(out=hT[:, f, :], in_=sqt[:], func=AF.Square), reads=[sqk], writes=["hT"])
                        else:
                            P.op("dve", lambda e, bank=bank, sqt=sqt: e.tensor_scalar(out=sqt[:], in0=bank[:, 0:256], scalar1=0.0, scalar2=None, op0=ALU.max), reads=[bkey], writes=[sqk])
                            P.op("dve", lambda e, f=f, sqt=sqt: e.tensor_tensor(out=hT[:, f, :], in0=sqt[:], in1=sqt[:], op=ALU.mult), reads=[sqk], writes=["hT"])

                def p4b_down(gi):
                    sl = gi % 2
                    gk = "xg%d" % sl
                    epi = []
                    for t in range(2):
                        for n in range(2):
                            for f in range(32):
                                P.op("pe", lambda e, t=t, n=n, f=f: e.matmul(pd[n][:], lhsT=hT[:, f, t * 128:(t + 1) * 128], rhs=wdn[:, f, n * 512:(n + 1) * 512], start=(f == 0), stop=(f == 31)), reads=["hT", "wdn"], writes=["pd%d" % n])
                            cs_ = slice(n * 512, (n + 1) * 512)
                            P.op("dve", lambda e, t=t, n=n, cs_=cs_: e.tensor_tensor(out=xg[sl][:, t, cs_], in0=pd[n][:], in1=xg[sl][:, t, cs_], op=ALU.add), reads=["pd%d" % n, gk], writes=[gk])

                        def fin(t=t):
                            osl = oc[0] % 2
                            oc[0] += 1
                            ok = "ot%d" % osl
                            rstd_ops(st[2], "st5_2", xg[sl][:, t, :], gk, ot[osl][:], ok)
                            P.op("dve", lambda e: e.scalar_tensor_tensor(out=ot[osl][:], in0=xg[sl][:, t, :], scalar=st[2][:, 2:3], in1=nfin[:], op0=ALU.mult, op1=ALU.mult), reads=[gk, "st5_2", "nfin"], writes=[ok])
                            r0 = gi * 256 + t * 128
                            P.op("sp", lambda e: e.dma_start(out=y_out[r0:r0 + 128, :], in_=ot[osl][:]), reads=[ok], writes=["y_out"], dma=True)
                        fin()
                    return epi

                NG = NM // 2
                p4b_load(0)
                for f_ in p4b_norm_stages(0):
                    f_()
                p4b_tr(0)
                for gi in range(NG):
                    if gi + 1 < NG:
                        p4b_load(gi + 1)
                        ns_ = p4b_norm_stages(gi + 1)
                        hk = {9: ns_[0], 11: ns_[1], 13: ns_[2], 15: ns_[3], 18: ns_[4], 20: ns_[5], 22: ns_[6], 24: ns_[7]}
                    else:
                        hk = None
                    p4b_up(gi, 0, 32, hk)
                    if gi + 1 < NG:
                        p4b_tr(gi + 1)
                    p4b_down(gi)
                P.flush()
        P.flush(final=True)
    return nc, P


def _other_chunks(h):
    return np.arange(0, 64) if h == 1 else np.arange(127, 63, -1)


def _consts(h):
    bf = ml_dtypes.bfloat16
    c = {}
    c["ident"] = np.eye(128, dtype=np.float32).astype(bf)
    inv = (np.float32(10000.0) ** (-(np.arange(0, 256, 2, dtype=np.float32)) / np.float32(256))).astype(np.float32)

    def rot(pos):
        ang = (pos.astype(np.float32)[:, None] * inv[None, :]).astype(np.float32).astype(np.float64)
        return np.concatenate([np.cos(ang), np.sin(ang)], axis=1).astype(np.float32)
    pos_m = h * 8192 + np.arange(8192)
    oc = _other_chunks(h)
    pos_o = (oc[:, None] * 128 + np.arange(128)[None, :]).reshape(-1)
    c["rotm"] = rot(pos_m)
    c["roto"] = rot(pos_o)
    dk = np.zeros((128, DKW), np.float32)
    j = np.arange(128)[:, None].astype(np.float64)
    i = np.arange(128)[None, :].astype(np.float64)
    dk[:, O_E0F:O_E0F + 128] = np.maximum(i - j, 0)
    dk[:, O_MF:O_MF + 128] = (i >= j)
    dk[:, O_E0B:O_E0B + 128] = np.maximum(j - i, 0)
    dk[:, O_MB:O_MB + 128] = (j > i)
    dk[:, O_XF:O_XF + 128] = i + 1
    dk[:, O_XB:O_XB + 128] = 128 - i
    zf = 127 - j
    zb = j
    dk[:, O_ZF:O_ZF + 256] = zf
    dk[:, O_ZB:O_ZB + 256] = zb
    dk[:, O_ZO:O_ZO + 256] = zf if h == 1 else zb
    dk[:, O_MSKF] = 1.0 if h == 1 else 0.0
    dk[:, O_MSKB] = 1.0 if h == 0 else 0.0
    dk[:, O_EPS6] = 1e-6
    dk[:, O_EPS5] = 1e-5
    dk[:, O_ONE] = 1.0
    dk[:, O_NH:O_NH + 8] = -0.5
    c["dk"] = dk
    a = np.concatenate([64 * h + np.arange(64), oc]).astype(np.float64)
    d = np.arange(128, dtype=np.float64)
    th = 2 * np.pi * ((a[:, None] * d[None, :]) % 128) / 128.0
    fcs = np.zeros((128, 2, 256), np.float64)
    fcs[:, 0, :128] = np.cos(th)
    fcs[:, 0, 128:] = -np.sin(th)
    fcs[:, 1, :128] = np.sin(th)
    fcs[:, 1, 128:] = np.cos(th)
    c["fcs"] = fcs.astype(np.float32).astype(bf)
    b = np.arange(128, dtype=np.int64)[:, None, None]
    dd = np.arange(128, dtype=np.int64)[None, :, None]
    cg = (64 * h + np.arange(64, dtype=np.int64))[None, None, :]
    num = (cg * b * 128 + dd * b) % 16384
    th2 = 2 * np.pi * num.astype(np.float64) / 16384.0
    mt = np.zeros((128, 128, 2, 64), np.float64)
    mt[:, :, 0, :] = np.cos(th2)
    mt[:, :, 1, :] = np.sin(th2)
    c["mt"] = mt.astype(np.float32).astype(bf)
    ch = (np.arange(2)[None, :, None] * 128 + np.arange(128)[:, None, None]).astype(np.int64)
    jj = np.arange(256, dtype=np.int64)[None, None, :]
    th3 = 2 * np.pi * ((ch * jj) % 256).astype(np.float64) / 256.0
    cs = np.concatenate([np.cos(th3), -np.sin(th3)], axis=2)
    c["cs"] = cs.astype(np.float32).astype(bf)
    return c


_CACHE = {}


def _rep(v):
    return np.ascontiguousarray(np.broadcast_to(np.asarray(v, np.float32).reshape(1, -1), (128, v.size)))


def make_in_maps(inp):
    seqs = [(inp["x_prompt"][0], inp["mem_prompt"][0]), (inp["x_prompt"][1], inp["mem_prompt"][1]), (inp["x_sample"][0], inp["mem_sample"][0])]
    shared = {
        "w_in": np.ascontiguousarray(inp["w_in"][0]), "w_ro": np.ascontiguousarray(inp["w_ret_out"][0]),
        "w_4": np.ascontiguousarray(inp["w_four_out"][0]), "w_mx": np.ascontiguousarray(inp["w_mix_out"][0]),
        "w_cq": np.ascontiguousarray(inp["w_cq"][0]), "w_ck": np.ascontiguousarray(inp["w_ck"][0]),
        "w_cv": np.ascontiguousarray(inp["w_cv"][0]), "w_co": np.ascontiguousarray(inp["w_co"][0]),
        "w_up": np.ascontiguousarray(inp["w_up"][0]), "w_dn": np.ascontiguousarray(inp["w_down"][0]),
        "nmix": _rep(inp["norm_mix_w"][0]), "nca": _rep(inp["norm_ca_w"][0]), "nmem": _rep(inp["norm_mem_w"][0]),
        "nmlp": _rep(inp["norm_mlp_w"][0]), "nfin": _rep(inp["norm_final_w"]), "gnw": _rep(inp["ret_gn_w"][0]),
    }
    consts = [_consts(0), _consts(1)]
    in_maps = []
    for core in range(8):
        si = min(core // 2, 2)
        h = core % 2
        x, mem = seqs[si]
        x = np.asarray(x, np.float32)
        xm = np.ascontiguousarray(x[h * 8192:(h + 1) * 8192])
        oc = _other_chunks(h)
        xo = np.ascontiguousarray(x.reshape(128, 128, D)[oc].reshape(8192, D))
        df = np.asarray(inp["ret_decay_fwd"][0], np.float32)
        db = np.asarray(inp["ret_decay_bwd"][0], np.float32)
        do = df if h == 1 else db
        m = dict(shared)
        m.update(consts[h])
        m.update({"xm": xm, "xo": xo, "mem": np.ascontiguousarray(np.asarray(mem, np.float32)),
                  "dec": _rep(np.concatenate([df, db, do]))})
        in_maps.append(m)
    return in_maps


def kernel(**inputs):
    inp = {k: np.asarray(v) for k, v in inputs.items()}
    if "nc" not in _CACHE:
        _CACHE["nc"] = build()[0]
    nc = _CACHE["nc"]
    in_maps = make_in_maps(inp)
    res = run_bass_kernel_spmd(nc, in_maps, core_ids=list(range(8)))
    ys = [np.asarray(r["y"], np.float32) for r in res.results]
    y_prompt = np.stack([np.concatenate([ys[0], ys[1]], 0), np.concatenate([ys[2], ys[3]], 0)], 0)
    y_sample = np.concatenate([ys[4], ys[5]], 0)[None]
    return (y_prompt, y_sample)
```
